# Optimizing a Trainium2 kernel written in Bass

```python
import math
import jax
import jax.numpy as jnp
from jax import lax
import numpy as np

D_MODEL = 1024
BATCH = 16
SEQ = 256
DEPTH = 4
DEC_BATCH = 4
DEC_SEQ = 4096
PAST_LEN = 512

GRID_W = 64
ROPE_BASE = 10000.0
Q_BLOCK = 128
EPS = 1e-6
N_GROUPS = 4
D_MIX = D_MODEL
BRANCH = D_MIX // N_GROUPS
MLA_HEADS = 4
MLA_NOPE = 64
MLA_ROPE = 32
MLA_V = BRANCH // MLA_HEADS
MLA_Q_LORA = D_MODEL // 4
MLA_KV_LORA = D_MODEL // 8
MLA_SCALE = (MLA_NOPE + MLA_ROPE) ** -0.5
DIFF_HEADS = 4
DIFF_HD = BRANCH // (2 * DIFF_HEADS)
DIFF_SCALE = DIFF_HD ** -0.5
HGRN_HEADS = 4
HGRN_DK = BRANCH // HGRN_HEADS
HGRN_DV = BRANCH // HGRN_HEADS
HGRN_CHUNK = 64
HY_CH = BRANCH
HY_ORDER = 2
HY_EMB = 33
HY_BANDS = (HY_EMB - 1) // 2
HY_FH = 64
HY_DECAY_TARGET = 0.01
HY_FAST_DECAY = 0.3
HY_SLOW_DECAY = 1.5
IN_SPLITS = (MLA_Q_LORA, MLA_KV_LORA, MLA_ROPE, BRANCH,
             BRANCH, BRANCH, BRANCH, BRANCH,
             HGRN_HEADS * HGRN_DK, HGRN_HEADS * HGRN_DK, HGRN_HEADS * HGRN_DK, HGRN_HEADS * HGRN_DV, BRANCH,
             3 * HY_CH, BRANCH)
IN_COLS = sum(IN_SPLITS)

kernel_name = 'hybrid_diffusion_trunk_step'


def rms_norm(x, g):
    xf = x.astype(jnp.float32)
    y = xf * lax.rsqrt(jnp.mean(xf * xf, axis=-1, keepdims=True) + EPS)
    return (y * g.astype(jnp.float32)).astype(x.dtype)


def _rotate(x, ang):
    n = x.shape[-1] // 2
    shape = (x.shape[1],) + (1,) * (x.ndim - 3) + (n,)
    cos = jnp.cos(ang).reshape(shape).astype(x.dtype)
    sin = jnp.sin(ang).reshape(shape).astype(x.dtype)
    x1, x2 = x[..., :n], x[..., n:]
    return jnp.concatenate([x1 * cos - x2 * sin, x2 * cos + x1 * sin], axis=-1)


def rope2d(x, pos):
    half = x.shape[-1] // 2
    inv = ROPE_BASE ** (-jnp.arange(0, half, 2, dtype=jnp.float32) / half)
    row, col = pos
    return jnp.concatenate([_rotate(x[..., :half], row[:, None] * inv),
                            _rotate(x[..., half:], col[:, None] * inv)], axis=-1)


def blockwise(fn, q):
    B, L = q.shape[:2]
    qb = q.reshape((B, L // Q_BLOCK, Q_BLOCK) + q.shape[2:]).swapaxes(0, 1)
    out = lax.map(fn, qb)
    return out.swapaxes(0, 1).reshape((B, L) + out.shape[3:])


def attend(q, k, v, scale):
    def block(qb):
        s = jnp.einsum('bqhd,bkhd->bhqk', qb, k).astype(jnp.float32) * scale
        p = jax.nn.softmax(s, axis=-1).astype(v.dtype)
        return jnp.einsum('bhqk,bkhe->bqhe', p, v)
    return blockwise(block, q)


def diff_attend(q, k, v, lam):
    def block(qb):
        s = jnp.einsum('bqhjd,bkhjd->bhjqk', qb, k).astype(jnp.float32) * DIFF_SCALE
        p = jax.nn.softmax(s, axis=-1)
        w = (p[:, :, 0] - lam * p[:, :, 1]).astype(v.dtype)
        return jnp.einsum('bhqk,bkhe->bqhe', w, v)
    return blockwise(block, q)


def mla_expand(ckv, krope_h, w_ukv, k_nope_g):
    kv = (ckv @ w_ukv).reshape(ckv.shape[:2] + (MLA_HEADS, MLA_NOPE + MLA_V))
    k_nope = rms_norm(kv[..., :MLA_NOPE], k_nope_g)
    k = jnp.concatenate([k_nope, jnp.broadcast_to(krope_h, k_nope.shape[:-1] + (MLA_ROPE,))], axis=-1)
    return k, kv[..., MLA_NOPE:]


def hgrn_scan(q, k, v, logf, s0):
    B, L, H, _ = q.shape
    nc = L // HGRN_CHUNK

    def chunks(t):
        return t.reshape((B, nc, HGRN_CHUNK) + t.shape[2:]).swapaxes(0, 1)

    mask = jnp.tril(jnp.ones((HGRN_CHUNK, HGRN_CHUNK), bool))[None, :, :, None, None]

    def step(S, inp):
        qc, kc, vc, gc = inp
        b = jnp.cumsum(gc, axis=1)
        diff = jnp.where(mask, b[:, :, None] - b[:, None, :], 0.0)
        decay = jnp.where(mask, jnp.exp(diff), 0.0)
        a = jnp.einsum('bthd,bshd,btshd->bhts', qc, kc, decay)
        o = jnp.einsum('bhts,bshe->bthe', a, vc) + jnp.einsum('bthd,bhde->bthe', qc * jnp.exp(b), S)
        b_last = b[:, -1]
        S = jnp.exp(b_last)[..., None] * S + jnp.einsum('bshd,bshe->bhde', kc * jnp.exp(b_last[:, None] - b), vc)
        return S, o

    S, o = lax.scan(step, s0, (chunks(q), chunks(k), chunks(v), chunks(logf)))
    return o.swapaxes(0, 1).reshape(B, L, H, v.shape[-1]), S


def hgrn_mixer(q_raw, zf, zb, i_raw, lb_f, lb_b, s0, out_g):
    B, L, _ = q_raw.shape

    def heads(t):
        return t.astype(jnp.float32).reshape(B, L, HGRN_HEADS, -1)

    def gates(z, lb):
        z = z.astype(jnp.float32)
        f = lb + (1.0 - lb) * jax.nn.sigmoid(z)
        k = (1.0 - lb) * jax.nn.sigmoid(-z)
        return heads(jnp.log(f)), heads(k)

    q, v = heads(q_raw), heads(i_raw)
    gf, kf = gates(zf, lb_f)
    gb, kb = gates(zb, lb_b)
    s0 = s0.astype(jnp.float32)
    o_f, s_f = hgrn_scan(q, kf, v, gf, s0[:, 0])
    flip = lambda t: jnp.flip(t, axis=1)
    o_b, s_b = hgrn_scan(flip(q), flip(kb), flip(v), flip(gb), s0[:, 1])
    o = rms_norm(o_f + flip(o_b), out_g).reshape(B, L, BRANCH).astype(q_raw.dtype)
    return o, jnp.stack([s_f, s_b], axis=1).astype(q_raw.dtype)


def hyena_filters(L, w1, b1, w2, b2, w3, freq):
    t = jnp.linspace(0.0, 1.0, L, dtype=jnp.float32)[:, None]
    w = 2.0 * math.pi * jnp.arange(L, dtype=jnp.float32) / L
    f = jnp.linspace(1e-4, HY_BANDS - 1, HY_BANDS, dtype=jnp.float32)
    ang = w[:, None] * f[None, :]
    feats = jnp.concatenate([t, jnp.cos(ang), -jnp.sin(ang)], axis=-1)
    h = jnp.sin(freq[0] * (feats @ w1 + b1))
    h = jnp.sin(freq[1] * (h @ w2 + b2))
    h = (h @ w3).astype(jnp.float32).reshape(L, HY_ORDER, 2, HY_CH)
    min_decay = math.log(HY_DECAY_TARGET) / HY_SLOW_DECAY
    max_decay = math.log(HY_DECAY_TARGET) / HY_FAST_DECAY
    deltas = jnp.linspace(min_decay, max_decay, HY_CH, dtype=jnp.float32)
    window = jnp.exp(-t * jnp.abs(deltas))
    return h * window[:, None, None, :]


def long_conv(z, hf, hb):
    L = z.shape[1]
    filt = jnp.concatenate([hf, hb[::-1]], axis=0)
    zf = jnp.fft.rfft(z, n=2 * L, axis=1)
    ff = jnp.fft.rfft(filt, n=2 * L, axis=0)
    return jnp.fft.irfft(zf * ff[None], n=2 * L, axis=1)[:, :L]


def hyena_mixer(u, conv_w, conv_b, filt, bias):
    dtype = u.dtype
    up = jnp.pad(u, ((0, 0), (1, 1), (0, 0)))
    u = up[:, :-2] * conv_w[0] + up[:, 1:-1] * conv_w[1] + up[:, 2:] * conv_w[2] + conv_b
    v, x1, x2 = jnp.split(u.astype(jnp.float32), 3, axis=-1)
    z = v
    for n, xg in enumerate((x1, x2)):
        z = xg * (long_conv(z, filt[:, n, 0], filt[:, n, 1]) + z * bias[n])
    return z.astype(dtype)


def trunk_layer(x, cvec, l, W, ctx, pos):
    B, L, _ = x.shape
    mod = (jax.nn.silu(cvec) @ W['w_mod'][l] + W['b_mod'][l])[:, None, :]
    shift, scale, gate = jnp.split(mod, 3, axis=-1)
    h = rms_norm(x, W['norm_g'][l]) * (1 + scale) + shift
    split_at = np.cumsum(IN_SPLITS)[:-1].tolist()
    (cq, ckv_raw, krope_raw, g_a, dq, dk, dv, g_b,
     hq, hzf, hzb, hi, g_c, hu, g_d) = jnp.split(h @ W['w_in'][l], split_at, axis=-1)

    q = (rms_norm(cq, W['mla_q_norm_g'][l]) @ W['mla_w_uq'][l]).reshape(B, L, MLA_HEADS, MLA_NOPE + MLA_ROPE)
    q_nope = rms_norm(q[..., :MLA_NOPE], W['mla_nope_g'][l, 0])
    q_rope = rms_norm(q[..., MLA_NOPE:], W['mla_rope_g'][l, 0])
    ckv = rms_norm(ckv_raw, W['mla_kv_norm_g'][l])
    krope = rms_norm(krope_raw, W['mla_rope_g'][l, 1])
    krope_h = krope[:, :, None, :]
    if pos is not None:
        q_rope = rope2d(q_rope, pos)
        krope_h = rope2d(krope_h, pos)
    k_a, v_a = mla_expand(ckv, krope_h, W['mla_w_ukv'][l], W['mla_nope_g'][l, 1])
    if ctx is not None:
        k_c, v_c = mla_expand(ctx[0], ctx[1][:, :, None, :], W['mla_w_ukv'][l], W['mla_nope_g'][l, 1])
        k_a = jnp.concatenate([k_a, k_c], axis=1)
        v_a = jnp.concatenate([v_a, v_c], axis=1)
    out_a = attend(jnp.concatenate([q_nope, q_rope], axis=-1), k_a, v_a, MLA_SCALE).reshape(B, L, BRANCH)

    qd = rms_norm(dq.reshape(B, L, DIFF_HEADS, 2, DIFF_HD), W['diff_qk_g'][l, 0])
    kd = rms_norm(dk.reshape(B, L, DIFF_HEADS, 2, DIFF_HD), W['diff_qk_g'][l, 1])
    vd = dv.reshape(B, L, DIFF_HEADS, 2 * DIFF_HD)
    k_b, v_b = kd, vd
    if pos is not None:
        qd = rope2d(qd, pos)
        k_b = rope2d(kd, pos)
    if ctx is not None:
        k_b = jnp.concatenate([k_b, ctx[2]], axis=1)
        v_b = jnp.concatenate([v_b, ctx[3]], axis=1)
    lam_init = 0.8 - 0.6 * math.exp(-0.3 * l)
    lp = W['diff_lambda'][l].astype(jnp.float32)
    lam = jnp.exp(jnp.sum(lp[0] * lp[1])) - jnp.exp(jnp.sum(lp[2] * lp[3])) + lam_init
    out_b = rms_norm(diff_attend(qd, k_b, v_b, lam), W['diff_subln_g'][l]) * (1.0 - lam_init)
    out_b = out_b.reshape(B, L, BRANCH)

    s0 = jnp.zeros((B, 2, HGRN_HEADS, HGRN_DK, HGRN_DV), x.dtype) if ctx is None else ctx[4]
    out_c, states = hgrn_mixer(hq, hzf, hzb, hi, W['hgrn_lb'][0, l], W['hgrn_lb'][1, l], s0, W['hgrn_out_g'][l])

    filt = hyena_filters(L, W['hy_w1'][l], W['hy_b1'][l], W['hy_w2'][l], W['hy_b2'][l], W['hy_w3'][l], W['hy_sin_freq'][l])
    out_d = hyena_mixer(hu, W['hy_conv_w'][l], W['hy_conv_b'][l], filt, W['hy_bias'][l])

    y = jnp.concatenate([out_a, out_b, out_c, out_d], axis=-1) * jax.nn.silu(jnp.concatenate([g_a, g_b, g_c, g_d], axis=-1))
    x = x + gate * (y @ W['w_out'][l])
    if ctx is None:
        return x, (ckv, krope, kd, vd, states)
    return x


def setup_inputs(seed: int = 0) -> dict:
    key = jax.random.key(seed)
    ks = iter(jax.random.split(key, 48))

    def nrm(shape, s=1.0):
        return s * jax.random.normal(next(ks), shape, jnp.float32)

    def gain(shape):
        return 1.0 + 0.02 * nrm(shape)

    return {
        'x_prompt': nrm((BATCH, SEQ, D_MODEL)),
        'x_sample': nrm((DEC_BATCH, DEC_SEQ, D_MODEL)),
        'cache_mla_ckv': nrm((DEC_BATCH, DEPTH, PAST_LEN, MLA_KV_LORA)),
        'cache_mla_krope': nrm((DEC_BATCH, DEPTH, PAST_LEN, MLA_ROPE)),
        'cache_diff_k': nrm((DEC_BATCH, DEPTH, PAST_LEN, DIFF_HEADS, 2, DIFF_HD)),
        'cache_diff_v': nrm((DEC_BATCH, DEPTH, PAST_LEN, DIFF_HEADS, 2 * DIFF_HD)),
        'state_hgrn': nrm((DEC_BATCH, DEPTH, 2, HGRN_HEADS, HGRN_DK, HGRN_DV), 0.5),
        'c': nrm((DEC_BATCH, D_MODEL)),
        'c_ctx': nrm((D_MODEL,)),
        'norm_g': gain((DEPTH, D_MODEL)),
        'w_mod': nrm((DEPTH, D_MODEL, 3 * D_MODEL), 0.5 * D_MODEL ** -0.5),
        'b_mod': nrm((DEPTH, 3 * D_MODEL), 0.02),
        'w_in': nrm((DEPTH, D_MODEL, IN_COLS), D_MODEL ** -0.5),
        'w_out': nrm((DEPTH, D_MIX, D_MODEL), D_MIX ** -0.5),
        'mla_q_norm_g': gain((DEPTH, MLA_Q_LORA)),
        'mla_w_uq': nrm((DEPTH, MLA_Q_LORA, MLA_HEADS * (MLA_NOPE + MLA_ROPE)), MLA_Q_LORA ** -0.5),
        'mla_kv_norm_g': gain((DEPTH, MLA_KV_LORA)),
        'mla_w_ukv': nrm((DEPTH, MLA_KV_LORA, MLA_HEADS * (MLA_NOPE + MLA_V)), MLA_KV_LORA ** -0.5),
        'mla_nope_g': gain((DEPTH, 2, MLA_NOPE)),
        'mla_rope_g': gain((DEPTH, 2, MLA_ROPE)),
        'diff_qk_g': gain((DEPTH, 2, DIFF_HD)),
        'diff_lambda': nrm((DEPTH, 4, DIFF_HD), 0.1),
        'diff_subln_g': gain((DEPTH, 2 * DIFF_HD)),
        'hgrn_lb_logits': nrm((2, DEPTH, HGRN_HEADS * HGRN_DK), 0.5),
        'hgrn_out_g': gain((DEPTH, HGRN_DV)),
        'hy_conv_w': nrm((DEPTH, 3, 3 * HY_CH), 3 ** -0.5),
        'hy_conv_b': nrm((DEPTH, 3 * HY_CH), 0.02),
        'hy_w1': nrm((DEPTH, HY_EMB, HY_FH), HY_EMB ** -0.5),
        'hy_b1': nrm((DEPTH, HY_FH), 0.1),
        'hy_w2': nrm((DEPTH, HY_FH, HY_FH), HY_FH ** -0.5),
        'hy_b2': nrm((DEPTH, HY_FH), 0.1),
        'hy_w3': nrm((DEPTH, HY_FH, HY_ORDER * 2 * HY_CH), 0.003),
        'hy_sin_freq': gain((DEPTH, 2, HY_FH)),
        'hy_bias': nrm((DEPTH, HY_ORDER, HY_CH)),
    }


def reference(x_prompt, x_sample, cache_mla_ckv, cache_mla_krope, cache_diff_k, cache_diff_v, state_hgrn,
              c, c_ctx, norm_g, w_mod, b_mod, w_in, w_out, mla_q_norm_g, mla_w_uq, mla_kv_norm_g, mla_w_ukv,
              mla_nope_g, mla_rope_g, diff_qk_g, diff_lambda, diff_subln_g, hgrn_lb_logits, hgrn_out_g,
              hy_conv_w, hy_conv_b, hy_w1, hy_b1, hy_w2, hy_b2, hy_w3, hy_sin_freq, hy_bias):
    p = jax.nn.softmax(hgrn_lb_logits.astype(jnp.float32), axis=1)
    hgrn_lb = jnp.cumsum(p, axis=1) - p[:, :1]
    W = dict(norm_g=norm_g, w_mod=w_mod, b_mod=b_mod, w_in=w_in, w_out=w_out,
             mla_q_norm_g=mla_q_norm_g, mla_w_uq=mla_w_uq, mla_kv_norm_g=mla_kv_norm_g, mla_w_ukv=mla_w_ukv,
             mla_nope_g=mla_nope_g, mla_rope_g=mla_rope_g, diff_qk_g=diff_qk_g, diff_lambda=diff_lambda,
             diff_subln_g=diff_subln_g, hgrn_lb=hgrn_lb, hgrn_out_g=hgrn_out_g, hy_conv_w=hy_conv_w,
             hy_conv_b=hy_conv_b, hy_w1=hy_w1, hy_b1=hy_b1, hy_w2=hy_w2, hy_b2=hy_b2, hy_w3=hy_w3,
             hy_sin_freq=hy_sin_freq, hy_bias=hy_bias)

    y_prompt = x_prompt
    ctx_c = c_ctx[None, :]
    per_layer = []
    for l in range(DEPTH):
        y_prompt, new = trunk_layer(y_prompt, ctx_c, l, W, None, None)
        per_layer.append(new)
    new_mla_ckv = jnp.stack([s[0] for s in per_layer], axis=1)
    new_mla_krope = jnp.stack([s[1] for s in per_layer], axis=1)
    new_diff_k = jnp.stack([s[2] for s in per_layer], axis=1)
    new_diff_v = jnp.stack([s[3] for s in per_layer], axis=1)
    new_hgrn_state = jnp.stack([s[4] for s in per_layer], axis=1)

    L = x_sample.shape[1]
    rows = L // GRID_W
    row = jnp.repeat(jnp.arange(rows, dtype=jnp.float32), GRID_W)
    col = (jnp.arange(rows * GRID_W) % GRID_W).astype(jnp.float32)
    y_sample = x_sample
    for l in range(DEPTH):
        ctx = (cache_mla_ckv[:, l], cache_mla_krope[:, l], cache_diff_k[:, l], cache_diff_v[:, l], state_hgrn[:, l])
        y_sample = trunk_layer(y_sample, c, l, W, ctx, (row, col))

    return (y_prompt, y_sample, new_mla_ckv, new_mla_krope, new_diff_k, new_diff_v, new_hgrn_state)
```

```python
import math
import os
from contextlib import ExitStack

import numpy as np
import ml_dtypes

import concourse.bass as bass
import concourse.mybir as mybir
from concourse.bass_utils import run_bass_kernel_spmd

F32 = mybir.dt.float32
BF16 = mybir.dt.bfloat16
AF = mybir.ActivationFunctionType
ALU = mybir.AluOpType
AX = mybir.AxisListType

D = 1024
T = 4096
DEPTH = 4
PAST = 512
NKEY = T + PAST
EPS = 1e-6
IN_COLS = 4000
NT = T // 128
NST = T // 512
BIG = 30000.0

C_CQ, C_CKV, C_KR, C_GA = 0, 256, 384, 416
C_DQ, C_DK, C_DV, C_GB = 672, 928, 1184, 1440
C_HQ, C_HZF, C_HZB, C_HI, C_GC = 1696, 1952, 2208, 2464, 2720
C_HU, C_GD = 2976, 3744

EPOCH = 30000
ENGS = ("pe", "act", "dve", "pool", "sp")


class Sched:
    def __init__(self, nc, n_dma_sems=40, self_wait=True):
        self.nc = nc
        self.h = {"pe": nc.tensor, "act": nc.scalar, "dve": nc.vector, "pool": nc.gpsimd, "sp": nc.sync}
        self.sems = {e: [] for e in ENGS}
        self.cnt = {e: 0 for e in ENGS}
        self.seen = {e: {f: 0 for f in ENGS} for e in ENGS}
        self.clk = {e: [None] for e in ENGS}
        self.qrange = {"sp": (0, 24), "pool": (24, 40), "act": (40, 48)}
        n_dma_sems = 48
        self.nd = n_dma_sems
        self.dsem = [nc.alloc_semaphore(f"dma{i}") for i in range(n_dma_sems)]
        self.dval = [0] * n_dma_sems
        self.dclk = [None] * n_dma_sems
        self.dseen = {e: [0] * n_dma_sems for e in ENGS}
        self.drr = {q: r[0] for q, r in self.qrange.items()}
        self.kw = {}
        self.kr = {}
        self.self_wait = self_wait
        self.n_wait = 0
        self.n_inst = 0

    def _sem(self, e, idx):
        ep = idx // EPOCH
        while len(self.sems[e]) <= ep:
            self.sems[e].append(self.nc.alloc_semaphore(f"s_{e}_{len(self.sems[e])}"))
        return self.sems[e][ep], idx % EPOCH + 1

    def _snap(self, e):
        return (tuple(self.seen[e][f] for f in ENGS), tuple(self.dseen[e]))

    def _merge(self, e, snap):
        if snap is None:
            return
        s, d = snap
        se = self.seen[e]
        for f, v in zip(ENGS, s):
            if v > se[f]:
                se[f] = v
        de = self.dseen[e]
        for i, v in enumerate(d):
            if v > de[i]:
                de[i] = v

    def _wait_event(self, e, ev):
        if ev[0] == "E":
            _, f, c = ev
            if f == e:
                if e in ("pe", "sp"):
                    return
                if (not self.self_wait) or self.cnt[e] - c > 10 or self.seen[e][e] >= c:
                    return
            elif self.seen[e][f] >= c:
                return
            sem, val = self._sem(f, c - 1)
            self.h[e].wait_ge(sem, val)
            self.n_wait += 1
            if c > self.seen[e][f]:
                self.seen[e][f] = c
            self._merge(e, self.clk[f][c])
        else:
            _, i, v, snap = ev
            if self.dseen[e][i] >= v:
                return
            self.h[e].wait_ge(self.dsem[i], v)
            self.n_wait += 1
            self.dseen[e][i] = v
            self._merge(e, snap)

    def _deps(self, e, reads, writes):
        evs = []
        for k in reads:
            w = self.kw.get(k)
            if w is not None:
                evs.append(w)
        for k in writes:
            w = self.kw.get(k)
            if w is not None:
                evs.append(w)
            for r in self.kr.get(k, ()):
                if r[0] == "E" and r[1] == e:
                    continue
                evs.append(r)
        for ev in evs:
            self._wait_event(e, ev)

    def _record(self, ev, reads, writes):
        for k in writes:
            self.kw[k] = ev
            self.kr[k] = []
        for k in reads:
            lst = self.kr.setdefault(k, [])
            if ev[0] == "E":
                lst[:] = [r for r in lst if not (r[0] == "E" and r[1] == ev[1])]
            lst.append(ev)

    def op(self, e, name, *args, reads=(), writes=(), **kw):
        self._deps(e, reads, writes)
        ins = getattr(self.h[e], name)(*args, **kw)
        idx = self.cnt[e]
        sem, _ = self._sem(e, idx)
        ins.then_inc(sem, 1)
        self.cnt[e] = idx + 1
        self.n_inst += 1
        self.clk[e].append(self._snap(e))
        ev = ("E", e, idx + 1)
        self._record(ev, reads, writes)
        return ev

    def dma(self, q, out, in_, reads=(), writes=(), **kw):
        self._deps(q, reads, writes)
        i = self.drr[q]
        lo, hi = self.qrange[q]
        self.drr[q] = lo + (i + 1 - lo) % (hi - lo)
        if self.dval[i] > 0 and self.dseen[q][i] < self.dval[i]:
            self.h[q].wait_ge(self.dsem[i], self.dval[i])
            self.n_wait += 1
            self.dseen[q][i] = self.dval[i]
            self._merge(q, self.dclk[i])
        ins = self.h[q].dma_start(out=out, in_=in_, **kw)
        self.dval[i] += 16
        ins.then_inc(self.dsem[i], 16)
        self.n_inst += 1
        snap = self._snap(q)
        self.dclk[i] = snap
        ev = ("D", i, self.dval[i], snap)
        self._record(ev, reads, writes)
        return ev

    def barrier(self):
        for e in ENGS:
            for f in ENGS:
                if f != e and self.cnt[f] > 0:
                    self._wait_event(e, ("E", f, self.cnt[f]))
            for i in range(self.nd):
                if self.dval[i] > 0:
                    self._wait_event(e, ("D", i, self.dval[i], self.dclk[i]))
        self.kw.clear()
        self.kr.clear()


class Builder:
    def __init__(self, debug=False, n_layers=DEPTH, stages=("mla", "diff", "hgrn", "hyena")):
        self.debug = debug
        self.n_layers = n_layers
        self.stages = stages
        self.nc = bass.Bass("TRN2", target_bir_lowering=False)
        self.S = Sched(self.nc)
        self.inputs = {}
        self.outputs = {}
        self.build()

    def din(self, name, shape, dt=F32):
        t = self.nc.dram_tensor(name, list(shape), dt, kind="ExternalInput").ap()
        self.inputs[name] = t
        return t

    def dout(self, name, shape, dt=F32):
        t = self.nc.dram_tensor(name, list(shape), dt, kind="ExternalOutput").ap()
        self.outputs[name] = t
        return t

    def dscr(self, name, shape, dt=F32):
        kind = "ExternalOutput" if self.debug else "Internal"
        t = self.nc.dram_tensor(name, list(shape), dt, kind=kind).ap()
        if self.debug:
            self.outputs[name] = t
        return t

    def sb(self, es, name, shape, dt=F32):
        self._uid = getattr(self, "_uid", 0) + 1
        return es.enter_context(self.nc.sbuf_tensor(f"{name}_{self._uid}", list(shape), dt))

    def build(self):
        nc, S = self.nc, self.S
        self.x_in = self.din("x", [T, D])
        self.cvecT = self.din("cvecT", [128, 8])
        self.norm_g = self.din("norm_g", [DEPTH, D])
        self.w_mod = self.din("w_mod", [DEPTH, D, 3 * D])
        self.b_mod = self.din("b_mod", [DEPTH, 3 * D])
        self.w_in = self.din("w_in", [DEPTH, D, IN_COLS])
        self.w_out = self.din("w_out", [DEPTH, D, D])
        self.mla_q_norm_g = self.din("mla_q_norm_g", [DEPTH, 256])
        self.mla_w_uq = self.din("mla_w_uq", [DEPTH, 256, 384])
        self.mla_kv_norm_g = self.din("mla_kv_norm_g", [DEPTH, 128])
        self.mla_w_ukv = self.din("mla_w_ukv", [DEPTH, 128, 512])
        self.mla_nope_g = self.din("mla_nope_g", [DEPTH, 2, 64])
        self.mla_rope_g = self.din("mla_rope_g", [DEPTH, 2, 32])
        self.diff_qk_g = self.din("diff_qk_g", [DEPTH, 2, 32])
        self.diff_lambda = self.din("diff_lambda", [DEPTH, 128])
        self.diff_subln_g = self.din("diff_subln_gT", [DEPTH, 64, 1])
        self.c_ckv = self.din("c_ckv", [DEPTH, PAST, 128])
        self.c_krope = self.din("c_krope", [DEPTH, PAST, 32])
        self.c_dk = self.din("c_dk", [DEPTH, PAST, 256])
        self.c_dv = self.din("c_dv", [DEPTH, PAST, 256])
        self.ropeC = self.din("ropeC", [T, 256])
        self.ropeS = self.din("ropeS", [T, 256])
        self.qaug = self.din("qaug", [T, 17])
        self.kaug = self.din("kaug", [NKEY, 17])
        self.hy_w1 = self.din("hy_w1", [DEPTH, 33, 64])
        self.hy_w2 = self.din("hy_w2", [DEPTH, 64, 64])
        self.hy_w3 = self.din("hy_w3", [DEPTH, 64, 1024])
        self.hy_cols = self.din("hy_cols", [DEPTH, 64, 4])
        self.hy_convT = self.din("hy_convT", [DEPTH, 768, 4])
        self.hy_biasT = self.din("hy_biasT", [DEPTH, 2, 256, 1])
        self.negdelta = self.din("negdelta", [256, 1])
        self.featsT = self.din("featsT", [33, 8192])
        self.dmask = self.din("dmask", [2, 8192])
        self.bmask = self.din("bmask", [2, 128, 16])
        self.F1z = self.din("F1z", [32, 128])
        self.F1f = self.din("F1f", [64, 128])
        self.Gm = self.din("Gm", [64, 128, 384], BF16)
        self.Em = self.din("Em", [128, 512], BF16)
        self.Qm = self.din("Qm", [64, 128 * 64], BF16)
        self.s_filtT = self.dscr("s_filtT", [2, 256, 8192])
        self.s_H = self.dscr("s_H", [4, 128, 2 * 64 * 128])
        self.s_z = self.dscr("s_z", [4, 128, T])
        self.lbT = self.din("lbT", [256, 8])
        self.hgrn_out_g = self.din("hgrn_out_gT", [DEPTH, 64, 1])
        self.s0 = self.din("s0", [DEPTH, 2, 4, 64, 64])
        self.carry = self.din("carry", [2, 128, 64])
        self.maskT = self.din("maskT", [64, 256])
        self.o_state = self.dout("o_state", [DEPTH, 16, 2, 4, 64, 64])
        self.y = self.dout("y", [T, D])
        self.o_dv = self.dout("o_dv", [DEPTH, T, 256])
        self.o_ckv = self.dout("o_ckv", [DEPTH, T, 128])
        self.o_krope = self.dout("o_krope", [DEPTH, T, 32])
        self.o_dk = self.dout("o_dk", [DEPTH, T, 256])
        self.s_cq = self.dscr("s_cq", [T, 416])
        self.s_dqk = self.dscr("s_dqk", [T, 512])
        self.s_vd = self.dscr("s_vd", [NKEY, 4 * 65], BF16)
        self.s_hv = self.dscr("s_hv", [T, 256], BF16)
        self.s_gT = self.dscr("s_gT", [D, T], BF16)
        self.s_hT = self.dscr("s_hT", [768, T])
        self.s_huT = self.dscr("s_huT", [768, T])
        self.s_yT = self.dscr("s_yT", [D, T], BF16)

        with ExitStack() as es:
            self.ident_bf = self.sb(es, "ident_bf", [128, 128], BF16)
            self.ident_f = self.sb(es, "ident_f", [128, 128], F32)
            self.ones_f = self.sb(es, "ones_f", [128, 128], F32)
            self.ident_in = self.din("ident", [128, 128])
            self.modA = self.sb(es, "modA", [128, D], F32)
            self.modB = self.sb(es, "modB", [128, D], F32)
            self.gate_b = self.sb(es, "gate_b", [128, D], F32)
            self.crep = self.sb(es, "crep", [128, 8, 128], F32)
            self.psum = [es.enter_context(nc.psum_tensor(f"ps{i}", [128, 512], F32)) for i in range(8)]

            S.dma("sp", self.ident_f[:], self.ident_in[:, :], writes=["ident_f"])
            S.op("dve", "tensor_copy", self.ident_bf[:], self.ident_f[:], reads=["ident_f"], writes=["ident_bf"])
            S.op("dve", "memset", self.ones_f[:], 1.0, writes=["ones_f"])
            with ExitStack() as es0:
                cv = self.sb(es0, "cv", [128, 8], F32)
                sc = self.sb(es0, "sc", [128, 8], F32)
                S.dma("sp", cv[:], self.cvecT[:, :], writes=["cv"])
                S.op("act", "activation", out=sc[:], in_=cv[:], func=AF.Silu, reads=["cv"], writes=["sc"])
                for k in range(8):
                    S.op("dve", "tensor_scalar", self.crep[:, k, :], self.ones_f[:], sc[:, k:k + 1], None,
                         ALU.mult, reads=["sc", "ones_f"], writes=["crep"])
                S.barrier()

            for l in range(self.n_layers):
                self.layer(l)
            S.barrier()

    def layer(self, l):
        S = self.S
        self.phase_mod(l)
        S.barrier()
        self.phase_p1(l)
        S.barrier()
        if "mla" in self.stages:
            self.phase_mla(l)
            S.barrier()
        if "diff" in self.stages:
            self.phase_diff(l)
            S.barrier()
        if "hgrn" in self.stages:
            self.phase_hgrn(l)
            S.barrier()
        if "hyena" in self.stages:
            self.phase_hyena(l)
            S.barrier()
        if "stub" in self.stages:
            for c in range(8):
                S.dma("sp", self.s_yT[c * 128:(c + 1) * 128, :], self.s_gT[c * 128:(c + 1) * 128, :])
            S.barrier()
        self.phase_po(l)
        S.barrier()

    def phase_mod(self, l):
        nc, S = self.nc, self.S
        with ExitStack() as es:
            wm = self.sb(es, "wm", [128, 8, 3 * D], F32)
            bm = self.sb(es, "bm", [128, 3 * D], F32)
            gb = self.sb(es, "gb", [128, D], F32)
            for k in range(8):
                S.dma("sp" if k % 2 == 0 else "pool", wm[:, k, :], self.w_mod[l, k * 128:(k + 1) * 128, :],
                      writes=[("wm", k)])
            S.dma("sp", bm[:], self.b_mod[l:l + 1, :].partition_broadcast(128) if False else
                  self.b_mod[l, :].partition_broadcast(128), writes=["bm"])
            S.dma("sp", gb[:], self.norm_g[l, :].partition_broadcast(128), writes=["gb"])
            for nt in range(6):
                ps = self.psum[nt % 2]
                for k in range(8):
                    S.op("pe", "matmul", ps[:, :], self.crep[:, k, :], wm[:, k, nt * 512:(nt + 1) * 512],
                         start=(k == 0), stop=(k == 7), reads=["crep", ("wm", k)], writes=[("ps", nt % 2)])
                sl = slice((nt % 2) * 512, (nt % 2) * 512 + 512)
                bsl = slice(nt * 512, nt * 512 + 512)
                if nt < 2:
                    S.op("dve", "tensor_tensor", self.modB[:, sl], ps[:, :], bm[:, bsl], ALU.add,
                         reads=[("ps", nt % 2), "bm"], writes=["modB"])
                elif nt < 4:
                    S.op("dve", "scalar_tensor_tensor", self.modA[:, sl], ps[:, :], 1.0, bm[:, bsl], ALU.add, ALU.add,
                         reads=[("ps", nt % 2), "bm"], writes=["modA"])
                    S.op("dve", "tensor_tensor", self.modA[:, sl], self.modA[:, sl], gb[:, sl], ALU.mult,
                         reads=["modA", "gb"], writes=["modA"])
                else:
                    S.op("dve", "tensor_tensor", self.gate_b[:, sl], ps[:, :], bm[:, bsl], ALU.add,
                         reads=[("ps", nt % 2), "bm"], writes=["gate_b"])

    def phase_p1(self, l):
        nc, S = self.nc, self.S
        xsrc = self.x_in if l == 0 else self.y
        with ExitStack() as es:
            w16 = self.sb(es, "w16", [128, 8, IN_COLS], BF16)
            xin = [self.sb(es, f"xin{i}", [128, 4, D], F32) for i in range(2)]
            hT = [self.sb(es, f"hT{i}", [128, 8, 512], BF16) for i in range(2)]
            hb = self.sb(es, "hb", [128, 4, D], BF16)
            t32 = self.sb(es, "t32", [128, D], F32)
            junk = self.sb(es, "junk", [128, D], BF16)
            ss = self.sb(es, "ss", [128, 8], F32)
            tm_st = [self.sb(es, f"tmst{i}", [128, 928], F32) for i in range(2)]
            dv_st = [self.sb(es, f"dvst{i}", [128, 256], F32) for i in range(2)]
            hv_st = [self.sb(es, f"hvst{i}", [128, 256], BF16) for i in range(2)]
            g_st = [self.sb(es, f"gst{i}", [128, 512], BF16) for i in range(2)]
            f_st = [self.sb(es, f"fst{i}", [128, 512], F32) for i in range(3)]
            vd_st = [self.sb(es, f"vdst{i}", [128, 4, 65], BF16) for i in range(2)]

            for k in range(8):
                for hh in range(2):
                    S.dma("pool", w16[:, k, hh * 2000:(hh + 1) * 2000],
                          self.w_in[l, k * 128:(k + 1) * 128, hh * 2000:(hh + 1) * 2000], writes=[("w16", k)])
            for i in range(2):
                S.op("pool", "memset", vd_st[i][:], 1.0, writes=[("vdst", i)])

            def prep(st):
                b = st % 2
                S.dma("sp", xin[b][:], xsrc[st * 512:(st + 1) * 512, :].rearrange("(j p) d -> p j d", p=128),
                      writes=[("xin", b)])
                kd = os.environ.get("KDBG", "")
                for j in range(0 if "nostat" in kd else 4):
                    S.op("act", "activation", out=junk[:], in_=xin[b][:, j, :], func=AF.Square,
                         accum_out=ss[:, j:j + 1], reads=[("xin", b)], writes=["junk", ("ss", j)])
                if "nostat" not in kd:
                    S.op("dve", "tensor_scalar", ss[:, 4:8], ss[:, 0:4], 1.0 / D, EPS, ALU.mult, ALU.add,
                         reads=[("ss", j) for j in range(4)], writes=["ms"])
                    S.op("act", "activation", out=ss[:, 4:8], in_=ss[:, 4:8], func=AF.Sqrt, reads=["ms"], writes=["ms"])
                    S.op("dve", "reciprocal", ss[:, 4:8], ss[:, 4:8], reads=["ms"], writes=["ms"])
                else:
                    S.op("dve", "memset", ss[:], 1.0, writes=["ms"])
                for j in range(4):
                    S.op("dve", "scalar_tensor_tensor", t32[:], xin[b][:, j, :], ss[:, 4 + j:5 + j], self.modA[:],
                         ALU.mult, ALU.mult, reads=[("xin", b), "ms"], writes=["t32"])
                    S.op("pool", "tensor_tensor", hb[:, j, :], t32[:], self.modB[:], ALU.add,
                         reads=["t32"], writes=[("hb", j)])
                for j in range(4):
                    pi = j % 2
                    pst = self.psum[pi][:, :].bitcast(BF16)
                    for k in range(8):
                        S.op("pe", "transpose", pst[:, k * 128:(k + 1) * 128], hb[:, j, k * 128:(k + 1) * 128],
                             self.ident_bf[:], reads=[("hb", j)], writes=[("ps", pi)])
                    eng = "act" if (j % 2 == 0 or "tract" in os.environ.get("KDBG", "")) else "dve"
                    src = pst.rearrange("p (k t) -> p k t", k=8)
                    if eng == "act":
                        S.op("act", "activation", out=hT[b][:, :, j * 128:(j + 1) * 128], in_=src, func=AF.Copy,
                             reads=[("ps", pi)], writes=[("hT", b)])
                    else:
                        S.op("dve", "tensor_copy", hT[b][:, :, j * 128:(j + 1) * 128], src,
                             reads=[("ps", pi)], writes=[("hT", b)])

            fm_chunks = ([("g", 0, C_GA), ("g", 1, C_GA + 128), ("g", 2, C_GB), ("g", 3, C_GB + 128),
                          ("g", 4, C_GC), ("g", 5, C_GC + 128), ("g", 6, C_GD), ("g", 7, C_GD + 128)]
                         + [("h", i, C_HQ + 128 * i) for i in range(6)]
                         + [("u", i, C_HU + 128 * i) for i in range(6)])

            def mm(st):
                b = st % 2
                cnt = 0
                for j in range(0 if "notm" in os.environ.get("KDBG", "") else 4):
                    tok0 = st * 512 + j * 128
                    lhs = lambda k: hT[b][:, k, j * 128:(j + 1) * 128]
                    sb_i = j % 2
                    ps = self.psum[2]
                    for k in range(0 if "noa" in os.environ.get("KDBG", "") else 8):
                        S.op("pe", "matmul", ps[:, 0:416], lhs(k), w16[:, k, 0:416], start=(k == 0), stop=(k == 7),
                             reads=[("hT", b), ("w16", k)], writes=[("ps", 2)])
                    S.op("act", "activation", out=tm_st[sb_i][:, 0:416], in_=ps[:, 0:416], func=AF.Copy,
                         reads=[("ps", 2)], writes=[("tmst", sb_i, 0)])
                    S.dma("sp", self.s_cq[tok0:tok0 + 128, :], tm_st[sb_i][:, 0:416], reads=[("tmst", sb_i, 0)])
                    ps = self.psum[3]
                    for k in range(0 if "nob" in os.environ.get("KDBG", "") else 8):
                        S.op("pe", "matmul", ps[:, :], lhs(k), w16[:, k, C_DQ:C_DQ + 512], start=(k == 0), stop=(k == 7),
                             reads=[("hT", b), ("w16", k)], writes=[("ps", 3)])
                    S.op("dve", "tensor_copy", tm_st[sb_i][:, 416:928], ps[:, :],
                         reads=[("ps", 3)], writes=[("tmst", sb_i, 1)])
                    S.dma("sp", self.s_dqk[tok0:tok0 + 128, :], tm_st[sb_i][:, 416:928], reads=[("tmst", sb_i, 1)])
                    ps = self.psum[2]
                    if "noc" in os.environ.get("KDBG", ""):
                        continue
                    for k in range(8):
                        S.op("pe", "matmul", ps[:, 0:256], lhs(k), w16[:, k, C_DV:C_DV + 256], start=(k == 0), stop=(k == 7),
                             reads=[("hT", b), ("w16", k)], writes=[("ps", 2)])
                    for k in range(8):
                        S.op("pe", "matmul", ps[:, 256:512], lhs(k), w16[:, k, C_HI:C_HI + 256], start=(k == 0), stop=(k == 7),
                             reads=[("hT", b), ("w16", k)], writes=[("ps", 2)])
                    kd = os.environ.get("KDBG", "")
                    if "c1" not in kd:
                        S.op("act", "activation", out=dv_st[sb_i][:], in_=ps[:, 0:256], func=AF.Copy,
                             reads=[("ps", 2)], writes=[("dvst", sb_i)])
                        S.dma("sp", self.o_dv[l, tok0:tok0 + 128, :], dv_st[sb_i][:], reads=[("dvst", sb_i)])
                    if "c2" not in kd:
                        S.op("act", "activation", out=vd_st[sb_i][:, :, 0:64], in_=ps[:, 0:256].rearrange("p (h e) -> p h e", h=4),
                             func=AF.Copy, reads=[("ps", 2)], writes=[("vdst", sb_i)])
                        S.dma("sp", self.s_vd[tok0:tok0 + 128, :], vd_st[sb_i][:].rearrange("p h e -> p (h e)"),
                              reads=[("vdst", sb_i)])
                    if "c3" not in kd:
                        if "hv32" in kd:
                            S.op("dve", "tensor_copy", dv_st[sb_i][:], ps[:, 256:512], reads=[("ps", 2)], writes=[("dvst", sb_i)])
                            S.op("act", "activation", out=hv_st[sb_i][:], in_=dv_st[sb_i][:], func=AF.Copy, reads=[("dvst", sb_i)], writes=[("hvst", sb_i)])
                        elif "hvdve" not in kd:
                            S.op("act", "activation", out=hv_st[sb_i][:], in_=ps[:, 256:512], func=AF.Copy, reads=[("ps", 2)], writes=[("hvst", sb_i)])
                        else:
                            S.op("dve", "tensor_copy", hv_st[sb_i][:], ps[:, 256:512], reads=[("ps", 2)], writes=[("hvst", sb_i)])
                        S.dma("sp", self.s_hv[tok0:tok0 + 128, :], hv_st[sb_i][:], reads=[("hvst", sb_i)])
                for ci, (kind, idx, col) in enumerate(fm_chunks):
                    if "nofm" in os.environ.get("KDBG", ""):
                        break
                    pi = 4 + ci % 3
                    ps = self.psum[pi]
                    for k in range(8):
                        S.op("pe", "matmul", ps[:, :], w16[:, k, col:col + 128], hT[b][:, k, :], start=(k == 0), stop=(k == 7),
                             reads=[("hT", b), ("w16", k)], writes=[("ps", pi)])
                    tsl = slice(st * 512, (st + 1) * 512)
                    if kind == "g":
                        gi = ci % 2
                        S.op("act", "activation", out=g_st[gi][:], in_=ps[:, :], func=AF.Silu,
                             reads=[("ps", pi)], writes=[("gst", gi)])
                        S.dma("sp", self.s_gT[idx * 128:(idx + 1) * 128, tsl], g_st[gi][:], reads=[("gst", gi)])
                    else:
                        fi = ci % 3
                        if ci % 2 == 0:
                            S.op("dve", "tensor_copy", f_st[fi][:], ps[:, :], reads=[("ps", pi)], writes=[("fst", fi)])
                        else:
                            S.op("act", "activation", out=f_st[fi][:], in_=ps[:, :], func=AF.Copy,
                                 reads=[("ps", pi)], writes=[("fst", fi)])
                        dst = self.s_hT if kind == "h" else self.s_huT
                        S.dma("sp", dst[idx * 128:(idx + 1) * 128, tsl], f_st[fi][:], reads=[("fst", fi)])

            import os
            dbg = os.environ.get("KDBG", "")
            nst = int(os.environ.get("KNST", NST))
            if "nomm" in dbg:
                mm = lambda st: None
            prep(0)
            for st in range(nst):
                if st + 1 < nst:
                    prep(st + 1)
                mm(st)

    def rms_tm(self, src3, dst3, G, W, gain_b, sq, ssb, rk, wk, gain_eng="dve"):
        S = self.S
        sqv = sq[:, 0:G * W].rearrange("p (g w) -> p g w", g=G)
        S.op("dve", "tensor_tensor", sqv, src3, src3, ALU.mult, reads=rk, writes=["sq"])
        S.op("dve", "tensor_reduce", ssb[:, 0:G], sqv, AX.X, ALU.add, reads=["sq"], writes=["ssb"])
        S.op("dve", "tensor_scalar", ssb[:, G:2 * G], ssb[:, 0:G], 1.0 / W, EPS, ALU.mult, ALU.add,
             reads=["ssb"], writes=["ssb2"])
        S.op("act", "activation", out=ssb[:, G:2 * G], in_=ssb[:, G:2 * G], func=AF.Sqrt, reads=["ssb2"], writes=["ssb2"])
        S.op("dve", "reciprocal", ssb[:, G:2 * G], ssb[:, G:2 * G], reads=["ssb2"], writes=["ssb2"])
        if gain_b is None:
            S.op("dve", "tensor_tensor", dst3, src3, ssb[:, G:2 * G].unsqueeze(2).to_broadcast([128, G, W]), ALU.mult,
                 reads=list(rk) + ["ssb2"], writes=wk)
        else:
            S.op("dve", "tensor_tensor", sqv, src3, ssb[:, G:2 * G].unsqueeze(2).to_broadcast([128, G, W]), ALU.mult,
                 reads=list(rk) + ["ssb2"], writes=["sq"])
            S.op(gain_eng, "tensor_tensor", dst3, sqv, gain_b.unsqueeze(1).to_broadcast([128, G, W]), ALU.mult,
                 reads=["sq"], writes=wk)

    def rope_tm(self, x3, out3, G, rc, rs, t1, rk, wk):
        S = self.S
        xv = x3.rearrange("p g (r h e) -> p (g r) h e", r=2, h=2)
        sv = rs[:, 0:G * 32].rearrange("p (g h e) -> p g h e", h=2, e=8)
        tv = t1[:, 0:G * 32].rearrange("p (g h e) -> p g h e", h=2, e=8)
        S.op("dve", "tensor_tensor", tv[:, :, 0, :], xv[:, :, 1, :], sv[:, :, 0, :], ALU.mult, reads=list(rk) + ["rs"], writes=["t1"])
        S.op("dve", "tensor_tensor", tv[:, :, 1, :], xv[:, :, 0, :], sv[:, :, 1, :], ALU.mult, reads=list(rk) + ["rs"], writes=["t1"])
        x2 = x3.rearrange("p g w -> p (g w)") if False else x3
        S.op("dve", "tensor_tensor", x3, x3, rc[:, 0:G * 32].rearrange("p (g w) -> p g w", g=G), ALU.mult,
             reads=list(rk) + ["rc", "t1"], writes=rk)
        S.op("pool", "tensor_tensor", out3, x3, t1[:, 0:G * 32].rearrange("p (g w) -> p g w", g=G), ALU.add,
             reads=list(rk) + ["t1"], writes=wk)

    def bcast_load(self, q, tile_ap, dram_ap, key):
        self.S.dma(q, tile_ap, dram_ap.partition_broadcast(128), writes=[key])

    def attend(self, n_groups, Krows, QT, KT, Vt, v_of_g, scale, finalize, order=None):
        S = self.S
        NKC = NKEY // 128
        it = 0
        if order is None:
            order = [(g, qs) for g in range(n_groups) for qs in range(NST)]
        if True:
            for (g, qs) in order:
                pO = self.psum[2 + (it % 2)]
                for kc in range(NKC):
                    b = kc % 2
                    pS = self.psum[b]
                    S.op("pe", "matmul", pS[:, :], KT(g, kc), QT(g, qs), start=True, stop=True,
                         reads=["QT", "KT"], writes=[("ps", b)])
                    S.op("act", "activation", out=self.pt[b][:], in_=pS[:, :], func=AF.Exp, scale=scale,
                         reads=[("ps", b)], writes=[("pt", b)])
                    S.op("pe", "matmul", pO[0:65, :], Vt(v_of_g(g), kc), self.pt[b][:], start=(kc == 0), stop=(kc == NKC - 1),
                         reads=[("pt", b), "V"], writes=[("ps", 2 + it % 2)])
                finalize(g, qs, pO, 2 + it % 2)
                it += 1

    def phase_mla(self, l):
        nc, S = self.nc, self.S
        with ExitStack() as es:
            QT = self.sb(es, "QT", [128, 4, T], BF16)
            KT = self.sb(es, "KT", [128, 4, NKEY], BF16)
            Va = self.sb(es, "Va", [128, NKEY // 128, 4, 65], BF16)
            self.pt = [self.sb(es, f"pt{i}", [128, 512], BF16) for i in range(2)]
            wuq32 = self.sb(es, "wuq32", [128, 2, 384], F32)
            wuq = self.sb(es, "wuq", [128, 2, 384], BF16)
            wukv32 = self.sb(es, "wukv32", [128, 512], F32)
            wukv = self.sb(es, "wukv", [128, 512], BF16)
            g_q = self.sb(es, "g_q", [128, 256], F32)
            g_kv = self.sb(es, "g_kv", [128, 128], F32)
            g_nope = self.sb(es, "g_nope", [128, 128], F32)
            g_rope = self.sb(es, "g_rope", [128, 64], F32)
            S.op("pool", "memset", Va[:], 1.0, writes=["V"])
            S.dma("sp", wuq32[:], self.mla_w_uq[l].rearrange("(c p) n -> p c n", p=128), writes=["wuq32"])
            S.op("dve", "tensor_copy", wuq[:], wuq32[:], reads=["wuq32"], writes=["wuq"])
            S.dma("sp", wukv32[:], self.mla_w_ukv[l], writes=["wukv32"])
            S.op("dve", "tensor_copy", wukv[:], wukv32[:], reads=["wukv32"], writes=["wukv"])
            self.bcast_load("sp", g_q[:], self.mla_q_norm_g[l, :], "g_q")
            self.bcast_load("sp", g_kv[:], self.mla_kv_norm_g[l, :], "g_kv")
            self.bcast_load("sp", g_nope[:], self.mla_nope_g[l].rearrange("a b -> (a b)"), "g_nope")
            self.bcast_load("sp", g_rope[:], self.mla_rope_g[l].rearrange("a b -> (a b)"), "g_rope")
            with ExitStack() as es2:
                cqt = [self.sb(es2, f"cqt{i}", [128, 416], F32) for i in range(2)]
                rc = [self.sb(es2, f"rc{i}", [128, 256], F32) for i in range(2)]
                rs = [self.sb(es2, f"rs{i}", [128, 256], F32) for i in range(2)]
                qa = [self.sb(es2, f"qa{i}", [128, 17], F32) for i in range(2)]
                ka = [self.sb(es2, f"ka{i}", [128, 17], F32) for i in range(2)]
                sq = self.sb(es2, "sq", [128, 256], F32)
                ssb = self.sb(es2, "ssb", [128, 16], F32)
                t1 = self.sb(es2, "t1", [128, 256], F32)
                cqb = self.sb(es2, "cqb", [128, 256], BF16)
                cqT = self.sb(es2, "cqT", [128, 2, 128], BF16)
                q32 = self.sb(es2, "q32", [128, 384], F32)
                qr = self.sb(es2, "qr", [128, 4, 32], F32)
                Qst = self.sb(es2, "Qst", [128, 4, 113], BF16)
                Kst = self.sb(es2, "Kst", [128, 4, 113], BF16)
                ckn = [self.sb(es2, f"ckn{i}", [128, 128], F32) for i in range(2)]
                ckb = self.sb(es2, "ckb", [128, 128], BF16)
                ckT = self.sb(es2, "ckT", [128, 128], BF16)
                kv32 = self.sb(es2, "kv32", [128, 512], F32)
                krn = [self.sb(es2, f"krn{i}", [128, 32], F32) for i in range(2)]
                krr = self.sb(es2, "krr", [128, 32], F32)
                q32v = q32[:].rearrange("p (h w) -> p h w", h=4)
                kv32v = kv32[:].rearrange("p (h w) -> p h w", h=4)

                def transposes(src_tile, n, width, dst_ap, rkey, wkey, pi):
                    pst = self.psum[pi][:, :].bitcast(BF16)
                    for h in range(n):
                        S.op("pe", "transpose", pst[0:width, h * 128:(h + 1) * 128], src_tile(h), self.ident_bf[:],
                             reads=[rkey], writes=[("ps", pi)])
                    S.op("act", "activation", out=dst_ap, in_=pst[0:width, 0:n * 128].rearrange("p (h t) -> p h t", h=n),
                         func=AF.Copy, reads=[("ps", pi)], writes=[wkey])

                for t in range(NKEY // 128):
                    b = t % 2
                    tok0 = t * 128
                    ctx = t >= NT
                    S.dma("sp", ka[b][:], self.kaug[tok0:tok0 + 128, :], writes=[("ka", b)])
                    if not ctx:
                        S.dma("sp", cqt[b][:], self.s_cq[tok0:tok0 + 128, :], writes=[("cqt", b)])
                        S.dma("sp", rc[b][:], self.ropeC[tok0:tok0 + 128, :], writes=["rc"])
                        S.dma("sp", rs[b][:], self.ropeS[tok0:tok0 + 128, :], writes=["rs"])
                        S.dma("sp", qa[b][:], self.qaug[tok0:tok0 + 128, :], writes=[("qa", b)])
                        self.rms_tm(cqt[b][:, 0:256].unsqueeze(1), cqb[:].unsqueeze(1), 1, 256, g_q[:], sq, ssb,
                                    [("cqt", b)], ["cqb"], gain_eng="pool")
                        transposes(lambda c: cqb[:, c * 128:(c + 1) * 128], 2, 128, cqT[:, :, :], "cqb", "cqT", 4)
                        for c in range(2):
                            S.op("pe", "matmul", self.psum[5][:, 0:384], cqT[:, c, :], wuq[:, c, :], start=(c == 0), stop=(c == 1),
                                 reads=["cqT", "wuq"], writes=[("ps", 5)])
                        S.op("act", "activation", out=q32[:], in_=self.psum[5][:, 0:384], func=AF.Copy,
                             reads=[("ps", 5)], writes=["q32"])
                        self.rms_tm(q32v[:, :, 0:64], Qst[:, :, 0:64], 4, 64, g_nope[:, 0:64], sq, ssb, ["q32"], ["Qst"],
                                    gain_eng="pool")
                        self.rms_tm(q32v[:, :, 64:96], qr[:], 4, 32, g_rope[:, 0:32], sq, ssb, ["q32"], ["qr"])
                        self.rope_tm(qr[:], Qst[:, :, 64:96], 4, rc[b], rs[b], t1, ["qr"], ["Qst"])
                        S.op("pool", "tensor_copy", Qst[:, :, 96:113], qa[b][:].unsqueeze(1).to_broadcast([128, 4, 17]),
                             reads=[("qa", b)], writes=["Qst"])
                        transposes(lambda h: Qst[:, h, :], 4, 113, QT[0:113, :, tok0:tok0 + 128], "Qst", "QT", 4)
                        self.rms_tm(cqt[b][:, 256:384].unsqueeze(1), ckn[b][:].unsqueeze(1), 1, 128, g_kv[:], sq, ssb,
                                    [("cqt", b)], [("ckn", b)])
                        S.dma("sp", self.o_ckv[l, tok0:tok0 + 128, :], ckn[b][:], reads=[("ckn", b)])
                        self.rms_tm(cqt[b][:, 384:416].unsqueeze(1), krn[b][:].unsqueeze(1), 1, 32, g_rope[:, 32:64], sq, ssb,
                                    [("cqt", b)], [("krn", b)])
                        S.dma("sp", self.o_krope[l, tok0:tok0 + 128, :], krn[b][:], reads=[("krn", b)])
                        S.op("dve", "tensor_copy", krr[:], krn[b][:], reads=[("krn", b)], writes=["krr"])
                        self.rope_tm(krr[:].unsqueeze(1), krr[:].unsqueeze(1), 1, rc[b], rs[b], t1, ["krr"], ["krr"])
                    else:
                        c0 = tok0 - T
                        S.dma("sp", ckn[b][:], self.c_ckv[l, c0:c0 + 128, :], writes=[("ckn", b)])
                        S.dma("sp", krr[:], self.c_krope[l, c0:c0 + 128, :], writes=["krr"])
                    S.op("pool", "tensor_copy", ckb[:], ckn[b][:], reads=[("ckn", b)], writes=["ckb"])
                    transposes(lambda h: ckb[:], 1, 128, ckT[:].unsqueeze(1), "ckb", "ckT", 6)
                    S.op("pe", "matmul", self.psum[7][:, :], ckT[:], wukv[:], start=True, stop=True,
                         reads=["ckT", "wukv"], writes=[("ps", 7)])
                    S.op("act", "activation", out=kv32[:], in_=self.psum[7][:, :], func=AF.Copy, reads=[("ps", 7)], writes=["kv32"])
                    self.rms_tm(kv32v[:, :, 0:64], Kst[:, :, 0:64], 4, 64, g_nope[:, 64:128], sq, ssb, ["kv32"], ["Kst"],
                                gain_eng="pool")
                    S.op("pool", "tensor_copy", Va[:, t, :, 0:64], kv32v[:, :, 64:128], reads=["kv32"], writes=["V"])
                    S.op("pool", "tensor_copy", Kst[:, :, 64:96], krr[:].unsqueeze(1).to_broadcast([128, 4, 32]),
                         reads=["krr"], writes=["Kst"])
                    S.op("pool", "tensor_copy", Kst[:, :, 96:113], ka[b][:].unsqueeze(1).to_broadcast([128, 4, 17]),
                         reads=[("ka", b)], writes=["Kst"])
                    transposes(lambda h: Kst[:, h, :], 4, 113, KT[0:113, :, tok0:tok0 + 128], "Kst", "KT", 6)
                S.barrier()

            with ExitStack() as es3:
                o32 = [self.sb(es3, f"o32{i}", [128, 512], F32) for i in range(2)]
                yf = self.sb(es3, "yf", [64, 512], F32)
                gt = [self.sb(es3, f"gt{i}", [64, 512], BF16) for i in range(2)]
                yb = [self.sb(es3, f"yb{i}", [64, 512], BF16) for i in range(2)]
                cnt = [0]

                def fin(h, qs, pO, pkey):
                    i = cnt[0] % 2
                    cnt[0] += 1
                    S.dma("sp", gt[i][:], self.s_gT[h * 64:(h + 1) * 64, qs * 512:(qs + 1) * 512], writes=[("gt", i)])
                    S.op("act", "activation", out=o32[i][0:65, :], in_=pO[0:65, :], func=AF.Copy,
                         reads=[("ps", pkey)], writes=[("o32", i)])
                    S.op("dve", "reciprocal", o32[i][64:65, :], o32[i][64:65, :], reads=[("o32", i)], writes=[("o32", i)])
                    S.op("pe", "matmul", self.psum[4][0:64, :], self.ones_f[64:65, 0:64], o32[i][64:65, :], start=True, stop=True,
                         reads=[("o32", i)], writes=[("ps", 4)])
                    S.op("dve", "tensor_tensor", yf[:], o32[i][0:64, :], self.psum[4][0:64, :], ALU.mult,
                         reads=[("o32", i), ("ps", 4)], writes=["yf"])
                    S.op("pool", "tensor_tensor", yb[i][:], yf[:], gt[i][:], ALU.mult, reads=["yf", ("gt", i)], writes=[("yb", i)])
                    S.dma("sp", self.s_yT[h * 64:(h + 1) * 64, qs * 512:(qs + 1) * 512], yb[i][:], reads=[("yb", i)])

                self.attend(4, 113,
                            lambda g, qs: QT[0:113, g, qs * 512:(qs + 1) * 512],
                            lambda g, kc: KT[0:113, g, kc * 128:(kc + 1) * 128],
                            lambda h, kc: Va[:, kc, h, :],
                            lambda g: g, float((64 + 32) ** -0.5), fin)

    def phase_diff(self, l):
        nc, S = self.nc, self.S
        lam_init = 0.8 - 0.6 * math.exp(-0.3 * l)
        with ExitStack() as es:
            QdT = self.sb(es, "QdT", [128, 4, T], BF16)
            KdT = self.sb(es, "KdT", [128, 4, NKEY], BF16)
            Vd = self.sb(es, "Vd", [128, NKEY // 128, 4, 65], BF16)
            self.pt = [self.sb(es, f"pt{i}", [128, 512], BF16) for i in range(2)]
            g_qk = self.sb(es, "g_qk", [128, 64], F32)
            lpb = self.sb(es, "lpb", [128, 128], F32)
            lsm = self.sb(es, "lsm", [128, 72], F32)
            neglam = self.sb(es, "neglam", [128, 1], F32)
            gsub = self.sb(es, "gsub", [64, 1], F32)
            self.bcast_load("sp", g_qk[:], self.diff_qk_g[l].rearrange("a b -> (a b)"), "g_qk")
            self.bcast_load("sp", lpb[:], self.diff_lambda[l, :], "lpb")
            S.dma("sp", gsub[:], self.diff_subln_g[l], writes=["gsub"])
            S.op("dve", "tensor_scalar", gsub[:], gsub[:], float(1.0 - lam_init), None, ALU.mult, reads=["gsub"], writes=["gsub"])
            lp4 = lpb[:].rearrange("p (a b w) -> p a b w", a=2, b=2)
            S.op("dve", "tensor_tensor", lsm[:, 0:64].rearrange("p (a w) -> p a w", a=2), lp4[:, :, 0, :], lp4[:, :, 1, :], ALU.mult,
                 reads=["lpb"], writes=["lsm"])
            S.op("dve", "tensor_reduce", lsm[:, 64:66], lsm[:, 0:64].rearrange("p (a w) -> p a w", a=2), AX.X, ALU.add,
                 reads=["lsm"], writes=["lsm2"])
            S.op("act", "activation", out=lsm[:, 66:68], in_=lsm[:, 64:66], func=AF.Exp, reads=["lsm2"], writes=["lsm3"])
            S.op("dve", "tensor_tensor", neglam[:], lsm[:, 67:68], lsm[:, 66:67], ALU.subtract, reads=["lsm3"], writes=["neglam"])
            S.op("dve", "tensor_scalar", neglam[:], neglam[:], float(-lam_init), None, ALU.add, reads=["neglam"], writes=["neglam"])
            S.dma("sp", Vd[:, 0:NT, :, :].rearrange("p c h e -> p c (h e)"),
                  self.s_vd[0:T, :].rearrange("(c p) e -> p c e", p=128), writes=["V"])
            with ExitStack() as es1:
                cv32 = self.sb(es1, "cv32", [128, 4, 256], F32)
                S.dma("sp", cv32[:], self.c_dv[l].rearrange("(c p) e -> p c e", p=128), writes=["cv32"])
                S.op("pool", "memset", Vd[:, NT:NT + 4, :, :], 1.0, reads=["V"], writes=["V"])
                for c in range(4):
                    S.op("pool", "tensor_copy", Vd[:, NT + c, :, 0:64], cv32[:, c, :].rearrange("p (h e) -> p h e", h=4),
                         reads=["cv32", "V"], writes=["V"])
                S.barrier()
            with ExitStack() as es2:
                dqk = [self.sb(es2, f"dqk{i}", [128, 512], F32) for i in range(2)]
                rc = [self.sb(es2, f"rc{i}", [128, 256], F32) for i in range(2)]
                rs = [self.sb(es2, f"rs{i}", [128, 256], F32) for i in range(2)]
                qa = [self.sb(es2, f"qa{i}", [128, 17], F32) for i in range(2)]
                ka = [self.sb(es2, f"ka{i}", [128, 17], F32) for i in range(2)]
                sq = self.sb(es2, "sq", [128, 256], F32)
                ssb = self.sb(es2, "ssb", [128, 16], F32)
                t1 = self.sb(es2, "t1", [128, 256], F32)
                qn = self.sb(es2, "qn", [128, 8, 32], F32)
                kn = [self.sb(es2, f"kn{i}", [128, 8, 32], F32) for i in range(2)]
                kr = self.sb(es2, "kr", [128, 8, 32], F32)
                Qdst = self.sb(es2, "Qdst", [128, 8, 64], BF16)
                Kdst = self.sb(es2, "Kdst", [128, 8, 64], BF16)
                S.op("pool", "memset", Qdst[:], 0.0, writes=["Qdst"])
                S.op("pool", "memset", Kdst[:], 0.0, writes=["Kdst"])

                def transposes(src, dst_ap, rkey, wkey, pi):
                    pst = self.psum[pi][:, :].bitcast(BF16)
                    for c in range(4):
                        S.op("pe", "transpose", pst[:, c * 128:(c + 1) * 128],
                             src[:, 2 * c:2 * c + 2, :].rearrange("p a w -> p (a w)"), self.ident_bf[:],
                             reads=[rkey], writes=[("ps", pi)])
                    S.op("act", "activation", out=dst_ap, in_=pst[:, 0:512].rearrange("p (c t) -> p c t", c=4),
                         func=AF.Copy, reads=[("ps", pi)], writes=[wkey])

                for t in range(NKEY // 128):
                    b = t % 2
                    tok0 = t * 128
                    ctx = t >= NT
                    S.dma("sp", ka[b][:], self.kaug[tok0:tok0 + 128, :], writes=[("ka", b)])
                    if not ctx:
                        S.dma("sp", dqk[b][:], self.s_dqk[tok0:tok0 + 128, :], writes=[("dqk", b)])
                        S.dma("sp", rc[b][:], self.ropeC[tok0:tok0 + 128, :], writes=["rc"])
                        S.dma("sp", rs[b][:], self.ropeS[tok0:tok0 + 128, :], writes=["rs"])
                        S.dma("sp", qa[b][:], self.qaug[tok0:tok0 + 128, :], writes=[("qa", b)])
                        qv = dqk[b][:, 0:256].rearrange("p (g w) -> p g w", g=8)
                        kv = dqk[b][:, 256:512].rearrange("p (g w) -> p g w", g=8)
                        self.rms_tm(qv, qn[:], 8, 32, g_qk[:, 0:32], sq, ssb, [("dqk", b)], ["qn"])
                        self.rope_tm(qn[:], Qdst[:, :, 0:32], 8, rc[b], rs[b], t1, ["qn"], ["Qdst"])
                        S.op("pool", "tensor_copy", Qdst[:, :, 32:49], qa[b][:].unsqueeze(1).to_broadcast([128, 8, 17]),
                             reads=[("qa", b)], writes=["Qdst"])
                        transposes(Qdst, QdT[:, :, tok0:tok0 + 128], "Qdst", "QT", 4)
                        self.rms_tm(kv, kn[b][:], 8, 32, g_qk[:, 32:64], sq, ssb, [("dqk", b)], [("kn", b)])
                        S.dma("sp", self.o_dk[l, tok0:tok0 + 128, :], kn[b][:].rearrange("p g w -> p (g w)"), reads=[("kn", b)])
                        S.op("dve", "tensor_copy", kr[:], kn[b][:], reads=[("kn", b)], writes=["kr"])
                        self.rope_tm(kr[:], Kdst[:, :, 0:32], 8, rc[b], rs[b], t1, ["kr"], ["Kdst"])
                    else:
                        c0 = tok0 - T
                        S.dma("sp", kn[b][:].rearrange("p g w -> p (g w)"), self.c_dk[l, c0:c0 + 128, :], writes=[("kn", b)])
                        S.op("pool", "tensor_copy", Kdst[:, :, 0:32], kn[b][:], reads=[("kn", b)], writes=["Kdst"])
                    S.op("pool", "tensor_copy", Kdst[:, :, 32:49], ka[b][:].unsqueeze(1).to_broadcast([128, 8, 17]),
                         reads=[("ka", b)], writes=["Kdst"])
                    transposes(Kdst, KdT[:, :, tok0:tok0 + 128], "Kdst", "KT", 6)
                S.barrier()

            with ExitStack() as es3:
                o32 = [self.sb(es3, f"o32{i}", [128, 512], F32) for i in range(2)]
                y1 = self.sb(es3, "y1", [64, 512], F32)
                y2 = self.sb(es3, "y2", [64, 512], F32)
                sqf = self.sb(es3, "sqf", [64, 512], F32)
                gt = [self.sb(es3, f"gt{i}", [64, 512], BF16) for i in range(2)]
                yb = [self.sb(es3, f"yb{i}", [64, 512], BF16) for i in range(2)]
                cnt = [0]

                def fin(g, qs, pO, pkey):
                    h, j = g // 2, g % 2
                    S.op("act", "activation", out=o32[j][0:65, :], in_=pO[0:65, :], func=AF.Copy,
                         reads=[("ps", pkey)], writes=[("o32", j)])
                    S.op("dve", "reciprocal", o32[j][64:65, :], o32[j][64:65, :], reads=[("o32", j)], writes=[("o32", j)])
                    if j == 0:
                        return
                    i = cnt[0] % 2
                    cnt[0] += 1
                    r0 = 256 + h * 64
                    S.dma("sp", gt[i][:], self.s_gT[r0:r0 + 64, qs * 512:(qs + 1) * 512], writes=[("gt", i)])
                    for jj in range(2):
                        S.op("pe", "matmul", self.psum[4 + jj][0:64, :], self.ones_f[64:65, 0:64], o32[jj][64:65, :],
                             start=True, stop=True, reads=[("o32", jj)], writes=[("ps", 4 + jj)])
                    S.op("dve", "tensor_tensor", y1[:], o32[0][0:64, :], self.psum[4][0:64, :], ALU.mult,
                         reads=[("o32", 0), ("ps", 4)], writes=["y1"])
                    S.op("dve", "tensor_tensor", y2[:], o32[1][0:64, :], self.psum[5][0:64, :], ALU.mult,
                         reads=[("o32", 1), ("ps", 5)], writes=["y2"])
                    S.op("dve", "scalar_tensor_tensor", y1[:], y2[:], neglam[0:64, 0:1], y1[:], ALU.mult, ALU.add,
                         reads=["y1", "y2", "neglam"], writes=["y1"])
                    S.op("pool", "tensor_tensor", sqf[:], y1[:], y1[:], ALU.mult, reads=["y1"], writes=["sqf"])
                    S.op("pe", "matmul", self.psum[6][0:64, :], self.ones_f[0:64, 0:64], sqf[:], start=True, stop=True,
                         reads=["sqf"], writes=[("ps", 6)])
                    S.op("dve", "tensor_scalar", y2[:], self.psum[6][0:64, :], 1.0 / 64, EPS, ALU.mult, ALU.add,
                         reads=[("ps", 6)], writes=["y2"])
                    S.op("act", "activation", out=y2[:], in_=y2[:], func=AF.Sqrt, reads=["y2"], writes=["y2"])
                    S.op("dve", "reciprocal", y2[:], y2[:], reads=["y2"], writes=["y2"])
                    S.op("dve", "scalar_tensor_tensor", y1[:], y1[:], gsub[0:64, 0:1], y2[:], ALU.mult, ALU.mult,
                         reads=["y1", "y2", "gsub"], writes=["y1"])
                    S.op("pool", "tensor_tensor", yb[i][:], y1[:], gt[i][:], ALU.mult, reads=["y1", ("gt", i)], writes=[("yb", i)])
                    S.dma("sp", self.s_yT[r0:r0 + 64, qs * 512:(qs + 1) * 512], yb[i][:], reads=[("yb", i)])

                order = [(2 * h + j, qs) for h in range(4) for qs in range(NST) for j in range(2)]
                self.attend(8, 64,
                            lambda g, qs: QdT[64 * (g % 2):64 * (g % 2) + 64, g // 2, qs * 512:(qs + 1) * 512],
                            lambda g, kc: KdT[64 * (g % 2):64 * (g % 2) + 64, g // 2, kc * 128:(kc + 1) * 128],
                            lambda h, kc: Vd[:, kc, h, :],
                            lambda g: g // 2, float(32 ** -0.5), fin, order=order)

    def phase_hgrn(self, l):
        nc, S = self.nc, self.S
        NCH = T // 64
        SL = 512
        with ExitStack() as es:
            rmask = self.sb(es, "rmask", [128, SL], F32)
            mT = self.sb(es, "mT", [64, 256], F32)
            og = self.sb(es, "og", [64, 1], F32)
            S.op("pool", "memset", rmask[:], 1.0, writes=["rmask"])
            S.op("pool", "memset", rmask[:].rearrange("p (c s) -> p c s", s=64)[:, :, 0:1], 0.0, writes=["rmask"])
            S.dma("sp", mT[:], self.maskT[:, :], writes=["mT"])
            S.dma("sp", og[:], self.hgrn_out_g[l], writes=["og"])
            for hp in range(2):
                with ExitStack() as es1:
                    lbl = self.sb(es1, "lbl", [128, 24], F32)
                    vsb = self.sb(es1, "vsb", [64, NCH, 128], BF16)
                    qt = [self.sb(es1, f"qt{d}", [128, T], BF16) for d in range(2)]
                    kt = [self.sb(es1, f"kt{d}", [128, T], BF16) for d in range(2)]
                    qo = [self.sb(es1, f"qo{d}", [128, T], BF16) for d in range(2)]
                    ko = [self.sb(es1, f"ko{d}", [128, T], BF16) for d in range(2)]
                    qb = [self.sb(es1, f"qb{d}", [128, T], BF16) for d in range(2)]
                    khat = [self.sb(es1, f"khat{d}", [64, NCH, 128], BF16) for d in range(2)]
                    stab = [self.sb(es1, f"stab{d}", [128, NCH, 128], BF16) for d in range(2)]
                    etot = [self.sb(es1, f"etot{d}", [128, NCH], F32) for d in range(2)]
                    car = [self.sb(es1, f"car{d}", [128, NCH], F32) for d in range(2)]
                    S.dma("sp", vsb[:], self.s_hv[:, hp * 128:(hp + 1) * 128].rearrange("(c s) e -> s c e", s=64), writes=["vsb"])
                    for d in range(2):
                        S.dma("sp", car[d][:], self.carry[d], writes=[("car", d)])
                    S.dma("sp", lbl[:, 0:8], self.lbT[hp * 128:(hp + 1) * 128, :], writes=["lbl"])
                    S.op("act", "activation", out=lbl[:, 8:16], in_=lbl[:, 0:8], func=AF.Exp, reads=["lbl"], writes=["lbe"])
                    S.op("dve", "tensor_reduce", lbl[:, 16:18], lbl[:, 8:16].rearrange("p (a b) -> p a b", a=2), AX.X, ALU.add,
                         reads=["lbe"], writes=["lbs"])
                    S.op("dve", "reciprocal", lbl[:, 16:18], lbl[:, 16:18], reads=["lbs"], writes=["lbs"])
                    for d in range(2):
                        if l == 0:
                            S.op("dve", "memset", lbl[:, 18 + d:19 + d], 0.0, writes=[("lb", d)])
                        else:
                            S.op("dve", "tensor_reduce", lbl[:, 18 + d:19 + d], lbl[:, 8 + 4 * d + 1:8 + 4 * d + 1 + l], AX.X, ALU.add,
                                 reads=["lbe"], writes=[("lb", d)])
                            S.op("dve", "tensor_tensor", lbl[:, 18 + d:19 + d], lbl[:, 18 + d:19 + d], lbl[:, 16 + d:17 + d], ALU.mult,
                                 reads=[("lb", d), "lbs"], writes=[("lb", d)])
                        S.op("dve", "tensor_scalar", lbl[:, 20 + d:21 + d], lbl[:, 18 + d:19 + d], -1.0, 1.0, ALU.mult, ALU.add,
                             reads=[("lb", d)], writes=[("oml", d)])
                    if int(os.environ.get("KHG", "3")) < 1:
                        continue
                    with ExitStack() as es2:
                        q32 = self.sb(es2, "hq32", [128, SL], F32)
                        z = self.sb(es2, "hz", [128, SL], F32)
                        g = self.sb(es2, "hg", [128, SL], F32)
                        k = self.sb(es2, "hk", [128, SL], F32)
                        bb = self.sb(es2, "hb_", [128, SL], F32)
                        bc = self.sb(es2, "hbc", [128, SL], F32)
                        e1 = self.sb(es2, "he1", [128, SL], F32)
                        khT = self.sb(es2, "khT", [128, SL], BF16)
                        k2 = self.sb(es2, "hk2", [128, SL], F32)
                        c3 = lambda t_: t_[:].rearrange("p (c s) -> p c s", s=64)
                        ncs = SL // 64
                        for sl in range(T // SL):
                            tsl = slice(sl * SL, (sl + 1) * SL)
                            S.dma("sp", q32[:], self.s_hT[hp * 128:(hp + 1) * 128, tsl], writes=["q32"])
                            for d in range(2):
                                r0 = 256 * (1 + d) + hp * 128
                                S.dma("sp", z[:], self.s_hT[r0:r0 + 128, tsl], writes=["z"])
                                S.op("act", "activation", out=z[:], in_=z[:], func=AF.Sigmoid, reads=["z"], writes=["z"])
                                S.op("dve", "tensor_scalar", z[:], z[:], lbl[:, 20 + d:21 + d], lbl[:, 18 + d:19 + d], ALU.mult, ALU.add,
                                     reads=["z", ("lb", d), ("oml", d)], writes=["z"])
                                S.op("act", "activation", out=g[:], in_=z[:], func=AF.Ln, reads=["z"], writes=["g"])
                                S.op("pool", "tensor_scalar", k[:], z[:], -1.0, 1.0, ALU.mult, ALU.add, reads=["z"], writes=["k"])
                                S.op("dve", "tensor_tensor_scan", bb[:], rmask[:], g[:], 0.0, ALU.mult, ALU.add,
                                     reads=["g", "rmask"], writes=["bb"])
                                tot = c3(bb)[:, :, 63:64]
                                totb = tot.to_broadcast([128, ncs, 64])
                                if d == 1:
                                    S.op("dve", "tensor_tensor", bc[:], g[:], bb[:], ALU.subtract, reads=["g", "bb"], writes=["bc"])
                                    S.op("dve", "tensor_tensor", c3(e1), c3(bc), totb, ALU.add, reads=["bc", "bb"], writes=["e1"])
                                    S.op("act", "activation", out=etot[d][:, sl * ncs:(sl + 1) * ncs].unsqueeze(2), in_=tot, func=AF.Exp,
                                         reads=["bb"], writes=[("etot", d)])
                                    S.op("dve", "tensor_copy", bb[:], e1[:], reads=["e1"], writes=["bb"])
                                    totb = c3(bb)[:, :, 0:1].to_broadcast([128, ncs, 64])
                                else:
                                    S.op("act", "activation", out=etot[d][:, sl * ncs:(sl + 1) * ncs].unsqueeze(2), in_=tot, func=AF.Exp,
                                         reads=["bb"], writes=[("etot", d)])
                                c32 = lambda t_: t_[:].rearrange("p (c s) -> p c s", s=32)
                                rix = 31 if d == 0 else 32
                                S.op("dve", "tensor_tensor", c3(bc), c3(bb), c3(bb)[:, :, rix:rix + 1].to_broadcast([128, ncs, 64]),
                                     ALU.subtract, reads=["bb"], writes=["bc"])
                                S.op("pool", "tensor_scalar", k2[:], bc[:], 0.0, None, ALU.max, reads=["bc"], writes=["k2"])
                                S.op("dve", "tensor_scalar", bc[:], bc[:], 0.0, None, ALU.min, reads=["bc"], writes=["bc"])
                                S.op("act", "activation", out=e1[:], in_=bc[:], func=AF.Exp, reads=["bc"], writes=["e1"])
                                S.op("pool", "tensor_tensor", qo[d][:, tsl], q32[:], e1[:], ALU.mult, reads=["q32", "e1"], writes=[("qo", d)])
                                S.op("act", "activation", out=e1[:], in_=k2[:], func=AF.Exp, scale=-1.0, reads=["k2"], writes=["e1"])
                                S.op("pool", "tensor_tensor", ko[d][:, tsl], k[:], e1[:], ALU.mult, reads=["k", "e1"], writes=[("ko", d)])
                                S.op("dve", "tensor_tensor", c32(bc), c32(bb), c32(bb)[:, :, 15:16].to_broadcast([128, 2 * ncs, 32]),
                                     ALU.subtract, reads=["bb"], writes=["bc"])
                                S.op("dve", "tensor_scalar", bc[:], bc[:], 40.0, -40.0, ALU.min, ALU.max, reads=["bc"], writes=["bc"])
                                S.op("act", "activation", out=e1[:], in_=bc[:], func=AF.Exp, reads=["bc"], writes=["e1"])
                                S.op("pool", "tensor_tensor", qt[d][:, tsl], q32[:], e1[:], ALU.mult, reads=["q32", "e1"], writes=[("qt", d)])
                                S.op("dve", "reciprocal", e1[:], e1[:], reads=["e1"], writes=["e1"])
                                S.op("pool", "tensor_tensor", kt[d][:, tsl], k[:], e1[:], ALU.mult, reads=["k", "e1"], writes=[("kt", d)])
                                S.op("act", "activation", out=e1[:], in_=bb[:], func=AF.Exp, reads=["bb"], writes=["e1"])
                                S.op("pool", "tensor_tensor", qb[d][:, tsl], q32[:], e1[:], ALU.mult, reads=["q32", "e1"], writes=[("qb", d)])
                                S.op("dve", "tensor_tensor", c3(bc), totb, c3(bb), ALU.subtract, reads=["bb"], writes=["bc"])
                                S.op("act", "activation", out=e1[:], in_=bc[:], func=AF.Exp, reads=["bc"], writes=["e1"])
                                S.op("pool", "tensor_tensor", khT[:], k[:], e1[:], ALU.mult, reads=["k", "e1"], writes=["khT"])
                                for c4 in range(ncs // 8):
                                    pi = 4 + c4 % 2
                                    pst = self.psum[pi][:, :].bitcast(BF16)
                                    for cc in range(8):
                                        c = c4 * 8 + cc
                                        S.op("pe", "transpose", pst[0:64, cc * 128:(cc + 1) * 128], khT[:, c * 64:(c + 1) * 64],
                                             self.ident_bf[:], reads=["khT"], writes=[("ps", pi)])
                                    c0 = sl * ncs + c4 * 8
                                    S.op("act", "activation", out=khat[d][:, c0:c0 + 8, :],
                                         in_=pst[0:64, 0:1024].rearrange("p (c e) -> p c e", c=8), func=AF.Copy,
                                         reads=[("ps", pi)], writes=[("khat", d)])
                        S.barrier()
                    khg = int(os.environ.get("KHG", "3"))
                    if khg < 2:
                        continue
                    with ExitStack() as es3:
                        Scur = [self.sb(es3, f"Scur{d}", [128, 128], F32) for d in range(2)]
                        Sout = [[self.sb(es3, f"Sout{d}{i}", [128, 128], F32) for i in range(2)] for d in range(2)]
                        for d in range(2):
                            S.op("pool", "memset", Scur[d][:], 0.0, writes=[("Scur", d)])
                            for hh in range(2):
                                S.dma("sp", Scur[d][hh * 64:(hh + 1) * 64, hh * 64:(hh + 1) * 64], self.s0[l, d, 2 * hp + hh],
                                      reads=[("Scur", d)], writes=[("Scur", d)])
                        for step in range(NCH):
                            for d in range(2):
                                c = step if d == 0 else NCH - 1 - step
                                i = step % 2
                                pi = 4 + d
                                S.op("pool", "tensor_copy", stab[d][:, c, :], Scur[d][:], reads=[("Scur", d)], writes=[("stab", d)])
                                S.op("pe", "matmul", self.psum[pi][:, 0:128], khat[d][:, c, :], vsb[:, c, :], start=True, stop=True,
                                     reads=[("khat", d), "vsb"], writes=[("ps", pi)])
                                S.op("dve", "scalar_tensor_tensor", Sout[d][i][:], Scur[d][:], etot[d][:, c:c + 1], self.psum[pi][:, 0:128],
                                     ALU.mult, ALU.add, reads=[("Scur", d), ("etot", d), ("ps", pi)], writes=[("Sout", d, i)])
                                is_end = (c % 4 == 3) if d == 0 else (c % 4 == 0)
                                if is_end:
                                    for hh in range(2):
                                        S.dma("sp", self.o_state[l, c // 4, d, 2 * hp + hh],
                                              Sout[d][i][hh * 64:(hh + 1) * 64, hh * 64:(hh + 1) * 64], reads=[("Sout", d, i)])
                                cn = c + 1 if d == 0 else c - 1
                                if 0 <= cn < NCH:
                                    S.op("dve", "tensor_scalar", Scur[d][:], Sout[d][i][:], car[d][:, cn:cn + 1], None, ALU.mult,
                                         reads=[("Sout", d, i), ("car", d)], writes=[("Scur", d)])
                        S.barrier()
                    if khg < 3:
                        continue
                    with ExitStack() as es4:
                        Af = [self.sb(es4, f"Af{i}", [64, 256], F32) for i in range(2)]
                        Am = [self.sb(es4, f"Am{i}", [64, 256], BF16) for i in range(2)]
                        o32 = self.sb(es4, "ho32", [64, 512], F32)
                        sqf = self.sb(es4, "hsq", [64, 512], F32)
                        rst = self.sb(es4, "hrst", [64, 512], F32)
                        gt = [self.sb(es4, f"hgt{i}", [64, 512], BF16) for i in range(2)]
                        yb = [self.sb(es4, f"hyb{i}", [64, 512], BF16) for i in range(2)]
                        for st in range(NST):
                            for cc in range(8):
                                c = st * 8 + cc
                                i = c % 2
                                csl = slice(c * 64, (c + 1) * 64)
                                for hh in range(2):
                                    pA = self.psum[hh]
                                    hs_ = slice(hh * 64, (hh + 1) * 64)
                                    h1 = slice(c * 64, c * 64 + 32)
                                    h2 = slice(c * 64 + 32, c * 64 + 64)
                                    for d in range(2):
                                        col = d * 64
                                        for (sh, th, so, to) in ((h1, h1, 0, 0), (h2, h2, 32, 32), (h1, h2, 0, 32), (h2, h1, 32, 0)):
                                            cross_valid = (so == 0 and to == 32) if d == 0 else (so == 32 and to == 0)
                                            kk_, qq_ = (ko[d], qo[d]) if cross_valid else (kt[d], qt[d])
                                            S.op("pe", "matmul", pA[so:so + 32, col + to:col + to + 32], kk_[hs_, sh], qq_[hs_, th],
                                                 start=True, stop=True, reads=[("kt", d), ("qt", d), ("ko", d), ("qo", d)],
                                                 writes=[("ps", hh)])
                                for hh in range(2):
                                    S.op("dve", "tensor_tensor", Af[i][:, hh * 128:(hh + 1) * 128], self.psum[hh][0:64, 0:128],
                                         mT[:, hh * 128:(hh + 1) * 128], ALU.mult, reads=[("ps", hh), "mT"], writes=[("Af", i)])
                                S.op("pool", "tensor_copy", Am[i][:], Af[i][:], reads=[("Af", i)], writes=[("Am", i)])
                                for hh in range(2):
                                    pO = self.psum[2 + hh]
                                    pI = self.psum[4 + hh]
                                    hs = slice(hh * 64, (hh + 1) * 64)
                                    ocol = slice(cc * 64, (cc + 1) * 64)
                                    kp2 = os.environ.get("KP2", "")
                                    if "nointra" in kp2:
                                        continue
                                    S.op("pe", "matmul", pO[0:64, ocol], vsb[:, c, hs], Am[i][:, (hh * 2) * 64:(hh * 2 + 1) * 64],
                                         start=True, stop=False, reads=["vsb", ("Am", i)], writes=[("ps", 2 + hh)])
                                    S.op("pe", "matmul", pO[0:64, ocol], vsb[:, c, hs], Am[i][:, (hh * 2 + 1) * 64:(hh * 2 + 2) * 64],
                                         start=False, stop=True, reads=["vsb", ("Am", i)], writes=[("ps", 2 + hh)])
                                    for d in range(0 if "nointer" in kp2 else 2):
                                        S.op("pe", "matmul", pI[0:64, ocol], stab[d][hs, c, hs], qb[d][hs, csl],
                                             start=(d == 0), stop=(d == 1), reads=[("stab", d), ("qb", d)], writes=[("ps", 4 + hh)])
                            for hh in range(2):
                                h = 2 * hp + hh
                                gi = hh
                                r0 = 512 + h * 64
                                tsl = slice(st * 512, (st + 1) * 512)
                                S.dma("sp", gt[gi][:], self.s_gT[r0:r0 + 64, tsl], writes=[("gt", gi)])
                                S.op("act", "activation", out=o32[:], in_=self.psum[2 + hh][0:64, :], func=AF.Copy,
                                     reads=[("ps", 2 + hh)], writes=["o32"])
                                S.op("dve", "tensor_tensor", o32[:], o32[:], self.psum[4 + hh][0:64, :], ALU.add,
                                     reads=["o32", ("ps", 4 + hh)], writes=["o32"])
                                S.op("pool", "tensor_tensor", sqf[:], o32[:], o32[:], ALU.mult, reads=["o32"], writes=["sqf"])
                                S.op("pe", "matmul", self.psum[6][0:64, :], self.ones_f[0:64, 0:64], sqf[:], start=True, stop=True,
                                     reads=["sqf"], writes=[("ps", 6)])
                                S.op("dve", "tensor_scalar", rst[:], self.psum[6][0:64, :], 1.0 / 64, EPS, ALU.mult, ALU.add,
                                     reads=[("ps", 6)], writes=["rst"])
                                S.op("act", "activation", out=rst[:], in_=rst[:], func=AF.Sqrt, reads=["rst"], writes=["rst"])
                                S.op("dve", "reciprocal", rst[:], rst[:], reads=["rst"], writes=["rst"])
                                S.op("dve", "scalar_tensor_tensor", o32[:], o32[:], og[0:64, 0:1], rst[:], ALU.mult, ALU.mult,
                                     reads=["o32", "rst", "og"], writes=["o32"])
                                S.op("pool", "tensor_tensor", yb[gi][:], o32[:], gt[gi][:], ALU.mult, reads=["o32", ("gt", gi)],
                                     writes=[("yb", gi)])
                                S.dma("sp", self.s_yT[r0:r0 + 64, tsl], yb[gi][:], reads=[("yb", gi)])
                        S.barrier()

    def hy_wrap(self, z, m, key):
        S = self.S
        PI = math.pi
        S.op("dve", "tensor_scalar", m, z, PI, None, ALU.is_gt, reads=[key], writes=[key + "m"])
        S.op("dve", "scalar_tensor_tensor", z, m, -2 * PI, z, ALU.mult, ALU.add, reads=[key, key + "m"], writes=[key])
        S.op("dve", "tensor_scalar", m, z, -PI, None, ALU.is_lt, reads=[key], writes=[key + "m"])
        S.op("dve", "scalar_tensor_tensor", z, m, 2 * PI, z, ALU.mult, ALU.add, reads=[key, key + "m"], writes=[key])

    def phase_hyena(self, l):
        nc, S = self.nc, self.S
        with ExitStack() as es:
            F1z = self.sb(es, "F1z", [32, 128], F32)
            F1f = self.sb(es, "F1f", [64, 128], F32)
            Em = self.sb(es, "Em", [128, 512], BF16)
            Qm = self.sb(es, "Qm", [64, 128 * 64], BF16)
            S.dma("sp", F1z[:], self.F1z[:, :], writes=["F1z"])
            S.dma("sp", F1f[:], self.F1f[:, :], writes=["F1f"])
            S.dma("sp", Em[:], self.Em[:, :], writes=["Em"])
            S.dma("sp", Qm[:], self.Qm[:, :], writes=["Qm"])
            with ExitStack() as es1:
                w1 = self.sb(es1, "hw1", [33, 64], F32)
                w2 = self.sb(es1, "hw2", [64, 64], F32)
                w3 = self.sb(es1, "hw3", [64, 1024], F32)
                cols = self.sb(es1, "hcols", [64, 8], F32)
                nd = self.sb(es1, "hnd", [128, 2], F32)
                ft = [self.sb(es1, f"hft{i}", [33, 512], F32) for i in range(2)]
                tl = [self.sb(es1, f"htl{i}", [128, 512], F32) for i in range(2)]
                dm = [self.sb(es1, f"hdm{i}", [128, 2, 512], F32) for i in range(2)]
                z1 = self.sb(es1, "hz1", [64, 512], F32)
                m1 = self.sb(es1, "hm1", [64, 512], F32)
                h1 = self.sb(es1, "hh1", [64, 512], F32)
                h2 = self.sb(es1, "hh2", [64, 512], F32)
                win = self.sb(es1, "hwin", [128, 2, 512], F32)
                fa = self.sb(es1, "hfa", [128, 512], F32)
                fo = [self.sb(es1, f"hfo{i}", [128, 512], F32) for i in range(2)]
                S.dma("sp", w1[:], self.hy_w1[l], writes=["w1"])
                S.dma("sp", w2[:], self.hy_w2[l], writes=["w2"])
                S.dma("sp", w3[:], self.hy_w3[l], writes=["w3"])
                S.dma("sp", cols[:, 0:4], self.hy_cols[l], writes=["cols"])
                S.dma("sp", nd[:, 0:1], self.negdelta[0:128, :], writes=["nd"])
                S.dma("sp", nd[:, 1:2], self.negdelta[128:256, :], writes=["nd"])
                S.op("dve", "tensor_tensor", cols[:, 4:6], cols[:, 0:2], cols[:, 2:4], ALU.mult, reads=["cols"], writes=["cols2"])
                for sl in range(16):
                    b = sl % 2
                    ssl = slice(sl * 512, (sl + 1) * 512)
                    S.dma("sp", ft[b][:], self.featsT[:, ssl], writes=[("ft", b)])
                    S.dma("sp", tl[b][:], self.featsT[0, ssl].partition_broadcast(128), writes=[("tl", b)])
                    for dd in range(2):
                        S.dma("sp", dm[b][:, dd, :], self.dmask[dd, ssl].partition_broadcast(128), writes=[("dm", b)])
                    S.op("pe", "matmul", self.psum[0][0:64, :], w1[:], ft[b][:], start=True, stop=True,
                         reads=["w1", ("ft", b)], writes=[("ps", 0)])
                    S.op("dve", "tensor_scalar", z1[:], self.psum[0][0:64, :], cols[:, 2:3], cols[:, 4:5], ALU.mult, ALU.add,
                         reads=[("ps", 0), "cols", "cols2"], writes=["z1"])
                    self.hy_wrap(z1[:], m1[:], "z1")
                    S.op("act", "activation", out=h1[:], in_=z1[:], func=AF.Sin, reads=["z1"], writes=["h1"])
                    S.op("pe", "matmul", self.psum[1][0:64, :], w2[:], h1[:], start=True, stop=True,
                         reads=["w2", "h1"], writes=[("ps", 1)])
                    S.op("dve", "tensor_scalar", z1[:], self.psum[1][0:64, :], cols[:, 3:4], cols[:, 5:6], ALU.mult, ALU.add,
                         reads=[("ps", 1), "cols", "cols2"], writes=["z1"])
                    self.hy_wrap(z1[:], m1[:], "z1")
                    S.op("act", "activation", out=h2[:], in_=z1[:], func=AF.Sin, reads=["z1"], writes=["h2"])
                    for cch in range(2):
                        S.op("act", "activation", out=win[:, cch, :], in_=tl[b][:], func=AF.Exp, scale=nd[:, cch:cch + 1],
                             reads=[("tl", b), "nd"], writes=[("win", cch)])
                    it = 0
                    for o in range(2):
                        for cch in range(2):
                            for dd in range(2):
                                c0 = (o * 2 + dd) * 256 + cch * 128
                                S.op("pe", "matmul", self.psum[2 + dd][:, :], w3[:, c0:c0 + 128], h2[:], start=True, stop=True,
                                     reads=["w3", "h2"], writes=[("ps", 2 + dd)])
                            S.op("dve", "tensor_tensor", fa[:], self.psum[2][:, :], dm[b][:, 0, :], ALU.mult,
                                 reads=[("ps", 2), ("dm", b)], writes=["fa"])
                            S.op("dve", "tensor_tensor", fo[it % 2][:], self.psum[3][:, :], dm[b][:, 1, :], ALU.mult,
                                 reads=[("ps", 3), ("dm", b)], writes=[("fo", it % 2)])
                            S.op("pool", "tensor_tensor", fo[it % 2][:], fo[it % 2][:], fa[:], ALU.add,
                                 reads=[("fo", it % 2), "fa"], writes=[("fo", it % 2)])
                            S.op("pool", "tensor_tensor", fo[it % 2][:], fo[it % 2][:], win[:, cch, :], ALU.mult,
                                 reads=[("fo", it % 2), ("win", cch)], writes=[("fo", it % 2)])
                            S.dma("sp", self.s_filtT[o, cch * 128:(cch + 1) * 128, ssl], fo[it % 2][:], reads=[("fo", it % 2)])
                            it += 1
                S.barrier()

            with ExitStack() as es2:
                AB = self.sb(es2, "hyAB", [128, 16384], BF16)
                Y = self.sb(es2, "hyY", [128, 2, 128, 64], BF16)
                xb = [self.sb(es2, f"hyxb{i}", [64, 8, 128], F32) for i in range(2)]
                Gt = [self.sb(es2, f"hyG{i}", [128, 4, 384], BF16) for i in range(2)]
                Xg = [self.sb(es2, f"hyXg{i}", [128, 2, 4, 128], F32) for i in range(2)]
                Hg = [self.sb(es2, f"hyHg{i}", [128, 2, 4, 128], F32) for i in range(2)]
                pt1 = self.sb(es2, "hyp1", [128, 4, 128], F32)
                pt2 = self.sb(es2, "hyp2", [128, 4, 128], F32)
                A4 = AB[:, :].rearrange("p (r k c) -> p r k c", r=2, k=64)
                Bt = AB[0:64, :].rearrange("p (r m c) -> p r m c", r=2, m=128)

                def forward(src, K1, F1, key_f1, mode, hidx):
                    for cg in range(16):
                        b = cg % 2
                        S.dma("sp", xb[b][0:K1, :, :], src[cg * 8:(cg + 1) * 8, :].rearrange("c (a n) -> a c n", n=128),
                              writes=[("xb", b)])
                        for q4 in range(2):
                            pi = q4 % 2
                            for cc in range(4):
                                ci = q4 * 4 + cc
                                S.op("pe", "matmul", self.psum[pi][:, cc * 128:(cc + 1) * 128], xb[b][0:K1, ci, :], F1[0:K1, :],
                                     start=True, stop=True, reads=[("xb", b), key_f1], writes=[("ps", pi)])
                            c0 = cg * 8 + q4 * 4
                            S.op("act", "activation", out=A4[:, :, :, c0:c0 + 4].rearrange("p r k c -> p c r k"),
                                 in_=self.psum[pi][:, :].rearrange("p (c r k) -> p c r k", c=4, r=2), func=AF.Copy,
                                 reads=[("ps", pi)], writes=["A"])
                    for kg in range(16):
                        b = kg % 2
                        S.dma("sp", Gt[b][:], self.Gm[kg * 4:(kg + 1) * 4].rearrange("k n x -> n k x"), writes=[("G", b)])
                        if mode == "Y":
                            S.dma("sp", Hg[b][:], self.s_H[hidx].rearrange("p (r k c) -> p r k c", r=2, k=64)[:, :, kg * 4:(kg + 1) * 4, :],
                                  writes=[("Hg", b)])
                        for kk in range(4):
                            k1 = kg * 4 + kk
                            pi = 2 + kk % 2
                            ps = self.psum[pi]
                            S.op("pe", "matmul", ps[:, 0:128], Gt[b][:, kk, 0:128], A4[:, 0, k1, :], start=True, stop=False,
                                 reads=[("G", b), "A"], writes=[("ps", pi)])
                            S.op("pe", "matmul", ps[:, 0:128], Gt[b][:, kk, 256:384], A4[:, 1, k1, :], start=False, stop=True,
                                 reads=[("G", b), "A"], writes=[("ps", pi)])
                            S.op("pe", "matmul", ps[:, 128:256], Gt[b][:, kk, 128:256], A4[:, 0, k1, :], start=True, stop=False,
                                 reads=[("G", b), "A"], writes=[("ps", pi)])
                            S.op("pe", "matmul", ps[:, 128:256], Gt[b][:, kk, 0:128], A4[:, 1, k1, :], start=False, stop=True,
                                 reads=[("G", b), "A"], writes=[("ps", pi)])
                            S.op("act", "activation", out=Xg[b][:, :, kk, :], in_=ps[:, 0:256].rearrange("p (r c) -> p r c", r=2),
                                 func=AF.Copy, reads=[("ps", pi)], writes=[("Xg", b)])
                        if mode == "H":
                            S.dma("sp", self.s_H[hidx].rearrange("p (r k c) -> p r k c", r=2, k=64)[:, :, kg * 4:(kg + 1) * 4, :],
                                  Xg[b][:], reads=[("Xg", b)])
                        else:
                            Xr, Xi, Hr, Hi = Xg[b][:, 0], Xg[b][:, 1], Hg[b][:, 0], Hg[b][:, 1]
                            ksl = slice(kg * 4, (kg + 1) * 4)
                            S.op("dve", "tensor_tensor", pt1[:], Xr, Hr, ALU.mult, reads=[("Xg", b), ("Hg", b)], writes=["pt1"])
                            S.op("pool", "tensor_tensor", pt2[:], Xi, Hi, ALU.mult, reads=[("Xg", b), ("Hg", b)], writes=["pt2"])
                            S.op("pool", "tensor_tensor", Y[:, 0, :, ksl].rearrange("p c k -> p k c"), pt1[:], pt2[:], ALU.subtract,
                                 reads=["pt1", "pt2"], writes=["Y"])
                            S.op("dve", "tensor_tensor", pt1[:], Xr, Hi, ALU.mult, reads=[("Xg", b), ("Hg", b)], writes=["pt1"])
                            S.op("pool", "tensor_tensor", pt2[:], Xi, Hr, ALU.mult, reads=[("Xg", b), ("Hg", b)], writes=["pt2"])
                            S.op("pool", "tensor_tensor", Y[:, 1, :, ksl].rearrange("p c k -> p k c"), pt1[:], pt2[:], ALU.add,
                                 reads=["pt1", "pt2"], writes=["Y"])

                def inverse(yT):
                    E3 = Em[:, :].rearrange("p (v x) -> p v x", v=2)
                    Q4 = Qm[:, :].rearrange("p (m v b) -> p m v b", m=128, v=2)
                    for half in range(2):
                        for cp in range(32):
                            pi = 4 + cp % 2
                            for cc in range(2):
                                c = half * 64 + cp * 2 + cc
                                S.op("pe", "matmul", self.psum[pi][0:64, cc * 256:(cc + 1) * 256], Y[:, 0, c, :], E3[:, 0, :],
                                     start=True, stop=False, reads=["Y", "Em"], writes=[("ps", pi)])
                                S.op("pe", "matmul", self.psum[pi][0:64, cc * 256:(cc + 1) * 256], Y[:, 1, c, :], E3[:, 1, :],
                                     start=False, stop=True, reads=["Y", "Em"], writes=[("ps", pi)])
                            S.op("act", "activation", out=Bt[:, :, :, cp * 2:cp * 2 + 2].rearrange("p r m c -> p c r m"),
                                 in_=self.psum[pi][0:64, :].rearrange("p (c r m) -> p c r m", c=2, r=2), func=AF.Copy,
                                 reads=[("ps", pi)], writes=["Bt"])
                        for mg in range(8):
                            pi = 6 + mg % 2
                            for mm in range(16):
                                ma = mg * 16 + mm
                                S.op("pe", "matmul", self.psum[pi][0:64, mm * 32:(mm + 1) * 32], Bt[:, 0, ma, :], Q4[:, ma, 0, :],
                                     start=True, stop=False, reads=["Bt", "Qm"], writes=[("ps", pi)])
                                S.op("pe", "matmul", self.psum[pi][0:64, mm * 32:(mm + 1) * 32], Bt[:, 1, ma, :], Q4[:, ma, 1, :],
                                     start=False, stop=True, reads=["Bt", "Qm"], writes=[("ps", pi)])
                            S.op("act", "activation",
                                 out=yT[half * 64:(half + 1) * 64, :].rearrange("p (b a) -> p a b", a=128)[:, mg * 16:(mg + 1) * 16, :],
                                 in_=self.psum[pi][0:64, :].rearrange("p (a b) -> p a b", a=16), func=AF.Copy,
                                 reads=[("ps", pi)], writes=["yT"])

                for o in range(2):
                    for cch in range(2):
                        forward(self.s_filtT[o, cch * 128:(cch + 1) * 128, :], 64, F1f, "F1f", "H", o * 2 + cch)
                S.barrier()

                with ExitStack() as es3:
                    u = self.sb(es3, "hyu", [128, T], F32)
                    us = self.sb(es3, "hyus", [128, T], F32)
                    zc = self.sb(es3, "hyz", [128, T], F32)
                    cw = self.sb(es3, "hycw", [128, 4], F32)
                    bm = self.sb(es3, "hybm", [128, 2, 16], F32)
                    bia = self.sb(es3, "hybia", [128, 2], F32)
                    yT = us
                    ub = u[:, :].bitcast(BF16)
                    S.dma("sp", bm[:, 0, :], self.bmask[0], writes=["bm"])
                    S.dma("sp", bm[:, 1, :], self.bmask[1], writes=["bm"])
                    u3 = lambda t_: t_[:].rearrange("p (s t) -> p s t", s=16)
                    for cch in range(2):
                        for o in range(2):
                            S.dma("sp", bia[:, o:o + 1], self.hy_biasT[l, o, cch * 128:(cch + 1) * 128, :], writes=["bia"])
                        for part in range(3):
                            r0 = part * 256 + cch * 128
                            S.dma("sp", u[:], self.s_huT[r0:r0 + 128, :], writes=["u"])
                            S.dma("sp", cw[:], self.hy_convT[l, r0:r0 + 128, :], writes=["cw"])
                            S.op("pool", "memset", us[:, 0:1], 0.0, writes=["us"])
                            S.op("pool", "tensor_copy", us[:, 1:T], u[:, 0:T - 1], reads=["u"], writes=["us"])
                            S.op("dve", "tensor_tensor", u3(us)[:, :, 0:1], u3(us)[:, :, 0:1], bm[:, 0, :].unsqueeze(2), ALU.mult,
                                 reads=["us", "bm"], writes=["us"])
                            S.op("dve", "tensor_scalar", zc[:], u[:], cw[:, 1:2], cw[:, 3:4], ALU.mult, ALU.add,
                                 reads=["u", "cw"], writes=["zc"])
                            S.op("dve", "scalar_tensor_tensor", zc[:], us[:], cw[:, 0:1], zc[:], ALU.mult, ALU.add,
                                 reads=["us", "zc", "cw"], writes=["zc"])
                            S.op("pool", "memset", us[:, T - 1:T], 0.0, reads=["us"], writes=["us"])
                            S.op("pool", "tensor_copy", us[:, 0:T - 1], u[:, 1:T], reads=["u"], writes=["us"])
                            S.op("dve", "tensor_tensor", u3(us)[:, :, 255:256], u3(us)[:, :, 255:256], bm[:, 1, :].unsqueeze(2), ALU.mult,
                                 reads=["us", "bm"], writes=["us"])
                            S.op("dve", "scalar_tensor_tensor", zc[:], us[:], cw[:, 2:3], zc[:], ALU.mult, ALU.add,
                                 reads=["us", "zc", "cw"], writes=["zc"])
                            S.dma("sp", self.s_z[part, :, :], zc[:], reads=["zc"])
                        S.barrier()
                        for o in range(2):
                            zsrc = self.s_z[0] if o == 0 else self.s_z[3]
                            forward(zsrc, 32, F1z, "F1z", "Y", o * 2 + cch)
                            inverse(yT)
                            S.dma("sp", zc[:], zsrc, writes=["zc"])
                            S.dma("sp", u[:], self.s_z[1 + o], writes=["u"])
                            S.op("dve", "scalar_tensor_tensor", zc[:], zc[:], bia[:, o:o + 1], yT[:], ALU.mult, ALU.add,
                                 reads=["zc", "yT", "bia"], writes=["zc"])
                            S.op("pool", "tensor_tensor", zc[:], zc[:], u[:], ALU.mult, reads=["zc", "u"], writes=["zc"])
                            if o == 0:
                                S.dma("sp", self.s_z[3], zc[:], reads=["zc"])
                            else:
                                r0 = 768 + cch * 128
                                S.dma("sp", ub[:, 0:T], self.s_gT[r0:r0 + 128, :], reads=["u"], writes=["u"])
                                S.op("pool", "tensor_tensor", ub[:, T:2 * T], zc[:], ub[:, 0:T], ALU.mult, reads=["zc", "u"], writes=["u"])
                                S.dma("sp", self.s_yT[r0:r0 + 128, :], ub[:, T:2 * T], reads=["u"])
                            S.barrier()

    def phase_po(self, l):
        nc, S = self.nc, self.S
        xsrc = self.x_in if l == 0 else self.y
        with ExitStack() as es:
            wo32 = self.sb(es, "wo32", [128, 8, D], F32)
            wo16 = self.sb(es, "wo16", [128, 8, D], BF16)
            yT = [self.sb(es, f"yT{i}", [128, 8, 512], BF16) for i in range(2)]
            xin = [self.sb(es, f"xo{i}", [128, 4, D], F32) for i in range(2)]
            xo = [self.sb(es, f"xn{i}", [128, D], F32) for i in range(2)]
            for k in range(8):
                S.dma("sp", wo32[:, k, :], self.w_out[l, k * 128:(k + 1) * 128, :], writes=[("wo32", k)])
                S.op("dve" if k % 2 == 0 else "pool", "tensor_tensor", wo16[:, k, :], wo32[:, k, :], self.gate_b[:], ALU.mult,
                     reads=[("wo32", k)], writes=[("wo16", k)])

            def load(st):
                b = st % 2
                S.dma("sp", yT[b][:], self.s_yT[:, st * 512:(st + 1) * 512].rearrange("(k p) t -> p k t", p=128),
                      writes=[("yT", b)])
                S.dma("sp", xin[b][:], xsrc[st * 512:(st + 1) * 512, :].rearrange("(j p) d -> p j d", p=128),
                      writes=[("xo", b)])

            load(0)
            for st in range(NST):
                b = st % 2
                if st + 1 < NST:
                    load(st + 1)
                for j in range(4):
                    tok0 = st * 512 + j * 128
                    xb = j % 2
                    for n in range(2):
                        pi = (j * 2 + n) % 4
                        ps = self.psum[pi]
                        for k in range(8):
                            S.op("pe", "matmul", ps[:, :], yT[b][:, k, j * 128:(j + 1) * 128], wo16[:, k, n * 512:(n + 1) * 512],
                                 start=(k == 0), stop=(k == 7), reads=[("yT", b), ("wo16", k)], writes=[("ps", pi)])
                        S.op("dve", "tensor_tensor", xo[xb][:, n * 512:(n + 1) * 512], ps[:, :], xin[b][:, j, n * 512:(n + 1) * 512],
                             ALU.add, reads=[("ps", pi), ("xo", b)], writes=[("xn", xb, n)])
                    S.dma("sp", self.y[tok0:tok0 + 128, :], xo[xb][:], reads=[("xn", xb, 0), ("xn", xb, 1)])


def host_consts():
    c = {}
    c["ident"] = np.eye(128, dtype=np.float32)
    return c


def hyena_tables(is_prompt):
    f32 = np.float32
    N = 8192
    n = np.arange(N)
    t = {}
    if not is_prompt:
        L = 4096
        tt = np.where(n < L, n, 2 * L - 1 - n)
        dm0 = (n < L)
        dm1 = ~dm0
    else:
        L = 256
        tt = np.clip(np.where(n < 256, n, 511 - n), 0, 255)
        dm0 = (n < 256)
        dm1 = (n >= 256) & (n < 512)
    tl = np.linspace(0.0, 1.0, L, dtype=f32)[tt]
    w = (f32(2.0 * math.pi) * np.arange(L, dtype=f32) / f32(L)).astype(f32)[tt]
    fb = np.linspace(1e-4, 15, 16, dtype=f32)
    ang = (w[:, None] * fb[None, :]).astype(f32)
    feats = np.concatenate([tl[:, None], np.cos(ang), -np.sin(ang)], axis=-1).astype(f32)
    t["featsT"] = np.ascontiguousarray(feats.T)
    t["dmask"] = np.stack([dm0, dm1]).astype(f32)
    bm = np.ones((2, 128, 16), f32)
    if is_prompt:
        bm[:] = 0.0
    t["bmask"] = bm
    two_pi = 2.0 * math.pi
    k1 = np.arange(64)
    if not is_prompt:
        n1 = np.arange(64)
        a = two_pi * ((n1[:, None] * k1[None, :]) % 64) / 64.0
        F1 = np.concatenate([np.cos(a), -np.sin(a)], axis=1)
        F1z, F1f = F1[:32], F1
        n2 = np.arange(128)[None, :, None]
        k2 = np.arange(128)[None, None, :]
        idx = (n2 * (k1[:, None, None] + 64 * k2)) % 8192
        th = two_pi * idx / 8192.0
        ma = np.arange(128)[None, :, None]
        mb = np.arange(32)[None, None, :]
        qi = (128 * mb * k1[:, None, None] + ma * k1[:, None, None]) % 8192
        ps = two_pi * qi / 8192.0
        Qr, Qi = np.cos(ps) / 8192.0, np.sin(ps) / 8192.0
    else:
        p = k1 // 4
        kp = k1 % 4
        n1 = np.arange(64)
        a = two_pi * (((n1[:, None] % 2) * kp[None, :]) % 4) / 4.0
        sel = ((n1[:, None] // 2) == p[None, :])
        F1z = np.concatenate([np.cos(a) * sel, -np.sin(a) * sel], axis=1)[:32]
        a2 = two_pi * ((n1[:, None] * kp[None, :]) % 4) / 4.0
        sel2 = (n1[:, None] < 4)
        F1f = np.concatenate([np.cos(a2) * sel2, -np.sin(a2) * sel2], axis=1)
        n2 = np.arange(128)[None, :, None]
        k2 = np.arange(128)[None, None, :]
        idx = (n2 * (kp[:, None, None] + 4 * k2)) % 512
        th = two_pi * idx / 512.0
        ma = np.arange(128)[None, :, None]
        mb = np.arange(32)[None, None, :]
        qi = (128 * (mb % 2) * kp[:, None, None] + ma * kp[:, None, None]) % 512
        ps = two_pi * qi / 512.0
        selq = ((mb // 2) == p[:, None, None])
        Qr, Qi = np.cos(ps) / 512.0 * selq, np.sin(ps) / 512.0 * selq
    t["F1z"] = np.ascontiguousarray(F1z.astype(f32))
    t["F1f"] = np.ascontiguousarray(F1f.astype(f32))
    G = np.concatenate([np.cos(th), -np.sin(th), np.sin(th)], axis=2)
    t["Gm"] = np.ascontiguousarray(G.astype(ml_dtypes.bfloat16))
    k2v = np.arange(128)[:, None]
    mav = np.arange(128)[None, :]
    ph = two_pi * ((k2v * mav) % 128) / 128.0
    Er, Ei = np.cos(ph), np.sin(ph)
    t["Em"] = np.ascontiguousarray(np.concatenate([Er, Ei, -Ei, Er], axis=1).astype(ml_dtypes.bfloat16))
    Q = np.stack([Qr, -Qi], axis=2)
    t["Qm"] = np.ascontiguousarray(Q.reshape(64, 128 * 64).astype(ml_dtypes.bfloat16))
    dl = np.linspace(math.log(0.01) / 1.5, math.log(0.01) / 0.3, 256, dtype=f32)
    t["negdelta"] = (-np.abs(dl)).reshape(256, 1).astype(f32)
    return t


def job_tables(is_prompt):
    t = {}
    if not is_prompt:
        pos = np.arange(T)
        row = (pos // 64).astype(np.float32)
        col = (pos % 64).astype(np.float32)
        inv = (np.float32(10000.0) ** (-np.arange(0, 16, 2, dtype=np.float32) / np.float32(16))).astype(np.float32)
        ar = (row[:, None] * inv[None, :]).astype(np.float32)
        ac = (col[:, None] * inv[None, :]).astype(np.float32)
        C = np.concatenate([np.cos(ar), np.cos(ar), np.cos(ac), np.cos(ac)], axis=1).astype(np.float32)
        Ssg = np.concatenate([-np.sin(ar), np.sin(ar), -np.sin(ac), np.sin(ac)], axis=1).astype(np.float32)
        qaug = np.zeros((T, 17), np.float32)
        kaug = np.zeros((NKEY, 17), np.float32)
    else:
        C = np.ones((T, 32), np.float32)
        Ssg = np.zeros((T, 32), np.float32)
        pid = np.arange(T) // 256
        qaug = np.zeros((T, 17), np.float32)
        qaug[np.arange(T), pid] = 1.0
        qaug[:, 16] = 1.0
        kaug = np.zeros((NKEY, 17), np.float32)
        kaug[np.arange(T), pid] = BIG
        kaug[:, 16] = -BIG
    carry = np.ones((2, 128, 64), np.float32)
    if is_prompt:
        cidx = np.arange(64)
        carry[0][:, cidx % 4 == 0] = 0.0
        carry[1][:, cidx % 4 == 3] = 0.0
    t["carry"] = carry
    sidx = np.arange(64)
    mf = (sidx[:, None] <= sidx[None, :]).astype(np.float32)
    mb = (sidx[:, None] >= sidx[None, :]).astype(np.float32)
    t["maskT"] = np.ascontiguousarray(np.concatenate([mf, mb, mf, mb], axis=1))
    t.update(hyena_tables(is_prompt))
    t["ropeC"] = np.ascontiguousarray(np.tile(C, (1, 8)))
    t["ropeS"] = np.ascontiguousarray(np.tile(Ssg, (1, 8)))
    t["qaug"] = qaug
    t["kaug"] = kaug
    return t


def make_in_maps(inp, n_cores=8):
    consts = host_consts()
    tabs = {False: job_tables(False), True: job_tables(True)}
    maps = []
    for core in range(n_cores):
        job = core if core < 5 else 4
        m = dict(consts)
        if job < 4:
            m["x"] = np.ascontiguousarray(inp["x_sample"][job])
            cv = inp["c"][job]
        else:
            m["x"] = np.ascontiguousarray(inp["x_prompt"].reshape(T, D))
            cv = inp["c_ctx"]
        m["cvecT"] = np.ascontiguousarray(cv.reshape(8, 128).T)
        m.update(tabs[job >= 4])
        if job < 4:
            m["c_ckv"] = np.ascontiguousarray(inp["cache_mla_ckv"][job])
            m["c_krope"] = np.ascontiguousarray(inp["cache_mla_krope"][job])
            m["c_dk"] = np.ascontiguousarray(inp["cache_diff_k"][job].reshape(DEPTH, PAST, 256))
            m["c_dv"] = np.ascontiguousarray(inp["cache_diff_v"][job].reshape(DEPTH, PAST, 256))
        else:
            m["c_ckv"] = np.zeros((DEPTH, PAST, 128), np.float32)
            m["c_krope"] = np.zeros((DEPTH, PAST, 32), np.float32)
            m["c_dk"] = np.zeros((DEPTH, PAST, 256), np.float32)
            m["c_dv"] = np.zeros((DEPTH, PAST, 256), np.float32)
        m["lbT"] = np.ascontiguousarray(inp["hgrn_lb_logits"].transpose(2, 0, 1).reshape(256, 8))
        m["hy_cols"] = np.ascontiguousarray(np.stack([inp["hy_b1"], inp["hy_b2"], inp["hy_sin_freq"][:, 0], inp["hy_sin_freq"][:, 1]], axis=-1))
        m["hy_convT"] = np.ascontiguousarray(np.concatenate([inp["hy_conv_w"].transpose(0, 2, 1), inp["hy_conv_b"][:, :, None]], axis=-1))
        m["hy_biasT"] = np.ascontiguousarray(inp["hy_bias"].reshape(DEPTH, 2, 256, 1))
        for k in ("hy_w1", "hy_w2", "hy_w3"):
            m[k] = inp[k]
        m["hgrn_out_gT"] = np.ascontiguousarray(inp["hgrn_out_g"].reshape(DEPTH, 64, 1))
        if job < 4:
            m["s0"] = np.ascontiguousarray(inp["state_hgrn"][job])
        else:
            m["s0"] = np.zeros((DEPTH, 2, 4, 64, 64), np.float32)
        m["diff_lambda"] = np.ascontiguousarray(inp["diff_lambda"].reshape(DEPTH, 128))
        m["diff_subln_gT"] = np.ascontiguousarray(inp["diff_subln_g"].reshape(DEPTH, 64, 1))
        for k in ("norm_g", "w_mod", "b_mod", "w_in", "w_out", "mla_q_norm_g", "mla_w_uq", "mla_kv_norm_g", "mla_w_ukv",
                  "mla_nope_g", "mla_rope_g", "diff_qk_g"):
            m[k] = inp[k]
        maps.append(m)
    return maps


_CACHE = {}


def kernel(**inputs):
    inp = {k: np.asarray(v) for k, v in inputs.items()}
    if "b" not in _CACHE:
        _CACHE["b"] = Builder()
    b = _CACHE["b"]
    maps = make_in_maps(inp)
    maps = [{k: v for k, v in m.items() if k in b.inputs} for m in maps]
    res = run_bass_kernel_spmd(b.nc, maps, core_ids=list(range(8)))
    r = res.results
    y_sample = np.stack([np.asarray(r[j]["y"], dtype=np.float32) for j in range(4)], axis=0)
    p = r[4]
    y_prompt = np.asarray(p["y"], dtype=np.float32).reshape(16, 256, D)
    new_ckv = np.ascontiguousarray(np.asarray(p["o_ckv"], dtype=np.float32).reshape(DEPTH, 16, 256, 128).transpose(1, 0, 2, 3))
    new_krope = np.ascontiguousarray(np.asarray(p["o_krope"], dtype=np.float32).reshape(DEPTH, 16, 256, 32).transpose(1, 0, 2, 3))
    new_dk = np.ascontiguousarray(np.asarray(p["o_dk"], dtype=np.float32).reshape(DEPTH, 16, 256, 4, 2, 32).transpose(1, 0, 2, 3, 4, 5))
    new_dv = np.ascontiguousarray(np.asarray(p["o_dv"], dtype=np.float32).reshape(DEPTH, 16, 256, 4, 64).transpose(1, 0, 2, 3, 4))
    new_st = np.ascontiguousarray(np.asarray(p["o_state"], dtype=np.float32).transpose(1, 0, 2, 3, 4, 5))
    return (y_prompt, y_sample, new_ckv, new_krope, new_dk, new_dv, new_st)
```

```python
import math
import os
from contextlib import ExitStack

import numpy as np
import ml_dtypes

import concourse.bass as bass
import concourse.mybir as mybir
from concourse.bass_utils import run_bass_kernel_spmd

F32 = mybir.dt.float32
BF16 = mybir.dt.bfloat16
AF = mybir.ActivationFunctionType
ALU = mybir.AluOpType
AX = mybir.AxisListType

D = 1024
T = 4096
DEPTH = 4
PAST = 512
NKEY = T + PAST
EPS = 1e-6
IN_COLS = 4000
NT = T // 128
NST = T // 512
BIG = 30000.0

C_CQ, C_CKV, C_KR, C_GA = 0, 256, 384, 416
C_DQ, C_DK, C_DV, C_GB = 672, 928, 1184, 1440
C_HQ, C_HZF, C_HZB, C_HI, C_GC = 1696, 1952, 2208, 2464, 2720
C_HU, C_GD = 2976, 3744

EPOCH = 30000
ENGS = ("pe", "act", "dve", "pool", "sp")


class Sched:
    def __init__(self, nc, n_dma_sems=40, self_wait=True):
        self.nc = nc
        self.h = {"pe": nc.tensor, "act": nc.scalar, "dve": nc.vector, "pool": nc.gpsimd, "sp": nc.sync}
        self.sems = {e: [] for e in ENGS}
        self.cnt = {e: 0 for e in ENGS}
        self.seen = {e: {f: 0 for f in ENGS} for e in ENGS}
        self.clk = {e: [None] for e in ENGS}
        self.qrange = {"sp": (0, 24), "pool": (24, 40), "act": (40, 48)}
        n_dma_sems = 48
        self.nd = n_dma_sems
        self.dsem = [nc.alloc_semaphore(f"dma{i}") for i in range(n_dma_sems)]
        self.dval = [0] * n_dma_sems
        self.dclk = [None] * n_dma_sems
        self.dseen = {e: [0] * n_dma_sems for e in ENGS}
        self.drr = {q: r[0] for q, r in self.qrange.items()}
        self.kw = {}
        self.kr = {}
        self.self_wait = self_wait
        self.n_wait = 0
        self.n_inst = 0

    def _sem(self, e, idx):
        ep = idx // EPOCH
        while len(self.sems[e]) <= ep:
            self.sems[e].append(self.nc.alloc_semaphore(f"s_{e}_{len(self.sems[e])}"))
        return self.sems[e][ep], idx % EPOCH + 1

    def _snap(self, e):
        return (tuple(self.seen[e][f] for f in ENGS), tuple(self.dseen[e]))

    def _merge(self, e, snap):
        if snap is None:
            return
        s, d = snap
        se = self.seen[e]
        for f, v in zip(ENGS, s):
            if v > se[f]:
                se[f] = v
        de = self.dseen[e]
        for i, v in enumerate(d):
            if v > de[i]:
                de[i] = v

    def _wait_event(self, e, ev):
        if ev[0] == "E":
            _, f, c = ev
            if f == e:
                if e in ("pe", "sp"):
                    return
                if (not self.self_wait) or self.cnt[e] - c > 10 or self.seen[e][e] >= c:
                    return
            elif self.seen[e][f] >= c:
                return
            sem, val = self._sem(f, c - 1)
            self.h[e].wait_ge(sem, val)
            self.n_wait += 1
            if c > self.seen[e][f]:
                self.seen[e][f] = c
            self._merge(e, self.clk[f][c])
        else:
            _, i, v, snap = ev
            if self.dseen[e][i] >= v:
                return
            self.h[e].wait_ge(self.dsem[i], v)
            self.n_wait += 1
            self.dseen[e][i] = v
            self._merge(e, snap)

    def _deps(self, e, reads, writes):
        evs = []
        for k in reads:
            w = self.kw.get(k)
            if w is not None:
                evs.append(w)
        for k in writes:
            w = self.kw.get(k)
            if w is not None:
                evs.append(w)
            for r in self.kr.get(k, ()):
                if r[0] == "E" and r[1] == e:
                    continue
                evs.append(r)
        for ev in evs:
            self._wait_event(e, ev)

    def _record(self, ev, reads, writes):
        for k in writes:
            self.kw[k] = ev
            self.kr[k] = []
        for k in reads:
            lst = self.kr.setdefault(k, [])
            if ev[0] == "E":
                lst[:] = [r for r in lst if not (r[0] == "E" and r[1] == ev[1])]
            lst.append(ev)

    def op(self, e, name, *args, reads=(), writes=(), **kw):
        self._deps(e, reads, writes)
        ins = getattr(self.h[e], name)(*args, **kw)
        idx = self.cnt[e]
        sem, _ = self._sem(e, idx)
        ins.then_inc(sem, 1)
        self.cnt[e] = idx + 1
        self.n_inst += 1
        self.clk[e].append(self._snap(e))
        ev = ("E", e, idx + 1)
        self._record(ev, reads, writes)
        return ev

    def dma(self, q, out, in_, reads=(), writes=(), **kw):
        self._deps(q, reads, writes)
        i = self.drr[q]
        lo, hi = self.qrange[q]
        self.drr[q] = lo + (i + 1 - lo) % (hi - lo)
        if self.dval[i] > 0 and self.dseen[q][i] < self.dval[i]:
            self.h[q].wait_ge(self.dsem[i], self.dval[i])
            self.n_wait += 1
            self.dseen[q][i] = self.dval[i]
            self._merge(q, self.dclk[i])
        ins = self.h[q].dma_start(out=out, in_=in_, **kw)
        self.dval[i] += 16
        ins.then_inc(self.dsem[i], 16)
        self.n_inst += 1
        snap = self._snap(q)
        self.dclk[i] = snap
        ev = ("D", i, self.dval[i], snap)
        self._record(ev, reads, writes)
        return ev

    def barrier(self):
        for e in ENGS:
            for f in ENGS:
                if f != e and self.cnt[f] > 0:
                    self._wait_event(e, ("E", f, self.cnt[f]))
            for i in range(self.nd):
                if self.dval[i] > 0:
                    self._wait_event(e, ("D", i, self.dval[i], self.dclk[i]))
        self.kw.clear()
        self.kr.clear()


class Builder:
    def __init__(self, debug=False, n_layers=DEPTH, stages=("mla", "diff", "hgrn", "hyena")):
        self.debug = debug
        self.n_layers = n_layers
        self.stages = stages
        self.nc = bass.Bass("TRN2", target_bir_lowering=False)
        self.S = Sched(self.nc)
        self.inputs = {}
        self.outputs = {}
        self.build()

    def din(self, name, shape, dt=F32):
        t = self.nc.dram_tensor(name, list(shape), dt, kind="ExternalInput").ap()
        self.inputs[name] = t
        return t

    def dout(self, name, shape, dt=F32):
        t = self.nc.dram_tensor(name, list(shape), dt, kind="ExternalOutput").ap()
        self.outputs[name] = t
        return t

    def dscr(self, name, shape, dt=F32):
        kind = "ExternalOutput" if self.debug else "Internal"
        t = self.nc.dram_tensor(name, list(shape), dt, kind=kind).ap()
        if self.debug:
            self.outputs[name] = t
        return t

    def sb(self, es, name, shape, dt=F32):
        self._uid = getattr(self, "_uid", 0) + 1
        return es.enter_context(self.nc.sbuf_tensor(f"{name}_{self._uid}", list(shape), dt))

    def build(self):
        nc, S = self.nc, self.S
        self.x_in = self.din("x", [T, D])
        self.cvecT = self.din("cvecT", [128, 8])
        self.norm_g = self.din("norm_g", [DEPTH, D])
        self.w_mod = self.din("w_mod", [DEPTH, D, 3 * D])
        self.b_mod = self.din("b_mod", [DEPTH, 3 * D])
        self.w_in = self.din("w_in", [DEPTH, D, IN_COLS])
        self.w_out = self.din("w_out", [DEPTH, D, D])
        self.mla_q_norm_g = self.din("mla_q_norm_g", [DEPTH, 256])
        self.mla_w_uq = self.din("mla_w_uq", [DEPTH, 256, 384])
        self.mla_kv_norm_g = self.din("mla_kv_norm_g", [DEPTH, 128])
        self.mla_w_ukv = self.din("mla_w_ukv", [DEPTH, 128, 512])
        self.mla_nope_g = self.din("mla_nope_g", [DEPTH, 2, 64])
        self.mla_rope_g = self.din("mla_rope_g", [DEPTH, 2, 32])
        self.diff_qk_g = self.din("diff_qk_g", [DEPTH, 2, 32])
        self.diff_lambda = self.din("diff_lambda", [DEPTH, 128])
        self.diff_subln_g = self.din("diff_subln_gT", [DEPTH, 64, 1])
        self.c_ckv = self.din("c_ckv", [DEPTH, PAST, 128])
        self.c_krope = self.din("c_krope", [DEPTH, PAST, 32])
        self.c_dk = self.din("c_dk", [DEPTH, PAST, 256])
        self.c_dv = self.din("c_dv", [DEPTH, PAST, 256])
        self.ropeC = self.din("ropeC", [T, 256])
        self.ropeS = self.din("ropeS", [T, 256])
        self.qaug = self.din("qaug", [T, 17])
        self.kaug = self.din("kaug", [NKEY, 17])
        self.hy_w1 = self.din("hy_w1", [DEPTH, 33, 64])
        self.hy_w2 = self.din("hy_w2", [DEPTH, 64, 64])
        self.hy_w3 = self.din("hy_w3", [DEPTH, 64, 1024])
        self.hy_cols = self.din("hy_cols", [DEPTH, 64, 4])
        self.hy_convT = self.din("hy_convT", [DEPTH, 768, 4])
        self.hy_biasT = self.din("hy_biasT", [DEPTH, 2, 256, 1])
        self.negdelta = self.din("negdelta", [256, 1])
        self.featsT = self.din("featsT", [33, 8192])
        self.dmask = self.din("dmask", [2, 8192])
        self.bmask = self.din("bmask", [2, 128, 16])
        self.F1z = self.din("F1z", [32, 128])
        self.F1f = self.din("F1f", [64, 128])
        self.Gm = self.din("Gm", [64, 128, 384], BF16)
        self.Em = self.din("Em", [128, 512], BF16)
        self.Qm = self.din("Qm", [64, 128 * 64], BF16)
        self.s_filtT = self.dscr("s_filtT", [2, 256, 8192])
        self.s_H = self.dscr("s_H", [4, 128, 2 * 64 * 128])
        self.s_z = self.dscr("s_z", [4, 128, T])
        self.lbT = self.din("lbT", [256, 8])
        self.hgrn_out_g = self.din("hgrn_out_gT", [DEPTH, 64, 1])
        self.s0 = self.din("s0", [DEPTH, 2, 4, 64, 64])
        self.carry = self.din("carry", [2, 128, 64])
        self.maskT = self.din("maskT", [64, 256])
        self.o_state = self.dout("o_state", [DEPTH, 16, 2, 4, 64, 64])
        self.y = self.dout("y", [T, D])
        self.o_dv = self.dout("o_dv", [DEPTH, T, 256])
        self.o_ckv = self.dout("o_ckv", [DEPTH, T, 128])
        self.o_krope = self.dout("o_krope", [DEPTH, T, 32])
        self.o_dk = self.dout("o_dk", [DEPTH, T, 256])
        self.s_cq = self.dscr("s_cq", [T, 416])
        self.s_dqk = self.dscr("s_dqk", [T, 512])
        self.s_vd = self.dscr("s_vd", [NKEY, 4 * 65], BF16)
        self.s_hv = self.dscr("s_hv", [T, 256], BF16)
        self.s_gT = self.dscr("s_gT", [D, T], BF16)
        self.s_hT = self.dscr("s_hT", [768, T])
        self.s_huT = self.dscr("s_huT", [768, T])
        self.s_yT = self.dscr("s_yT", [D, T], BF16)

        with ExitStack() as es:
            self.ident_bf = self.sb(es, "ident_bf", [128, 128], BF16)
            self.ident_f = self.sb(es, "ident_f", [128, 128], F32)
            self.ones_f = self.sb(es, "ones_f", [128, 128], F32)
            self.ident_in = self.din("ident", [128, 128])
            self.modA = self.sb(es, "modA", [128, D], F32)
            self.modB = self.sb(es, "modB", [128, D], F32)
            self.gate_b = self.sb(es, "gate_b", [128, D], F32)
            self.crep = self.sb(es, "crep", [128, 8, 128], F32)
            self.psum = [es.enter_context(nc.psum_tensor(f"ps{i}", [128, 512], F32)) for i in range(8)]

            S.dma("sp", self.ident_f[:], self.ident_in[:, :], writes=["ident_f"])
            S.op("dve", "tensor_copy", self.ident_bf[:], self.ident_f[:], reads=["ident_f"], writes=["ident_bf"])
            S.op("dve", "memset", self.ones_f[:], 1.0, writes=["ones_f"])
            with ExitStack() as es0:
                cv = self.sb(es0, "cv", [128, 8], F32)
                sc = self.sb(es0, "sc", [128, 8], F32)
                S.dma("sp", cv[:], self.cvecT[:, :], writes=["cv"])
                S.op("act", "activation", out=sc[:], in_=cv[:], func=AF.Silu, reads=["cv"], writes=["sc"])
                for k in range(8):
                    S.op("dve", "tensor_scalar", self.crep[:, k, :], self.ones_f[:], sc[:, k:k + 1], None,
                         ALU.mult, reads=["sc", "ones_f"], writes=["crep"])
                S.barrier()

            for l in range(self.n_layers):
                self.layer(l)
            S.barrier()

    def layer(self, l):
        S = self.S
        self.phase_mod(l)
        S.barrier()
        self.phase_p1(l)
        S.barrier()
        if "mla" in self.stages:
            self.phase_mla(l)
            S.barrier()
        if "diff" in self.stages:
            self.phase_diff(l)
            S.barrier()
        if "hgrn" in self.stages:
            self.phase_hgrn(l)
            S.barrier()
        if "hyena" in self.stages:
            self.phase_hyena(l)
            S.barrier()
        if "stub" in self.stages:
            for c in range(8):
                S.dma("sp", self.s_yT[c * 128:(c + 1) * 128, :], self.s_gT[c * 128:(c + 1) * 128, :])
            S.barrier()
        self.phase_po(l)
        S.barrier()

    def phase_mod(self, l):
        nc, S = self.nc, self.S
        with ExitStack() as es:
            wm = self.sb(es, "wm", [128, 8, 3 * D], F32)
            bm = self.sb(es, "bm", [128, 3 * D], F32)
            gb = self.sb(es, "gb", [128, D], F32)
            for k in range(8):
                S.dma("sp" if k % 2 == 0 else "pool", wm[:, k, :], self.w_mod[l, k * 128:(k + 1) * 128, :],
                      writes=[("wm", k)])
            S.dma("sp", bm[:], self.b_mod[l:l + 1, :].partition_broadcast(128) if False else
                  self.b_mod[l, :].partition_broadcast(128), writes=["bm"])
            S.dma("sp", gb[:], self.norm_g[l, :].partition_broadcast(128), writes=["gb"])
            for nt in range(6):
                ps = self.psum[nt % 2]
                for k in range(8):
                    S.op("pe", "matmul", ps[:, :], self.crep[:, k, :], wm[:, k, nt * 512:(nt + 1) * 512],
                         start=(k == 0), stop=(k == 7), reads=["crep", ("wm", k)], writes=[("ps", nt % 2)])
                sl = slice((nt % 2) * 512, (nt % 2) * 512 + 512)
                bsl = slice(nt * 512, nt * 512 + 512)
                if nt < 2:
                    S.op("dve", "tensor_tensor", self.modB[:, sl], ps[:, :], bm[:, bsl], ALU.add,
                         reads=[("ps", nt % 2), "bm"], writes=["modB"])
                elif nt < 4:
                    S.op("dve", "scalar_tensor_tensor", self.modA[:, sl], ps[:, :], 1.0, bm[:, bsl], ALU.add, ALU.add,
                         reads=[("ps", nt % 2), "bm"], writes=["modA"])
                    S.op("dve", "tensor_tensor", self.modA[:, sl], self.modA[:, sl], gb[:, sl], ALU.mult,
                         reads=["modA", "gb"], writes=["modA"])
                else:
                    S.op("dve", "tensor_tensor", self.gate_b[:, sl], ps[:, :], bm[:, bsl], ALU.add,
                         reads=[("ps", nt % 2), "bm"], writes=["gate_b"])

    def phase_p1(self, l):
        nc, S = self.nc, self.S
        xsrc = self.x_in if l == 0 else self.y
        with ExitStack() as es:
            w16 = self.sb(es, "w16", [128, 8, IN_COLS], BF16)
            xin = [self.sb(es, f"xin{i}", [128, 4, D], F32) for i in range(2)]
            hT = [self.sb(es, f"hT{i}", [128, 8, 512], BF16) for i in range(2)]
            hb = self.sb(es, "hb", [128, 4, D], BF16)
            t32 = self.sb(es, "t32", [128, D], F32)
            junk = self.sb(es, "junk", [128, D], BF16)
            ss = self.sb(es, "ss", [128, 8], F32)
            tm_st = [self.sb(es, f"tmst{i}", [128, 928], F32) for i in range(2)]
            dv_st = [self.sb(es, f"dvst{i}", [128, 256], F32) for i in range(2)]
            hv_st = [self.sb(es, f"hvst{i}", [128, 256], BF16) for i in range(2)]
            g_st = [self.sb(es, f"gst{i}", [128, 512], BF16) for i in range(2)]
            f_st = [self.sb(es, f"fst{i}", [128, 512], F32) for i in range(3)]
            vd_st = [self.sb(es, f"vdst{i}", [128, 4, 65], BF16) for i in range(2)]

            for k in range(8):
                for hh in range(2):
                    S.dma("pool", w16[:, k, hh * 2000:(hh + 1) * 2000],
                          self.w_in[l, k * 128:(k + 1) * 128, hh * 2000:(hh + 1) * 2000], writes=[("w16", k)])
            for i in range(2):
                S.op("pool", "memset", vd_st[i][:], 1.0, writes=[("vdst", i)])

            def prep(st):
                b = st % 2
                S.dma("sp", xin[b][:], xsrc[st * 512:(st + 1) * 512, :].rearrange("(j p) d -> p j d", p=128),
                      writes=[("xin", b)])
                kd = os.environ.get("KDBG", "")
                for j in range(0 if "nostat" in kd else 4):
                    S.op("act", "activation", out=junk[:], in_=xin[b][:, j, :], func=AF.Square,
                         accum_out=ss[:, j:j + 1], reads=[("xin", b)], writes=["junk", ("ss", j)])
                if "nostat" not in kd:
                    S.op("dve", "tensor_scalar", ss[:, 4:8], ss[:, 0:4], 1.0 / D, EPS, ALU.mult, ALU.add,
                         reads=[("ss", j) for j in range(4)], writes=["ms"])
                    S.op("act", "activation", out=ss[:, 4:8], in_=ss[:, 4:8], func=AF.Sqrt, reads=["ms"], writes=["ms"])
                    S.op("dve", "reciprocal", ss[:, 4:8], ss[:, 4:8], reads=["ms"], writes=["ms"])
                else:
                    S.op("dve", "memset", ss[:], 1.0, writes=["ms"])
                for j in range(4):
                    S.op("dve", "scalar_tensor_tensor", t32[:], xin[b][:, j, :], ss[:, 4 + j:5 + j], self.modA[:],
                         ALU.mult, ALU.mult, reads=[("xin", b), "ms"], writes=["t32"])
                    S.op("pool", "tensor_tensor", hb[:, j, :], t32[:], self.modB[:], ALU.add,
                         reads=["t32"], writes=[("hb", j)])
                for j in range(4):
                    pi = j % 2
                    pst = self.psum[pi][:, :].bitcast(BF16)
                    for k in range(8):
                        S.op("pe", "transpose", pst[:, k * 128:(k + 1) * 128], hb[:, j, k * 128:(k + 1) * 128],
                             self.ident_bf[:], reads=[("hb", j)], writes=[("ps", pi)])
                    eng = "act" if (j % 2 == 0 or "tract" in os.environ.get("KDBG", "")) else "dve"
                    src = pst.rearrange("p (k t) -> p k t", k=8)
                    if eng == "act":
                        S.op("act", "activation", out=hT[b][:, :, j * 128:(j + 1) * 128], in_=src, func=AF.Copy,
                             reads=[("ps", pi)], writes=[("hT", b)])
                    else:
                        S.op("dve", "tensor_copy", hT[b][:, :, j * 128:(j + 1) * 128], src,
                             reads=[("ps", pi)], writes=[("hT", b)])

            fm_chunks = ([("g", 0, C_GA), ("g", 1, C_GA + 128), ("g", 2, C_GB), ("g", 3, C_GB + 128),
                          ("g", 4, C_GC), ("g", 5, C_GC + 128), ("g", 6, C_GD), ("g", 7, C_GD + 128)]
                         + [("h", i, C_HQ + 128 * i) for i in range(6)]
                         + [("u", i, C_HU + 128 * i) for i in range(6)])

            def mm(st):
                b = st % 2
                cnt = 0
                for j in range(0 if "notm" in os.environ.get("KDBG", "") else 4):
                    tok0 = st * 512 + j * 128
                    lhs = lambda k: hT[b][:, k, j * 128:(j + 1) * 128]
                    sb_i = j % 2
                    ps = self.psum[2]
                    for k in range(0 if "noa" in os.environ.get("KDBG", "") else 8):
                        S.op("pe", "matmul", ps[:, 0:416], lhs(k), w16[:, k, 0:416], start=(k == 0), stop=(k == 7),
                             reads=[("hT", b), ("w16", k)], writes=[("ps", 2)])
                    S.op("act", "activation", out=tm_st[sb_i][:, 0:416], in_=ps[:, 0:416], func=AF.Copy,
                         reads=[("ps", 2)], writes=[("tmst", sb_i, 0)])
                    S.dma("sp", self.s_cq[tok0:tok0 + 128, :], tm_st[sb_i][:, 0:416], reads=[("tmst", sb_i, 0)])
                    ps = self.psum[3]
                    for k in range(0 if "nob" in os.environ.get("KDBG", "") else 8):
                        S.op("pe", "matmul", ps[:, :], lhs(k), w16[:, k, C_DQ:C_DQ + 512], start=(k == 0), stop=(k == 7),
                             reads=[("hT", b), ("w16", k)], writes=[("ps", 3)])
                    S.op("dve", "tensor_copy", tm_st[sb_i][:, 416:928], ps[:, :],
                         reads=[("ps", 3)], writes=[("tmst", sb_i, 1)])
                    S.dma("sp", self.s_dqk[tok0:tok0 + 128, :], tm_st[sb_i][:, 416:928], reads=[("tmst", sb_i, 1)])
                    ps = self.psum[2]
                    if "noc" in os.environ.get("KDBG", ""):
                        continue
                    for k in range(8):
                        S.op("pe", "matmul", ps[:, 0:256], lhs(k), w16[:, k, C_DV:C_DV + 256], start=(k == 0), stop=(k == 7),
                             reads=[("hT", b), ("w16", k)], writes=[("ps", 2)])
                    for k in range(8):
                        S.op("pe", "matmul", ps[:, 256:512], lhs(k), w16[:, k, C_HI:C_HI + 256], start=(k == 0), stop=(k == 7),
                             reads=[("hT", b), ("w16", k)], writes=[("ps", 2)])
                    kd = os.environ.get("KDBG", "")
                    if "c1" not in kd:
                        S.op("act", "activation", out=dv_st[sb_i][:], in_=ps[:, 0:256], func=AF.Copy,
                             reads=[("ps", 2)], writes=[("dvst", sb_i)])
                        S.dma("sp", self.o_dv[l, tok0:tok0 + 128, :], dv_st[sb_i][:], reads=[("dvst", sb_i)])
                    if "c2" not in kd:
                        S.op("act", "activation", out=vd_st[sb_i][:, :, 0:64], in_=ps[:, 0:256].rearrange("p (h e) -> p h e", h=4),
                             func=AF.Copy, reads=[("ps", 2)], writes=[("vdst", sb_i)])
                        S.dma("sp", self.s_vd[tok0:tok0 + 128, :], vd_st[sb_i][:].rearrange("p h e -> p (h e)"),
                              reads=[("vdst", sb_i)])
                    if "c3" not in kd:
                        if "hv32" in kd:
                            S.op("dve", "tensor_copy", dv_st[sb_i][:], ps[:, 256:512], reads=[("ps", 2)], writes=[("dvst", sb_i)])
                            S.op("act", "activation", out=hv_st[sb_i][:], in_=dv_st[sb_i][:], func=AF.Copy, reads=[("dvst", sb_i)], writes=[("hvst", sb_i)])
                        elif "hvdve" not in kd:
                            S.op("act", "activation", out=hv_st[sb_i][:], in_=ps[:, 256:512], func=AF.Copy, reads=[("ps", 2)], writes=[("hvst", sb_i)])
                        else:
                            S.op("dve", "tensor_copy", hv_st[sb_i][:], ps[:, 256:512], reads=[("ps", 2)], writes=[("hvst", sb_i)])
                        S.dma("sp", self.s_hv[tok0:tok0 + 128, :], hv_st[sb_i][:], reads=[("hvst", sb_i)])
                for ci, (kind, idx, col) in enumerate(fm_chunks):
                    if "nofm" in os.environ.get("KDBG", ""):
                        break
                    pi = 4 + ci % 3
                    ps = self.psum[pi]
                    for k in range(8):
                        S.op("pe", "matmul", ps[:, :], w16[:, k, col:col + 128], hT[b][:, k, :], start=(k == 0), stop=(k == 7),
                             reads=[("hT", b), ("w16", k)], writes=[("ps", pi)])
                    tsl = slice(st * 512, (st + 1) * 512)
                    if kind == "g":
                        gi = ci % 2
                        S.op("act", "activation", out=g_st[gi][:], in_=ps[:, :], func=AF.Silu,
                             reads=[("ps", pi)], writes=[("gst", gi)])
                        S.dma("sp", self.s_gT[idx * 128:(idx + 1) * 128, tsl], g_st[gi][:], reads=[("gst", gi)])
                    else:
                        fi = ci % 3
                        if ci % 2 == 0:
                            S.op("dve", "tensor_copy", f_st[fi][:], ps[:, :], reads=[("ps", pi)], writes=[("fst", fi)])
                        else:
                            S.op("act", "activation", out=f_st[fi][:], in_=ps[:, :], func=AF.Copy,
                                 reads=[("ps", pi)], writes=[("fst", fi)])
                        dst = self.s_hT if kind == "h" else self.s_huT
                        S.dma("sp", dst[idx * 128:(idx + 1) * 128, tsl], f_st[fi][:], reads=[("fst", fi)])

            import os
            dbg = os.environ.get("KDBG", "")
            nst = int(os.environ.get("KNST", NST))
            if "nomm" in dbg:
                mm = lambda st: None
            prep(0)
            for st in range(nst):
                if st + 1 < nst:
                    prep(st + 1)
                mm(st)

    def rms_tm(self, src3, dst3, G, W, gain_b, sq, ssb, rk, wk, gain_eng="dve"):
        S = self.S
        sqv = sq[:, 0:G * W].rearrange("p (g w) -> p g w", g=G)
        S.op("dve", "tensor_tensor", sqv, src3, src3, ALU.mult, reads=rk, writes=["sq"])
        S.op("dve", "tensor_reduce", ssb[:, 0:G], sqv, AX.X, ALU.add, reads=["sq"], writes=["ssb"])
        S.op("dve", "tensor_scalar", ssb[:, G:2 * G], ssb[:, 0:G], 1.0 / W, EPS, ALU.mult, ALU.add,
             reads=["ssb"], writes=["ssb2"])
        S.op("act", "activation", out=ssb[:, G:2 * G], in_=ssb[:, G:2 * G], func=AF.Sqrt, reads=["ssb2"], writes=["ssb2"])
        S.op("dve", "reciprocal", ssb[:, G:2 * G], ssb[:, G:2 * G], reads=["ssb2"], writes=["ssb2"])
        if gain_b is None:
            S.op("dve", "tensor_tensor", dst3, src3, ssb[:, G:2 * G].unsqueeze(2).to_broadcast([128, G, W]), ALU.mult,
                 reads=list(rk) + ["ssb2"], writes=wk)
        else:
            S.op("dve", "tensor_tensor", sqv, src3, ssb[:, G:2 * G].unsqueeze(2).to_broadcast([128, G, W]), ALU.mult,
                 reads=list(rk) + ["ssb2"], writes=["sq"])
            S.op(gain_eng, "tensor_tensor", dst3, sqv, gain_b.unsqueeze(1).to_broadcast([128, G, W]), ALU.mult,
                 reads=["sq"], writes=wk)

    def rope_tm(self, x3, out3, G, rc, rs, t1, rk, wk):
        S = self.S
        xv = x3.rearrange("p g (r h e) -> p (g r) h e", r=2, h=2)
        sv = rs[:, 0:G * 32].rearrange("p (g h e) -> p g h e", h=2, e=8)
        tv = t1[:, 0:G * 32].rearrange("p (g h e) -> p g h e", h=2, e=8)
        S.op("dve", "tensor_tensor", tv[:, :, 0, :], xv[:, :, 1, :], sv[:, :, 0, :], ALU.mult, reads=list(rk) + ["rs"], writes=["t1"])
        S.op("dve", "tensor_tensor", tv[:, :, 1, :], xv[:, :, 0, :], sv[:, :, 1, :], ALU.mult, reads=list(rk) + ["rs"], writes=["t1"])
        x2 = x3.rearrange("p g w -> p (g w)") if False else x3
        S.op("dve", "tensor_tensor", x3, x3, rc[:, 0:G * 32].rearrange("p (g w) -> p g w", g=G), ALU.mult,
             reads=list(rk) + ["rc", "t1"], writes=rk)
        S.op("pool", "tensor_tensor", out3, x3, t1[:, 0:G * 32].rearrange("p (g w) -> p g w", g=G), ALU.add,
             reads=list(rk) + ["t1"], writes=wk)

    def bcast_load(self, q, tile_ap, dram_ap, key):
        self.S.dma(q, tile_ap, dram_ap.partition_broadcast(128), writes=[key])

    def attend(self, n_groups, Krows, QT, KT, Vt, v_of_g, scale, finalize, order=None):
        S = self.S
        NKC = NKEY // 128
        if order is None:
            order = [(g, qs) for g in range(n_groups) for qs in range(NST)]
        seq = [(oi, g, qs, kc) for oi, (g, qs) in enumerate(order) for kc in range(NKC)]
        banks = [0, 1, 7]
        nb = len(banks)

        def emit_s(idx):
            oi, g, qs, kc = seq[idx]
            b = idx % nb
            S.op("pe", "matmul", self.psum[banks[b]][:, :], KT(g, kc), QT(g, qs), start=True, stop=True,
                 reads=["QT", "KT"], writes=[("ps", banks[b])])

        emit_s(0)
        if len(seq) > 1:
            emit_s(1)
        for idx, (oi, g, qs, kc) in enumerate(seq):
            b = idx % nb
            pkey = 2 + (oi % 2)
            pO = self.psum[pkey]
            if idx + 2 < len(seq):
                emit_s(idx + 2)
            S.op("act", "activation", out=self.pt[b][:], in_=self.psum[banks[b]][:, :], func=AF.Exp, scale=scale,
                 reads=[("ps", banks[b])], writes=[("pt", b)])
            S.op("pe", "matmul", pO[0:65, :], Vt(v_of_g(g), kc), self.pt[b][:], start=(kc == 0), stop=(kc == NKC - 1),
                 reads=[("pt", b), "V"], writes=[("ps", pkey)])
            if kc == NKC - 1:
                finalize(g, qs, pO, pkey)

    def phase_mla(self, l):
        nc, S = self.nc, self.S
        with ExitStack() as es:
            QT = self.sb(es, "QT", [128, 4, T], BF16)
            KT = self.sb(es, "KT", [128, 4, NKEY], BF16)
            Va = self.sb(es, "Va", [128, NKEY // 128, 4, 65], BF16)
            self.pt = [self.sb(es, f"pt{i}", [128, 512], BF16) for i in range(3)]
            wuq32 = self.sb(es, "wuq32", [128, 2, 384], F32)
            wuq = self.sb(es, "wuq", [128, 2, 384], BF16)
            wukv32 = self.sb(es, "wukv32", [128, 512], F32)
            wukv = self.sb(es, "wukv", [128, 512], BF16)
            g_q = self.sb(es, "g_q", [128, 256], F32)
            g_kv = self.sb(es, "g_kv", [128, 128], F32)
            g_nope = self.sb(es, "g_nope", [128, 128], F32)
            g_rope = self.sb(es, "g_rope", [128, 64], F32)
            S.op("pool", "memset", Va[:], 1.0, writes=["V"])
            S.dma("sp", wuq32[:], self.mla_w_uq[l].rearrange("(c p) n -> p c n", p=128), writes=["wuq32"])
            S.op("dve", "tensor_copy", wuq[:], wuq32[:], reads=["wuq32"], writes=["wuq"])
            S.dma("sp", wukv32[:], self.mla_w_ukv[l], writes=["wukv32"])
            S.op("dve", "tensor_copy", wukv[:], wukv32[:], reads=["wukv32"], writes=["wukv"])
            self.bcast_load("sp", g_q[:], self.mla_q_norm_g[l, :], "g_q")
            self.bcast_load("sp", g_kv[:], self.mla_kv_norm_g[l, :], "g_kv")
            self.bcast_load("sp", g_nope[:], self.mla_nope_g[l].rearrange("a b -> (a b)"), "g_nope")
            self.bcast_load("sp", g_rope[:], self.mla_rope_g[l].rearrange("a b -> (a b)"), "g_rope")
            with ExitStack() as es2:
                cqt = [self.sb(es2, f"cqt{i}", [128, 416], F32) for i in range(2)]
                rc = [self.sb(es2, f"rc{i}", [128, 256], F32) for i in range(2)]
                rs = [self.sb(es2, f"rs{i}", [128, 256], F32) for i in range(2)]
                qa = [self.sb(es2, f"qa{i}", [128, 17], F32) for i in range(2)]
                ka = [self.sb(es2, f"ka{i}", [128, 17], F32) for i in range(2)]
                sq = self.sb(es2, "sq", [128, 256], F32)
                ssb = self.sb(es2, "ssb", [128, 16], F32)
                t1 = self.sb(es2, "t1", [128, 256], F32)
                cqb = self.sb(es2, "cqb", [128, 256], BF16)
                cqT = self.sb(es2, "cqT", [128, 2, 128], BF16)
                q32 = self.sb(es2, "q32", [128, 384], F32)
                qr = self.sb(es2, "qr", [128, 4, 32], F32)
                Qst = self.sb(es2, "Qst", [128, 4, 113], BF16)
                Kst = self.sb(es2, "Kst", [128, 4, 113], BF16)
                ckn = [self.sb(es2, f"ckn{i}", [128, 128], F32) for i in range(2)]
                ckb = self.sb(es2, "ckb", [128, 128], BF16)
                ckT = self.sb(es2, "ckT", [128, 128], BF16)
                kv32 = self.sb(es2, "kv32", [128, 512], F32)
                krn = [self.sb(es2, f"krn{i}", [128, 32], F32) for i in range(2)]
                krr = self.sb(es2, "krr", [128, 32], F32)
                q32v = q32[:].rearrange("p (h w) -> p h w", h=4)
                kv32v = kv32[:].rearrange("p (h w) -> p h w", h=4)

                def transposes(src_tile, n, width, dst_ap, rkey, wkey, pi):
                    pst = self.psum[pi][:, :].bitcast(BF16)
                    for h in range(n):
                        S.op("pe", "transpose", pst[0:width, h * 128:(h + 1) * 128], src_tile(h), self.ident_bf[:],
                             reads=[rkey], writes=[("ps", pi)])
                    S.op("act", "activation", out=dst_ap, in_=pst[0:width, 0:n * 128].rearrange("p (h t) -> p h t", h=n),
                         func=AF.Copy, reads=[("ps", pi)], writes=[wkey])

                for t in range(NKEY // 128):
                    b = t % 2
                    tok0 = t * 128
                    ctx = t >= NT
                    S.dma("sp", ka[b][:], self.kaug[tok0:tok0 + 128, :], writes=[("ka", b)])
                    if not ctx:
                        S.dma("sp", cqt[b][:], self.s_cq[tok0:tok0 + 128, :], writes=[("cqt", b)])
                        S.dma("sp", rc[b][:], self.ropeC[tok0:tok0 + 128, :], writes=["rc"])
                        S.dma("sp", rs[b][:], self.ropeS[tok0:tok0 + 128, :], writes=["rs"])
                        S.dma("sp", qa[b][:], self.qaug[tok0:tok0 + 128, :], writes=[("qa", b)])
                        self.rms_tm(cqt[b][:, 0:256].unsqueeze(1), cqb[:].unsqueeze(1), 1, 256, g_q[:], sq, ssb,
                                    [("cqt", b)], ["cqb"], gain_eng="pool")
                        transposes(lambda c: cqb[:, c * 128:(c + 1) * 128], 2, 128, cqT[:, :, :], "cqb", "cqT", 4)
                        for c in range(2):
                            S.op("pe", "matmul", self.psum[5][:, 0:384], cqT[:, c, :], wuq[:, c, :], start=(c == 0), stop=(c == 1),
                                 reads=["cqT", "wuq"], writes=[("ps", 5)])
                        S.op("act", "activation", out=q32[:], in_=self.psum[5][:, 0:384], func=AF.Copy,
                             reads=[("ps", 5)], writes=["q32"])
                        self.rms_tm(q32v[:, :, 0:64], Qst[:, :, 0:64], 4, 64, g_nope[:, 0:64], sq, ssb, ["q32"], ["Qst"],
                                    gain_eng="pool")
                        self.rms_tm(q32v[:, :, 64:96], qr[:], 4, 32, g_rope[:, 0:32], sq, ssb, ["q32"], ["qr"])
                        self.rope_tm(qr[:], Qst[:, :, 64:96], 4, rc[b], rs[b], t1, ["qr"], ["Qst"])
                        S.op("pool", "tensor_copy", Qst[:, :, 96:113], qa[b][:].unsqueeze(1).to_broadcast([128, 4, 17]),
                             reads=[("qa", b)], writes=["Qst"])
                        transposes(lambda h: Qst[:, h, :], 4, 113, QT[0:113, :, tok0:tok0 + 128], "Qst", "QT", 4)
                        self.rms_tm(cqt[b][:, 256:384].unsqueeze(1), ckn[b][:].unsqueeze(1), 1, 128, g_kv[:], sq, ssb,
                                    [("cqt", b)], [("ckn", b)])
                        S.dma("sp", self.o_ckv[l, tok0:tok0 + 128, :], ckn[b][:], reads=[("ckn", b)])
                        self.rms_tm(cqt[b][:, 384:416].unsqueeze(1), krn[b][:].unsqueeze(1), 1, 32, g_rope[:, 32:64], sq, ssb,
                                    [("cqt", b)], [("krn", b)])
                        S.dma("sp", self.o_krope[l, tok0:tok0 + 128, :], krn[b][:], reads=[("krn", b)])
                        S.op("dve", "tensor_copy", krr[:], krn[b][:], reads=[("krn", b)], writes=["krr"])
                        self.rope_tm(krr[:].unsqueeze(1), krr[:].unsqueeze(1), 1, rc[b], rs[b], t1, ["krr"], ["krr"])
                    else:
                        c0 = tok0 - T
                        S.dma("sp", ckn[b][:], self.c_ckv[l, c0:c0 + 128, :], writes=[("ckn", b)])
                        S.dma("sp", krr[:], self.c_krope[l, c0:c0 + 128, :], writes=["krr"])
                    S.op("pool", "tensor_copy", ckb[:], ckn[b][:], reads=[("ckn", b)], writes=["ckb"])
                    transposes(lambda h: ckb[:], 1, 128, ckT[:].unsqueeze(1), "ckb", "ckT", 6)
                    S.op("pe", "matmul", self.psum[7][:, :], ckT[:], wukv[:], start=True, stop=True,
                         reads=["ckT", "wukv"], writes=[("ps", 7)])
                    S.op("act", "activation", out=kv32[:], in_=self.psum[7][:, :], func=AF.Copy, reads=[("ps", 7)], writes=["kv32"])
                    self.rms_tm(kv32v[:, :, 0:64], Kst[:, :, 0:64], 4, 64, g_nope[:, 64:128], sq, ssb, ["kv32"], ["Kst"],
                                gain_eng="pool")
                    S.op("pool", "tensor_copy", Va[:, t, :, 0:64], kv32v[:, :, 64:128], reads=["kv32"], writes=["V"])
                    S.op("pool", "tensor_copy", Kst[:, :, 64:96], krr[:].unsqueeze(1).to_broadcast([128, 4, 32]),
                         reads=["krr"], writes=["Kst"])
                    S.op("pool", "tensor_copy", Kst[:, :, 96:113], ka[b][:].unsqueeze(1).to_broadcast([128, 4, 17]),
                         reads=[("ka", b)], writes=["Kst"])
                    transposes(lambda h: Kst[:, h, :], 4, 113, KT[0:113, :, tok0:tok0 + 128], "Kst", "KT", 6)
                S.barrier()

            with ExitStack() as es3:
                o32 = [self.sb(es3, f"o32{i}", [128, 512], F32) for i in range(2)]
                yf = self.sb(es3, "yf", [64, 512], F32)
                gt = [self.sb(es3, f"gt{i}", [64, 512], BF16) for i in range(2)]
                yb = [self.sb(es3, f"yb{i}", [64, 512], BF16) for i in range(2)]
                cnt = [0]

                def fin(h, qs, pO, pkey):
                    i = cnt[0] % 2
                    cnt[0] += 1
                    S.dma("sp", gt[i][:], self.s_gT[h * 64:(h + 1) * 64, qs * 512:(qs + 1) * 512], writes=[("gt", i)])
                    S.op("act", "activation", out=o32[i][0:65, :], in_=pO[0:65, :], func=AF.Copy,
                         reads=[("ps", pkey)], writes=[("o32", i)])
                    S.op("dve", "reciprocal", o32[i][64:65, :], o32[i][64:65, :], reads=[("o32", i)], writes=[("o32", i)])
                    S.op("pe", "matmul", self.psum[4][0:64, :], self.ones_f[64:65, 0:64], o32[i][64:65, :], start=True, stop=True,
                         reads=[("o32", i)], writes=[("ps", 4)])
                    S.op("dve", "tensor_tensor", yf[:], o32[i][0:64, :], self.psum[4][0:64, :], ALU.mult,
                         reads=[("o32", i), ("ps", 4)], writes=["yf"])
                    S.op("pool", "tensor_tensor", yb[i][:], yf[:], gt[i][:], ALU.mult, reads=["yf", ("gt", i)], writes=[("yb", i)])
                    S.dma("sp", self.s_yT[h * 64:(h + 1) * 64, qs * 512:(qs + 1) * 512], yb[i][:], reads=[("yb", i)])

                self.attend(4, 113,
                            lambda g, qs: QT[0:113, g, qs * 512:(qs + 1) * 512],
                            lambda g, kc: KT[0:113, g, kc * 128:(kc + 1) * 128],
                            lambda h, kc: Va[:, kc, h, :],
                            lambda g: g, float((64 + 32) ** -0.5), fin)

    def phase_diff(self, l):
        nc, S = self.nc, self.S
        lam_init = 0.8 - 0.6 * math.exp(-0.3 * l)
        with ExitStack() as es:
            QdT = self.sb(es, "QdT", [128, 4, T], BF16)
            KdT = self.sb(es, "KdT", [128, 4, NKEY], BF16)
            Vd = self.sb(es, "Vd", [128, NKEY // 128, 4, 65], BF16)
            self.pt = [self.sb(es, f"pt{i}", [128, 512], BF16) for i in range(3)]
            g_qk = self.sb(es, "g_qk", [128, 64], F32)
            lpb = self.sb(es, "lpb", [128, 128], F32)
            lsm = self.sb(es, "lsm", [128, 72], F32)
            neglam = self.sb(es, "neglam", [128, 1], F32)
            gsub = self.sb(es, "gsub", [64, 1], F32)
            self.bcast_load("sp", g_qk[:], self.diff_qk_g[l].rearrange("a b -> (a b)"), "g_qk")
            self.bcast_load("sp", lpb[:], self.diff_lambda[l, :], "lpb")
            S.dma("sp", gsub[:], self.diff_subln_g[l], writes=["gsub"])
            S.op("dve", "tensor_scalar", gsub[:], gsub[:], float(1.0 - lam_init), None, ALU.mult, reads=["gsub"], writes=["gsub"])
            lp4 = lpb[:].rearrange("p (a b w) -> p a b w", a=2, b=2)
            S.op("dve", "tensor_tensor", lsm[:, 0:64].rearrange("p (a w) -> p a w", a=2), lp4[:, :, 0, :], lp4[:, :, 1, :], ALU.mult,
                 reads=["lpb"], writes=["lsm"])
            S.op("dve", "tensor_reduce", lsm[:, 64:66], lsm[:, 0:64].rearrange("p (a w) -> p a w", a=2), AX.X, ALU.add,
                 reads=["lsm"], writes=["lsm2"])
            S.op("act", "activation", out=lsm[:, 66:68], in_=lsm[:, 64:66], func=AF.Exp, reads=["lsm2"], writes=["lsm3"])
            S.op("dve", "tensor_tensor", neglam[:], lsm[:, 67:68], lsm[:, 66:67], ALU.subtract, reads=["lsm3"], writes=["neglam"])
            S.op("dve", "tensor_scalar", neglam[:], neglam[:], float(-lam_init), None, ALU.add, reads=["neglam"], writes=["neglam"])
            S.dma("sp", Vd[:, 0:NT, :, :].rearrange("p c h e -> p c (h e)"),
                  self.s_vd[0:T, :].rearrange("(c p) e -> p c e", p=128), writes=["V"])
            with ExitStack() as es1:
                cv32 = self.sb(es1, "cv32", [128, 4, 256], F32)
                S.dma("sp", cv32[:], self.c_dv[l].rearrange("(c p) e -> p c e", p=128), writes=["cv32"])
                S.op("pool", "memset", Vd[:, NT:NT + 4, :, :], 1.0, reads=["V"], writes=["V"])
                for c in range(4):
                    S.op("pool", "tensor_copy", Vd[:, NT + c, :, 0:64], cv32[:, c, :].rearrange("p (h e) -> p h e", h=4),
                         reads=["cv32", "V"], writes=["V"])
                S.barrier()
            with ExitStack() as es2:
                dqk = [self.sb(es2, f"dqk{i}", [128, 512], F32) for i in range(2)]
                rc = [self.sb(es2, f"rc{i}", [128, 256], F32) for i in range(2)]
                rs = [self.sb(es2, f"rs{i}", [128, 256], F32) for i in range(2)]
                qa = [self.sb(es2, f"qa{i}", [128, 17], F32) for i in range(2)]
                ka = [self.sb(es2, f"ka{i}", [128, 17], F32) for i in range(2)]
                sq = self.sb(es2, "sq", [128, 256], F32)
                ssb = self.sb(es2, "ssb", [128, 16], F32)
                t1 = self.sb(es2, "t1", [128, 256], F32)
                qn = self.sb(es2, "qn", [128, 8, 32], F32)
                kn = [self.sb(es2, f"kn{i}", [128, 8, 32], F32) for i in range(2)]
                kr = self.sb(es2, "kr", [128, 8, 32], F32)
                Qdst = self.sb(es2, "Qdst", [128, 8, 64], BF16)
                Kdst = self.sb(es2, "Kdst", [128, 8, 64], BF16)
                S.op("pool", "memset", Qdst[:], 0.0, writes=["Qdst"])
                S.op("pool", "memset", Kdst[:], 0.0, writes=["Kdst"])

                def transposes(src, dst_ap, rkey, wkey, pi):
                    pst = self.psum[pi][:, :].bitcast(BF16)
                    for c in range(4):
                        S.op("pe", "transpose", pst[:, c * 128:(c + 1) * 128],
                             src[:, 2 * c:2 * c + 2, :].rearrange("p a w -> p (a w)"), self.ident_bf[:],
                             reads=[rkey], writes=[("ps", pi)])
                    S.op("act", "activation", out=dst_ap, in_=pst[:, 0:512].rearrange("p (c t) -> p c t", c=4),
                         func=AF.Copy, reads=[("ps", pi)], writes=[wkey])

                for t in range(NKEY // 128):
                    b = t % 2
                    tok0 = t * 128
                    ctx = t >= NT
                    S.dma("sp", ka[b][:], self.kaug[tok0:tok0 + 128, :], writes=[("ka", b)])
                    if not ctx:
                        S.dma("sp", dqk[b][:], self.s_dqk[tok0:tok0 + 128, :], writes=[("dqk", b)])
                        S.dma("sp", rc[b][:], self.ropeC[tok0:tok0 + 128, :], writes=["rc"])
                        S.dma("sp", rs[b][:], self.ropeS[tok0:tok0 + 128, :], writes=["rs"])
                        S.dma("sp", qa[b][:], self.qaug[tok0:tok0 + 128, :], writes=[("qa", b)])
                        qv = dqk[b][:, 0:256].rearrange("p (g w) -> p g w", g=8)
                        kv = dqk[b][:, 256:512].rearrange("p (g w) -> p g w", g=8)
                        self.rms_tm(qv, qn[:], 8, 32, g_qk[:, 0:32], sq, ssb, [("dqk", b)], ["qn"])
                        self.rope_tm(qn[:], Qdst[:, :, 0:32], 8, rc[b], rs[b], t1, ["qn"], ["Qdst"])
                        S.op("pool", "tensor_copy", Qdst[:, :, 32:49], qa[b][:].unsqueeze(1).to_broadcast([128, 8, 17]),
                             reads=[("qa", b)], writes=["Qdst"])
                        transposes(Qdst, QdT[:, :, tok0:tok0 + 128], "Qdst", "QT", 4)
                        self.rms_tm(kv, kn[b][:], 8, 32, g_qk[:, 32:64], sq, ssb, [("dqk", b)], [("kn", b)])
                        S.dma("sp", self.o_dk[l, tok0:tok0 + 128, :], kn[b][:].rearrange("p g w -> p (g w)"), reads=[("kn", b)])
                        S.op("dve", "tensor_copy", kr[:], kn[b][:], reads=[("kn", b)], writes=["kr"])
                        self.rope_tm(kr[:], Kdst[:, :, 0:32], 8, rc[b], rs[b], t1, ["kr"], ["Kdst"])
                    else:
                        c0 = tok0 - T
                        S.dma("sp", kn[b][:].rearrange("p g w -> p (g w)"), self.c_dk[l, c0:c0 + 128, :], writes=[("kn", b)])
                        S.op("pool", "tensor_copy", Kdst[:, :, 0:32], kn[b][:], reads=[("kn", b)], writes=["Kdst"])
                    S.op("pool", "tensor_copy", Kdst[:, :, 32:49], ka[b][:].unsqueeze(1).to_broadcast([128, 8, 17]),
                         reads=[("ka", b)], writes=["Kdst"])
                    transposes(Kdst, KdT[:, :, tok0:tok0 + 128], "Kdst", "KT", 6)
                S.barrier()

            with ExitStack() as es3:
                o32 = [self.sb(es3, f"o32{i}", [128, 512], F32) for i in range(2)]
                y1 = self.sb(es3, "y1", [64, 512], F32)
                y2 = self.sb(es3, "y2", [64, 512], F32)
                sqf = self.sb(es3, "sqf", [64, 512], F32)
                gt = [self.sb(es3, f"gt{i}", [64, 512], BF16) for i in range(2)]
                yb = [self.sb(es3, f"yb{i}", [64, 512], BF16) for i in range(2)]
                cnt = [0]

                def fin(g, qs, pO, pkey):
                    h, j = g // 2, g % 2
                    S.op("act", "activation", out=o32[j][0:65, :], in_=pO[0:65, :], func=AF.Copy,
                         reads=[("ps", pkey)], writes=[("o32", j)])
                    S.op("dve", "reciprocal", o32[j][64:65, :], o32[j][64:65, :], reads=[("o32", j)], writes=[("o32", j)])
                    if j == 0:
                        return
                    i = cnt[0] % 2
                    cnt[0] += 1
                    r0 = 256 + h * 64
                    S.dma("sp", gt[i][:], self.s_gT[r0:r0 + 64, qs * 512:(qs + 1) * 512], writes=[("gt", i)])
                    for jj in range(2):
                        S.op("pe", "matmul", self.psum[4 + jj][0:64, :], self.ones_f[64:65, 0:64], o32[jj][64:65, :],
                             start=True, stop=True, reads=[("o32", jj)], writes=[("ps", 4 + jj)])
                    S.op("dve", "tensor_tensor", y1[:], o32[0][0:64, :], self.psum[4][0:64, :], ALU.mult,
                         reads=[("o32", 0), ("ps", 4)], writes=["y1"])
                    S.op("dve", "tensor_tensor", y2[:], o32[1][0:64, :], self.psum[5][0:64, :], ALU.mult,
                         reads=[("o32", 1), ("ps", 5)], writes=["y2"])
                    S.op("dve", "scalar_tensor_tensor", y1[:], y2[:], neglam[0:64, 0:1], y1[:], ALU.mult, ALU.add,
                         reads=["y1", "y2", "neglam"], writes=["y1"])
                    S.op("pool", "tensor_tensor", sqf[:], y1[:], y1[:], ALU.mult, reads=["y1"], writes=["sqf"])
                    S.op("pe", "matmul", self.psum[6][0:64, :], self.ones_f[0:64, 0:64], sqf[:], start=True, stop=True,
                         reads=["sqf"], writes=[("ps", 6)])
                    S.op("dve", "tensor_scalar", y2[:], self.psum[6][0:64, :], 1.0 / 64, EPS, ALU.mult, ALU.add,
                         reads=[("ps", 6)], writes=["y2"])
                    S.op("act", "activation", out=y2[:], in_=y2[:], func=AF.Sqrt, reads=["y2"], writes=["y2"])
                    S.op("dve", "reciprocal", y2[:], y2[:], reads=["y2"], writes=["y2"])
                    S.op("dve", "scalar_tensor_tensor", y1[:], y1[:], gsub[0:64, 0:1], y2[:], ALU.mult, ALU.mult,
                         reads=["y1", "y2", "gsub"], writes=["y1"])
                    S.op("pool", "tensor_tensor", yb[i][:], y1[:], gt[i][:], ALU.mult, reads=["y1", ("gt", i)], writes=[("yb", i)])
                    S.dma("sp", self.s_yT[r0:r0 + 64, qs * 512:(qs + 1) * 512], yb[i][:], reads=[("yb", i)])

                order = [(2 * h + j, qs) for h in range(4) for qs in range(NST) for j in range(2)]
                self.attend(8, 64,
                            lambda g, qs: QdT[64 * (g % 2):64 * (g % 2) + 64, g // 2, qs * 512:(qs + 1) * 512],
                            lambda g, kc: KdT[64 * (g % 2):64 * (g % 2) + 64, g // 2, kc * 128:(kc + 1) * 128],
                            lambda h, kc: Vd[:, kc, h, :],
                            lambda g: g // 2, float(32 ** -0.5), fin, order=order)

    def phase_hgrn(self, l):
        nc, S = self.nc, self.S
        NCH = T // 64
        SL = 512
        with ExitStack() as es:
            rmask = self.sb(es, "rmask", [128, SL], F32)
            mT = self.sb(es, "mT", [64, 256], F32)
            og = self.sb(es, "og", [64, 1], F32)
            S.op("pool", "memset", rmask[:], 1.0, writes=["rmask"])
            S.op("pool", "memset", rmask[:].rearrange("p (c s) -> p c s", s=64)[:, :, 0:1], 0.0, writes=["rmask"])
            S.dma("sp", mT[:], self.maskT[:, :], writes=["mT"])
            S.dma("sp", og[:], self.hgrn_out_g[l], writes=["og"])
            for hp in range(2):
                with ExitStack() as es1:
                    lbl = self.sb(es1, "lbl", [128, 24], F32)
                    vsb = self.sb(es1, "vsb", [64, NCH, 128], BF16)
                    qt = [self.sb(es1, f"qt{d}", [128, T], BF16) for d in range(2)]
                    kt = [self.sb(es1, f"kt{d}", [128, T], BF16) for d in range(2)]
                    qo = [self.sb(es1, f"qo{d}", [128, T], BF16) for d in range(2)]
                    ko = [self.sb(es1, f"ko{d}", [128, T], BF16) for d in range(2)]
                    qb = [self.sb(es1, f"qb{d}", [128, T], BF16) for d in range(2)]
                    khat = [self.sb(es1, f"khat{d}", [64, NCH, 128], BF16) for d in range(2)]
                    stab = [self.sb(es1, f"stab{d}", [128, NCH, 128], BF16) for d in range(2)]
                    etot = [self.sb(es1, f"etot{d}", [128, NCH], F32) for d in range(2)]
                    car = [self.sb(es1, f"car{d}", [128, NCH], F32) for d in range(2)]
                    S.dma("sp", vsb[:], self.s_hv[:, hp * 128:(hp + 1) * 128].rearrange("(c s) e -> s c e", s=64), writes=["vsb"])
                    for d in range(2):
                        S.dma("sp", car[d][:], self.carry[d], writes=[("car", d)])
                    S.dma("sp", lbl[:, 0:8], self.lbT[hp * 128:(hp + 1) * 128, :], writes=["lbl"])
                    S.op("act", "activation", out=lbl[:, 8:16], in_=lbl[:, 0:8], func=AF.Exp, reads=["lbl"], writes=["lbe"])
                    S.op("dve", "tensor_reduce", lbl[:, 16:18], lbl[:, 8:16].rearrange("p (a b) -> p a b", a=2), AX.X, ALU.add,
                         reads=["lbe"], writes=["lbs"])
                    S.op("dve", "reciprocal", lbl[:, 16:18], lbl[:, 16:18], reads=["lbs"], writes=["lbs"])
                    for d in range(2):
                        if l == 0:
                            S.op("dve", "memset", lbl[:, 18 + d:19 + d], 0.0, writes=[("lb", d)])
                        else:
                            S.op("dve", "tensor_reduce", lbl[:, 18 + d:19 + d], lbl[:, 8 + 4 * d + 1:8 + 4 * d + 1 + l], AX.X, ALU.add,
                                 reads=["lbe"], writes=[("lb", d)])
                            S.op("dve", "tensor_tensor", lbl[:, 18 + d:19 + d], lbl[:, 18 + d:19 + d], lbl[:, 16 + d:17 + d], ALU.mult,
                                 reads=[("lb", d), "lbs"], writes=[("lb", d)])
                        S.op("dve", "tensor_scalar", lbl[:, 20 + d:21 + d], lbl[:, 18 + d:19 + d], -1.0, 1.0, ALU.mult, ALU.add,
                             reads=[("lb", d)], writes=[("oml", d)])
                    if int(os.environ.get("KHG", "3")) < 1:
                        continue
                    with ExitStack() as es2:
                        q32 = self.sb(es2, "hq32", [128, SL], F32)
                        z = self.sb(es2, "hz", [128, SL], F32)
                        g = self.sb(es2, "hg", [128, SL], F32)
                        k = self.sb(es2, "hk", [128, SL], F32)
                        bb = self.sb(es2, "hb_", [128, SL], F32)
                        bc = self.sb(es2, "hbc", [128, SL], F32)
                        e1 = self.sb(es2, "he1", [128, SL], F32)
                        khT = self.sb(es2, "khT", [128, SL], BF16)
                        k2 = self.sb(es2, "hk2", [128, SL], F32)
                        c3 = lambda t_: t_[:].rearrange("p (c s) -> p c s", s=64)
                        ncs = SL // 64
                        for sl in range(T // SL):
                            tsl = slice(sl * SL, (sl + 1) * SL)
                            S.dma("sp", q32[:], self.s_hT[hp * 128:(hp + 1) * 128, tsl], writes=["q32"])
                            for d in range(2):
                                r0 = 256 * (1 + d) + hp * 128
                                S.dma("sp", z[:], self.s_hT[r0:r0 + 128, tsl], writes=["z"])
                                S.op("act", "activation", out=z[:], in_=z[:], func=AF.Sigmoid, reads=["z"], writes=["z"])
                                S.op("dve", "tensor_scalar", z[:], z[:], lbl[:, 20 + d:21 + d], lbl[:, 18 + d:19 + d], ALU.mult, ALU.add,
                                     reads=["z", ("lb", d), ("oml", d)], writes=["z"])
                                S.op("act", "activation", out=g[:], in_=z[:], func=AF.Ln, reads=["z"], writes=["g"])
                                S.op("pool", "tensor_scalar", k[:], z[:], -1.0, 1.0, ALU.mult, ALU.add, reads=["z"], writes=["k"])
                                S.op("dve", "tensor_tensor_scan", bb[:], rmask[:], g[:], 0.0, ALU.mult, ALU.add,
                                     reads=["g", "rmask"], writes=["bb"])
                                tot = c3(bb)[:, :, 63:64]
                                totb = tot.to_broadcast([128, ncs, 64])
                                if d == 1:
                                    S.op("dve", "tensor_tensor", bc[:], g[:], bb[:], ALU.subtract, reads=["g", "bb"], writes=["bc"])
                                    S.op("dve", "tensor_tensor", c3(e1), c3(bc), totb, ALU.add, reads=["bc", "bb"], writes=["e1"])
                                    S.op("act", "activation", out=etot[d][:, sl * ncs:(sl + 1) * ncs].unsqueeze(2), in_=tot, func=AF.Exp,
                                         reads=["bb"], writes=[("etot", d)])
                                    S.op("dve", "tensor_copy", bb[:], e1[:], reads=["e1"], writes=["bb"])
                                    totb = c3(bb)[:, :, 0:1].to_broadcast([128, ncs, 64])
                                else:
                                    S.op("act", "activation", out=etot[d][:, sl * ncs:(sl + 1) * ncs].unsqueeze(2), in_=tot, func=AF.Exp,
                                         reads=["bb"], writes=[("etot", d)])
                                c32 = lambda t_: t_[:].rearrange("p (c s) -> p c s", s=32)
                                rix = 31 if d == 0 else 32
                                S.op("dve", "tensor_tensor", c3(bc), c3(bb), c3(bb)[:, :, rix:rix + 1].to_broadcast([128, ncs, 64]),
                                     ALU.subtract, reads=["bb"], writes=["bc"])
                                S.op("pool", "tensor_scalar", k2[:], bc[:], 0.0, None, ALU.max, reads=["bc"], writes=["k2"])
                                S.op("dve", "tensor_scalar", bc[:], bc[:], 0.0, None, ALU.min, reads=["bc"], writes=["bc"])
                                S.op("act", "activation", out=e1[:], in_=bc[:], func=AF.Exp, reads=["bc"], writes=["e1"])
                                S.op("pool", "tensor_tensor", qo[d][:, tsl], q32[:], e1[:], ALU.mult, reads=["q32", "e1"], writes=[("qo", d)])
                                S.op("act", "activation", out=e1[:], in_=k2[:], func=AF.Exp, scale=-1.0, reads=["k2"], writes=["e1"])
                                S.op("pool", "tensor_tensor", ko[d][:, tsl], k[:], e1[:], ALU.mult, reads=["k", "e1"], writes=[("ko", d)])
                                S.op("dve", "tensor_tensor", c32(bc), c32(bb), c32(bb)[:, :, 15:16].to_broadcast([128, 2 * ncs, 32]),
                                     ALU.subtract, reads=["bb"], writes=["bc"])
                                S.op("dve", "tensor_scalar", bc[:], bc[:], 40.0, -40.0, ALU.min, ALU.max, reads=["bc"], writes=["bc"])
                                S.op("act", "activation", out=e1[:], in_=bc[:], func=AF.Exp, reads=["bc"], writes=["e1"])
                                S.op("pool", "tensor_tensor", qt[d][:, tsl], q32[:], e1[:], ALU.mult, reads=["q32", "e1"], writes=[("qt", d)])
                                S.op("dve", "reciprocal", e1[:], e1[:], reads=["e1"], writes=["e1"])
                                S.op("pool", "tensor_tensor", kt[d][:, tsl], k[:], e1[:], ALU.mult, reads=["k", "e1"], writes=[("kt", d)])
                                S.op("act", "activation", out=e1[:], in_=bb[:], func=AF.Exp, reads=["bb"], writes=["e1"])
                                S.op("pool", "tensor_tensor", qb[d][:, tsl], q32[:], e1[:], ALU.mult, reads=["q32", "e1"], writes=[("qb", d)])
                                S.op("dve", "tensor_tensor", c3(bc), totb, c3(bb), ALU.subtract, reads=["bb"], writes=["bc"])
                                S.op("act", "activation", out=e1[:], in_=bc[:], func=AF.Exp, reads=["bc"], writes=["e1"])
                                S.op("pool", "tensor_tensor", khT[:], k[:], e1[:], ALU.mult, reads=["k", "e1"], writes=["khT"])
                                for c4 in range(ncs // 8):
                                    pi = 4 + c4 % 2
                                    pst = self.psum[pi][:, :].bitcast(BF16)
                                    for cc in range(8):
                                        c = c4 * 8 + cc
                                        S.op("pe", "transpose", pst[0:64, cc * 128:(cc + 1) * 128], khT[:, c * 64:(c + 1) * 64],
                                             self.ident_bf[:], reads=["khT"], writes=[("ps", pi)])
                                    c0 = sl * ncs + c4 * 8
                                    S.op("act", "activation", out=khat[d][:, c0:c0 + 8, :],
                                         in_=pst[0:64, 0:1024].rearrange("p (c e) -> p c e", c=8), func=AF.Copy,
                                         reads=[("ps", pi)], writes=[("khat", d)])
                        S.barrier()
                    khg = int(os.environ.get("KHG", "3"))
                    if khg < 2:
                        continue
                    with ExitStack() as es3:
                        Scur = [self.sb(es3, f"Scur{d}", [128, 128], F32) for d in range(2)]
                        Sout = [[self.sb(es3, f"Sout{d}{i}", [128, 128], F32) for i in range(2)] for d in range(2)]
                        for d in range(2):
                            S.op("pool", "memset", Scur[d][:], 0.0, writes=[("Scur", d)])
                            for hh in range(2):
                                S.dma("sp", Scur[d][hh * 64:(hh + 1) * 64, hh * 64:(hh + 1) * 64], self.s0[l, d, 2 * hp + hh],
                                      reads=[("Scur", d)], writes=[("Scur", d)])
                        for step in range(NCH):
                            for d in range(2):
                                c = step if d == 0 else NCH - 1 - step
                                i = step % 2
                                pi = 4 + d
                                S.op("pool", "tensor_copy", stab[d][:, c, :], Scur[d][:], reads=[("Scur", d)], writes=[("stab", d)])
                                S.op("pe", "matmul", self.psum[pi][:, 0:128], khat[d][:, c, :], vsb[:, c, :], start=True, stop=True,
                                     reads=[("khat", d), "vsb"], writes=[("ps", pi)])
                                S.op("dve", "scalar_tensor_tensor", Sout[d][i][:], Scur[d][:], etot[d][:, c:c + 1], self.psum[pi][:, 0:128],
                                     ALU.mult, ALU.add, reads=[("Scur", d), ("etot", d), ("ps", pi)], writes=[("Sout", d, i)])
                                is_end = (c % 4 == 3) if d == 0 else (c % 4 == 0)
                                if is_end:
                                    for hh in range(2):
                                        S.dma("sp", self.o_state[l, c // 4, d, 2 * hp + hh],
                                              Sout[d][i][hh * 64:(hh + 1) * 64, hh * 64:(hh + 1) * 64], reads=[("Sout", d, i)])
                                cn = c + 1 if d == 0 else c - 1
                                if 0 <= cn < NCH:
                                    S.op("dve", "tensor_scalar", Scur[d][:], Sout[d][i][:], car[d][:, cn:cn + 1], None, ALU.mult,
                                         reads=[("Sout", d, i), ("car", d)], writes=[("Scur", d)])
                        S.barrier()
                    if khg < 3:
                        continue
                    with ExitStack() as es4:
                        Af = [self.sb(es4, f"Af{i}", [64, 256], F32) for i in range(2)]
                        Am = [self.sb(es4, f"Am{i}", [64, 256], BF16) for i in range(2)]
                        o32 = self.sb(es4, "ho32", [64, 512], F32)
                        sqf = self.sb(es4, "hsq", [64, 512], F32)
                        rst = self.sb(es4, "hrst", [64, 512], F32)
                        gt = [self.sb(es4, f"hgt{i}", [64, 512], BF16) for i in range(2)]
                        yb = [self.sb(es4, f"hyb{i}", [64, 512], BF16) for i in range(2)]
                        for st in range(NST):
                            for cc in range(8):
                                c = st * 8 + cc
                                i = c % 2
                                csl = slice(c * 64, (c + 1) * 64)
                                for hh in range(2):
                                    pA = self.psum[hh]
                                    hs_ = slice(hh * 64, (hh + 1) * 64)
                                    h1 = slice(c * 64, c * 64 + 32)
                                    h2 = slice(c * 64 + 32, c * 64 + 64)
                                    for d in range(2):
                                        col = d * 64
                                        for (sh, th, so, to) in ((h1, h1, 0, 0), (h2, h2, 32, 32), (h1, h2, 0, 32), (h2, h1, 32, 0)):
                                            cross_valid = (so == 0 and to == 32) if d == 0 else (so == 32 and to == 0)
                                            kk_, qq_ = (ko[d], qo[d]) if cross_valid else (kt[d], qt[d])
                                            S.op("pe", "matmul", pA[so:so + 32, col + to:col + to + 32], kk_[hs_, sh], qq_[hs_, th],
                                                 start=True, stop=True, reads=[("kt", d), ("qt", d), ("ko", d), ("qo", d)],
                                                 writes=[("ps", hh)])
                                for hh in range(2):
                                    S.op("dve", "tensor_tensor", Af[i][:, hh * 128:(hh + 1) * 128], self.psum[hh][0:64, 0:128],
                                         mT[:, hh * 128:(hh + 1) * 128], ALU.mult, reads=[("ps", hh), "mT"], writes=[("Af", i)])
                                S.op("pool", "tensor_copy", Am[i][:], Af[i][:], reads=[("Af", i)], writes=[("Am", i)])
                                for hh in range(2):
                                    pO = self.psum[2 + hh]
                                    pI = self.psum[4 + hh]
                                    hs = slice(hh * 64, (hh + 1) * 64)
                                    ocol = slice(cc * 64, (cc + 1) * 64)
                                    kp2 = os.environ.get("KP2", "")
                                    if "nointra" in kp2:
                                        continue
                                    S.op("pe", "matmul", pO[0:64, ocol], vsb[:, c, hs], Am[i][:, (hh * 2) * 64:(hh * 2 + 1) * 64],
                                         start=True, stop=False, reads=["vsb", ("Am", i)], writes=[("ps", 2 + hh)])
                                    S.op("pe", "matmul", pO[0:64, ocol], vsb[:, c, hs], Am[i][:, (hh * 2 + 1) * 64:(hh * 2 + 2) * 64],
                                         start=False, stop=True, reads=["vsb", ("Am", i)], writes=[("ps", 2 + hh)])
                                    for d in range(0 if "nointer" in kp2 else 2):
                                        S.op("pe", "matmul", pI[0:64, ocol], stab[d][hs, c, hs], qb[d][hs, csl],
                                             start=(d == 0), stop=(d == 1), reads=[("stab", d), ("qb", d)], writes=[("ps", 4 + hh)])
                            for hh in range(2):
                                h = 2 * hp + hh
                                gi = hh
                                r0 = 512 + h * 64
                                tsl = slice(st * 512, (st + 1) * 512)
                                S.dma("sp", gt[gi][:], self.s_gT[r0:r0 + 64, tsl], writes=[("gt", gi)])
                                S.op("act", "activation", out=o32[:], in_=self.psum[2 + hh][0:64, :], func=AF.Copy,
                                     reads=[("ps", 2 + hh)], writes=["o32"])
                                S.op("dve", "tensor_tensor", o32[:], o32[:], self.psum[4 + hh][0:64, :], ALU.add,
                                     reads=["o32", ("ps", 4 + hh)], writes=["o32"])
                                S.op("pool", "tensor_tensor", sqf[:], o32[:], o32[:], ALU.mult, reads=["o32"], writes=["sqf"])
                                S.op("pe", "matmul", self.psum[6][0:64, :], self.ones_f[0:64, 0:64], sqf[:], start=True, stop=True,
                                     reads=["sqf"], writes=[("ps", 6)])
                                S.op("dve", "tensor_scalar", rst[:], self.psum[6][0:64, :], 1.0 / 64, EPS, ALU.mult, ALU.add,
                                     reads=[("ps", 6)], writes=["rst"])
                                S.op("act", "activation", out=rst[:], in_=rst[:], func=AF.Sqrt, reads=["rst"], writes=["rst"])
                                S.op("dve", "reciprocal", rst[:], rst[:], reads=["rst"], writes=["rst"])
                                S.op("dve", "scalar_tensor_tensor", o32[:], o32[:], og[0:64, 0:1], rst[:], ALU.mult, ALU.mult,
                                     reads=["o32", "rst", "og"], writes=["o32"])
                                S.op("pool", "tensor_tensor", yb[gi][:], o32[:], gt[gi][:], ALU.mult, reads=["o32", ("gt", gi)],
                                     writes=[("yb", gi)])
                                S.dma("sp", self.s_yT[r0:r0 + 64, tsl], yb[gi][:], reads=[("yb", gi)])
                        S.barrier()

    def hy_wrap(self, z, m, key):
        S = self.S
        PI = math.pi
        S.op("dve", "tensor_scalar", m, z, PI, None, ALU.is_gt, reads=[key], writes=[key + "m"])
        S.op("dve", "scalar_tensor_tensor", z, m, -2 * PI, z, ALU.mult, ALU.add, reads=[key, key + "m"], writes=[key])
        S.op("dve", "tensor_scalar", m, z, -PI, None, ALU.is_lt, reads=[key], writes=[key + "m"])
        S.op("dve", "scalar_tensor_tensor", z, m, 2 * PI, z, ALU.mult, ALU.add, reads=[key, key + "m"], writes=[key])

    def phase_hyena(self, l):
        nc, S = self.nc, self.S
        with ExitStack() as es:
            F1z = self.sb(es, "F1z", [32, 128], F32)
            F1f = self.sb(es, "F1f", [64, 128], F32)
            Em = self.sb(es, "Em", [128, 512], BF16)
            Qm = self.sb(es, "Qm", [64, 128 * 64], BF16)
            S.dma("sp", F1z[:], self.F1z[:, :], writes=["F1z"])
            S.dma("sp", F1f[:], self.F1f[:, :], writes=["F1f"])
            S.dma("sp", Em[:], self.Em[:, :], writes=["Em"])
            S.dma("sp", Qm[:], self.Qm[:, :], writes=["Qm"])
            with ExitStack() as es1:
                w1 = self.sb(es1, "hw1", [33, 64], F32)
                w2 = self.sb(es1, "hw2", [64, 64], F32)
                w3 = self.sb(es1, "hw3", [64, 1024], F32)
                cols = self.sb(es1, "hcols", [64, 8], F32)
                nd = self.sb(es1, "hnd", [128, 2], F32)
                ft = [self.sb(es1, f"hft{i}", [33, 512], F32) for i in range(2)]
                tl = [self.sb(es1, f"htl{i}", [128, 512], F32) for i in range(2)]
                dm = [self.sb(es1, f"hdm{i}", [128, 2, 512], F32) for i in range(2)]
                z1 = self.sb(es1, "hz1", [64, 512], F32)
                m1 = self.sb(es1, "hm1", [64, 512], F32)
                h1 = self.sb(es1, "hh1", [64, 512], F32)
                h2 = self.sb(es1, "hh2", [64, 512], F32)
                win = self.sb(es1, "hwin", [128, 2, 512], F32)
                fa = self.sb(es1, "hfa", [128, 512], F32)
                fo = [self.sb(es1, f"hfo{i}", [128, 512], F32) for i in range(2)]
                S.dma("sp", w1[:], self.hy_w1[l], writes=["w1"])
                S.dma("sp", w2[:], self.hy_w2[l], writes=["w2"])
                S.dma("sp", w3[:], self.hy_w3[l], writes=["w3"])
                S.dma("sp", cols[:, 0:4], self.hy_cols[l], writes=["cols"])
                S.dma("sp", nd[:, 0:1], self.negdelta[0:128, :], writes=["nd"])
                S.dma("sp", nd[:, 1:2], self.negdelta[128:256, :], writes=["nd"])
                S.op("dve", "tensor_tensor", cols[:, 4:6], cols[:, 0:2], cols[:, 2:4], ALU.mult, reads=["cols"], writes=["cols2"])
                for sl in range(16):
                    b = sl % 2
                    ssl = slice(sl * 512, (sl + 1) * 512)
                    S.dma("sp", ft[b][:], self.featsT[:, ssl], writes=[("ft", b)])
                    S.dma("sp", tl[b][:], self.featsT[0, ssl].partition_broadcast(128), writes=[("tl", b)])
                    for dd in range(2):
                        S.dma("sp", dm[b][:, dd, :], self.dmask[dd, ssl].partition_broadcast(128), writes=[("dm", b)])
                    S.op("pe", "matmul", self.psum[0][0:64, :], w1[:], ft[b][:], start=True, stop=True,
                         reads=["w1", ("ft", b)], writes=[("ps", 0)])
                    S.op("dve", "tensor_scalar", z1[:], self.psum[0][0:64, :], cols[:, 2:3], cols[:, 4:5], ALU.mult, ALU.add,
                         reads=[("ps", 0), "cols", "cols2"], writes=["z1"])
                    self.hy_wrap(z1[:], m1[:], "z1")
                    S.op("act", "activation", out=h1[:], in_=z1[:], func=AF.Sin, reads=["z1"], writes=["h1"])
                    S.op("pe", "matmul", self.psum[1][0:64, :], w2[:], h1[:], start=True, stop=True,
                         reads=["w2", "h1"], writes=[("ps", 1)])
                    S.op("dve", "tensor_scalar", z1[:], self.psum[1][0:64, :], cols[:, 3:4], cols[:, 5:6], ALU.mult, ALU.add,
                         reads=[("ps", 1), "cols", "cols2"], writes=["z1"])
                    self.hy_wrap(z1[:], m1[:], "z1")
                    S.op("act", "activation", out=h2[:], in_=z1[:], func=AF.Sin, reads=["z1"], writes=["h2"])
                    for cch in range(2):
                        S.op("act", "activation", out=win[:, cch, :], in_=tl[b][:], func=AF.Exp, scale=nd[:, cch:cch + 1],
                             reads=[("tl", b), "nd"], writes=[("win", cch)])
                    it = 0
                    for o in range(2):
                        for cch in range(2):
                            for dd in range(2):
                                c0 = (o * 2 + dd) * 256 + cch * 128
                                S.op("pe", "matmul", self.psum[2 + dd][:, :], w3[:, c0:c0 + 128], h2[:], start=True, stop=True,
                                     reads=["w3", "h2"], writes=[("ps", 2 + dd)])
                            S.op("dve", "tensor_tensor", fa[:], self.psum[2][:, :], dm[b][:, 0, :], ALU.mult,
                                 reads=[("ps", 2), ("dm", b)], writes=["fa"])
                            S.op("dve", "tensor_tensor", fo[it % 2][:], self.psum[3][:, :], dm[b][:, 1, :], ALU.mult,
                                 reads=[("ps", 3), ("dm", b)], writes=[("fo", it % 2)])
                            S.op("pool", "tensor_tensor", fo[it % 2][:], fo[it % 2][:], fa[:], ALU.add,
                                 reads=[("fo", it % 2), "fa"], writes=[("fo", it % 2)])
                            S.op("pool", "tensor_tensor", fo[it % 2][:], fo[it % 2][:], win[:, cch, :], ALU.mult,
                                 reads=[("fo", it % 2), ("win", cch)], writes=[("fo", it % 2)])
                            S.dma("sp", self.s_filtT[o, cch * 128:(cch + 1) * 128, ssl], fo[it % 2][:], reads=[("fo", it % 2)])
                            it += 1
                S.barrier()

            with ExitStack() as es2:
                AB = self.sb(es2, "hyAB", [128, 16384], BF16)
                Y = self.sb(es2, "hyY", [128, 2, 128, 64], BF16)
                xb = [self.sb(es2, f"hyxb{i}", [64, 8, 128], F32) for i in range(2)]
                Gt = [self.sb(es2, f"hyG{i}", [128, 4, 384], BF16) for i in range(2)]
                Xg = [self.sb(es2, f"hyXg{i}", [128, 2, 4, 128], F32) for i in range(2)]
                Hg = [self.sb(es2, f"hyHg{i}", [128, 2, 4, 128], F32) for i in range(2)]
                pt1 = self.sb(es2, "hyp1", [128, 4, 128], F32)
                pt2 = self.sb(es2, "hyp2", [128, 4, 128], F32)
                A4 = AB[:, :].rearrange("p (r k c) -> p r k c", r=2, k=64)
                Bt = AB[0:64, :].rearrange("p (r m c) -> p r m c", r=2, m=128)

                def forward(src, K1, F1, key_f1, mode, hidx):
                    for cg in range(16):
                        b = cg % 2
                        S.dma("sp", xb[b][0:K1, :, :], src[cg * 8:(cg + 1) * 8, :].rearrange("c (a n) -> a c n", n=128),
                              writes=[("xb", b)])
                        for q4 in range(2):
                            pi = q4 % 2
                            for cc in range(4):
                                ci = q4 * 4 + cc
                                S.op("pe", "matmul", self.psum[pi][:, cc * 128:(cc + 1) * 128], xb[b][0:K1, ci, :], F1[0:K1, :],
                                     start=True, stop=True, reads=[("xb", b), key_f1], writes=[("ps", pi)])
                            c0 = cg * 8 + q4 * 4
                            S.op("act", "activation", out=A4[:, :, :, c0:c0 + 4].rearrange("p r k c -> p c r k"),
                                 in_=self.psum[pi][:, :].rearrange("p (c r k) -> p c r k", c=4, r=2), func=AF.Copy,
                                 reads=[("ps", pi)], writes=["A"])
                    for kg in range(16):
                        b = kg % 2
                        S.dma("sp", Gt[b][:], self.Gm[kg * 4:(kg + 1) * 4].rearrange("k n x -> n k x"), writes=[("G", b)])
                        if mode == "Y":
                            S.dma("sp", Hg[b][:], self.s_H[hidx].rearrange("p (r k c) -> p r k c", r=2, k=64)[:, :, kg * 4:(kg + 1) * 4, :],
                                  writes=[("Hg", b)])
                        for kk in range(4):
                            k1 = kg * 4 + kk
                            pi = 2 + kk % 2
                            ps = self.psum[pi]
                            S.op("pe", "matmul", ps[:, 0:128], Gt[b][:, kk, 0:128], A4[:, 0, k1, :], start=True, stop=False,
                                 reads=[("G", b), "A"], writes=[("ps", pi)])
                            S.op("pe", "matmul", ps[:, 0:128], Gt[b][:, kk, 256:384], A4[:, 1, k1, :], start=False, stop=True,
                                 reads=[("G", b), "A"], writes=[("ps", pi)])
                            S.op("pe", "matmul", ps[:, 128:256], Gt[b][:, kk, 128:256], A4[:, 0, k1, :], start=True, stop=False,
                                 reads=[("G", b), "A"], writes=[("ps", pi)])
                            S.op("pe", "matmul", ps[:, 128:256], Gt[b][:, kk, 0:128], A4[:, 1, k1, :], start=False, stop=True,
                                 reads=[("G", b), "A"], writes=[("ps", pi)])
                            S.op("act", "activation", out=Xg[b][:, :, kk, :], in_=ps[:, 0:256].rearrange("p (r c) -> p r c", r=2),
                                 func=AF.Copy, reads=[("ps", pi)], writes=[("Xg", b)])
                        if mode == "H":
                            S.dma("sp", self.s_H[hidx].rearrange("p (r k c) -> p r k c", r=2, k=64)[:, :, kg * 4:(kg + 1) * 4, :],
                                  Xg[b][:], reads=[("Xg", b)])
                        else:
                            Xr, Xi, Hr, Hi = Xg[b][:, 0], Xg[b][:, 1], Hg[b][:, 0], Hg[b][:, 1]
                            ksl = slice(kg * 4, (kg + 1) * 4)
                            S.op("dve", "tensor_tensor", pt1[:], Xr, Hr, ALU.mult, reads=[("Xg", b), ("Hg", b)], writes=["pt1"])
                            S.op("pool", "tensor_tensor", pt2[:], Xi, Hi, ALU.mult, reads=[("Xg", b), ("Hg", b)], writes=["pt2"])
                            S.op("pool", "tensor_tensor", Y[:, 0, :, ksl].rearrange("p c k -> p k c"), pt1[:], pt2[:], ALU.subtract,
                                 reads=["pt1", "pt2"], writes=["Y"])
                            S.op("dve", "tensor_tensor", pt1[:], Xr, Hi, ALU.mult, reads=[("Xg", b), ("Hg", b)], writes=["pt1"])
                            S.op("pool", "tensor_tensor", pt2[:], Xi, Hr, ALU.mult, reads=[("Xg", b), ("Hg", b)], writes=["pt2"])
                            S.op("pool", "tensor_tensor", Y[:, 1, :, ksl].rearrange("p c k -> p k c"), pt1[:], pt2[:], ALU.add,
                                 reads=["pt1", "pt2"], writes=["Y"])

                def inverse(yT):
                    E3 = Em[:, :].rearrange("p (v x) -> p v x", v=2)
                    Q4 = Qm[:, :].rearrange("p (m v b) -> p m v b", m=128, v=2)
                    for half in range(2):
                        for cp in range(32):
                            pi = 4 + cp % 2
                            for cc in range(2):
                                c = half * 64 + cp * 2 + cc
                                S.op("pe", "matmul", self.psum[pi][0:64, cc * 256:(cc + 1) * 256], Y[:, 0, c, :], E3[:, 0, :],
                                     start=True, stop=False, reads=["Y", "Em"], writes=[("ps", pi)])
                                S.op("pe", "matmul", self.psum[pi][0:64, cc * 256:(cc + 1) * 256], Y[:, 1, c, :], E3[:, 1, :],
                                     start=False, stop=True, reads=["Y", "Em"], writes=[("ps", pi)])
                            S.op("act", "activation", out=Bt[:, :, :, cp * 2:cp * 2 + 2].rearrange("p r m c -> p c r m"),
                                 in_=self.psum[pi][0:64, :].rearrange("p (c r m) -> p c r m", c=2, r=2), func=AF.Copy,
                                 reads=[("ps", pi)], writes=["Bt"])
                        for mg in range(8):
                            pi = 6 + mg % 2
                            for mm in range(16):
                                ma = mg * 16 + mm
                                S.op("pe", "matmul", self.psum[pi][0:64, mm * 32:(mm + 1) * 32], Bt[:, 0, ma, :], Q4[:, ma, 0, :],
                                     start=True, stop=False, reads=["Bt", "Qm"], writes=[("ps", pi)])
                                S.op("pe", "matmul", self.psum[pi][0:64, mm * 32:(mm + 1) * 32], Bt[:, 1, ma, :], Q4[:, ma, 1, :],
                                     start=False, stop=True, reads=["Bt", "Qm"], writes=[("ps", pi)])
                            S.op("act", "activation",
                                 out=yT[half * 64:(half + 1) * 64, :].rearrange("p (b a) -> p a b", a=128)[:, mg * 16:(mg + 1) * 16, :],
                                 in_=self.psum[pi][0:64, :].rearrange("p (a b) -> p a b", a=16), func=AF.Copy,
                                 reads=[("ps", pi)], writes=["yT"])

                for o in range(2):
                    for cch in range(2):
                        forward(self.s_filtT[o, cch * 128:(cch + 1) * 128, :], 64, F1f, "F1f", "H", o * 2 + cch)
                S.barrier()

                with ExitStack() as es3:
                    u = self.sb(es3, "hyu", [128, T], F32)
                    us = self.sb(es3, "hyus", [128, T], F32)
                    zc = self.sb(es3, "hyz", [128, T], F32)
                    cw = self.sb(es3, "hycw", [128, 4], F32)
                    bm = self.sb(es3, "hybm", [128, 2, 16], F32)
                    bia = self.sb(es3, "hybia", [128, 2], F32)
                    yT = us
                    ub = u[:, :].bitcast(BF16)
                    S.dma("sp", bm[:, 0, :], self.bmask[0], writes=["bm"])
                    S.dma("sp", bm[:, 1, :], self.bmask[1], writes=["bm"])
                    u3 = lambda t_: t_[:].rearrange("p (s t) -> p s t", s=16)
                    for cch in range(2):
                        for o in range(2):
                            S.dma("sp", bia[:, o:o + 1], self.hy_biasT[l, o, cch * 128:(cch + 1) * 128, :], writes=["bia"])
                        for part in range(3):
                            r0 = part * 256 + cch * 128
                            S.dma("sp", u[:], self.s_huT[r0:r0 + 128, :], writes=["u"])
                            S.dma("sp", cw[:], self.hy_convT[l, r0:r0 + 128, :], writes=["cw"])
                            S.op("pool", "memset", us[:, 0:1], 0.0, writes=["us"])
                            S.op("pool", "tensor_copy", us[:, 1:T], u[:, 0:T - 1], reads=["u"], writes=["us"])
                            S.op("dve", "tensor_tensor", u3(us)[:, :, 0:1], u3(us)[:, :, 0:1], bm[:, 0, :].unsqueeze(2), ALU.mult,
                                 reads=["us", "bm"], writes=["us"])
                            S.op("dve", "tensor_scalar", zc[:], u[:], cw[:, 1:2], cw[:, 3:4], ALU.mult, ALU.add,
                                 reads=["u", "cw"], writes=["zc"])
                            S.op("dve", "scalar_tensor_tensor", zc[:], us[:], cw[:, 0:1], zc[:], ALU.mult, ALU.add,
                                 reads=["us", "zc", "cw"], writes=["zc"])
                            S.op("pool", "memset", us[:, T - 1:T], 0.0, reads=["us"], writes=["us"])
                            S.op("pool", "tensor_copy", us[:, 0:T - 1], u[:, 1:T], reads=["u"], writes=["us"])
                            S.op("dve", "tensor_tensor", u3(us)[:, :, 255:256], u3(us)[:, :, 255:256], bm[:, 1, :].unsqueeze(2), ALU.mult,
                                 reads=["us", "bm"], writes=["us"])
                            S.op("dve", "scalar_tensor_tensor", zc[:], us[:], cw[:, 2:3], zc[:], ALU.mult, ALU.add,
                                 reads=["us", "zc", "cw"], writes=["zc"])
                            S.dma("sp", self.s_z[part, :, :], zc[:], reads=["zc"])
                        S.barrier()
                        for o in range(2):
                            zsrc = self.s_z[0] if o == 0 else self.s_z[3]
                            forward(zsrc, 32, F1z, "F1z", "Y", o * 2 + cch)
                            inverse(yT)
                            S.dma("sp", zc[:], zsrc, writes=["zc"])
                            S.dma("sp", u[:], self.s_z[1 + o], writes=["u"])
                            S.op("dve", "scalar_tensor_tensor", zc[:], zc[:], bia[:, o:o + 1], yT[:], ALU.mult, ALU.add,
                                 reads=["zc", "yT", "bia"], writes=["zc"])
                            S.op("pool", "tensor_tensor", zc[:], zc[:], u[:], ALU.mult, reads=["zc", "u"], writes=["zc"])
                            if o == 0:
                                S.dma("sp", self.s_z[3], zc[:], reads=["zc"])
                            else:
                                r0 = 768 + cch * 128
                                S.dma("sp", ub[:, 0:T], self.s_gT[r0:r0 + 128, :], reads=["u"], writes=["u"])
                                S.op("pool", "tensor_tensor", ub[:, T:2 * T], zc[:], ub[:, 0:T], ALU.mult, reads=["zc", "u"], writes=["u"])
                                S.dma("sp", self.s_yT[r0:r0 + 128, :], ub[:, T:2 * T], reads=["u"])
                            S.barrier()

    def phase_po(self, l):
        nc, S = self.nc, self.S
        xsrc = self.x_in if l == 0 else self.y
        with ExitStack() as es:
            wo32 = self.sb(es, "wo32", [128, 8, D], F32)
            wo16 = self.sb(es, "wo16", [128, 8, D], BF16)
            yT = [self.sb(es, f"yT{i}", [128, 8, 512], BF16) for i in range(2)]
            xin = [self.sb(es, f"xo{i}", [128, 4, D], F32) for i in range(2)]
            xo = [self.sb(es, f"xn{i}", [128, D], F32) for i in range(2)]
            for k in range(8):
                S.dma("sp", wo32[:, k, :], self.w_out[l, k * 128:(k + 1) * 128, :], writes=[("wo32", k)])
                S.op("dve" if k % 2 == 0 else "pool", "tensor_tensor", wo16[:, k, :], wo32[:, k, :], self.gate_b[:], ALU.mult,
                     reads=[("wo32", k)], writes=[("wo16", k)])

            def load(st):
                b = st % 2
                S.dma("sp", yT[b][:], self.s_yT[:, st * 512:(st + 1) * 512].rearrange("(k p) t -> p k t", p=128),
                      writes=[("yT", b)])
                S.dma("sp", xin[b][:], xsrc[st * 512:(st + 1) * 512, :].rearrange("(j p) d -> p j d", p=128),
                      writes=[("xo", b)])

            load(0)
            for st in range(NST):
                b = st % 2
                if st + 1 < NST:
                    load(st + 1)
                for j in range(4):
                    tok0 = st * 512 + j * 128
                    xb = j % 2
                    for n in range(2):
                        pi = (j * 2 + n) % 4
                        ps = self.psum[pi]
                        for k in range(8):
                            S.op("pe", "matmul", ps[:, :], yT[b][:, k, j * 128:(j + 1) * 128], wo16[:, k, n * 512:(n + 1) * 512],
                                 start=(k == 0), stop=(k == 7), reads=[("yT", b), ("wo16", k)], writes=[("ps", pi)])
                        S.op("dve", "tensor_tensor", xo[xb][:, n * 512:(n + 1) * 512], ps[:, :], xin[b][:, j, n * 512:(n + 1) * 512],
                             ALU.add, reads=[("ps", pi), ("xo", b)], writes=[("xn", xb, n)])
                    S.dma("sp", self.y[tok0:tok0 + 128, :], xo[xb][:], reads=[("xn", xb, 0), ("xn", xb, 1)])


def host_consts():
    c = {}
    c["ident"] = np.eye(128, dtype=np.float32)
    return c


def hyena_tables(is_prompt):
    f32 = np.float32
    N = 8192
    n = np.arange(N)
    t = {}
    if not is_prompt:
        L = 4096
        tt = np.where(n < L, n, 2 * L - 1 - n)
        dm0 = (n < L)
        dm1 = ~dm0
    else:
        L = 256
        tt = np.clip(np.where(n < 256, n, 511 - n), 0, 255)
        dm0 = (n < 256)
        dm1 = (n >= 256) & (n < 512)
    tl = np.linspace(0.0, 1.0, L, dtype=f32)[tt]
    w = (f32(2.0 * math.pi) * np.arange(L, dtype=f32) / f32(L)).astype(f32)[tt]
    fb = np.linspace(1e-4, 15, 16, dtype=f32)
    ang = (w[:, None] * fb[None, :]).astype(f32)
    feats = np.concatenate([tl[:, None], np.cos(ang), -np.sin(ang)], axis=-1).astype(f32)
    t["featsT"] = np.ascontiguousarray(feats.T)
    t["dmask"] = np.stack([dm0, dm1]).astype(f32)
    bm = np.ones((2, 128, 16), f32)
    if is_prompt:
        bm[:] = 0.0
    t["bmask"] = bm
    two_pi = 2.0 * math.pi
    k1 = np.arange(64)
    if not is_prompt:
        n1 = np.arange(64)
        a = two_pi * ((n1[:, None] * k1[None, :]) % 64) / 64.0
        F1 = np.concatenate([np.cos(a), -np.sin(a)], axis=1)
        F1z, F1f = F1[:32], F1
        n2 = np.arange(128)[None, :, None]
        k2 = np.arange(128)[None, None, :]
        idx = (n2 * (k1[:, None, None] + 64 * k2)) % 8192
        th = two_pi * idx / 8192.0
        ma = np.arange(128)[None, :, None]
        mb = np.arange(32)[None, None, :]
        qi = (128 * mb * k1[:, None, None] + ma * k1[:, None, None]) % 8192
        ps = two_pi * qi / 8192.0
        Qr, Qi = np.cos(ps) / 8192.0, np.sin(ps) / 8192.0
    else:
        p = k1 // 4
        kp = k1 % 4
        n1 = np.arange(64)
        a = two_pi * (((n1[:, None] % 2) * kp[None, :]) % 4) / 4.0
        sel = ((n1[:, None] // 2) == p[None, :])
        F1z = np.concatenate([np.cos(a) * sel, -np.sin(a) * sel], axis=1)[:32]
        a2 = two_pi * ((n1[:, None] * kp[None, :]) % 4) / 4.0
        sel2 = (n1[:, None] < 4)
        F1f = np.concatenate([np.cos(a2) * sel2, -np.sin(a2) * sel2], axis=1)
        n2 = np.arange(128)[None, :, None]
        k2 = np.arange(128)[None, None, :]
        idx = (n2 * (kp[:, None, None] + 4 * k2)) % 512
        th = two_pi * idx / 512.0
        ma = np.arange(128)[None, :, None]
        mb = np.arange(32)[None, None, :]
        qi = (128 * (mb % 2) * kp[:, None, None] + ma * kp[:, None, None]) % 512
        ps = two_pi * qi / 512.0
        selq = ((mb // 2) == p[:, None, None])
        Qr, Qi = np.cos(ps) / 512.0 * selq, np.sin(ps) / 512.0 * selq
    t["F1z"] = np.ascontiguousarray(F1z.astype(f32))
    t["F1f"] = np.ascontiguousarray(F1f.astype(f32))
    G = np.concatenate([np.cos(th), -np.sin(th), np.sin(th)], axis=2)
    t["Gm"] = np.ascontiguousarray(G.astype(ml_dtypes.bfloat16))
    k2v = np.arange(128)[:, None]
    mav = np.arange(128)[None, :]
    ph = two_pi * ((k2v * mav) % 128) / 128.0
    Er, Ei = np.cos(ph), np.sin(ph)
    t["Em"] = np.ascontiguousarray(np.concatenate([Er, Ei, -Ei, Er], axis=1).astype(ml_dtypes.bfloat16))
    Q = np.stack([Qr, -Qi], axis=2)
    t["Qm"] = np.ascontiguousarray(Q.reshape(64, 128 * 64).astype(ml_dtypes.bfloat16))
    dl = np.linspace(math.log(0.01) / 1.5, math.log(0.01) / 0.3, 256, dtype=f32)
    t["negdelta"] = (-np.abs(dl)).reshape(256, 1).astype(f32)
    return t


def job_tables(is_prompt):
    t = {}
    if not is_prompt:
        pos = np.arange(T)
        row = (pos // 64).astype(np.float32)
        col = (pos % 64).astype(np.float32)
        inv = (np.float32(10000.0) ** (-np.arange(0, 16, 2, dtype=np.float32) / np.float32(16))).astype(np.float32)
        ar = (row[:, None] * inv[None, :]).astype(np.float32)
        ac = (col[:, None] * inv[None, :]).astype(np.float32)
        C = np.concatenate([np.cos(ar), np.cos(ar), np.cos(ac), np.cos(ac)], axis=1).astype(np.float32)
        Ssg = np.concatenate([-np.sin(ar), np.sin(ar), -np.sin(ac), np.sin(ac)], axis=1).astype(np.float32)
        qaug = np.zeros((T, 17), np.float32)
        kaug = np.zeros((NKEY, 17), np.float32)
    else:
        C = np.ones((T, 32), np.float32)
        Ssg = np.zeros((T, 32), np.float32)
        pid = np.arange(T) // 256
        qaug = np.zeros((T, 17), np.float32)
        qaug[np.arange(T), pid] = 1.0
        qaug[:, 16] = 1.0
        kaug = np.zeros((NKEY, 17), np.float32)
        kaug[np.arange(T), pid] = BIG
        kaug[:, 16] = -BIG
    carry = np.ones((2, 128, 64), np.float32)
    if is_prompt:
        cidx = np.arange(64)
        carry[0][:, cidx % 4 == 0] = 0.0
        carry[1][:, cidx % 4 == 3] = 0.0
    t["carry"] = carry
    sidx = np.arange(64)
    mf = (sidx[:, None] <= sidx[None, :]).astype(np.float32)
    mb = (sidx[:, None] >= sidx[None, :]).astype(np.float32)
    t["maskT"] = np.ascontiguousarray(np.concatenate([mf, mb, mf, mb], axis=1))
    t.update(hyena_tables(is_prompt))
    t["ropeC"] = np.ascontiguousarray(np.tile(C, (1, 8)))
    t["ropeS"] = np.ascontiguousarray(np.tile(Ssg, (1, 8)))
    t["qaug"] = qaug
    t["kaug"] = kaug
    return t


def make_in_maps(inp, n_cores=8):
    consts = host_consts()
    tabs = {False: job_tables(False), True: job_tables(True)}
    maps = []
    for core in range(n_cores):
        job = core if core < 5 else 4
        m = dict(consts)
        if job < 4:
            m["x"] = np.ascontiguousarray(inp["x_sample"][job])
            cv = inp["c"][job]
        else:
            m["x"] = np.ascontiguousarray(inp["x_prompt"].reshape(T, D))
            cv = inp["c_ctx"]
        m["cvecT"] = np.ascontiguousarray(cv.reshape(8, 128).T)
        m.update(tabs[job >= 4])
        if job < 4:
            m["c_ckv"] = np.ascontiguousarray(inp["cache_mla_ckv"][job])
            m["c_krope"] = np.ascontiguousarray(inp["cache_mla_krope"][job])
            m["c_dk"] = np.ascontiguousarray(inp["cache_diff_k"][job].reshape(DEPTH, PAST, 256))
            m["c_dv"] = np.ascontiguousarray(inp["cache_diff_v"][job].reshape(DEPTH, PAST, 256))
        else:
            m["c_ckv"] = np.zeros((DEPTH, PAST, 128), np.float32)
            m["c_krope"] = np.zeros((DEPTH, PAST, 32), np.float32)
            m["c_dk"] = np.zeros((DEPTH, PAST, 256), np.float32)
            m["c_dv"] = np.zeros((DEPTH, PAST, 256), np.float32)
        m["lbT"] = np.ascontiguousarray(inp["hgrn_lb_logits"].transpose(2, 0, 1).reshape(256, 8))
        m["hy_cols"] = np.ascontiguousarray(np.stack([inp["hy_b1"], inp["hy_b2"], inp["hy_sin_freq"][:, 0], inp["hy_sin_freq"][:, 1]], axis=-1))
        m["hy_convT"] = np.ascontiguousarray(np.concatenate([inp["hy_conv_w"].transpose(0, 2, 1), inp["hy_conv_b"][:, :, None]], axis=-1))
        m["hy_biasT"] = np.ascontiguousarray(inp["hy_bias"].reshape(DEPTH, 2, 256, 1))
        for k in ("hy_w1", "hy_w2", "hy_w3"):
            m[k] = inp[k]
        m["hgrn_out_gT"] = np.ascontiguousarray(inp["hgrn_out_g"].reshape(DEPTH, 64, 1))
        if job < 4:
            m["s0"] = np.ascontiguousarray(inp["state_hgrn"][job])
        else:
            m["s0"] = np.zeros((DEPTH, 2, 4, 64, 64), np.float32)
        m["diff_lambda"] = np.ascontiguousarray(inp["diff_lambda"].reshape(DEPTH, 128))
        m["diff_subln_gT"] = np.ascontiguousarray(inp["diff_subln_g"].reshape(DEPTH, 64, 1))
        for k in ("norm_g", "w_mod", "b_mod", "w_in", "w_out", "mla_q_norm_g", "mla_w_uq", "mla_kv_norm_g", "mla_w_ukv",
                  "mla_nope_g", "mla_rope_g", "diff_qk_g"):
            m[k] = inp[k]
        maps.append(m)
    return maps


_CACHE = {}


def kernel(**inputs):
    inp = {k: np.asarray(v) for k, v in inputs.items()}
    if "b" not in _CACHE:
        _CACHE["b"] = Builder()
    b = _CACHE["b"]
    maps = make_in_maps(inp)
    maps = [{k: v for k, v in m.items() if k in b.inputs} for m in maps]
    res = run_bass_kernel_spmd(b.nc, maps, core_ids=list(range(8)))
    r = res.results
    y_sample = np.stack([np.asarray(r[j]["y"], dtype=np.float32) for j in range(4)], axis=0)
    p = r[4]
    y_prompt = np.asarray(p["y"], dtype=np.float32).reshape(16, 256, D)
    new_ckv = np.ascontiguousarray(np.asarray(p["o_ckv"], dtype=np.float32).reshape(DEPTH, 16, 256, 128).transpose(1, 0, 2, 3))
    new_krope = np.ascontiguousarray(np.asarray(p["o_krope"], dtype=np.float32).reshape(DEPTH, 16, 256, 32).transpose(1, 0, 2, 3))
    new_dk = np.ascontiguousarray(np.asarray(p["o_dk"], dtype=np.float32).reshape(DEPTH, 16, 256, 4, 2, 32).transpose(1, 0, 2, 3, 4, 5))
    new_dv = np.ascontiguousarray(np.asarray(p["o_dv"], dtype=np.float32).reshape(DEPTH, 16, 256, 4, 64).transpose(1, 0, 2, 3, 4))
    new_st = np.ascontiguousarray(np.asarray(p["o_state"], dtype=np.float32).transpose(1, 0, 2, 3, 4, 5))
    return (y_prompt, y_sample, new_ckv, new_krope, new_dk, new_dv, new_st)
```

```python
import math
import os
from contextlib import ExitStack

import numpy as np
import ml_dtypes

import concourse.bass as bass
import concourse.mybir as mybir
from concourse.bass_utils import run_bass_kernel_spmd

F32 = mybir.dt.float32
BF16 = mybir.dt.bfloat16
AF = mybir.ActivationFunctionType
ALU = mybir.AluOpType
AX = mybir.AxisListType

D = 1024
T = 4096
DEPTH = 4
PAST = 512
NKEY = T + PAST
EPS = 1e-6
IN_COLS = 4000
NT = T // 128
NST = T // 512
BIG = 30000.0

C_CQ, C_CKV, C_KR, C_GA = 0, 256, 384, 416
C_DQ, C_DK, C_DV, C_GB = 672, 928, 1184, 1440
C_HQ, C_HZF, C_HZB, C_HI, C_GC = 1696, 1952, 2208, 2464, 2720
C_HU, C_GD = 2976, 3744

EPOCH = 30000
ENGS = ("pe", "act", "dve", "pool", "sp")


class Sched:
    def __init__(self, nc, n_dma_sems=40, self_wait=True):
        self.nc = nc
        self.h = {"pe": nc.tensor, "act": nc.scalar, "dve": nc.vector, "pool": nc.gpsimd, "sp": nc.sync}
        self.sems = {e: [] for e in ENGS}
        self.cnt = {e: 0 for e in ENGS}
        self.seen = {e: {f: 0 for f in ENGS} for e in ENGS}
        self.clk = {e: [None] for e in ENGS}
        self.qrange = {"sp": (0, 24), "pool": (24, 40), "act": (40, 48)}
        n_dma_sems = 48
        self.nd = n_dma_sems
        self.dsem = [nc.alloc_semaphore(f"dma{i}") for i in range(n_dma_sems)]
        self.dval = [0] * n_dma_sems
        self.dclk = [None] * n_dma_sems
        self.dseen = {e: [0] * n_dma_sems for e in ENGS}
        self.drr = {q: r[0] for q, r in self.qrange.items()}
        self.kw = {}
        self.kr = {}
        self.self_wait = self_wait
        self.n_wait = 0
        self.n_inst = 0

    def _sem(self, e, idx):
        ep = idx // EPOCH
        while len(self.sems[e]) <= ep:
            self.sems[e].append(self.nc.alloc_semaphore(f"s_{e}_{len(self.sems[e])}"))
        return self.sems[e][ep], idx % EPOCH + 1

    def _snap(self, e):
        return (tuple(self.seen[e][f] for f in ENGS), tuple(self.dseen[e]))

    def _merge(self, e, snap):
        if snap is None:
            return
        s, d = snap
        se = self.seen[e]
        for f, v in zip(ENGS, s):
            if v > se[f]:
                se[f] = v
        de = self.dseen[e]
        for i, v in enumerate(d):
            if v > de[i]:
                de[i] = v

    def _wait_event(self, e, ev):
        if ev[0] == "E":
            _, f, c = ev
            if f == e:
                if e in ("pe", "sp"):
                    return
                if (not self.self_wait) or self.cnt[e] - c > 10 or self.seen[e][e] >= c:
                    return
            elif self.seen[e][f] >= c:
                return
            sem, val = self._sem(f, c - 1)
            self.h[e].wait_ge(sem, val)
            self.n_wait += 1
            if c > self.seen[e][f]:
                self.seen[e][f] = c
            self._merge(e, self.clk[f][c])
        else:
            _, i, v, snap = ev
            if self.dseen[e][i] >= v:
                return
            self.h[e].wait_ge(self.dsem[i], v)
            self.n_wait += 1
            self.dseen[e][i] = v
            self._merge(e, snap)

    def _deps(self, e, reads, writes):
        evs = []
        for k in reads:
            w = self.kw.get(k)
            if w is not None:
                evs.append(w)
        for k in writes:
            w = self.kw.get(k)
            if w is not None:
                evs.append(w)
            for r in self.kr.get(k, ()):
                if r[0] == "E" and r[1] == e:
                    continue
                evs.append(r)
        for ev in evs:
            self._wait_event(e, ev)

    def _record(self, ev, reads, writes):
        for k in writes:
            self.kw[k] = ev
            self.kr[k] = []
        for k in reads:
            lst = self.kr.setdefault(k, [])
            if ev[0] == "E":
                lst[:] = [r for r in lst if not (r[0] == "E" and r[1] == ev[1])]
            lst.append(ev)

    def op(self, e, name, *args, reads=(), writes=(), **kw):
        self._deps(e, reads, writes)
        ins = getattr(self.h[e], name)(*args, **kw)
        idx = self.cnt[e]
        sem, _ = self._sem(e, idx)
        ins.then_inc(sem, 1)
        self.cnt[e] = idx + 1
        self.n_inst += 1
        self.clk[e].append(self._snap(e))
        ev = ("E", e, idx + 1)
        self._record(ev, reads, writes)
        return ev

    def dma(self, q, out, in_, reads=(), writes=(), **kw):
        self._deps(q, reads, writes)
        i = self.drr[q]
        lo, hi = self.qrange[q]
        self.drr[q] = lo + (i + 1 - lo) % (hi - lo)
        if self.dval[i] > 0 and self.dseen[q][i] < self.dval[i]:
            self.h[q].wait_ge(self.dsem[i], self.dval[i])
            self.n_wait += 1
            self.dseen[q][i] = self.dval[i]
            self._merge(q, self.dclk[i])
        ins = self.h[q].dma_start(out=out, in_=in_, **kw)
        self.dval[i] += 16
        ins.then_inc(self.dsem[i], 16)
        self.n_inst += 1
        snap = self._snap(q)
        self.dclk[i] = snap
        ev = ("D", i, self.dval[i], snap)
        self._record(ev, reads, writes)
        return ev

    def barrier(self):
        for e in ENGS:
            for f in ENGS:
                if f != e and self.cnt[f] > 0:
                    self._wait_event(e, ("E", f, self.cnt[f]))
            for i in range(self.nd):
                if self.dval[i] > 0:
                    self._wait_event(e, ("D", i, self.dval[i], self.dclk[i]))
        self.kw.clear()
        self.kr.clear()


class Rec:
    def __init__(self, si):
        self.si = si
        self.ops = []

    def _k(self, ks):
        return [(self.si, k) for k in ks]

    def op(self, e, name, *a, reads=(), writes=(), **kw):
        self.ops.append((0, e, name, a, self._k(reads), self._k(writes), kw))

    def dma(self, q, out, in_, reads=(), writes=(), **kw):
        self.ops.append((1, q, out, in_, self._k(reads), self._k(writes), kw))


def emit_interleaved(S, recs):
    n = max(len(r.ops) for r in recs)
    for i in range(n):
        for r in recs:
            if i < len(r.ops):
                o = r.ops[i]
                if o[0] == 0:
                    S.op(o[1], o[2], *o[3], reads=o[4], writes=o[5], **o[6])
                else:
                    S.dma(o[1], o[2], o[3], reads=o[4], writes=o[5], **o[6])


class Builder:
    def __init__(self, debug=False, n_layers=DEPTH, stages=("mla", "diff", "hgrn", "hyena")):
        self.debug = debug
        self.n_layers = n_layers
        self.stages = stages
        self.nc = bass.Bass("TRN2", target_bir_lowering=False)
        self.S = Sched(self.nc)
        self.inputs = {}
        self.outputs = {}
        self.build()

    def din(self, name, shape, dt=F32):
        t = self.nc.dram_tensor(name, list(shape), dt, kind="ExternalInput").ap()
        self.inputs[name] = t
        return t

    def dout(self, name, shape, dt=F32):
        t = self.nc.dram_tensor(name, list(shape), dt, kind="ExternalOutput").ap()
        self.outputs[name] = t
        return t

    def dscr(self, name, shape, dt=F32):
        kind = "ExternalOutput" if self.debug else "Internal"
        t = self.nc.dram_tensor(name, list(shape), dt, kind=kind).ap()
        if self.debug:
            self.outputs[name] = t
        return t

    def sb(self, es, name, shape, dt=F32):
        self._uid = getattr(self, "_uid", 0) + 1
        return es.enter_context(self.nc.sbuf_tensor(f"{name}_{self._uid}", list(shape), dt))

    def build(self):
        nc, S = self.nc, self.S
        self.x_in = self.din("x", [T, D])
        self.cvecT = self.din("cvecT", [128, 8])
        self.norm_g = self.din("norm_g", [DEPTH, D])
        self.w_mod = self.din("w_mod", [DEPTH, D, 3 * D])
        self.b_mod = self.din("b_mod", [DEPTH, 3 * D])
        self.w_in = self.din("w_in", [DEPTH, D, IN_COLS])
        self.w_out = self.din("w_out", [DEPTH, D, D])
        self.mla_q_norm_g = self.din("mla_q_norm_g", [DEPTH, 256])
        self.mla_w_uq = self.din("mla_w_uq", [DEPTH, 256, 384])
        self.mla_kv_norm_g = self.din("mla_kv_norm_g", [DEPTH, 128])
        self.mla_w_ukv = self.din("mla_w_ukv", [DEPTH, 128, 512])
        self.mla_nope_g = self.din("mla_nope_g", [DEPTH, 2, 64])
        self.mla_rope_g = self.din("mla_rope_g", [DEPTH, 2, 32])
        self.diff_qk_g = self.din("diff_qk_g", [DEPTH, 2, 32])
        self.diff_lambda = self.din("diff_lambda", [DEPTH, 128])
        self.diff_subln_g = self.din("diff_subln_gT", [DEPTH, 64, 1])
        self.c_ckv = self.din("c_ckv", [DEPTH, PAST, 128])
        self.c_krope = self.din("c_krope", [DEPTH, PAST, 32])
        self.c_dk = self.din("c_dk", [DEPTH, PAST, 256])
        self.c_dv = self.din("c_dv", [DEPTH, PAST, 256])
        self.ropeC = self.din("ropeC", [T, 256])
        self.ropeS = self.din("ropeS", [T, 256])
        self.qaug = self.din("qaug", [T, 17])
        self.kaug = self.din("kaug", [NKEY, 17])
        self.hy_w1 = self.din("hy_w1", [DEPTH, 33, 64])
        self.hy_w2 = self.din("hy_w2", [DEPTH, 64, 64])
        self.hy_w3 = self.din("hy_w3", [DEPTH, 64, 1024])
        self.hy_cols = self.din("hy_cols", [DEPTH, 64, 4])
        self.hy_convT = self.din("hy_convT", [DEPTH, 768, 4])
        self.hy_biasT = self.din("hy_biasT", [DEPTH, 2, 256, 1])
        self.negdelta = self.din("negdelta", [256, 1])
        self.featsT = self.din("featsT", [33, 8192])
        self.dmask = self.din("dmask", [2, 8192])
        self.bmask = self.din("bmask", [2, 128, 16])
        self.F1z = self.din("F1z", [32, 128])
        self.F1f = self.din("F1f", [64, 128])
        self.Gm = self.din("Gm", [64, 128, 384], BF16)
        self.Em = self.din("Em", [128, 512], BF16)
        self.Qm = self.din("Qm", [64, 128 * 64], BF16)
        self.s_filtT = self.dscr("s_filtT", [2, 256, 8192])
        self.s_H = self.dscr("s_H", [4, 128, 2 * 64 * 128])
        self.s_z = self.dscr("s_z", [4, 128, T])
        self.lbT = self.din("lbT", [256, 8])
        self.hgrn_out_g = self.din("hgrn_out_gT", [DEPTH, 64, 1])
        self.s0 = self.din("s0", [DEPTH, 2, 4, 64, 64])
        self.carry = self.din("carry", [2, 128, 64])
        self.maskT = self.din("maskT", [64, 256])
        self.o_state = self.dout("o_state", [DEPTH, 16, 2, 4, 64, 64])
        self.y = self.dout("y", [T, D])
        self.o_dv = self.dout("o_dv", [DEPTH, T, 256])
        self.o_ckv = self.dout("o_ckv", [DEPTH, T, 128])
        self.o_krope = self.dout("o_krope", [DEPTH, T, 32])
        self.o_dk = self.dout("o_dk", [DEPTH, T, 256])
        self.s_cq = self.dscr("s_cq", [T, 416])
        self.s_dqk = self.dscr("s_dqk", [T, 512])
        self.s_vd = self.dscr("s_vd", [NKEY, 4 * 65], BF16)
        self.s_hv = self.dscr("s_hv", [T, 256], BF16)
        self.s_gT = self.dscr("s_gT", [D, T], BF16)
        self.s_hT = self.dscr("s_hT", [768, T])
        self.s_huT = self.dscr("s_huT", [768, T])
        self.s_yT = self.dscr("s_yT", [D, T], BF16)

        with ExitStack() as es:
            self.ident_bf = self.sb(es, "ident_bf", [128, 128], BF16)
            self.ident_f = self.sb(es, "ident_f", [128, 128], F32)
            self.ones_f = self.sb(es, "ones_f", [128, 128], F32)
            self.ident_in = self.din("ident", [128, 128])
            self.modA = self.sb(es, "modA", [128, D], F32)
            self.modB = self.sb(es, "modB", [128, D], F32)
            self.gate_b = self.sb(es, "gate_b", [128, D], F32)
            self.crep = self.sb(es, "crep", [128, 8, 128], F32)
            self.psum = [es.enter_context(nc.psum_tensor(f"ps{i}", [128, 512], F32)) for i in range(8)]

            S.dma("sp", self.ident_f[:], self.ident_in[:, :], writes=["ident_f"])
            S.op("dve", "tensor_copy", self.ident_bf[:], self.ident_f[:], reads=["ident_f"], writes=["ident_bf"])
            S.op("dve", "memset", self.ones_f[:], 1.0, writes=["ones_f"])
            with ExitStack() as es0:
                cv = self.sb(es0, "cv", [128, 8], F32)
                sc = self.sb(es0, "sc", [128, 8], F32)
                S.dma("sp", cv[:], self.cvecT[:, :], writes=["cv"])
                S.op("act", "activation", out=sc[:], in_=cv[:], func=AF.Silu, reads=["cv"], writes=["sc"])
                for k in range(8):
                    S.op("dve", "tensor_scalar", self.crep[:, k, :], self.ones_f[:], sc[:, k:k + 1], None,
                         ALU.mult, reads=["sc", "ones_f"], writes=["crep"])
                S.barrier()

            for l in range(self.n_layers):
                self.layer(l)
            S.barrier()

    def layer(self, l):
        S = self.S
        self.phase_mod(l)
        S.barrier()
        self.phase_p1(l)
        S.barrier()
        if "mla" in self.stages:
            self.phase_mla(l)
            S.barrier()
        if "diff" in self.stages:
            self.phase_diff(l)
            S.barrier()
        if "hgrn" in self.stages:
            self.phase_hgrn(l)
            S.barrier()
        if "hyena" in self.stages:
            self.phase_hyena(l)
            S.barrier()
        if "stub" in self.stages:
            for c in range(8):
                S.dma("sp", self.s_yT[c * 128:(c + 1) * 128, :], self.s_gT[c * 128:(c + 1) * 128, :])
            S.barrier()
        self.phase_po(l)
        S.barrier()

    def phase_mod(self, l):
        nc, S = self.nc, self.S
        with ExitStack() as es:
            wm = self.sb(es, "wm", [128, 8, 3 * D], F32)
            bm = self.sb(es, "bm", [128, 3 * D], F32)
            gb = self.sb(es, "gb", [128, D], F32)
            for k in range(8):
                S.dma("sp" if k % 2 == 0 else "pool", wm[:, k, :], self.w_mod[l, k * 128:(k + 1) * 128, :],
                      writes=[("wm", k)])
            S.dma("sp", bm[:], self.b_mod[l:l + 1, :].partition_broadcast(128) if False else
                  self.b_mod[l, :].partition_broadcast(128), writes=["bm"])
            S.dma("sp", gb[:], self.norm_g[l, :].partition_broadcast(128), writes=["gb"])
            for nt in range(6):
                ps = self.psum[nt % 2]
                for k in range(8):
                    S.op("pe", "matmul", ps[:, :], self.crep[:, k, :], wm[:, k, nt * 512:(nt + 1) * 512],
                         start=(k == 0), stop=(k == 7), reads=["crep", ("wm", k)], writes=[("ps", nt % 2)])
                sl = slice((nt % 2) * 512, (nt % 2) * 512 + 512)
                bsl = slice(nt * 512, nt * 512 + 512)
                if nt < 2:
                    S.op("dve", "tensor_tensor", self.modB[:, sl], ps[:, :], bm[:, bsl], ALU.add,
                         reads=[("ps", nt % 2), "bm"], writes=["modB"])
                elif nt < 4:
                    S.op("dve", "scalar_tensor_tensor", self.modA[:, sl], ps[:, :], 1.0, bm[:, bsl], ALU.add, ALU.add,
                         reads=[("ps", nt % 2), "bm"], writes=["modA"])
                    S.op("dve", "tensor_tensor", self.modA[:, sl], self.modA[:, sl], gb[:, sl], ALU.mult,
                         reads=["modA", "gb"], writes=["modA"])
                else:
                    S.op("dve", "tensor_tensor", self.gate_b[:, sl], ps[:, :], bm[:, bsl], ALU.add,
                         reads=[("ps", nt % 2), "bm"], writes=["gate_b"])

    def phase_p1(self, l):
        nc, S = self.nc, self.S
        xsrc = self.x_in if l == 0 else self.y
        with ExitStack() as es:
            w16 = self.sb(es, "w16", [128, 8, IN_COLS], BF16)
            xin = [self.sb(es, f"xin{i}", [128, 4, D], F32) for i in range(2)]
            hT = [self.sb(es, f"hT{i}", [128, 8, 512], BF16) for i in range(2)]
            hb = self.sb(es, "hb", [128, 4, D], BF16)
            t32 = self.sb(es, "t32", [128, D], F32)
            junk = self.sb(es, "junk", [128, D], BF16)
            ss = self.sb(es, "ss", [128, 8], F32)
            tm_st = [self.sb(es, f"tmst{i}", [128, 928], F32) for i in range(2)]
            dv_st = [self.sb(es, f"dvst{i}", [128, 256], F32) for i in range(2)]
            hv_st = [self.sb(es, f"hvst{i}", [128, 256], BF16) for i in range(2)]
            g_st = [self.sb(es, f"gst{i}", [128, 512], BF16) for i in range(2)]
            f_st = [self.sb(es, f"fst{i}", [128, 512], F32) for i in range(3)]
            vd_st = [self.sb(es, f"vdst{i}", [128, 4, 65], BF16) for i in range(2)]

            for k in range(8):
                for hh in range(2):
                    S.dma("pool", w16[:, k, hh * 2000:(hh + 1) * 2000],
                          self.w_in[l, k * 128:(k + 1) * 128, hh * 2000:(hh + 1) * 2000], writes=[("w16", k)])
            for i in range(2):
                S.op("pool", "memset", vd_st[i][:], 1.0, writes=[("vdst", i)])

            def prep(st):
                b = st % 2
                S.dma("sp", xin[b][:], xsrc[st * 512:(st + 1) * 512, :].rearrange("(j p) d -> p j d", p=128),
                      writes=[("xin", b)])
                kd = os.environ.get("KDBG", "")
                for j in range(0 if "nostat" in kd else 4):
                    S.op("act", "activation", out=junk[:], in_=xin[b][:, j, :], func=AF.Square,
                         accum_out=ss[:, j:j + 1], reads=[("xin", b)], writes=["junk", ("ss", j)])
                if "nostat" not in kd:
                    S.op("dve", "tensor_scalar", ss[:, 4:8], ss[:, 0:4], 1.0 / D, EPS, ALU.mult, ALU.add,
                         reads=[("ss", j) for j in range(4)], writes=["ms"])
                    S.op("act", "activation", out=ss[:, 4:8], in_=ss[:, 4:8], func=AF.Sqrt, reads=["ms"], writes=["ms"])
                    S.op("dve", "reciprocal", ss[:, 4:8], ss[:, 4:8], reads=["ms"], writes=["ms"])
                else:
                    S.op("dve", "memset", ss[:], 1.0, writes=["ms"])
                for j in range(4):
                    S.op("dve", "scalar_tensor_tensor", t32[:], xin[b][:, j, :], ss[:, 4 + j:5 + j], self.modA[:],
                         ALU.mult, ALU.mult, reads=[("xin", b), "ms"], writes=["t32"])
                    S.op("pool", "tensor_tensor", hb[:, j, :], t32[:], self.modB[:], ALU.add,
                         reads=["t32"], writes=[("hb", j)])
                for j in range(4):
                    pi = j % 2
                    pst = self.psum[pi][:, :].bitcast(BF16)
                    for k in range(8):
                        S.op("pe", "transpose", pst[:, k * 128:(k + 1) * 128], hb[:, j, k * 128:(k + 1) * 128],
                             self.ident_bf[:], reads=[("hb", j)], writes=[("ps", pi)])
                    eng = "act" if (j % 2 == 0 or "tract" in os.environ.get("KDBG", "")) else "dve"
                    src = pst.rearrange("p (k t) -> p k t", k=8)
                    if eng == "act":
                        S.op("act", "activation", out=hT[b][:, :, j * 128:(j + 1) * 128], in_=src, func=AF.Copy,
                             reads=[("ps", pi)], writes=[("hT", b)])
                    else:
                        S.op("dve", "tensor_copy", hT[b][:, :, j * 128:(j + 1) * 128], src,
                             reads=[("ps", pi)], writes=[("hT", b)])

            fm_chunks = ([("g", 0, C_GA), ("g", 1, C_GA + 128), ("g", 2, C_GB), ("g", 3, C_GB + 128),
                          ("g", 4, C_GC), ("g", 5, C_GC + 128), ("g", 6, C_GD), ("g", 7, C_GD + 128)]
                         + [("h", i, C_HQ + 128 * i) for i in range(6)]
                         + [("u", i, C_HU + 128 * i) for i in range(6)])

            def mm(st):
                b = st % 2
                cnt = 0
                for j in range(0 if "notm" in os.environ.get("KDBG", "") else 4):
                    tok0 = st * 512 + j * 128
                    lhs = lambda k: hT[b][:, k, j * 128:(j + 1) * 128]
                    sb_i = j % 2
                    ps = self.psum[2]
                    for k in range(0 if "noa" in os.environ.get("KDBG", "") else 8):
                        S.op("pe", "matmul", ps[:, 0:416], lhs(k), w16[:, k, 0:416], start=(k == 0), stop=(k == 7),
                             reads=[("hT", b), ("w16", k)], writes=[("ps", 2)])
                    S.op("act", "activation", out=tm_st[sb_i][:, 0:416], in_=ps[:, 0:416], func=AF.Copy,
                         reads=[("ps", 2)], writes=[("tmst", sb_i, 0)])
                    S.dma("sp", self.s_cq[tok0:tok0 + 128, :], tm_st[sb_i][:, 0:416], reads=[("tmst", sb_i, 0)])
                    ps = self.psum[3]
                    for k in range(0 if "nob" in os.environ.get("KDBG", "") else 8):
                        S.op("pe", "matmul", ps[:, :], lhs(k), w16[:, k, C_DQ:C_DQ + 512], start=(k == 0), stop=(k == 7),
                             reads=[("hT", b), ("w16", k)], writes=[("ps", 3)])
                    S.op("dve", "tensor_copy", tm_st[sb_i][:, 416:928], ps[:, :],
                         reads=[("ps", 3)], writes=[("tmst", sb_i, 1)])
                    S.dma("sp", self.s_dqk[tok0:tok0 + 128, :], tm_st[sb_i][:, 416:928], reads=[("tmst", sb_i, 1)])
                    ps = self.psum[2]
                    if "noc" in os.environ.get("KDBG", ""):
                        continue
                    for k in range(8):
                        S.op("pe", "matmul", ps[:, 0:256], lhs(k), w16[:, k, C_DV:C_DV + 256], start=(k == 0), stop=(k == 7),
                             reads=[("hT", b), ("w16", k)], writes=[("ps", 2)])
                    for k in range(8):
                        S.op("pe", "matmul", ps[:, 256:512], lhs(k), w16[:, k, C_HI:C_HI + 256], start=(k == 0), stop=(k == 7),
                             reads=[("hT", b), ("w16", k)], writes=[("ps", 2)])
                    kd = os.environ.get("KDBG", "")
                    if "c1" not in kd:
                        S.op("act", "activation", out=dv_st[sb_i][:], in_=ps[:, 0:256], func=AF.Copy,
                             reads=[("ps", 2)], writes=[("dvst", sb_i)])
                        S.dma("sp", self.o_dv[l, tok0:tok0 + 128, :], dv_st[sb_i][:], reads=[("dvst", sb_i)])
                    if "c2" not in kd:
                        S.op("act", "activation", out=vd_st[sb_i][:, :, 0:64], in_=ps[:, 0:256].rearrange("p (h e) -> p h e", h=4),
                             func=AF.Copy, reads=[("ps", 2)], writes=[("vdst", sb_i)])
                        S.dma("sp", self.s_vd[tok0:tok0 + 128, :], vd_st[sb_i][:].rearrange("p h e -> p (h e)"),
                              reads=[("vdst", sb_i)])
                    if "c3" not in kd:
                        if "hv32" in kd:
                            S.op("dve", "tensor_copy", dv_st[sb_i][:], ps[:, 256:512], reads=[("ps", 2)], writes=[("dvst", sb_i)])
                            S.op("act", "activation", out=hv_st[sb_i][:], in_=dv_st[sb_i][:], func=AF.Copy, reads=[("dvst", sb_i)], writes=[("hvst", sb_i)])
                        elif "hvdve" not in kd:
                            S.op("act", "activation", out=hv_st[sb_i][:], in_=ps[:, 256:512], func=AF.Copy, reads=[("ps", 2)], writes=[("hvst", sb_i)])
                        else:
                            S.op("dve", "tensor_copy", hv_st[sb_i][:], ps[:, 256:512], reads=[("ps", 2)], writes=[("hvst", sb_i)])
                        S.dma("sp", self.s_hv[tok0:tok0 + 128, :], hv_st[sb_i][:], reads=[("hvst", sb_i)])
                for ci, (kind, idx, col) in enumerate(fm_chunks):
                    if "nofm" in os.environ.get("KDBG", ""):
                        break
                    pi = 4 + ci % 3
                    ps = self.psum[pi]
                    for k in range(8):
                        S.op("pe", "matmul", ps[:, :], w16[:, k, col:col + 128], hT[b][:, k, :], start=(k == 0), stop=(k == 7),
                             reads=[("hT", b), ("w16", k)], writes=[("ps", pi)])
                    tsl = slice(st * 512, (st + 1) * 512)
                    if kind == "g":
                        gi = ci % 2
                        S.op("act", "activation", out=g_st[gi][:], in_=ps[:, :], func=AF.Silu,
                             reads=[("ps", pi)], writes=[("gst", gi)])
                        S.dma("sp", self.s_gT[idx * 128:(idx + 1) * 128, tsl], g_st[gi][:], reads=[("gst", gi)])
                    else:
                        fi = ci % 3
                        if ci % 2 == 0:
                            S.op("dve", "tensor_copy", f_st[fi][:], ps[:, :], reads=[("ps", pi)], writes=[("fst", fi)])
                        else:
                            S.op("act", "activation", out=f_st[fi][:], in_=ps[:, :], func=AF.Copy,
                                 reads=[("ps", pi)], writes=[("fst", fi)])
                        dst = self.s_hT if kind == "h" else self.s_huT
                        S.dma("sp", dst[idx * 128:(idx + 1) * 128, tsl], f_st[fi][:], reads=[("fst", fi)])

            import os
            dbg = os.environ.get("KDBG", "")
            nst = int(os.environ.get("KNST", NST))
            if "nomm" in dbg:
                mm = lambda st: None
            prep(0)
            for st in range(nst):
                if st + 1 < nst:
                    prep(st + 1)
                mm(st)

    def rms_tm(self, src3, dst3, G, W, gain_b, sq, ssb, rk, wk, gain_eng="dve"):
        S = self.S
        sqv = sq[:, 0:G * W].rearrange("p (g w) -> p g w", g=G)
        S.op("dve", "tensor_tensor", sqv, src3, src3, ALU.mult, reads=rk, writes=["sq"])
        S.op("dve", "tensor_reduce", ssb[:, 0:G], sqv, AX.X, ALU.add, reads=["sq"], writes=["ssb"])
        S.op("dve", "tensor_scalar", ssb[:, G:2 * G], ssb[:, 0:G], 1.0 / W, EPS, ALU.mult, ALU.add,
             reads=["ssb"], writes=["ssb2"])
        S.op("act", "activation", out=ssb[:, G:2 * G], in_=ssb[:, G:2 * G], func=AF.Sqrt, reads=["ssb2"], writes=["ssb2"])
        S.op("dve", "reciprocal", ssb[:, G:2 * G], ssb[:, G:2 * G], reads=["ssb2"], writes=["ssb2"])
        if gain_b is None:
            S.op("dve", "tensor_tensor", dst3, src3, ssb[:, G:2 * G].unsqueeze(2).to_broadcast([128, G, W]), ALU.mult,
                 reads=list(rk) + ["ssb2"], writes=wk)
        else:
            S.op("dve", "tensor_tensor", sqv, src3, ssb[:, G:2 * G].unsqueeze(2).to_broadcast([128, G, W]), ALU.mult,
                 reads=list(rk) + ["ssb2"], writes=["sq"])
            S.op(gain_eng, "tensor_tensor", dst3, sqv, gain_b.unsqueeze(1).to_broadcast([128, G, W]), ALU.mult,
                 reads=["sq"], writes=wk)

    def rope_tm(self, x3, out3, G, rc, rs, t1, rk, wk):
        S = self.S
        xv = x3.rearrange("p g (r h e) -> p (g r) h e", r=2, h=2)
        sv = rs[:, 0:G * 32].rearrange("p (g h e) -> p g h e", h=2, e=8)
        tv = t1[:, 0:G * 32].rearrange("p (g h e) -> p g h e", h=2, e=8)
        S.op("dve", "tensor_tensor", tv[:, :, 0, :], xv[:, :, 1, :], sv[:, :, 0, :], ALU.mult, reads=list(rk) + ["rs"], writes=["t1"])
        S.op("dve", "tensor_tensor", tv[:, :, 1, :], xv[:, :, 0, :], sv[:, :, 1, :], ALU.mult, reads=list(rk) + ["rs"], writes=["t1"])
        x2 = x3.rearrange("p g w -> p (g w)") if False else x3
        S.op("dve", "tensor_tensor", x3, x3, rc[:, 0:G * 32].rearrange("p (g w) -> p g w", g=G), ALU.mult,
             reads=list(rk) + ["rc", "t1"], writes=rk)
        S.op("pool", "tensor_tensor", out3, x3, t1[:, 0:G * 32].rearrange("p (g w) -> p g w", g=G), ALU.add,
             reads=list(rk) + ["t1"], writes=wk)

    def bcast_load(self, q, tile_ap, dram_ap, key):
        self.S.dma(q, tile_ap, dram_ap.partition_broadcast(128), writes=[key])

    def attend(self, n_groups, Krows, QT, KT, Vt, v_of_g, scale, finalize, order=None):
        S = self.S
        NKC = NKEY // 128
        if order is None:
            order = [(g, qs) for g in range(n_groups) for qs in range(NST)]
        seq = [(oi, g, qs, kc) for oi, (g, qs) in enumerate(order) for kc in range(NKC)]
        banks = [0, 1, 7]
        nb = len(banks)

        def emit_s(idx):
            oi, g, qs, kc = seq[idx]
            b = idx % nb
            S.op("pe", "matmul", self.psum[banks[b]][:, :], KT(g, kc), QT(g, qs), start=True, stop=True,
                 reads=["QT", "KT"], writes=[("ps", banks[b])])

        emit_s(0)
        if len(seq) > 1:
            emit_s(1)
        for idx, (oi, g, qs, kc) in enumerate(seq):
            b = idx % nb
            pkey = 2 + (oi % 2)
            pO = self.psum[pkey]
            if idx + 2 < len(seq):
                emit_s(idx + 2)
            S.op("act", "activation", out=self.pt[b][:], in_=self.psum[banks[b]][:, :], func=AF.Exp, scale=scale,
                 reads=[("ps", banks[b])], writes=[("pt", b)])
            S.op("pe", "matmul", pO[0:65, :], Vt(v_of_g(g), kc), self.pt[b][:], start=(kc == 0), stop=(kc == NKC - 1),
                 reads=[("pt", b), "V"], writes=[("ps", pkey)])
            if kc == NKC - 1:
                finalize(g, qs, pO, pkey)

    def phase_mla(self, l):
        nc, S = self.nc, self.S
        with ExitStack() as es:
            QT = self.sb(es, "QT", [128, 4, T], BF16)
            KT = self.sb(es, "KT", [128, 4, NKEY], BF16)
            Va = self.sb(es, "Va", [128, NKEY // 128, 4, 65], BF16)
            self.pt = [self.sb(es, f"pt{i}", [128, 512], BF16) for i in range(3)]
            wuq32 = self.sb(es, "wuq32", [128, 2, 384], F32)
            wuq = self.sb(es, "wuq", [128, 2, 384], BF16)
            wukv32 = self.sb(es, "wukv32", [128, 512], F32)
            wukv = self.sb(es, "wukv", [128, 512], BF16)
            g_q = self.sb(es, "g_q", [128, 256], F32)
            g_kv = self.sb(es, "g_kv", [128, 128], F32)
            g_nope = self.sb(es, "g_nope", [128, 128], F32)
            g_rope = self.sb(es, "g_rope", [128, 64], F32)
            S.op("pool", "memset", Va[:], 1.0, writes=["V"])
            S.dma("sp", wuq32[:], self.mla_w_uq[l].rearrange("(c p) n -> p c n", p=128), writes=["wuq32"])
            S.op("dve", "tensor_copy", wuq[:], wuq32[:], reads=["wuq32"], writes=["wuq"])
            S.dma("sp", wukv32[:], self.mla_w_ukv[l], writes=["wukv32"])
            S.op("dve", "tensor_copy", wukv[:], wukv32[:], reads=["wukv32"], writes=["wukv"])
            self.bcast_load("sp", g_q[:], self.mla_q_norm_g[l, :], "g_q")
            self.bcast_load("sp", g_kv[:], self.mla_kv_norm_g[l, :], "g_kv")
            self.bcast_load("sp", g_nope[:], self.mla_nope_g[l].rearrange("a b -> (a b)"), "g_nope")
            self.bcast_load("sp", g_rope[:], self.mla_rope_g[l].rearrange("a b -> (a b)"), "g_rope")
            with ExitStack() as es2:
                cqt = [self.sb(es2, f"cqt{i}", [128, 416], F32) for i in range(2)]
                rc = [self.sb(es2, f"rc{i}", [128, 256], F32) for i in range(2)]
                rs = [self.sb(es2, f"rs{i}", [128, 256], F32) for i in range(2)]
                qa = [self.sb(es2, f"qa{i}", [128, 17], F32) for i in range(2)]
                ka = [self.sb(es2, f"ka{i}", [128, 17], F32) for i in range(2)]
                ckn = [self.sb(es2, f"ckn{i}", [128, 128], F32) for i in range(2)]
                krn = [self.sb(es2, f"krn{i}", [128, 32], F32) for i in range(2)]
                BS = []
                for si_ in range(2):
                    BS.append(dict(
                        sq=self.sb(es2, "sq", [128, 256], F32), ssb=self.sb(es2, "ssb", [128, 16], F32),
                        t1=self.sb(es2, "t1", [128, 256], F32), cqb=self.sb(es2, "cqb", [128, 256], BF16),
                        cqT=self.sb(es2, "cqT", [128, 2, 128], BF16), q32=self.sb(es2, "q32", [128, 384], F32),
                        qr=self.sb(es2, "qr", [128, 4, 32], F32), Qst=self.sb(es2, "Qst", [128, 4, 113], BF16),
                        Kst=self.sb(es2, "Kst", [128, 4, 113], BF16), ckb=self.sb(es2, "ckb", [128, 128], BF16),
                        ckT=self.sb(es2, "ckT", [128, 128], BF16), kv32=self.sb(es2, "kv32", [128, 512], F32),
                        krr=self.sb(es2, "krr", [128, 32], F32)))
                realS = self.S
                realS.barrier()

                def transposes(src_tile, n, width, dst_ap, rkey, wkey, pi):
                    S = self.S
                    pst = self.psum[pi][:, :].bitcast(BF16)
                    for h in range(n):
                        S.op("pe", "transpose", pst[0:width, h * 128:(h + 1) * 128], src_tile(h), self.ident_bf[:],
                             reads=[rkey], writes=[("ps", pi)])
                    S.op("act", "activation", out=dst_ap, in_=pst[0:width, 0:n * 128].rearrange("p (h t) -> p h t", h=n),
                         func=AF.Copy, reads=[("ps", pi)], writes=[wkey])

                def mla_tile(t, S):
                    b = t % 2
                    PB = 4 - 4 * b
                    d_ = BS[b]
                    sq, ssb, t1, cqb, cqT, q32, qr, Qst, Kst, ckb, ckT, kv32, krr = (
                        d_["sq"], d_["ssb"], d_["t1"], d_["cqb"], d_["cqT"], d_["q32"], d_["qr"], d_["Qst"], d_["Kst"],
                        d_["ckb"], d_["ckT"], d_["kv32"], d_["krr"])
                    q32v = q32[:].rearrange("p (h w) -> p h w", h=4)
                    kv32v = kv32[:].rearrange("p (h w) -> p h w", h=4)
                    tok0 = t * 128
                    ctx = t >= NT
                    S.dma("sp", ka[b][:], self.kaug[tok0:tok0 + 128, :], writes=[("ka", b)])
                    if not ctx:
                        S.dma("sp", cqt[b][:], self.s_cq[tok0:tok0 + 128, :], writes=[("cqt", b)])
                        S.dma("sp", rc[b][:], self.ropeC[tok0:tok0 + 128, :], writes=["rc"])
                        S.dma("sp", rs[b][:], self.ropeS[tok0:tok0 + 128, :], writes=["rs"])
                        S.dma("sp", qa[b][:], self.qaug[tok0:tok0 + 128, :], writes=[("qa", b)])
                        self.rms_tm(cqt[b][:, 0:256].unsqueeze(1), cqb[:].unsqueeze(1), 1, 256, g_q[:], sq, ssb,
                                    [("cqt", b)], ["cqb"], gain_eng="pool")
                        transposes(lambda c: cqb[:, c * 128:(c + 1) * 128], 2, 128, cqT[:, :, :], "cqb", "cqT", PB + 0)
                        for c in range(2):
                            S.op("pe", "matmul", self.psum[PB + 1][:, 0:384], cqT[:, c, :], wuq[:, c, :], start=(c == 0), stop=(c == 1),
                                 reads=["cqT", "wuq"], writes=[("ps", PB + 1)])
                        S.op("act", "activation", out=q32[:], in_=self.psum[PB + 1][:, 0:384], func=AF.Copy,
                             reads=[("ps", PB + 1)], writes=["q32"])
                        self.rms_tm(q32v[:, :, 0:64], Qst[:, :, 0:64], 4, 64, g_nope[:, 0:64], sq, ssb, ["q32"], ["Qst"],
                                    gain_eng="pool")
                        self.rms_tm(q32v[:, :, 64:96], qr[:], 4, 32, g_rope[:, 0:32], sq, ssb, ["q32"], ["qr"])
                        self.rope_tm(qr[:], Qst[:, :, 64:96], 4, rc[b], rs[b], t1, ["qr"], ["Qst"])
                        S.op("pool", "tensor_copy", Qst[:, :, 96:113], qa[b][:].unsqueeze(1).to_broadcast([128, 4, 17]),
                             reads=[("qa", b)], writes=["Qst"])
                        transposes(lambda h: Qst[:, h, :], 4, 113, QT[0:113, :, tok0:tok0 + 128], "Qst", "QT", PB + 0)
                        self.rms_tm(cqt[b][:, 256:384].unsqueeze(1), ckn[b][:].unsqueeze(1), 1, 128, g_kv[:], sq, ssb,
                                    [("cqt", b)], [("ckn", b)])
                        S.dma("sp", self.o_ckv[l, tok0:tok0 + 128, :], ckn[b][:], reads=[("ckn", b)])
                        self.rms_tm(cqt[b][:, 384:416].unsqueeze(1), krn[b][:].unsqueeze(1), 1, 32, g_rope[:, 32:64], sq, ssb,
                                    [("cqt", b)], [("krn", b)])
                        S.dma("sp", self.o_krope[l, tok0:tok0 + 128, :], krn[b][:], reads=[("krn", b)])
                        S.op("dve", "tensor_copy", krr[:], krn[b][:], reads=[("krn", b)], writes=["krr"])
                        self.rope_tm(krr[:].unsqueeze(1), krr[:].unsqueeze(1), 1, rc[b], rs[b], t1, ["krr"], ["krr"])
                    else:
                        c0 = tok0 - T
                        S.dma("sp", ckn[b][:], self.c_ckv[l, c0:c0 + 128, :], writes=[("ckn", b)])
                        S.dma("sp", krr[:], self.c_krope[l, c0:c0 + 128, :], writes=["krr"])
                    S.op("pool", "tensor_copy", ckb[:], ckn[b][:], reads=[("ckn", b)], writes=["ckb"])
                    transposes(lambda h: ckb[:], 1, 128, ckT[:].unsqueeze(1), "ckb", "ckT", PB + 2)
                    S.op("pe", "matmul", self.psum[PB + 3][:, :], ckT[:], wukv[:], start=True, stop=True,
                         reads=["ckT", "wukv"], writes=[("ps", PB + 3)])
                    S.op("act", "activation", out=kv32[:], in_=self.psum[PB + 3][:, :], func=AF.Copy, reads=[("ps", PB + 3)], writes=["kv32"])
                    self.rms_tm(kv32v[:, :, 0:64], Kst[:, :, 0:64], 4, 64, g_nope[:, 64:128], sq, ssb, ["kv32"], ["Kst"],
                                gain_eng="pool")
                    S.op("pool", "tensor_copy", Va[:, t, :, 0:64], kv32v[:, :, 64:128], reads=["kv32"], writes=["V"])
                    S.op("pool", "tensor_copy", Kst[:, :, 64:96], krr[:].unsqueeze(1).to_broadcast([128, 4, 32]),
                         reads=["krr"], writes=["Kst"])
                    S.op("pool", "tensor_copy", Kst[:, :, 96:113], ka[b][:].unsqueeze(1).to_broadcast([128, 4, 17]),
                         reads=[("ka", b)], writes=["Kst"])
                    transposes(lambda h: Kst[:, h, :], 4, 113, KT[0:113, :, tok0:tok0 + 128], "Kst", "KT", PB + 2)
                for t0 in range(0, NKEY // 128, 2):
                    recs = []
                    for tt in (t0, t0 + 1):
                        r_ = Rec(tt % 2)
                        self.S = r_
                        mla_tile(tt, r_)
                        recs.append(r_)
                    self.S = realS
                    emit_interleaved(realS, recs)
                S = realS
                S.barrier()

            with ExitStack() as es3:
                o32 = [self.sb(es3, f"o32{i}", [128, 512], F32) for i in range(2)]
                yf = self.sb(es3, "yf", [64, 512], F32)
                gt = [self.sb(es3, f"gt{i}", [64, 512], BF16) for i in range(2)]
                yb = [self.sb(es3, f"yb{i}", [64, 512], BF16) for i in range(2)]
                cnt = [0]

                def fin(h, qs, pO, pkey):
                    i = cnt[0] % 2
                    cnt[0] += 1
                    S.dma("sp", gt[i][:], self.s_gT[h * 64:(h + 1) * 64, qs * 512:(qs + 1) * 512], writes=[("gt", i)])
                    S.op("act", "activation", out=o32[i][0:65, :], in_=pO[0:65, :], func=AF.Copy,
                         reads=[("ps", pkey)], writes=[("o32", i)])
                    S.op("dve", "reciprocal", o32[i][64:65, :], o32[i][64:65, :], reads=[("o32", i)], writes=[("o32", i)])
                    S.op("pe", "matmul", self.psum[4][0:64, :], self.ones_f[64:65, 0:64], o32[i][64:65, :], start=True, stop=True,
                         reads=[("o32", i)], writes=[("ps", 4)])
                    S.op("dve", "tensor_tensor", yf[:], o32[i][0:64, :], self.psum[4][0:64, :], ALU.mult,
                         reads=[("o32", i), ("ps", 4)], writes=["yf"])
                    S.op("pool", "tensor_tensor", yb[i][:], yf[:], gt[i][:], ALU.mult, reads=["yf", ("gt", i)], writes=[("yb", i)])
                    S.dma("sp", self.s_yT[h * 64:(h + 1) * 64, qs * 512:(qs + 1) * 512], yb[i][:], reads=[("yb", i)])

                self.attend(4, 113,
                            lambda g, qs: QT[0:113, g, qs * 512:(qs + 1) * 512],
                            lambda g, kc: KT[0:113, g, kc * 128:(kc + 1) * 128],
                            lambda h, kc: Va[:, kc, h, :],
                            lambda g: g, float((64 + 32) ** -0.5), fin)

    def phase_diff(self, l):
        nc, S = self.nc, self.S
        lam_init = 0.8 - 0.6 * math.exp(-0.3 * l)
        with ExitStack() as es:
            QdT = self.sb(es, "QdT", [128, 4, T], BF16)
            KdT = self.sb(es, "KdT", [128, 4, NKEY], BF16)
            Vd = self.sb(es, "Vd", [128, NKEY // 128, 4, 65], BF16)
            self.pt = [self.sb(es, f"pt{i}", [128, 512], BF16) for i in range(3)]
            g_qk = self.sb(es, "g_qk", [128, 64], F32)
            lpb = self.sb(es, "lpb", [128, 128], F32)
            lsm = self.sb(es, "lsm", [128, 72], F32)
            neglam = self.sb(es, "neglam", [128, 1], F32)
            gsub = self.sb(es, "gsub", [64, 1], F32)
            self.bcast_load("sp", g_qk[:], self.diff_qk_g[l].rearrange("a b -> (a b)"), "g_qk")
            self.bcast_load("sp", lpb[:], self.diff_lambda[l, :], "lpb")
            S.dma("sp", gsub[:], self.diff_subln_g[l], writes=["gsub"])
            S.op("dve", "tensor_scalar", gsub[:], gsub[:], float(1.0 - lam_init), None, ALU.mult, reads=["gsub"], writes=["gsub"])
            lp4 = lpb[:].rearrange("p (a b w) -> p a b w", a=2, b=2)
            S.op("dve", "tensor_tensor", lsm[:, 0:64].rearrange("p (a w) -> p a w", a=2), lp4[:, :, 0, :], lp4[:, :, 1, :], ALU.mult,
                 reads=["lpb"], writes=["lsm"])
            S.op("dve", "tensor_reduce", lsm[:, 64:66], lsm[:, 0:64].rearrange("p (a w) -> p a w", a=2), AX.X, ALU.add,
                 reads=["lsm"], writes=["lsm2"])
            S.op("act", "activation", out=lsm[:, 66:68], in_=lsm[:, 64:66], func=AF.Exp, reads=["lsm2"], writes=["lsm3"])
            S.op("dve", "tensor_tensor", neglam[:], lsm[:, 67:68], lsm[:, 66:67], ALU.subtract, reads=["lsm3"], writes=["neglam"])
            S.op("dve", "tensor_scalar", neglam[:], neglam[:], float(-lam_init), None, ALU.add, reads=["neglam"], writes=["neglam"])
            S.dma("sp", Vd[:, 0:NT, :, :].rearrange("p c h e -> p c (h e)"),
                  self.s_vd[0:T, :].rearrange("(c p) e -> p c e", p=128), writes=["V"])
            with ExitStack() as es1:
                cv32 = self.sb(es1, "cv32", [128, 4, 256], F32)
                S.dma("sp", cv32[:], self.c_dv[l].rearrange("(c p) e -> p c e", p=128), writes=["cv32"])
                S.op("pool", "memset", Vd[:, NT:NT + 4, :, :], 1.0, reads=["V"], writes=["V"])
                for c in range(4):
                    S.op("pool", "tensor_copy", Vd[:, NT + c, :, 0:64], cv32[:, c, :].rearrange("p (h e) -> p h e", h=4),
                         reads=["cv32", "V"], writes=["V"])
                S.barrier()
            with ExitStack() as es2:
                dqk = [self.sb(es2, f"dqk{i}", [128, 512], F32) for i in range(2)]
                rc = [self.sb(es2, f"rc{i}", [128, 256], F32) for i in range(2)]
                rs = [self.sb(es2, f"rs{i}", [128, 256], F32) for i in range(2)]
                qa = [self.sb(es2, f"qa{i}", [128, 17], F32) for i in range(2)]
                ka = [self.sb(es2, f"ka{i}", [128, 17], F32) for i in range(2)]
                kn = [self.sb(es2, f"kn{i}", [128, 8, 32], F32) for i in range(2)]
                BS = []
                for si_ in range(2):
                    BS.append(dict(
                        sq=self.sb(es2, "sq", [128, 256], F32), ssb=self.sb(es2, "ssb", [128, 16], F32),
                        t1=self.sb(es2, "t1", [128, 256], F32), qn=self.sb(es2, "qn", [128, 8, 32], F32),
                        kr=self.sb(es2, "kr", [128, 8, 32], F32), Qdst=self.sb(es2, "Qdst", [128, 8, 64], BF16),
                        Kdst=self.sb(es2, "Kdst", [128, 8, 64], BF16)))
                    S.op("pool", "memset", BS[si_]["Qdst"][:], 0.0, writes=[("Qdst0", si_)])
                    S.op("pool", "memset", BS[si_]["Kdst"][:], 0.0, writes=[("Kdst0", si_)])
                realS = self.S
                realS.barrier()

                def transposes(src, dst_ap, rkey, wkey, pi):
                    S = self.S
                    pst = self.psum[pi][:, :].bitcast(BF16)
                    for c in range(4):
                        S.op("pe", "transpose", pst[:, c * 128:(c + 1) * 128],
                             src[:, 2 * c:2 * c + 2, :].rearrange("p a w -> p (a w)"), self.ident_bf[:],
                             reads=[rkey], writes=[("ps", pi)])
                    S.op("act", "activation", out=dst_ap, in_=pst[:, 0:512].rearrange("p (c t) -> p c t", c=4),
                         func=AF.Copy, reads=[("ps", pi)], writes=[wkey])

                def diff_tile(t, S):
                    b = t % 2
                    PB = 4 - 4 * b
                    d_ = BS[b]
                    sq, ssb, t1, qn, kr, Qdst, Kdst = d_["sq"], d_["ssb"], d_["t1"], d_["qn"], d_["kr"], d_["Qdst"], d_["Kdst"]
                    tok0 = t * 128
                    ctx = t >= NT
                    S.dma("sp", ka[b][:], self.kaug[tok0:tok0 + 128, :], writes=[("ka", b)])
                    if not ctx:
                        S.dma("sp", dqk[b][:], self.s_dqk[tok0:tok0 + 128, :], writes=[("dqk", b)])
                        S.dma("sp", rc[b][:], self.ropeC[tok0:tok0 + 128, :], writes=["rc"])
                        S.dma("sp", rs[b][:], self.ropeS[tok0:tok0 + 128, :], writes=["rs"])
                        S.dma("sp", qa[b][:], self.qaug[tok0:tok0 + 128, :], writes=[("qa", b)])
                        qv = dqk[b][:, 0:256].rearrange("p (g w) -> p g w", g=8)
                        kv = dqk[b][:, 256:512].rearrange("p (g w) -> p g w", g=8)
                        self.rms_tm(qv, qn[:], 8, 32, g_qk[:, 0:32], sq, ssb, [("dqk", b)], ["qn"])
                        self.rope_tm(qn[:], Qdst[:, :, 0:32], 8, rc[b], rs[b], t1, ["qn"], ["Qdst"])
                        S.op("pool", "tensor_copy", Qdst[:, :, 32:49], qa[b][:].unsqueeze(1).to_broadcast([128, 8, 17]),
                             reads=[("qa", b)], writes=["Qdst"])
                        transposes(Qdst, QdT[:, :, tok0:tok0 + 128], "Qdst", "QT", PB + 0)
                        self.rms_tm(kv, kn[b][:], 8, 32, g_qk[:, 32:64], sq, ssb, [("dqk", b)], [("kn", b)])
                        S.dma("sp", self.o_dk[l, tok0:tok0 + 128, :], kn[b][:].rearrange("p g w -> p (g w)"), reads=[("kn", b)])
                        S.op("dve", "tensor_copy", kr[:], kn[b][:], reads=[("kn", b)], writes=["kr"])
                        self.rope_tm(kr[:], Kdst[:, :, 0:32], 8, rc[b], rs[b], t1, ["kr"], ["Kdst"])
                    else:
                        c0 = tok0 - T
                        S.dma("sp", kn[b][:].rearrange("p g w -> p (g w)"), self.c_dk[l, c0:c0 + 128, :], writes=[("kn", b)])
                        S.op("pool", "tensor_copy", Kdst[:, :, 0:32], kn[b][:], reads=[("kn", b)], writes=["Kdst"])
                    S.op("pool", "tensor_copy", Kdst[:, :, 32:49], ka[b][:].unsqueeze(1).to_broadcast([128, 8, 17]),
                         reads=[("ka", b)], writes=["Kdst"])
                    transposes(Kdst, KdT[:, :, tok0:tok0 + 128], "Kdst", "KT", PB + 2)
                for t0 in range(0, NKEY // 128, 2):
                    recs = []
                    for tt in (t0, t0 + 1):
                        r_ = Rec(tt % 2)
                        self.S = r_
                        diff_tile(tt, r_)
                        recs.append(r_)
                    self.S = realS
                    emit_interleaved(realS, recs)
                S = realS
                S.barrier()

            with ExitStack() as es3:
                o32 = [self.sb(es3, f"o32{i}", [128, 512], F32) for i in range(2)]
                y1 = self.sb(es3, "y1", [64, 512], F32)
                y2 = self.sb(es3, "y2", [64, 512], F32)
                sqf = self.sb(es3, "sqf", [64, 512], F32)
                gt = [self.sb(es3, f"gt{i}", [64, 512], BF16) for i in range(2)]
                yb = [self.sb(es3, f"yb{i}", [64, 512], BF16) for i in range(2)]
                cnt = [0]

                def fin(g, qs, pO, pkey):
                    h, j = g // 2, g % 2
                    S.op("act", "activation", out=o32[j][0:65, :], in_=pO[0:65, :], func=AF.Copy,
                         reads=[("ps", pkey)], writes=[("o32", j)])
                    S.op("dve", "reciprocal", o32[j][64:65, :], o32[j][64:65, :], reads=[("o32", j)], writes=[("o32", j)])
                    if j == 0:
                        return
                    i = cnt[0] % 2
                    cnt[0] += 1
                    r0 = 256 + h * 64
                    S.dma("sp", gt[i][:], self.s_gT[r0:r0 + 64, qs * 512:(qs + 1) * 512], writes=[("gt", i)])
                    for jj in range(2):
                        S.op("pe", "matmul", self.psum[4 + jj][0:64, :], self.ones_f[64:65, 0:64], o32[jj][64:65, :],
                             start=True, stop=True, reads=[("o32", jj)], writes=[("ps", 4 + jj)])
                    S.op("dve", "tensor_tensor", y1[:], o32[0][0:64, :], self.psum[4][0:64, :], ALU.mult,
                         reads=[("o32", 0), ("ps", 4)], writes=["y1"])
                    S.op("dve", "tensor_tensor", y2[:], o32[1][0:64, :], self.psum[5][0:64, :], ALU.mult,
                         reads=[("o32", 1), ("ps", 5)], writes=["y2"])
                    S.op("dve", "scalar_tensor_tensor", y1[:], y2[:], neglam[0:64, 0:1], y1[:], ALU.mult, ALU.add,
                         reads=["y1", "y2", "neglam"], writes=["y1"])
                    S.op("pool", "tensor_tensor", sqf[:], y1[:], y1[:], ALU.mult, reads=["y1"], writes=["sqf"])
                    S.op("pe", "matmul", self.psum[6][0:64, :], self.ones_f[0:64, 0:64], sqf[:], start=True, stop=True,
                         reads=["sqf"], writes=[("ps", 6)])
                    S.op("dve", "tensor_scalar", y2[:], self.psum[6][0:64, :], 1.0 / 64, EPS, ALU.mult, ALU.add,
                         reads=[("ps", 6)], writes=["y2"])
                    S.op("act", "activation", out=y2[:], in_=y2[:], func=AF.Sqrt, reads=["y2"], writes=["y2"])
                    S.op("dve", "reciprocal", y2[:], y2[:], reads=["y2"], writes=["y2"])
                    S.op("dve", "scalar_tensor_tensor", y1[:], y1[:], gsub[0:64, 0:1], y2[:], ALU.mult, ALU.mult,
                         reads=["y1", "y2", "gsub"], writes=["y1"])
                    S.op("pool", "tensor_tensor", yb[i][:], y1[:], gt[i][:], ALU.mult, reads=["y1", ("gt", i)], writes=[("yb", i)])
                    S.dma("sp", self.s_yT[r0:r0 + 64, qs * 512:(qs + 1) * 512], yb[i][:], reads=[("yb", i)])

                order = [(2 * h + j, qs) for h in range(4) for qs in range(NST) for j in range(2)]
                self.attend(8, 64,
                            lambda g, qs: QdT[64 * (g % 2):64 * (g % 2) + 64, g // 2, qs * 512:(qs + 1) * 512],
                            lambda g, kc: KdT[64 * (g % 2):64 * (g % 2) + 64, g // 2, kc * 128:(kc + 1) * 128],
                            lambda h, kc: Vd[:, kc, h, :],
                            lambda g: g // 2, float(32 ** -0.5), fin, order=order)

    def phase_hgrn(self, l):
        nc, S = self.nc, self.S
        NCH = T // 64
        SL = 512
        with ExitStack() as es:
            rmask = self.sb(es, "rmask", [128, SL], F32)
            mT = self.sb(es, "mT", [64, 256], F32)
            og = self.sb(es, "og", [64, 1], F32)
            S.op("pool", "memset", rmask[:], 1.0, writes=["rmask"])
            S.op("pool", "memset", rmask[:].rearrange("p (c s) -> p c s", s=64)[:, :, 0:1], 0.0, writes=["rmask"])
            S.dma("sp", mT[:], self.maskT[:, :], writes=["mT"])
            S.dma("sp", og[:], self.hgrn_out_g[l], writes=["og"])
            for hp in range(2):
                with ExitStack() as es1:
                    lbl = self.sb(es1, "lbl", [128, 24], F32)
                    vsb = self.sb(es1, "vsb", [64, NCH, 128], BF16)
                    qt = [self.sb(es1, f"qt{d}", [128, T], BF16) for d in range(2)]
                    kt = [self.sb(es1, f"kt{d}", [128, T], BF16) for d in range(2)]
                    qo = [self.sb(es1, f"qo{d}", [128, T], BF16) for d in range(2)]
                    ko = [self.sb(es1, f"ko{d}", [128, T], BF16) for d in range(2)]
                    qb = [self.sb(es1, f"qb{d}", [128, T], BF16) for d in range(2)]
                    khat = [self.sb(es1, f"khat{d}", [64, NCH, 128], BF16) for d in range(2)]
                    stab = [self.sb(es1, f"stab{d}", [128, NCH, 128], BF16) for d in range(2)]
                    etot = [self.sb(es1, f"etot{d}", [128, NCH], F32) for d in range(2)]
                    car = [self.sb(es1, f"car{d}", [128, NCH], F32) for d in range(2)]
                    S.dma("sp", vsb[:], self.s_hv[:, hp * 128:(hp + 1) * 128].rearrange("(c s) e -> s c e", s=64), writes=["vsb"])
                    for d in range(2):
                        S.dma("sp", car[d][:], self.carry[d], writes=[("car", d)])
                    S.dma("sp", lbl[:, 0:8], self.lbT[hp * 128:(hp + 1) * 128, :], writes=["lbl"])
                    S.op("act", "activation", out=lbl[:, 8:16], in_=lbl[:, 0:8], func=AF.Exp, reads=["lbl"], writes=["lbe"])
                    S.op("dve", "tensor_reduce", lbl[:, 16:18], lbl[:, 8:16].rearrange("p (a b) -> p a b", a=2), AX.X, ALU.add,
                         reads=["lbe"], writes=["lbs"])
                    S.op("dve", "reciprocal", lbl[:, 16:18], lbl[:, 16:18], reads=["lbs"], writes=["lbs"])
                    for d in range(2):
                        if l == 0:
                            S.op("dve", "memset", lbl[:, 18 + d:19 + d], 0.0, writes=[("lb", d)])
                        else:
                            S.op("dve", "tensor_reduce", lbl[:, 18 + d:19 + d], lbl[:, 8 + 4 * d + 1:8 + 4 * d + 1 + l], AX.X, ALU.add,
                                 reads=["lbe"], writes=[("lb", d)])
                            S.op("dve", "tensor_tensor", lbl[:, 18 + d:19 + d], lbl[:, 18 + d:19 + d], lbl[:, 16 + d:17 + d], ALU.mult,
                                 reads=[("lb", d), "lbs"], writes=[("lb", d)])
                        S.op("dve", "tensor_scalar", lbl[:, 20 + d:21 + d], lbl[:, 18 + d:19 + d], -1.0, 1.0, ALU.mult, ALU.add,
                             reads=[("lb", d)], writes=[("oml", d)])
                    if int(os.environ.get("KHG", "3")) < 1:
                        continue
                    with ExitStack() as es2:
                        q32 = self.sb(es2, "hq32", [128, SL], F32)
                        z = self.sb(es2, "hz", [128, SL], F32)
                        g = self.sb(es2, "hg", [128, SL], F32)
                        k = self.sb(es2, "hk", [128, SL], F32)
                        bb = self.sb(es2, "hb_", [128, SL], F32)
                        bc = self.sb(es2, "hbc", [128, SL], F32)
                        e1 = self.sb(es2, "he1", [128, SL], F32)
                        khT = self.sb(es2, "khT", [128, SL], BF16)
                        k2 = self.sb(es2, "hk2", [128, SL], F32)
                        c3 = lambda t_: t_[:].rearrange("p (c s) -> p c s", s=64)
                        ncs = SL // 64
                        for sl in range(T // SL):
                            tsl = slice(sl * SL, (sl + 1) * SL)
                            S.dma("sp", q32[:], self.s_hT[hp * 128:(hp + 1) * 128, tsl], writes=["q32"])
                            for d in range(2):
                                r0 = 256 * (1 + d) + hp * 128
                                S.dma("sp", z[:], self.s_hT[r0:r0 + 128, tsl], writes=["z"])
                                S.op("act", "activation", out=z[:], in_=z[:], func=AF.Sigmoid, reads=["z"], writes=["z"])
                                S.op("dve", "tensor_scalar", z[:], z[:], lbl[:, 20 + d:21 + d], lbl[:, 18 + d:19 + d], ALU.mult, ALU.add,
                                     reads=["z", ("lb", d), ("oml", d)], writes=["z"])
                                S.op("act", "activation", out=g[:], in_=z[:], func=AF.Ln, reads=["z"], writes=["g"])
                                S.op("pool", "tensor_scalar", k[:], z[:], -1.0, 1.0, ALU.mult, ALU.add, reads=["z"], writes=["k"])
                                S.op("dve", "tensor_tensor_scan", bb[:], rmask[:], g[:], 0.0, ALU.mult, ALU.add,
                                     reads=["g", "rmask"], writes=["bb"])
                                tot = c3(bb)[:, :, 63:64]
                                totb = tot.to_broadcast([128, ncs, 64])
                                if d == 1:
                                    S.op("dve", "tensor_tensor", bc[:], g[:], bb[:], ALU.subtract, reads=["g", "bb"], writes=["bc"])
                                    S.op("dve", "tensor_tensor", c3(e1), c3(bc), totb, ALU.add, reads=["bc", "bb"], writes=["e1"])
                                    S.op("act", "activation", out=etot[d][:, sl * ncs:(sl + 1) * ncs].unsqueeze(2), in_=tot, func=AF.Exp,
                                         reads=["bb"], writes=[("etot", d)])
                                    S.op("dve", "tensor_copy", bb[:], e1[:], reads=["e1"], writes=["bb"])
                                    totb = c3(bb)[:, :, 0:1].to_broadcast([128, ncs, 64])
                                else:
                                    S.op("act", "activation", out=etot[d][:, sl * ncs:(sl + 1) * ncs].unsqueeze(2), in_=tot, func=AF.Exp,
                                         reads=["bb"], writes=[("etot", d)])
                                c32 = lambda t_: t_[:].rearrange("p (c s) -> p c s", s=32)
                                rix = 31 if d == 0 else 32
                                S.op("dve", "tensor_tensor", c3(bc), c3(bb), c3(bb)[:, :, rix:rix + 1].to_broadcast([128, ncs, 64]),
                                     ALU.subtract, reads=["bb"], writes=["bc"])
                                S.op("pool", "tensor_scalar", k2[:], bc[:], 0.0, None, ALU.max, reads=["bc"], writes=["k2"])
                                S.op("dve", "tensor_scalar", bc[:], bc[:], 0.0, None, ALU.min, reads=["bc"], writes=["bc"])
                                S.op("act", "activation", out=e1[:], in_=bc[:], func=AF.Exp, reads=["bc"], writes=["e1"])
                                S.op("pool", "tensor_tensor", qo[d][:, tsl], q32[:], e1[:], ALU.mult, reads=["q32", "e1"], writes=[("qo", d)])
                                S.op("act", "activation", out=e1[:], in_=k2[:], func=AF.Exp, scale=-1.0, reads=["k2"], writes=["e1"])
                                S.op("pool", "tensor_tensor", ko[d][:, tsl], k[:], e1[:], ALU.mult, reads=["k", "e1"], writes=[("ko", d)])
                                S.op("dve", "tensor_tensor", c32(bc), c32(bb), c32(bb)[:, :, 15:16].to_broadcast([128, 2 * ncs, 32]),
                                     ALU.subtract, reads=["bb"], writes=["bc"])
                                S.op("dve", "tensor_scalar", bc[:], bc[:], 40.0, -40.0, ALU.min, ALU.max, reads=["bc"], writes=["bc"])
                                S.op("act", "activation", out=e1[:], in_=bc[:], func=AF.Exp, reads=["bc"], writes=["e1"])
                                S.op("pool", "tensor_tensor", qt[d][:, tsl], q32[:], e1[:], ALU.mult, reads=["q32", "e1"], writes=[("qt", d)])
                                S.op("dve", "reciprocal", e1[:], e1[:], reads=["e1"], writes=["e1"])
                                S.op("pool", "tensor_tensor", kt[d][:, tsl], k[:], e1[:], ALU.mult, reads=["k", "e1"], writes=[("kt", d)])
                                S.op("act", "activation", out=e1[:], in_=bb[:], func=AF.Exp, reads=["bb"], writes=["e1"])
                                S.op("pool", "tensor_tensor", qb[d][:, tsl], q32[:], e1[:], ALU.mult, reads=["q32", "e1"], writes=[("qb", d)])
                                S.op("dve", "tensor_tensor", c3(bc), totb, c3(bb), ALU.subtract, reads=["bb"], writes=["bc"])
                                S.op("act", "activation", out=e1[:], in_=bc[:], func=AF.Exp, reads=["bc"], writes=["e1"])
                                S.op("pool", "tensor_tensor", khT[:], k[:], e1[:], ALU.mult, reads=["k", "e1"], writes=["khT"])
                                for c4 in range(ncs // 8):
                                    pi = 4 + c4 % 2
                                    pst = self.psum[pi][:, :].bitcast(BF16)
                                    for cc in range(8):
                                        c = c4 * 8 + cc
                                        S.op("pe", "transpose", pst[0:64, cc * 128:(cc + 1) * 128], khT[:, c * 64:(c + 1) * 64],
                                             self.ident_bf[:], reads=["khT"], writes=[("ps", pi)])
                                    c0 = sl * ncs + c4 * 8
                                    S.op("act", "activation", out=khat[d][:, c0:c0 + 8, :],
                                         in_=pst[0:64, 0:1024].rearrange("p (c e) -> p c e", c=8), func=AF.Copy,
                                         reads=[("ps", pi)], writes=[("khat", d)])
                        S.barrier()
                    khg = int(os.environ.get("KHG", "3"))
                    if khg < 2:
                        continue
                    with ExitStack() as es3:
                        Scur = [self.sb(es3, f"Scur{d}", [128, 128], F32) for d in range(2)]
                        Sout = [[self.sb(es3, f"Sout{d}{i}", [128, 128], F32) for i in range(2)] for d in range(2)]
                        for d in range(2):
                            S.op("pool", "memset", Scur[d][:], 0.0, writes=[("Scur", d)])
                            for hh in range(2):
                                S.dma("sp", Scur[d][hh * 64:(hh + 1) * 64, hh * 64:(hh + 1) * 64], self.s0[l, d, 2 * hp + hh],
                                      reads=[("Scur", d)], writes=[("Scur", d)])
                        for step in range(NCH):
                            for d in range(2):
                                c = step if d == 0 else NCH - 1 - step
                                i = step % 2
                                pi = 4 + d
                                S.op("pool", "tensor_copy", stab[d][:, c, :], Scur[d][:], reads=[("Scur", d)], writes=[("stab", d)])
                                S.op("pe", "matmul", self.psum[pi][:, 0:128], khat[d][:, c, :], vsb[:, c, :], start=True, stop=True,
                                     reads=[("khat", d), "vsb"], writes=[("ps", pi)])
                                S.op("dve", "scalar_tensor_tensor", Sout[d][i][:], Scur[d][:], etot[d][:, c:c + 1], self.psum[pi][:, 0:128],
                                     ALU.mult, ALU.add, reads=[("Scur", d), ("etot", d), ("ps", pi)], writes=[("Sout", d, i)])
                                is_end = (c % 4 == 3) if d == 0 else (c % 4 == 0)
                                if is_end:
                                    for hh in range(2):
                                        S.dma("sp", self.o_state[l, c // 4, d, 2 * hp + hh],
                                              Sout[d][i][hh * 64:(hh + 1) * 64, hh * 64:(hh + 1) * 64], reads=[("Sout", d, i)])
                                cn = c + 1 if d == 0 else c - 1
                                if 0 <= cn < NCH:
                                    S.op("dve", "tensor_scalar", Scur[d][:], Sout[d][i][:], car[d][:, cn:cn + 1], None, ALU.mult,
                                         reads=[("Sout", d, i), ("car", d)], writes=[("Scur", d)])
                        S.barrier()
                    if khg < 3:
                        continue
                    with ExitStack() as es4:
                        Af = [self.sb(es4, f"Af{i}", [64, 256], F32) for i in range(2)]
                        Am = [self.sb(es4, f"Am{i}", [64, 256], BF16) for i in range(2)]
                        o32 = self.sb(es4, "ho32", [64, 512], F32)
                        sqf = self.sb(es4, "hsq", [64, 512], F32)
                        rst = self.sb(es4, "hrst", [64, 512], F32)
                        gt = [self.sb(es4, f"hgt{i}", [64, 512], BF16) for i in range(2)]
                        yb = [self.sb(es4, f"hyb{i}", [64, 512], BF16) for i in range(2)]
                        for st in range(NST):
                            for cc in range(8):
                                c = st * 8 + cc
                                i = c % 2
                                csl = slice(c * 64, (c + 1) * 64)
                                for hh in range(2):
                                    pA = self.psum[hh]
                                    hs_ = slice(hh * 64, (hh + 1) * 64)
                                    h1 = slice(c * 64, c * 64 + 32)
                                    h2 = slice(c * 64 + 32, c * 64 + 64)
                                    for d in range(2):
                                        col = d * 64
                                        for (sh, th, so, to) in ((h1, h1, 0, 0), (h2, h2, 32, 32), (h1, h2, 0, 32), (h2, h1, 32, 0)):
                                            cross_valid = (so == 0 and to == 32) if d == 0 else (so == 32 and to == 0)
                                            kk_, qq_ = (ko[d], qo[d]) if cross_valid else (kt[d], qt[d])
                                            S.op("pe", "matmul", pA[so:so + 32, col + to:col + to + 32], kk_[hs_, sh], qq_[hs_, th],
                                                 start=True, stop=True, reads=[("kt", d), ("qt", d), ("ko", d), ("qo", d)],
                                                 writes=[("ps", hh)])
                                for hh in range(2):
                                    S.op("dve", "tensor_tensor", Af[i][:, hh * 128:(hh + 1) * 128], self.psum[hh][0:64, 0:128],
                                         mT[:, hh * 128:(hh + 1) * 128], ALU.mult, reads=[("ps", hh), "mT"], writes=[("Af", i)])
                                S.op("pool", "tensor_copy", Am[i][:], Af[i][:], reads=[("Af", i)], writes=[("Am", i)])
                                for hh in range(2):
                                    pO = self.psum[2 + hh]
                                    pI = self.psum[4 + hh]
                                    hs = slice(hh * 64, (hh + 1) * 64)
                                    ocol = slice(cc * 64, (cc + 1) * 64)
                                    kp2 = os.environ.get("KP2", "")
                                    if "nointra" in kp2:
                                        continue
                                    S.op("pe", "matmul", pO[0:64, ocol], vsb[:, c, hs], Am[i][:, (hh * 2) * 64:(hh * 2 + 1) * 64],
                                         start=True, stop=False, reads=["vsb", ("Am", i)], writes=[("ps", 2 + hh)])
                                    S.op("pe", "matmul", pO[0:64, ocol], vsb[:, c, hs], Am[i][:, (hh * 2 + 1) * 64:(hh * 2 + 2) * 64],
                                         start=False, stop=True, reads=["vsb", ("Am", i)], writes=[("ps", 2 + hh)])
                                    for d in range(0 if "nointer" in kp2 else 2):
                                        S.op("pe", "matmul", pI[0:64, ocol], stab[d][hs, c, hs], qb[d][hs, csl],
                                             start=(d == 0), stop=(d == 1), reads=[("stab", d), ("qb", d)], writes=[("ps", 4 + hh)])
                            for hh in range(2):
                                h = 2 * hp + hh
                                gi = hh
                                r0 = 512 + h * 64
                                tsl = slice(st * 512, (st + 1) * 512)
                                S.dma("sp", gt[gi][:], self.s_gT[r0:r0 + 64, tsl], writes=[("gt", gi)])
                                S.op("act", "activation", out=o32[:], in_=self.psum[2 + hh][0:64, :], func=AF.Copy,
                                     reads=[("ps", 2 + hh)], writes=["o32"])
                                S.op("dve", "tensor_tensor", o32[:], o32[:], self.psum[4 + hh][0:64, :], ALU.add,
                                     reads=["o32", ("ps", 4 + hh)], writes=["o32"])
                                S.op("pool", "tensor_tensor", sqf[:], o32[:], o32[:], ALU.mult, reads=["o32"], writes=["sqf"])
                                S.op("pe", "matmul", self.psum[6][0:64, :], self.ones_f[0:64, 0:64], sqf[:], start=True, stop=True,
                                     reads=["sqf"], writes=[("ps", 6)])
                                S.op("dve", "tensor_scalar", rst[:], self.psum[6][0:64, :], 1.0 / 64, EPS, ALU.mult, ALU.add,
                                     reads=[("ps", 6)], writes=["rst"])
                                S.op("act", "activation", out=rst[:], in_=rst[:], func=AF.Sqrt, reads=["rst"], writes=["rst"])
                                S.op("dve", "reciprocal", rst[:], rst[:], reads=["rst"], writes=["rst"])
                                S.op("dve", "scalar_tensor_tensor", o32[:], o32[:], og[0:64, 0:1], rst[:], ALU.mult, ALU.mult,
                                     reads=["o32", "rst", "og"], writes=["o32"])
                                S.op("pool", "tensor_tensor", yb[gi][:], o32[:], gt[gi][:], ALU.mult, reads=["o32", ("gt", gi)],
                                     writes=[("yb", gi)])
                                S.dma("sp", self.s_yT[r0:r0 + 64, tsl], yb[gi][:], reads=[("yb", gi)])
                        S.barrier()

    def hy_wrap(self, z, m, key):
        S = self.S
        PI = math.pi
        S.op("dve", "tensor_scalar", m, z, PI, None, ALU.is_gt, reads=[key], writes=[key + "m"])
        S.op("dve", "scalar_tensor_tensor", z, m, -2 * PI, z, ALU.mult, ALU.add, reads=[key, key + "m"], writes=[key])
        S.op("dve", "tensor_scalar", m, z, -PI, None, ALU.is_lt, reads=[key], writes=[key + "m"])
        S.op("dve", "scalar_tensor_tensor", z, m, 2 * PI, z, ALU.mult, ALU.add, reads=[key, key + "m"], writes=[key])

    def phase_hyena(self, l):
        nc, S = self.nc, self.S
        with ExitStack() as es:
            F1z = self.sb(es, "F1z", [32, 128], F32)
            F1f = self.sb(es, "F1f", [64, 128], F32)
            Em = self.sb(es, "Em", [128, 512], BF16)
            Qm = self.sb(es, "Qm", [64, 128 * 64], BF16)
            S.dma("sp", F1z[:], self.F1z[:, :], writes=["F1z"])
            S.dma("sp", F1f[:], self.F1f[:, :], writes=["F1f"])
            S.dma("sp", Em[:], self.Em[:, :], writes=["Em"])
            S.dma("sp", Qm[:], self.Qm[:, :], writes=["Qm"])
            with ExitStack() as es1:
                w1 = self.sb(es1, "hw1", [33, 64], F32)
                w2 = self.sb(es1, "hw2", [64, 64], F32)
                w3 = self.sb(es1, "hw3", [64, 1024], F32)
                cols = self.sb(es1, "hcols", [64, 8], F32)
                nd = self.sb(es1, "hnd", [128, 2], F32)
                ft = [self.sb(es1, f"hft{i}", [33, 512], F32) for i in range(2)]
                tl = [self.sb(es1, f"htl{i}", [128, 512], F32) for i in range(2)]
                dm = [self.sb(es1, f"hdm{i}", [128, 2, 512], F32) for i in range(2)]
                z1 = self.sb(es1, "hz1", [64, 512], F32)
                m1 = self.sb(es1, "hm1", [64, 512], F32)
                h1 = self.sb(es1, "hh1", [64, 512], F32)
                h2 = self.sb(es1, "hh2", [64, 512], F32)
                win = self.sb(es1, "hwin", [128, 2, 512], F32)
                fa = self.sb(es1, "hfa", [128, 512], F32)
                fo = [self.sb(es1, f"hfo{i}", [128, 512], F32) for i in range(2)]
                S.dma("sp", w1[:], self.hy_w1[l], writes=["w1"])
                S.dma("sp", w2[:], self.hy_w2[l], writes=["w2"])
                S.dma("sp", w3[:], self.hy_w3[l], writes=["w3"])
                S.dma("sp", cols[:, 0:4], self.hy_cols[l], writes=["cols"])
                S.dma("sp", nd[:, 0:1], self.negdelta[0:128, :], writes=["nd"])
                S.dma("sp", nd[:, 1:2], self.negdelta[128:256, :], writes=["nd"])
                S.op("dve", "tensor_tensor", cols[:, 4:6], cols[:, 0:2], cols[:, 2:4], ALU.mult, reads=["cols"], writes=["cols2"])
                for sl in range(16):
                    b = sl % 2
                    ssl = slice(sl * 512, (sl + 1) * 512)
                    S.dma("sp", ft[b][:], self.featsT[:, ssl], writes=[("ft", b)])
                    S.dma("sp", tl[b][:], self.featsT[0, ssl].partition_broadcast(128), writes=[("tl", b)])
                    for dd in range(2):
                        S.dma("sp", dm[b][:, dd, :], self.dmask[dd, ssl].partition_broadcast(128), writes=[("dm", b)])
                    S.op("pe", "matmul", self.psum[0][0:64, :], w1[:], ft[b][:], start=True, stop=True,
                         reads=["w1", ("ft", b)], writes=[("ps", 0)])
                    S.op("dve", "tensor_scalar", z1[:], self.psum[0][0:64, :], cols[:, 2:3], cols[:, 4:5], ALU.mult, ALU.add,
                         reads=[("ps", 0), "cols", "cols2"], writes=["z1"])
                    self.hy_wrap(z1[:], m1[:], "z1")
                    S.op("act", "activation", out=h1[:], in_=z1[:], func=AF.Sin, reads=["z1"], writes=["h1"])
                    S.op("pe", "matmul", self.psum[1][0:64, :], w2[:], h1[:], start=True, stop=True,
                         reads=["w2", "h1"], writes=[("ps", 1)])
                    S.op("dve", "tensor_scalar", z1[:], self.psum[1][0:64, :], cols[:, 3:4], cols[:, 5:6], ALU.mult, ALU.add,
                         reads=[("ps", 1), "cols", "cols2"], writes=["z1"])
                    self.hy_wrap(z1[:], m1[:], "z1")
                    S.op("act", "activation", out=h2[:], in_=z1[:], func=AF.Sin, reads=["z1"], writes=["h2"])
                    for cch in range(2):
                        S.op("act", "activation", out=win[:, cch, :], in_=tl[b][:], func=AF.Exp, scale=nd[:, cch:cch + 1],
                             reads=[("tl", b), "nd"], writes=[("win", cch)])
                    it = 0
                    for o in range(2):
                        for cch in range(2):
                            for dd in range(2):
                                c0 = (o * 2 + dd) * 256 + cch * 128
                                S.op("pe", "matmul", self.psum[2 + dd][:, :], w3[:, c0:c0 + 128], h2[:], start=True, stop=True,
                                     reads=["w3", "h2"], writes=[("ps", 2 + dd)])
                            S.op("dve", "tensor_tensor", fa[:], self.psum[2][:, :], dm[b][:, 0, :], ALU.mult,
                                 reads=[("ps", 2), ("dm", b)], writes=["fa"])
                            S.op("dve", "tensor_tensor", fo[it % 2][:], self.psum[3][:, :], dm[b][:, 1, :], ALU.mult,
                                 reads=[("ps", 3), ("dm", b)], writes=[("fo", it % 2)])
                            S.op("pool", "tensor_tensor", fo[it % 2][:], fo[it % 2][:], fa[:], ALU.add,
                                 reads=[("fo", it % 2), "fa"], writes=[("fo", it % 2)])
                            S.op("pool", "tensor_tensor", fo[it % 2][:], fo[it % 2][:], win[:, cch, :], ALU.mult,
                                 reads=[("fo", it % 2), ("win", cch)], writes=[("fo", it % 2)])
                            S.dma("sp", self.s_filtT[o, cch * 128:(cch + 1) * 128, ssl], fo[it % 2][:], reads=[("fo", it % 2)])
                            it += 1
                S.barrier()

            with ExitStack() as es2:
                AB = self.sb(es2, "hyAB", [128, 16384], BF16)
                Y = self.sb(es2, "hyY", [128, 2, 128, 64], BF16)
                xb = [self.sb(es2, f"hyxb{i}", [64, 8, 128], F32) for i in range(2)]
                Gt = [self.sb(es2, f"hyG{i}", [128, 4, 384], BF16) for i in range(2)]
                Xg = [self.sb(es2, f"hyXg{i}", [128, 2, 4, 128], F32) for i in range(2)]
                Hg = [self.sb(es2, f"hyHg{i}", [128, 2, 4, 128], F32) for i in range(2)]
                pt1 = self.sb(es2, "hyp1", [128, 4, 128], F32)
                pt2 = self.sb(es2, "hyp2", [128, 4, 128], F32)
                A4 = AB[:, :].rearrange("p (r k c) -> p r k c", r=2, k=64)
                Bt = AB[0:64, :].rearrange("p (r m c) -> p r m c", r=2, m=128)

                def forward(src, K1, F1, key_f1, mode, hidx):
                    for cg in range(16):
                        b = cg % 2
                        S.dma("sp", xb[b][0:K1, :, :], src[cg * 8:(cg + 1) * 8, :].rearrange("c (a n) -> a c n", n=128),
                              writes=[("xb", b)])
                        for q4 in range(2):
                            pi = q4 % 2
                            for cc in range(4):
                                ci = q4 * 4 + cc
                                S.op("pe", "matmul", self.psum[pi][:, cc * 128:(cc + 1) * 128], xb[b][0:K1, ci, :], F1[0:K1, :],
                                     start=True, stop=True, reads=[("xb", b), key_f1], writes=[("ps", pi)])
                            c0 = cg * 8 + q4 * 4
                            S.op("act", "activation", out=A4[:, :, :, c0:c0 + 4].rearrange("p r k c -> p c r k"),
                                 in_=self.psum[pi][:, :].rearrange("p (c r k) -> p c r k", c=4, r=2), func=AF.Copy,
                                 reads=[("ps", pi)], writes=["A"])
                    for kg in range(16):
                        b = kg % 2
                        S.dma("sp", Gt[b][:], self.Gm[kg * 4:(kg + 1) * 4].rearrange("k n x -> n k x"), writes=[("G", b)])
                        if mode == "Y":
                            S.dma("sp", Hg[b][:], self.s_H[hidx].rearrange("p (r k c) -> p r k c", r=2, k=64)[:, :, kg * 4:(kg + 1) * 4, :],
                                  writes=[("Hg", b)])
                        for kk in range(4):
                            k1 = kg * 4 + kk
                            pi = 2 + kk % 2
                            ps = self.psum[pi]
                            S.op("pe", "matmul", ps[:, 0:128], Gt[b][:, kk, 0:128], A4[:, 0, k1, :], start=True, stop=False,
                                 reads=[("G", b), "A"], writes=[("ps", pi)])
                            S.op("pe", "matmul", ps[:, 0:128], Gt[b][:, kk, 256:384], A4[:, 1, k1, :], start=False, stop=True,
                                 reads=[("G", b), "A"], writes=[("ps", pi)])
                            S.op("pe", "matmul", ps[:, 128:256], Gt[b][:, kk, 128:256], A4[:, 0, k1, :], start=True, stop=False,
                                 reads=[("G", b), "A"], writes=[("ps", pi)])
                            S.op("pe", "matmul", ps[:, 128:256], Gt[b][:, kk, 0:128], A4[:, 1, k1, :], start=False, stop=True,
                                 reads=[("G", b), "A"], writes=[("ps", pi)])
                            S.op("act", "activation", out=Xg[b][:, :, kk, :], in_=ps[:, 0:256].rearrange("p (r c) -> p r c", r=2),
                                 func=AF.Copy, reads=[("ps", pi)], writes=[("Xg", b)])
                        if mode == "H":
                            S.dma("sp", self.s_H[hidx].rearrange("p (r k c) -> p r k c", r=2, k=64)[:, :, kg * 4:(kg + 1) * 4, :],
                                  Xg[b][:], reads=[("Xg", b)])
                        else:
                            Xr, Xi, Hr, Hi = Xg[b][:, 0], Xg[b][:, 1], Hg[b][:, 0], Hg[b][:, 1]
                            ksl = slice(kg * 4, (kg + 1) * 4)
                            S.op("dve", "tensor_tensor", pt1[:], Xr, Hr, ALU.mult, reads=[("Xg", b), ("Hg", b)], writes=["pt1"])
                            S.op("pool", "tensor_tensor", pt2[:], Xi, Hi, ALU.mult, reads=[("Xg", b), ("Hg", b)], writes=["pt2"])
                            S.op("pool", "tensor_tensor", Y[:, 0, :, ksl].rearrange("p c k -> p k c"), pt1[:], pt2[:], ALU.subtract,
                                 reads=["pt1", "pt2"], writes=["Y"])
                            S.op("dve", "tensor_tensor", pt1[:], Xr, Hi, ALU.mult, reads=[("Xg", b), ("Hg", b)], writes=["pt1"])
                            S.op("pool", "tensor_tensor", pt2[:], Xi, Hr, ALU.mult, reads=[("Xg", b), ("Hg", b)], writes=["pt2"])
                            S.op("pool", "tensor_tensor", Y[:, 1, :, ksl].rearrange("p c k -> p k c"), pt1[:], pt2[:], ALU.add,
                                 reads=["pt1", "pt2"], writes=["Y"])

                def inverse(yT):
                    E3 = Em[:, :].rearrange("p (v x) -> p v x", v=2)
                    Q4 = Qm[:, :].rearrange("p (m v b) -> p m v b", m=128, v=2)
                    for half in range(2):
                        for cp in range(32):
                            pi = 4 + cp % 2
                            for cc in range(2):
                                c = half * 64 + cp * 2 + cc
                                S.op("pe", "matmul", self.psum[pi][0:64, cc * 256:(cc + 1) * 256], Y[:, 0, c, :], E3[:, 0, :],
                                     start=True, stop=False, reads=["Y", "Em"], writes=[("ps", pi)])
                                S.op("pe", "matmul", self.psum[pi][0:64, cc * 256:(cc + 1) * 256], Y[:, 1, c, :], E3[:, 1, :],
                                     start=False, stop=True, reads=["Y", "Em"], writes=[("ps", pi)])
                            S.op("act", "activation", out=Bt[:, :, :, cp * 2:cp * 2 + 2].rearrange("p r m c -> p c r m"),
                                 in_=self.psum[pi][0:64, :].rearrange("p (c r m) -> p c r m", c=2, r=2), func=AF.Copy,
                                 reads=[("ps", pi)], writes=["Bt"])
                        for mg in range(8):
                            pi = 6 + mg % 2
                            for mm in range(16):
                                ma = mg * 16 + mm
                                S.op("pe", "matmul", self.psum[pi][0:64, mm * 32:(mm + 1) * 32], Bt[:, 0, ma, :], Q4[:, ma, 0, :],
                                     start=True, stop=False, reads=["Bt", "Qm"], writes=[("ps", pi)])
                                S.op("pe", "matmul", self.psum[pi][0:64, mm * 32:(mm + 1) * 32], Bt[:, 1, ma, :], Q4[:, ma, 1, :],
                                     start=False, stop=True, reads=["Bt", "Qm"], writes=[("ps", pi)])
                            S.op("act", "activation",
                                 out=yT[half * 64:(half + 1) * 64, :].rearrange("p (b a) -> p a b", a=128)[:, mg * 16:(mg + 1) * 16, :],
                                 in_=self.psum[pi][0:64, :].rearrange("p (a b) -> p a b", a=16), func=AF.Copy,
                                 reads=[("ps", pi)], writes=["yT"])

                for o in range(2):
                    for cch in range(2):
                        forward(self.s_filtT[o, cch * 128:(cch + 1) * 128, :], 64, F1f, "F1f", "H", o * 2 + cch)
                S.barrier()

                with ExitStack() as es3:
                    u = self.sb(es3, "hyu", [128, T], F32)
                    us = self.sb(es3, "hyus", [128, T], F32)
                    zc = self.sb(es3, "hyz", [128, T], F32)
                    cw = self.sb(es3, "hycw", [128, 4], F32)
                    bm = self.sb(es3, "hybm", [128, 2, 16], F32)
                    bia = self.sb(es3, "hybia", [128, 2], F32)
                    yT = us
                    ub = u[:, :].bitcast(BF16)
                    S.dma("sp", bm[:, 0, :], self.bmask[0], writes=["bm"])
                    S.dma("sp", bm[:, 1, :], self.bmask[1], writes=["bm"])
                    u3 = lambda t_: t_[:].rearrange("p (s t) -> p s t", s=16)
                    for cch in range(2):
                        for o in range(2):
                            S.dma("sp", bia[:, o:o + 1], self.hy_biasT[l, o, cch * 128:(cch + 1) * 128, :], writes=["bia"])
                        for part in range(3):
                            r0 = part * 256 + cch * 128
                            S.dma("sp", u[:], self.s_huT[r0:r0 + 128, :], writes=["u"])
                            S.dma("sp", cw[:], self.hy_convT[l, r0:r0 + 128, :], writes=["cw"])
                            S.op("pool", "memset", us[:, 0:1], 0.0, writes=["us"])
                            S.op("pool", "tensor_copy", us[:, 1:T], u[:, 0:T - 1], reads=["u"], writes=["us"])
                            S.op("dve", "tensor_tensor", u3(us)[:, :, 0:1], u3(us)[:, :, 0:1], bm[:, 0, :].unsqueeze(2), ALU.mult,
                                 reads=["us", "bm"], writes=["us"])
                            S.op("dve", "tensor_scalar", zc[:], u[:], cw[:, 1:2], cw[:, 3:4], ALU.mult, ALU.add,
                                 reads=["u", "cw"], writes=["zc"])
                            S.op("dve", "scalar_tensor_tensor", zc[:], us[:], cw[:, 0:1], zc[:], ALU.mult, ALU.add,
                                 reads=["us", "zc", "cw"], writes=["zc"])
                            S.op("pool", "memset", us[:, T - 1:T], 0.0, reads=["us"], writes=["us"])
                            S.op("pool", "tensor_copy", us[:, 0:T - 1], u[:, 1:T], reads=["u"], writes=["us"])
                            S.op("dve", "tensor_tensor", u3(us)[:, :, 255:256], u3(us)[:, :, 255:256], bm[:, 1, :].unsqueeze(2), ALU.mult,
                                 reads=["us", "bm"], writes=["us"])
                            S.op("dve", "scalar_tensor_tensor", zc[:], us[:], cw[:, 2:3], zc[:], ALU.mult, ALU.add,
                                 reads=["us", "zc", "cw"], writes=["zc"])
                            S.dma("sp", self.s_z[part, :, :], zc[:], reads=["zc"])
                        S.barrier()
                        for o in range(2):
                            zsrc = self.s_z[0] if o == 0 else self.s_z[3]
                            forward(zsrc, 32, F1z, "F1z", "Y", o * 2 + cch)
                            inverse(yT)
                            S.dma("sp", zc[:], zsrc, writes=["zc"])
                            S.dma("sp", u[:], self.s_z[1 + o], writes=["u"])
                            S.op("dve", "scalar_tensor_tensor", zc[:], zc[:], bia[:, o:o + 1], yT[:], ALU.mult, ALU.add,
                                 reads=["zc", "yT", "bia"], writes=["zc"])
                            S.op("pool", "tensor_tensor", zc[:], zc[:], u[:], ALU.mult, reads=["zc", "u"], writes=["zc"])
                            if o == 0:
                                S.dma("sp", self.s_z[3], zc[:], reads=["zc"])
                            else:
                                r0 = 768 + cch * 128
                                S.dma("sp", ub[:, 0:T], self.s_gT[r0:r0 + 128, :], reads=["u"], writes=["u"])
                                S.op("pool", "tensor_tensor", ub[:, T:2 * T], zc[:], ub[:, 0:T], ALU.mult, reads=["zc", "u"], writes=["u"])
                                S.dma("sp", self.s_yT[r0:r0 + 128, :], ub[:, T:2 * T], reads=["u"])
                            S.barrier()

    def phase_po(self, l):
        nc, S = self.nc, self.S
        xsrc = self.x_in if l == 0 else self.y
        with ExitStack() as es:
            wo32 = self.sb(es, "wo32", [128, 8, D], F32)
            wo16 = self.sb(es, "wo16", [128, 8, D], BF16)
            yT = [self.sb(es, f"yT{i}", [128, 8, 512], BF16) for i in range(2)]
            xin = [self.sb(es, f"xo{i}", [128, 4, D], F32) for i in range(2)]
            xo = [self.sb(es, f"xn{i}", [128, D], F32) for i in range(2)]
            for k in range(8):
                S.dma("sp", wo32[:, k, :], self.w_out[l, k * 128:(k + 1) * 128, :], writes=[("wo32", k)])
                S.op("dve" if k % 2 == 0 else "pool", "tensor_tensor", wo16[:, k, :], wo32[:, k, :], self.gate_b[:], ALU.mult,
                     reads=[("wo32", k)], writes=[("wo16", k)])

            def load(st):
                b = st % 2
                S.dma("sp", yT[b][:], self.s_yT[:, st * 512:(st + 1) * 512].rearrange("(k p) t -> p k t", p=128),
                      writes=[("yT", b)])
                S.dma("sp", xin[b][:], xsrc[st * 512:(st + 1) * 512, :].rearrange("(j p) d -> p j d", p=128),
                      writes=[("xo", b)])

            load(0)
            for st in range(NST):
                b = st % 2
                if st + 1 < NST:
                    load(st + 1)
                for j in range(4):
                    tok0 = st * 512 + j * 128
                    xb = j % 2
                    for n in range(2):
                        pi = (j * 2 + n) % 4
                        ps = self.psum[pi]
                        for k in range(8):
                            S.op("pe", "matmul", ps[:, :], yT[b][:, k, j * 128:(j + 1) * 128], wo16[:, k, n * 512:(n + 1) * 512],
                                 start=(k == 0), stop=(k == 7), reads=[("yT", b), ("wo16", k)], writes=[("ps", pi)])
                        S.op("dve", "tensor_tensor", xo[xb][:, n * 512:(n + 1) * 512], ps[:, :], xin[b][:, j, n * 512:(n + 1) * 512],
                             ALU.add, reads=[("ps", pi), ("xo", b)], writes=[("xn", xb, n)])
                    S.dma("sp", self.y[tok0:tok0 + 128, :], xo[xb][:], reads=[("xn", xb, 0), ("xn", xb, 1)])


def host_consts():
    c = {}
    c["ident"] = np.eye(128, dtype=np.float32)
    return c


def hyena_tables(is_prompt):
    f32 = np.float32
    N = 8192
    n = np.arange(N)
    t = {}
    if not is_prompt:
        L = 4096
        tt = np.where(n < L, n, 2 * L - 1 - n)
        dm0 = (n < L)
        dm1 = ~dm0
    else:
        L = 256
        tt = np.clip(np.where(n < 256, n, 511 - n), 0, 255)
        dm0 = (n < 256)
        dm1 = (n >= 256) & (n < 512)
    tl = np.linspace(0.0, 1.0, L, dtype=f32)[tt]
    w = (f32(2.0 * math.pi) * np.arange(L, dtype=f32) / f32(L)).astype(f32)[tt]
    fb = np.linspace(1e-4, 15, 16, dtype=f32)
    ang = (w[:, None] * fb[None, :]).astype(f32)
    feats = np.concatenate([tl[:, None], np.cos(ang), -np.sin(ang)], axis=-1).astype(f32)
    t["featsT"] = np.ascontiguousarray(feats.T)
    t["dmask"] = np.stack([dm0, dm1]).astype(f32)
    bm = np.ones((2, 128, 16), f32)
    if is_prompt:
        bm[:] = 0.0
    t["bmask"] = bm
    two_pi = 2.0 * math.pi
    k1 = np.arange(64)
    if not is_prompt:
        n1 = np.arange(64)
        a = two_pi * ((n1[:, None] * k1[None, :]) % 64) / 64.0
        F1 = np.concatenate([np.cos(a), -np.sin(a)], axis=1)
        F1z, F1f = F1[:32], F1
        n2 = np.arange(128)[None, :, None]
        k2 = np.arange(128)[None, None, :]
        idx = (n2 * (k1[:, None, None] + 64 * k2)) % 8192
        th = two_pi * idx / 8192.0
        ma = np.arange(128)[None, :, None]
        mb = np.arange(32)[None, None, :]
        qi = (128 * mb * k1[:, None, None] + ma * k1[:, None, None]) % 8192
        ps = two_pi * qi / 8192.0
        Qr, Qi = np.cos(ps) / 8192.0, np.sin(ps) / 8192.0
    else:
        p = k1 // 4
        kp = k1 % 4
        n1 = np.arange(64)
        a = two_pi * (((n1[:, None] % 2) * kp[None, :]) % 4) / 4.0
        sel = ((n1[:, None] // 2) == p[None, :])
        F1z = np.concatenate([np.cos(a) * sel, -np.sin(a) * sel], axis=1)[:32]
        a2 = two_pi * ((n1[:, None] * kp[None, :]) % 4) / 4.0
        sel2 = (n1[:, None] < 4)
        F1f = np.concatenate([np.cos(a2) * sel2, -np.sin(a2) * sel2], axis=1)
        n2 = np.arange(128)[None, :, None]
        k2 = np.arange(128)[None, None, :]
        idx = (n2 * (kp[:, None, None] + 4 * k2)) % 512
        th = two_pi * idx / 512.0
        ma = np.arange(128)[None, :, None]
        mb = np.arange(32)[None, None, :]
        qi = (128 * (mb % 2) * kp[:, None, None] + ma * kp[:, None, None]) % 512
        ps = two_pi * qi / 512.0
        selq = ((mb // 2) == p[:, None, None])
        Qr, Qi = np.cos(ps) / 512.0 * selq, np.sin(ps) / 512.0 * selq
    t["F1z"] = np.ascontiguousarray(F1z.astype(f32))
    t["F1f"] = np.ascontiguousarray(F1f.astype(f32))
    G = np.concatenate([np.cos(th), -np.sin(th), np.sin(th)], axis=2)
    t["Gm"] = np.ascontiguousarray(G.astype(ml_dtypes.bfloat16))
    k2v = np.arange(128)[:, None]
    mav = np.arange(128)[None, :]
    ph = two_pi * ((k2v * mav) % 128) / 128.0
    Er, Ei = np.cos(ph), np.sin(ph)
    t["Em"] = np.ascontiguousarray(np.concatenate([Er, Ei, -Ei, Er], axis=1).astype(ml_dtypes.bfloat16))
    Q = np.stack([Qr, -Qi], axis=2)
    t["Qm"] = np.ascontiguousarray(Q.reshape(64, 128 * 64).astype(ml_dtypes.bfloat16))
    dl = np.linspace(math.log(0.01) / 1.5, math.log(0.01) / 0.3, 256, dtype=f32)
    t["negdelta"] = (-np.abs(dl)).reshape(256, 1).astype(f32)
    return t


def job_tables(is_prompt):
    t = {}
    if not is_prompt:
        pos = np.arange(T)
        row = (pos // 64).astype(np.float32)
        col = (pos % 64).astype(np.float32)
        inv = (np.float32(10000.0) ** (-np.arange(0, 16, 2, dtype=np.float32) / np.float32(16))).astype(np.float32)
        ar = (row[:, None] * inv[None, :]).astype(np.float32)
        ac = (col[:, None] * inv[None, :]).astype(np.float32)
        C = np.concatenate([np.cos(ar), np.cos(ar), np.cos(ac), np.cos(ac)], axis=1).astype(np.float32)
        Ssg = np.concatenate([-np.sin(ar), np.sin(ar), -np.sin(ac), np.sin(ac)], axis=1).astype(np.float32)
        qaug = np.zeros((T, 17), np.float32)
        kaug = np.zeros((NKEY, 17), np.float32)
    else:
        C = np.ones((T, 32), np.float32)
        Ssg = np.zeros((T, 32), np.float32)
        pid = np.arange(T) // 256
        qaug = np.zeros((T, 17), np.float32)
        qaug[np.arange(T), pid] = 1.0
        qaug[:, 16] = 1.0
        kaug = np.zeros((NKEY, 17), np.float32)
        kaug[np.arange(T), pid] = BIG
        kaug[:, 16] = -BIG
    carry = np.ones((2, 128, 64), np.float32)
    if is_prompt:
        cidx = np.arange(64)
        carry[0][:, cidx % 4 == 0] = 0.0
        carry[1][:, cidx % 4 == 3] = 0.0
    t["carry"] = carry
    sidx = np.arange(64)
    mf = (sidx[:, None] <= sidx[None, :]).astype(np.float32)
    mb = (sidx[:, None] >= sidx[None, :]).astype(np.float32)
    t["maskT"] = np.ascontiguousarray(np.concatenate([mf, mb, mf, mb], axis=1))
    t.update(hyena_tables(is_prompt))
    t["ropeC"] = np.ascontiguousarray(np.tile(C, (1, 8)))
    t["ropeS"] = np.ascontiguousarray(np.tile(Ssg, (1, 8)))
    t["qaug"] = qaug
    t["kaug"] = kaug
    return t


def make_in_maps(inp, n_cores=8):
    consts = host_consts()
    tabs = {False: job_tables(False), True: job_tables(True)}
    maps = []
    for core in range(n_cores):
        job = core if core < 5 else 4
        m = dict(consts)
        if job < 4:
            m["x"] = np.ascontiguousarray(inp["x_sample"][job])
            cv = inp["c"][job]
        else:
            m["x"] = np.ascontiguousarray(inp["x_prompt"].reshape(T, D))
            cv = inp["c_ctx"]
        m["cvecT"] = np.ascontiguousarray(cv.reshape(8, 128).T)
        m.update(tabs[job >= 4])
        if job < 4:
            m["c_ckv"] = np.ascontiguousarray(inp["cache_mla_ckv"][job])
            m["c_krope"] = np.ascontiguousarray(inp["cache_mla_krope"][job])
            m["c_dk"] = np.ascontiguousarray(inp["cache_diff_k"][job].reshape(DEPTH, PAST, 256))
            m["c_dv"] = np.ascontiguousarray(inp["cache_diff_v"][job].reshape(DEPTH, PAST, 256))
        else:
            m["c_ckv"] = np.zeros((DEPTH, PAST, 128), np.float32)
            m["c_krope"] = np.zeros((DEPTH, PAST, 32), np.float32)
            m["c_dk"] = np.zeros((DEPTH, PAST, 256), np.float32)
            m["c_dv"] = np.zeros((DEPTH, PAST, 256), np.float32)
        m["lbT"] = np.ascontiguousarray(inp["hgrn_lb_logits"].transpose(2, 0, 1).reshape(256, 8))
        m["hy_cols"] = np.ascontiguousarray(np.stack([inp["hy_b1"], inp["hy_b2"], inp["hy_sin_freq"][:, 0], inp["hy_sin_freq"][:, 1]], axis=-1))
        m["hy_convT"] = np.ascontiguousarray(np.concatenate([inp["hy_conv_w"].transpose(0, 2, 1), inp["hy_conv_b"][:, :, None]], axis=-1))
        m["hy_biasT"] = np.ascontiguousarray(inp["hy_bias"].reshape(DEPTH, 2, 256, 1))
        for k in ("hy_w1", "hy_w2", "hy_w3"):
            m[k] = inp[k]
        m["hgrn_out_gT"] = np.ascontiguousarray(inp["hgrn_out_g"].reshape(DEPTH, 64, 1))
        if job < 4:
            m["s0"] = np.ascontiguousarray(inp["state_hgrn"][job])
        else:
            m["s0"] = np.zeros((DEPTH, 2, 4, 64, 64), np.float32)
        m["diff_lambda"] = np.ascontiguousarray(inp["diff_lambda"].reshape(DEPTH, 128))
        m["diff_subln_gT"] = np.ascontiguousarray(inp["diff_subln_g"].reshape(DEPTH, 64, 1))
        for k in ("norm_g", "w_mod", "b_mod", "w_in", "w_out", "mla_q_norm_g", "mla_w_uq", "mla_kv_norm_g", "mla_w_ukv",
                  "mla_nope_g", "mla_rope_g", "diff_qk_g"):
            m[k] = inp[k]
        maps.append(m)
    return maps


_CACHE = {}


def kernel(**inputs):
    inp = {k: np.asarray(v) for k, v in inputs.items()}
    if "b" not in _CACHE:
        _CACHE["b"] = Builder()
    b = _CACHE["b"]
    maps = make_in_maps(inp)
    maps = [{k: v for k, v in m.items() if k in b.inputs} for m in maps]
    res = run_bass_kernel_spmd(b.nc, maps, core_ids=list(range(8)))
    r = res.results
    y_sample = np.stack([np.asarray(r[j]["y"], dtype=np.float32) for j in range(4)], axis=0)
    p = r[4]
    y_prompt = np.asarray(p["y"], dtype=np.float32).reshape(16, 256, D)
    new_ckv = np.ascontiguousarray(np.asarray(p["o_ckv"], dtype=np.float32).reshape(DEPTH, 16, 256, 128).transpose(1, 0, 2, 3))
    new_krope = np.ascontiguousarray(np.asarray(p["o_krope"], dtype=np.float32).reshape(DEPTH, 16, 256, 32).transpose(1, 0, 2, 3))
    new_dk = np.ascontiguousarray(np.asarray(p["o_dk"], dtype=np.float32).reshape(DEPTH, 16, 256, 4, 2, 32).transpose(1, 0, 2, 3, 4, 5))
    new_dv = np.ascontiguousarray(np.asarray(p["o_dv"], dtype=np.float32).reshape(DEPTH, 16, 256, 4, 64).transpose(1, 0, 2, 3, 4))
    new_st = np.ascontiguousarray(np.asarray(p["o_state"], dtype=np.float32).transpose(1, 0, 2, 3, 4, 5))
    return (y_prompt, y_sample, new_ckv, new_krope, new_dk, new_dv, new_st)
```

```python
import math
import os
from contextlib import ExitStack

import numpy as np
import ml_dtypes

import concourse.bass as bass
import concourse.mybir as mybir
from concourse.bass_utils import run_bass_kernel_spmd

F32 = mybir.dt.float32
BF16 = mybir.dt.bfloat16
AF = mybir.ActivationFunctionType
ALU = mybir.AluOpType
AX = mybir.AxisListType

D = 1024
T = 4096
DEPTH = 4
PAST = 512
NKEY = T + PAST
EPS = 1e-6
IN_COLS = 4000
NT = T // 128
NST = T // 512
BIG = 30000.0

C_CQ, C_CKV, C_KR, C_GA = 0, 256, 384, 416
C_DQ, C_DK, C_DV, C_GB = 672, 928, 1184, 1440
C_HQ, C_HZF, C_HZB, C_HI, C_GC = 1696, 1952, 2208, 2464, 2720
C_HU, C_GD = 2976, 3744

EPOCH = 30000
ENGS = ("pe", "act", "dve", "pool", "sp")


class Sched:
    def __init__(self, nc, n_dma_sems=40, self_wait=True):
        self.nc = nc
        self.h = {"pe": nc.tensor, "act": nc.scalar, "dve": nc.vector, "pool": nc.gpsimd, "sp": nc.sync}
        self.sems = {e: [] for e in ENGS}
        self.cnt = {e: 0 for e in ENGS}
        self.seen = {e: {f: 0 for f in ENGS} for e in ENGS}
        self.clk = {e: [None] for e in ENGS}
        self.qrange = {"sp": (0, 24), "pool": (24, 40), "act": (40, 48)}
        n_dma_sems = 48
        self.nd = n_dma_sems
        self.dsem = [nc.alloc_semaphore(f"dma{i}") for i in range(n_dma_sems)]
        self.dval = [0] * n_dma_sems
        self.dclk = [None] * n_dma_sems
        self.dseen = {e: [0] * n_dma_sems for e in ENGS}
        self.drr = {q: r[0] for q, r in self.qrange.items()}
        self.kw = {}
        self.kr = {}
        self.self_wait = self_wait
        self.n_wait = 0
        self.n_inst = 0

    def _sem(self, e, idx):
        ep = idx // EPOCH
        while len(self.sems[e]) <= ep:
            self.sems[e].append(self.nc.alloc_semaphore(f"s_{e}_{len(self.sems[e])}"))
        return self.sems[e][ep], idx % EPOCH + 1

    def _snap(self, e):
        return (tuple(self.seen[e][f] for f in ENGS), tuple(self.dseen[e]))

    def _merge(self, e, snap):
        if snap is None:
            return
        s, d = snap
        se = self.seen[e]
        for f, v in zip(ENGS, s):
            if v > se[f]:
                se[f] = v
        de = self.dseen[e]
        for i, v in enumerate(d):
            if v > de[i]:
                de[i] = v

    def _wait_event(self, e, ev):
        if ev[0] == "E":
            _, f, c = ev
            if f == e:
                if e in ("pe", "sp"):
                    return
                if (not self.self_wait) or self.cnt[e] - c > 10 or self.seen[e][e] >= c:
                    return
            elif self.seen[e][f] >= c:
                return
            sem, val = self._sem(f, c - 1)
            self.h[e].wait_ge(sem, val)
            self.n_wait += 1
            if c > self.seen[e][f]:
                self.seen[e][f] = c
            self._merge(e, self.clk[f][c])
        else:
            _, i, v, snap = ev
            if self.dseen[e][i] >= v:
                return
            self.h[e].wait_ge(self.dsem[i], v)
            self.n_wait += 1
            self.dseen[e][i] = v
            self._merge(e, snap)

    def _deps(self, e, reads, writes):
        evs = []
        for k in reads:
            w = self.kw.get(k)
            if w is not None:
                evs.append(w)
        for k in writes:
            w = self.kw.get(k)
            if w is not None:
                evs.append(w)
            for r in self.kr.get(k, ()):
                if r[0] == "E" and r[1] == e:
                    continue
                evs.append(r)
        for ev in evs:
            self._wait_event(e, ev)

    def _record(self, ev, reads, writes):
        for k in writes:
            self.kw[k] = ev
            self.kr[k] = []
        for k in reads:
            lst = self.kr.setdefault(k, [])
            if ev[0] == "E":
                lst[:] = [r for r in lst if not (r[0] == "E" and r[1] == ev[1])]
            lst.append(ev)

    def op(self, e, name, *args, reads=(), writes=(), **kw):
        self._deps(e, reads, writes)
        ins = getattr(self.h[e], name)(*args, **kw)
        idx = self.cnt[e]
        sem, _ = self._sem(e, idx)
        ins.then_inc(sem, 1)
        self.cnt[e] = idx + 1
        self.n_inst += 1
        self.clk[e].append(self._snap(e))
        ev = ("E", e, idx + 1)
        self._record(ev, reads, writes)
        return ev

    def dma(self, q, out, in_, reads=(), writes=(), **kw):
        self._deps(q, reads, writes)
        i = self.drr[q]
        lo, hi = self.qrange[q]
        self.drr[q] = lo + (i + 1 - lo) % (hi - lo)
        if self.dval[i] > 0 and self.dseen[q][i] < self.dval[i]:
            self.h[q].wait_ge(self.dsem[i], self.dval[i])
            self.n_wait += 1
            self.dseen[q][i] = self.dval[i]
            self._merge(q, self.dclk[i])
        ins = self.h[q].dma_start(out=out, in_=in_, **kw)
        self.dval[i] += 16
        ins.then_inc(self.dsem[i], 16)
        self.n_inst += 1
        snap = self._snap(q)
        self.dclk[i] = snap
        ev = ("D", i, self.dval[i], snap)
        self._record(ev, reads, writes)
        return ev

    def barrier(self):
        for e in ENGS:
            for f in ENGS:
                if f != e and self.cnt[f] > 0:
                    self._wait_event(e, ("E", f, self.cnt[f]))
            for i in range(self.nd):
                if self.dval[i] > 0:
                    self._wait_event(e, ("D", i, self.dval[i], self.dclk[i]))
        self.kw.clear()
        self.kr.clear()


class Rec:
    def __init__(self, si):
        self.si = si
        self.ops = []

    def _k(self, ks):
        return [(self.si, k) for k in ks]

    def op(self, e, name, *a, reads=(), writes=(), **kw):
        self.ops.append((0, e, name, a, self._k(reads), self._k(writes), kw))

    def dma(self, q, out, in_, reads=(), writes=(), **kw):
        self.ops.append((1, q, out, in_, self._k(reads), self._k(writes), kw))


def emit_interleaved(S, recs):
    n = max(len(r.ops) for r in recs)
    for i in range(n):
        for r in recs:
            if i < len(r.ops):
                o = r.ops[i]
                if o[0] == 0:
                    S.op(o[1], o[2], *o[3], reads=o[4], writes=o[5], **o[6])
                else:
                    S.dma(o[1], o[2], o[3], reads=o[4], writes=o[5], **o[6])


class Builder:
    def __init__(self, debug=False, n_layers=DEPTH, stages=("mla", "diff", "hgrn", "hyena")):
        self.debug = debug
        self.n_layers = n_layers
        self.stages = stages
        self.nc = bass.Bass("TRN2", target_bir_lowering=False)
        self.S = Sched(self.nc)
        self.inputs = {}
        self.outputs = {}
        self.build()

    def din(self, name, shape, dt=F32):
        t = self.nc.dram_tensor(name, list(shape), dt, kind="ExternalInput").ap()
        self.inputs[name] = t
        return t

    def dout(self, name, shape, dt=F32):
        t = self.nc.dram_tensor(name, list(shape), dt, kind="ExternalOutput").ap()
        self.outputs[name] = t
        return t

    def dscr(self, name, shape, dt=F32):
        kind = "ExternalOutput" if self.debug else "Internal"
        t = self.nc.dram_tensor(name, list(shape), dt, kind=kind).ap()
        if self.debug:
            self.outputs[name] = t
        return t

    def sb(self, es, name, shape, dt=F32):
        self._uid = getattr(self, "_uid", 0) + 1
        return es.enter_context(self.nc.sbuf_tensor(f"{name}_{self._uid}", list(shape), dt))

    def build(self):
        nc, S = self.nc, self.S
        self.x_in = self.din("x", [T, D])
        self.cvecT = self.din("cvecT", [128, 8])
        self.norm_g = self.din("norm_g", [DEPTH, D])
        self.w_mod = self.din("w_mod", [DEPTH, D, 3 * D])
        self.b_mod = self.din("b_mod", [DEPTH, 3 * D])
        self.w_in = self.din("w_in", [DEPTH, D, IN_COLS])
        self.w_out = self.din("w_out", [DEPTH, D, D])
        self.mla_q_norm_g = self.din("mla_q_norm_g", [DEPTH, 256])
        self.mla_w_uq = self.din("mla_w_uq", [DEPTH, 256, 384])
        self.mla_kv_norm_g = self.din("mla_kv_norm_g", [DEPTH, 128])
        self.mla_w_ukv = self.din("mla_w_ukv", [DEPTH, 128, 512])
        self.mla_nope_g = self.din("mla_nope_g", [DEPTH, 2, 64])
        self.mla_rope_g = self.din("mla_rope_g", [DEPTH, 2, 32])
        self.diff_qk_g = self.din("diff_qk_g", [DEPTH, 2, 32])
        self.diff_lambda = self.din("diff_lambda", [DEPTH, 128])
        self.diff_subln_g = self.din("diff_subln_gT", [DEPTH, 64, 1])
        self.c_ckv = self.din("c_ckv", [DEPTH, PAST, 128])
        self.c_krope = self.din("c_krope", [DEPTH, PAST, 32])
        self.c_dk = self.din("c_dk", [DEPTH, PAST, 256])
        self.c_dv = self.din("c_dv", [DEPTH, PAST, 256])
        self.ropeC = self.din("ropeC", [T, 256])
        self.ropeS = self.din("ropeS", [T, 256])
        self.qaug = self.din("qaug", [T, 17])
        self.kaug = self.din("kaug", [NKEY, 17])
        self.hy_w1 = self.din("hy_w1", [DEPTH, 33, 64])
        self.hy_w2 = self.din("hy_w2", [DEPTH, 64, 64])
        self.hy_w3 = self.din("hy_w3", [DEPTH, 64, 1024])
        self.hy_cols = self.din("hy_cols", [DEPTH, 64, 4])
        self.hy_convT = self.din("hy_convT", [DEPTH, 768, 4])
        self.hy_biasT = self.din("hy_biasT", [DEPTH, 2, 256, 1])
        self.negdelta = self.din("negdelta", [256, 1])
        self.featsT = self.din("featsT", [33, 8192])
        self.dmask = self.din("dmask", [2, 8192])
        self.bmask = self.din("bmask", [2, 128, 16])
        self.F1z = self.din("F1z", [32, 128])
        self.F1f = self.din("F1f", [64, 128])
        self.Gm = self.din("Gm", [64, 128, 384], BF16)
        self.Em = self.din("Em", [128, 512], BF16)
        self.Qm = self.din("Qm", [64, 128 * 64], BF16)
        self.s_filtT = self.dscr("s_filtT", [2, 256, 8192])
        self.s_H = self.dscr("s_H", [4, 128, 2 * 64 * 128])
        self.s_z = self.dscr("s_z", [4, 128, T])
        self.lbT = self.din("lbT", [256, 8])
        self.hgrn_out_g = self.din("hgrn_out_gT", [DEPTH, 64, 1])
        self.s0 = self.din("s0", [DEPTH, 2, 4, 64, 64])
        self.carry = self.din("carry", [2, 128, 64])
        self.maskT = self.din("maskT", [64, 256])
        self.o_state = self.dout("o_state", [DEPTH, 16, 2, 4, 64, 64])
        self.y = self.dout("y", [T, D])
        self.o_dv = self.dout("o_dv", [DEPTH, T, 256])
        self.o_ckv = self.dout("o_ckv", [DEPTH, T, 128])
        self.o_krope = self.dout("o_krope", [DEPTH, T, 32])
        self.o_dk = self.dout("o_dk", [DEPTH, T, 256])
        self.s_cq = self.dscr("s_cq", [T, 416])
        self.s_dqk = self.dscr("s_dqk", [T, 512])
        self.s_vd = self.dscr("s_vd", [NKEY, 4 * 65], BF16)
        self.s_hv = self.dscr("s_hv", [T, 256], BF16)
        self.s_gT = self.dscr("s_gT", [D, T], BF16)
        self.s_hT = self.dscr("s_hT", [768, T])
        self.s_huT = self.dscr("s_huT", [768, T])
        self.s_yT = self.dscr("s_yT", [D, T], BF16)

        with ExitStack() as es:
            self.ident_bf = self.sb(es, "ident_bf", [128, 128], BF16)
            self.ident_f = self.sb(es, "ident_f", [128, 128], F32)
            self.ones_f = self.sb(es, "ones_f", [128, 128], F32)
            self.ident_in = self.din("ident", [128, 128])
            self.modA = self.sb(es, "modA", [128, D], F32)
            self.modB = self.sb(es, "modB", [128, D], F32)
            self.gate_b = self.sb(es, "gate_b", [128, D], F32)
            self.crep = self.sb(es, "crep", [128, 8, 128], F32)
            self.psum = [es.enter_context(nc.psum_tensor(f"ps{i}", [128, 512], F32)) for i in range(8)]

            S.dma("sp", self.ident_f[:], self.ident_in[:, :], writes=["ident_f"])
            S.op("dve", "tensor_copy", self.ident_bf[:], self.ident_f[:], reads=["ident_f"], writes=["ident_bf"])
            S.op("dve", "memset", self.ones_f[:], 1.0, writes=["ones_f"])
            with ExitStack() as es0:
                cv = self.sb(es0, "cv", [128, 8], F32)
                sc = self.sb(es0, "sc", [128, 8], F32)
                S.dma("sp", cv[:], self.cvecT[:, :], writes=["cv"])
                S.op("act", "activation", out=sc[:], in_=cv[:], func=AF.Silu, reads=["cv"], writes=["sc"])
                for k in range(8):
                    S.op("dve", "tensor_scalar", self.crep[:, k, :], self.ones_f[:], sc[:, k:k + 1], None,
                         ALU.mult, reads=["sc", "ones_f"], writes=["crep"])
                S.barrier()

            for l in range(self.n_layers):
                self.layer(l)
            S.barrier()

    def layer(self, l):
        S = self.S
        self.phase_mod(l)
        S.barrier()
        self.phase_p1(l)
        S.barrier()
        if "mla" in self.stages:
            self.phase_mla(l)
            S.barrier()
        if "diff" in self.stages:
            self.phase_diff(l)
            S.barrier()
        if "hgrn" in self.stages:
            self.phase_hgrn(l)
            S.barrier()
        if "hyena" in self.stages:
            self.phase_hyena(l)
            S.barrier()
        if "stub" in self.stages:
            for c in range(8):
                S.dma("sp", self.s_yT[c * 128:(c + 1) * 128, :], self.s_gT[c * 128:(c + 1) * 128, :])
            S.barrier()
        self.phase_po(l)
        S.barrier()

    def phase_mod(self, l):
        nc, S = self.nc, self.S
        with ExitStack() as es:
            wm = self.sb(es, "wm", [128, 8, 3 * D], F32)
            bm = self.sb(es, "bm", [128, 3 * D], F32)
            gb = self.sb(es, "gb", [128, D], F32)
            for k in range(8):
                S.dma("sp" if k % 2 == 0 else "pool", wm[:, k, :], self.w_mod[l, k * 128:(k + 1) * 128, :],
                      writes=[("wm", k)])
            S.dma("sp", bm[:], self.b_mod[l:l + 1, :].partition_broadcast(128) if False else
                  self.b_mod[l, :].partition_broadcast(128), writes=["bm"])
            S.dma("sp", gb[:], self.norm_g[l, :].partition_broadcast(128), writes=["gb"])
            for nt in range(6):
                ps = self.psum[nt % 2]
                for k in range(8):
                    S.op("pe", "matmul", ps[:, :], self.crep[:, k, :], wm[:, k, nt * 512:(nt + 1) * 512],
                         start=(k == 0), stop=(k == 7), reads=["crep", ("wm", k)], writes=[("ps", nt % 2)])
                sl = slice((nt % 2) * 512, (nt % 2) * 512 + 512)
                bsl = slice(nt * 512, nt * 512 + 512)
                if nt < 2:
                    S.op("dve", "tensor_tensor", self.modB[:, sl], ps[:, :], bm[:, bsl], ALU.add,
                         reads=[("ps", nt % 2), "bm"], writes=["modB"])
                elif nt < 4:
                    S.op("dve", "scalar_tensor_tensor", self.modA[:, sl], ps[:, :], 1.0, bm[:, bsl], ALU.add, ALU.add,
                         reads=[("ps", nt % 2), "bm"], writes=["modA"])
                    S.op("dve", "tensor_tensor", self.modA[:, sl], self.modA[:, sl], gb[:, sl], ALU.mult,
                         reads=["modA", "gb"], writes=["modA"])
                else:
                    S.op("dve", "tensor_tensor", self.gate_b[:, sl], ps[:, :], bm[:, bsl], ALU.add,
                         reads=[("ps", nt % 2), "bm"], writes=["gate_b"])

    def phase_p1(self, l):
        nc, S = self.nc, self.S
        xsrc = self.x_in if l == 0 else self.y
        with ExitStack() as es:
            w16 = self.sb(es, "w16", [128, 8, IN_COLS], BF16)
            xin = [self.sb(es, f"xin{i}", [128, 4, D], F32) for i in range(2)]
            hT = [self.sb(es, f"hT{i}", [128, 8, 512], BF16) for i in range(2)]
            hb = self.sb(es, "hb", [128, 4, D], BF16)
            t32 = self.sb(es, "t32", [128, D], F32)
            junk = self.sb(es, "junk", [128, D], BF16)
            ss = self.sb(es, "ss", [128, 8], F32)
            tm_st = [self.sb(es, f"tmst{i}", [128, 928], F32) for i in range(2)]
            dv_st = [self.sb(es, f"dvst{i}", [128, 256], F32) for i in range(2)]
            hv_st = [self.sb(es, f"hvst{i}", [128, 256], BF16) for i in range(2)]
            g_st = [self.sb(es, f"gst{i}", [128, 512], BF16) for i in range(2)]
            f_st = [self.sb(es, f"fst{i}", [128, 512], F32) for i in range(3)]
            vd_st = [self.sb(es, f"vdst{i}", [128, 4, 65], BF16) for i in range(2)]

            for k in range(8):
                for hh in range(2):
                    S.dma("pool", w16[:, k, hh * 2000:(hh + 1) * 2000],
                          self.w_in[l, k * 128:(k + 1) * 128, hh * 2000:(hh + 1) * 2000], writes=[("w16", k)])
            for i in range(2):
                S.op("pool", "memset", vd_st[i][:], 1.0, writes=[("vdst", i)])

            def prep(st):
                b = st % 2
                S.dma("sp", xin[b][:], xsrc[st * 512:(st + 1) * 512, :].rearrange("(j p) d -> p j d", p=128),
                      writes=[("xin", b)])
                kd = os.environ.get("KDBG", "")
                for j in range(0 if "nostat" in kd else 4):
                    S.op("act", "activation", out=junk[:], in_=xin[b][:, j, :], func=AF.Square,
                         accum_out=ss[:, j:j + 1], reads=[("xin", b)], writes=["junk", ("ss", j)])
                if "nostat" not in kd:
                    S.op("dve", "tensor_scalar", ss[:, 4:8], ss[:, 0:4], 1.0 / D, EPS, ALU.mult, ALU.add,
                         reads=[("ss", j) for j in range(4)], writes=["ms"])
                    S.op("act", "activation", out=ss[:, 4:8], in_=ss[:, 4:8], func=AF.Sqrt, reads=["ms"], writes=["ms"])
                    S.op("dve", "reciprocal", ss[:, 4:8], ss[:, 4:8], reads=["ms"], writes=["ms"])
                else:
                    S.op("dve", "memset", ss[:], 1.0, writes=["ms"])
                for j in range(4):
                    S.op("dve", "scalar_tensor_tensor", t32[:], xin[b][:, j, :], ss[:, 4 + j:5 + j], self.modA[:],
                         ALU.mult, ALU.mult, reads=[("xin", b), "ms"], writes=["t32"])
                    S.op("pool", "tensor_tensor", hb[:, j, :], t32[:], self.modB[:], ALU.add,
                         reads=["t32"], writes=[("hb", j)])
                for j in range(4):
                    pi = j % 2
                    pst = self.psum[pi][:, :].bitcast(BF16)
                    for k in range(8):
                        S.op("pe", "transpose", pst[:, k * 128:(k + 1) * 128], hb[:, j, k * 128:(k + 1) * 128],
                             self.ident_bf[:], reads=[("hb", j)], writes=[("ps", pi)])
                    eng = "act" if (j % 2 == 0 or "tract" in os.environ.get("KDBG", "")) else "dve"
                    src = pst.rearrange("p (k t) -> p k t", k=8)
                    if eng == "act":
                        S.op("act", "activation", out=hT[b][:, :, j * 128:(j + 1) * 128], in_=src, func=AF.Copy,
                             reads=[("ps", pi)], writes=[("hT", b)])
                    else:
                        S.op("dve", "tensor_copy", hT[b][:, :, j * 128:(j + 1) * 128], src,
                             reads=[("ps", pi)], writes=[("hT", b)])

            fm_chunks = ([("g", 0, C_GA), ("g", 1, C_GA + 128), ("g", 2, C_GB), ("g", 3, C_GB + 128),
                          ("g", 4, C_GC), ("g", 5, C_GC + 128), ("g", 6, C_GD), ("g", 7, C_GD + 128)]
                         + [("h", i, C_HQ + 128 * i) for i in range(6)]
                         + [("u", i, C_HU + 128 * i) for i in range(6)])

            def mm(st):
                b = st % 2
                cnt = 0
                for j in range(0 if "notm" in os.environ.get("KDBG", "") else 4):
                    tok0 = st * 512 + j * 128
                    lhs = lambda k: hT[b][:, k, j * 128:(j + 1) * 128]
                    sb_i = j % 2
                    ps = self.psum[2]
                    for k in range(0 if "noa" in os.environ.get("KDBG", "") else 8):
                        S.op("pe", "matmul", ps[:, 0:416], lhs(k), w16[:, k, 0:416], start=(k == 0), stop=(k == 7),
                             reads=[("hT", b), ("w16", k)], writes=[("ps", 2)])
                    S.op("act", "activation", out=tm_st[sb_i][:, 0:416], in_=ps[:, 0:416], func=AF.Copy,
                         reads=[("ps", 2)], writes=[("tmst", sb_i, 0)])
                    S.dma("sp", self.s_cq[tok0:tok0 + 128, :], tm_st[sb_i][:, 0:416], reads=[("tmst", sb_i, 0)])
                    ps = self.psum[3]
                    for k in range(0 if "nob" in os.environ.get("KDBG", "") else 8):
                        S.op("pe", "matmul", ps[:, :], lhs(k), w16[:, k, C_DQ:C_DQ + 512], start=(k == 0), stop=(k == 7),
                             reads=[("hT", b), ("w16", k)], writes=[("ps", 3)])
                    S.op("dve", "tensor_copy", tm_st[sb_i][:, 416:928], ps[:, :],
                         reads=[("ps", 3)], writes=[("tmst", sb_i, 1)])
                    S.dma("sp", self.s_dqk[tok0:tok0 + 128, :], tm_st[sb_i][:, 416:928], reads=[("tmst", sb_i, 1)])
                    ps = self.psum[2]
                    if "noc" in os.environ.get("KDBG", ""):
                        continue
                    for k in range(8):
                        S.op("pe", "matmul", ps[:, 0:256], lhs(k), w16[:, k, C_DV:C_DV + 256], start=(k == 0), stop=(k == 7),
                             reads=[("hT", b), ("w16", k)], writes=[("ps", 2)])
                    for k in range(8):
                        S.op("pe", "matmul", ps[:, 256:512], lhs(k), w16[:, k, C_HI:C_HI + 256], start=(k == 0), stop=(k == 7),
                             reads=[("hT", b), ("w16", k)], writes=[("ps", 2)])
                    kd = os.environ.get("KDBG", "")
                    if "c1" not in kd:
                        S.op("act", "activation", out=dv_st[sb_i][:], in_=ps[:, 0:256], func=AF.Copy,
                             reads=[("ps", 2)], writes=[("dvst", sb_i)])
                        S.dma("sp", self.o_dv[l, tok0:tok0 + 128, :], dv_st[sb_i][:], reads=[("dvst", sb_i)])
                    if "c2" not in kd:
                        S.op("act", "activation", out=vd_st[sb_i][:, :, 0:64], in_=ps[:, 0:256].rearrange("p (h e) -> p h e", h=4),
                             func=AF.Copy, reads=[("ps", 2)], writes=[("vdst", sb_i)])
                        S.dma("sp", self.s_vd[tok0:tok0 + 128, :], vd_st[sb_i][:].rearrange("p h e -> p (h e)"),
                              reads=[("vdst", sb_i)])
                    if "c3" not in kd:
                        if "hv32" in kd:
                            S.op("dve", "tensor_copy", dv_st[sb_i][:], ps[:, 256:512], reads=[("ps", 2)], writes=[("dvst", sb_i)])
                            S.op("act", "activation", out=hv_st[sb_i][:], in_=dv_st[sb_i][:], func=AF.Copy, reads=[("dvst", sb_i)], writes=[("hvst", sb_i)])
                        elif "hvdve" not in kd:
                            S.op("act", "activation", out=hv_st[sb_i][:], in_=ps[:, 256:512], func=AF.Copy, reads=[("ps", 2)], writes=[("hvst", sb_i)])
                        else:
                            S.op("dve", "tensor_copy", hv_st[sb_i][:], ps[:, 256:512], reads=[("ps", 2)], writes=[("hvst", sb_i)])
                        S.dma("sp", self.s_hv[tok0:tok0 + 128, :], hv_st[sb_i][:], reads=[("hvst", sb_i)])
                for ci, (kind, idx, col) in enumerate(fm_chunks):
                    if "nofm" in os.environ.get("KDBG", ""):
                        break
                    pi = 4 + ci % 3
                    ps = self.psum[pi]
                    for k in range(8):
                        S.op("pe", "matmul", ps[:, :], w16[:, k, col:col + 128], hT[b][:, k, :], start=(k == 0), stop=(k == 7),
                             reads=[("hT", b), ("w16", k)], writes=[("ps", pi)])
                    tsl = slice(st * 512, (st + 1) * 512)
                    if kind == "g":
                        gi = ci % 2
                        S.op("act", "activation", out=g_st[gi][:], in_=ps[:, :], func=AF.Silu,
                             reads=[("ps", pi)], writes=[("gst", gi)])
                        S.dma("sp", self.s_gT[idx * 128:(idx + 1) * 128, tsl], g_st[gi][:], reads=[("gst", gi)])
                    else:
                        fi = ci % 3
                        if ci % 2 == 0:
                            S.op("dve", "tensor_copy", f_st[fi][:], ps[:, :], reads=[("ps", pi)], writes=[("fst", fi)])
                        else:
                            S.op("act", "activation", out=f_st[fi][:], in_=ps[:, :], func=AF.Copy,
                                 reads=[("ps", pi)], writes=[("fst", fi)])
                        dst = self.s_hT if kind == "h" else self.s_huT
                        S.dma("sp", dst[idx * 128:(idx + 1) * 128, tsl], f_st[fi][:], reads=[("fst", fi)])

            import os
            dbg = os.environ.get("KDBG", "")
            nst = int(os.environ.get("KNST", NST))
            if "nomm" in dbg:
                mm = lambda st: None
            prep(0)
            for st in range(nst):
                if st + 1 < nst:
                    prep(st + 1)
                mm(st)

    def rms_tm(self, src3, dst3, G, W, gain_b, sq, ssb, rk, wk, gain_eng="dve"):
        S = self.S
        sqv = sq[:, 0:G * W].rearrange("p (g w) -> p g w", g=G)
        S.op("dve", "tensor_tensor", sqv, src3, src3, ALU.mult, reads=rk, writes=["sq"])
        S.op("dve", "tensor_reduce", ssb[:, 0:G], sqv, AX.X, ALU.add, reads=["sq"], writes=["ssb"])
        S.op("dve", "tensor_scalar", ssb[:, G:2 * G], ssb[:, 0:G], 1.0 / W, EPS, ALU.mult, ALU.add,
             reads=["ssb"], writes=["ssb2"])
        S.op("act", "activation", out=ssb[:, G:2 * G], in_=ssb[:, G:2 * G], func=AF.Sqrt, reads=["ssb2"], writes=["ssb2"])
        S.op("dve", "reciprocal", ssb[:, G:2 * G], ssb[:, G:2 * G], reads=["ssb2"], writes=["ssb2"])
        if gain_b is None:
            S.op("dve", "tensor_tensor", dst3, src3, ssb[:, G:2 * G].unsqueeze(2).to_broadcast([128, G, W]), ALU.mult,
                 reads=list(rk) + ["ssb2"], writes=wk)
        else:
            S.op("dve", "tensor_tensor", sqv, src3, ssb[:, G:2 * G].unsqueeze(2).to_broadcast([128, G, W]), ALU.mult,
                 reads=list(rk) + ["ssb2"], writes=["sq"])
            S.op(gain_eng, "tensor_tensor", dst3, sqv, gain_b.unsqueeze(1).to_broadcast([128, G, W]), ALU.mult,
                 reads=["sq"], writes=wk)

    def rope_tm(self, x3, out3, G, rc, rs, t1, rk, wk):
        S = self.S
        xv = x3.rearrange("p g (r h e) -> p (g r) h e", r=2, h=2)
        sv = rs[:, 0:G * 32].rearrange("p (g h e) -> p g h e", h=2, e=8)
        tv = t1[:, 0:G * 32].rearrange("p (g h e) -> p g h e", h=2, e=8)
        S.op("dve", "tensor_tensor", tv[:, :, 0, :], xv[:, :, 1, :], sv[:, :, 0, :], ALU.mult, reads=list(rk) + ["rs"], writes=["t1"])
        S.op("dve", "tensor_tensor", tv[:, :, 1, :], xv[:, :, 0, :], sv[:, :, 1, :], ALU.mult, reads=list(rk) + ["rs"], writes=["t1"])
        x2 = x3.rearrange("p g w -> p (g w)") if False else x3
        S.op("dve", "tensor_tensor", x3, x3, rc[:, 0:G * 32].rearrange("p (g w) -> p g w", g=G), ALU.mult,
             reads=list(rk) + ["rc", "t1"], writes=rk)
        S.op("pool", "tensor_tensor", out3, x3, t1[:, 0:G * 32].rearrange("p (g w) -> p g w", g=G), ALU.add,
             reads=list(rk) + ["t1"], writes=wk)

    def bcast_load(self, q, tile_ap, dram_ap, key):
        self.S.dma(q, tile_ap, dram_ap.partition_broadcast(128), writes=[key])

    def attend(self, n_groups, Krows, QT, KT, Vt, v_of_g, scale, finalize, order=None):
        S = self.S
        NKC = NKEY // 128
        if order is None:
            order = [(g, qs) for g in range(n_groups) for qs in range(NST)]
        seq = [(oi, g, qs, kc) for oi, (g, qs) in enumerate(order) for kc in range(NKC)]
        banks = [0, 1, 7]
        nb = len(banks)

        def emit_s(idx):
            oi, g, qs, kc = seq[idx]
            b = idx % nb
            S.op("pe", "matmul", self.psum[banks[b]][:, :], KT(g, kc), QT(g, qs), start=True, stop=True,
                 reads=["QT", "KT"], writes=[("ps", banks[b])])

        emit_s(0)
        if len(seq) > 1:
            emit_s(1)
        for idx, (oi, g, qs, kc) in enumerate(seq):
            b = idx % nb
            pkey = 2 + (oi % 2)
            pO = self.psum[pkey]
            if idx + 2 < len(seq):
                emit_s(idx + 2)
            S.op("act", "activation", out=self.pt[b][:], in_=self.psum[banks[b]][:, :], func=AF.Exp, scale=scale,
                 reads=[("ps", banks[b])], writes=[("pt", b)])
            S.op("pe", "matmul", pO[0:65, :], Vt(v_of_g(g), kc), self.pt[b][:], start=(kc == 0), stop=(kc == NKC - 1),
                 reads=[("pt", b), "V"], writes=[("ps", pkey)])
            if kc == NKC - 1:
                finalize(g, qs, pO, pkey)

    def phase_mla(self, l):
        nc, S = self.nc, self.S
        with ExitStack() as es:
            QT = self.sb(es, "QT", [128, 4, T], BF16)
            KT = self.sb(es, "KT", [128, 4, NKEY], BF16)
            Va = self.sb(es, "Va", [128, NKEY // 128, 4, 65], BF16)
            self.pt = [self.sb(es, f"pt{i}", [128, 512], BF16) for i in range(3)]
            wuq32 = self.sb(es, "wuq32", [128, 2, 384], F32)
            wuq = self.sb(es, "wuq", [128, 2, 384], BF16)
            wukv32 = self.sb(es, "wukv32", [128, 512], F32)
            wukv = self.sb(es, "wukv", [128, 512], BF16)
            g_q = self.sb(es, "g_q", [128, 256], F32)
            g_kv = self.sb(es, "g_kv", [128, 128], F32)
            g_nope = self.sb(es, "g_nope", [128, 128], F32)
            g_rope = self.sb(es, "g_rope", [128, 64], F32)
            S.op("pool", "memset", Va[:], 1.0, writes=["V"])
            S.dma("sp", wuq32[:], self.mla_w_uq[l].rearrange("(c p) n -> p c n", p=128), writes=["wuq32"])
            S.op("dve", "tensor_copy", wuq[:], wuq32[:], reads=["wuq32"], writes=["wuq"])
            S.dma("sp", wukv32[:], self.mla_w_ukv[l], writes=["wukv32"])
            S.op("dve", "tensor_copy", wukv[:], wukv32[:], reads=["wukv32"], writes=["wukv"])
            self.bcast_load("sp", g_q[:], self.mla_q_norm_g[l, :], "g_q")
            self.bcast_load("sp", g_kv[:], self.mla_kv_norm_g[l, :], "g_kv")
            self.bcast_load("sp", g_nope[:], self.mla_nope_g[l].rearrange("a b -> (a b)"), "g_nope")
            self.bcast_load("sp", g_rope[:], self.mla_rope_g[l].rearrange("a b -> (a b)"), "g_rope")
            with ExitStack() as es2:
                cqt = [self.sb(es2, f"cqt{i}", [128, 416], F32) for i in range(2)]
                rc = [self.sb(es2, f"rc{i}", [128, 256], F32) for i in range(2)]
                rs = [self.sb(es2, f"rs{i}", [128, 256], F32) for i in range(2)]
                qa = [self.sb(es2, f"qa{i}", [128, 17], F32) for i in range(2)]
                ka = [self.sb(es2, f"ka{i}", [128, 17], F32) for i in range(2)]
                ckn = [self.sb(es2, f"ckn{i}", [128, 128], F32) for i in range(2)]
                krn = [self.sb(es2, f"krn{i}", [128, 32], F32) for i in range(2)]
                BS = []
                for si_ in range(2):
                    BS.append(dict(
                        sq=self.sb(es2, "sq", [128, 256], F32), ssb=self.sb(es2, "ssb", [128, 16], F32),
                        t1=self.sb(es2, "t1", [128, 256], F32), cqb=self.sb(es2, "cqb", [128, 256], BF16),
                        cqT=self.sb(es2, "cqT", [128, 2, 128], BF16), q32=self.sb(es2, "q32", [128, 384], F32),
                        qr=self.sb(es2, "qr", [128, 4, 32], F32), Qst=self.sb(es2, "Qst", [128, 4, 113], BF16),
                        Kst=self.sb(es2, "Kst", [128, 4, 113], BF16), ckb=self.sb(es2, "ckb", [128, 128], BF16),
                        ckT=self.sb(es2, "ckT", [128, 128], BF16), kv32=self.sb(es2, "kv32", [128, 512], F32),
                        krr=self.sb(es2, "krr", [128, 32], F32)))
                realS = self.S
                realS.barrier()

                def transposes(src_tile, n, width, dst_ap, rkey, wkey, pi):
                    S = self.S
                    pst = self.psum[pi][:, :].bitcast(BF16)
                    for h in range(n):
                        S.op("pe", "transpose", pst[0:width, h * 128:(h + 1) * 128], src_tile(h), self.ident_bf[:],
                             reads=[rkey], writes=[("ps", pi)])
                    S.op("act", "activation", out=dst_ap, in_=pst[0:width, 0:n * 128].rearrange("p (h t) -> p h t", h=n),
                         func=AF.Copy, reads=[("ps", pi)], writes=[wkey])

                def mla_tile(t, S):
                    b = t % 2
                    PB = 4 - 4 * b
                    d_ = BS[b]
                    sq, ssb, t1, cqb, cqT, q32, qr, Qst, Kst, ckb, ckT, kv32, krr = (
                        d_["sq"], d_["ssb"], d_["t1"], d_["cqb"], d_["cqT"], d_["q32"], d_["qr"], d_["Qst"], d_["Kst"],
                        d_["ckb"], d_["ckT"], d_["kv32"], d_["krr"])
                    q32v = q32[:].rearrange("p (h w) -> p h w", h=4)
                    kv32v = kv32[:].rearrange("p (h w) -> p h w", h=4)
                    tok0 = t * 128
                    ctx = t >= NT
                    S.dma("sp", ka[b][:], self.kaug[tok0:tok0 + 128, :], writes=[("ka", b)])
                    if not ctx:
                        S.dma("sp", cqt[b][:], self.s_cq[tok0:tok0 + 128, :], writes=[("cqt", b)])
                        S.dma("sp", rc[b][:], self.ropeC[tok0:tok0 + 128, :], writes=["rc"])
                        S.dma("sp", rs[b][:], self.ropeS[tok0:tok0 + 128, :], writes=["rs"])
                        S.dma("sp", qa[b][:], self.qaug[tok0:tok0 + 128, :], writes=[("qa", b)])
                        self.rms_tm(cqt[b][:, 0:256].unsqueeze(1), cqb[:].unsqueeze(1), 1, 256, g_q[:], sq, ssb,
                                    [("cqt", b)], ["cqb"], gain_eng="pool")
                        transposes(lambda c: cqb[:, c * 128:(c + 1) * 128], 2, 128, cqT[:, :, :], "cqb", "cqT", PB + 0)
                        for c in range(2):
                            S.op("pe", "matmul", self.psum[PB + 1][:, 0:384], cqT[:, c, :], wuq[:, c, :], start=(c == 0), stop=(c == 1),
                                 reads=["cqT", "wuq"], writes=[("ps", PB + 1)])
                        S.op("act", "activation", out=q32[:], in_=self.psum[PB + 1][:, 0:384], func=AF.Copy,
                             reads=[("ps", PB + 1)], writes=["q32"])
                        self.rms_tm(q32v[:, :, 0:64], Qst[:, :, 0:64], 4, 64, g_nope[:, 0:64], sq, ssb, ["q32"], ["Qst"],
                                    gain_eng="pool")
                        self.rms_tm(q32v[:, :, 64:96], qr[:], 4, 32, g_rope[:, 0:32], sq, ssb, ["q32"], ["qr"])
                        self.rope_tm(qr[:], Qst[:, :, 64:96], 4, rc[b], rs[b], t1, ["qr"], ["Qst"])
                        S.op("pool", "tensor_copy", Qst[:, :, 96:113], qa[b][:].unsqueeze(1).to_broadcast([128, 4, 17]),
                             reads=[("qa", b)], writes=["Qst"])
                        transposes(lambda h: Qst[:, h, :], 4, 113, QT[0:113, :, tok0:tok0 + 128], "Qst", "QT", PB + 0)
                        self.rms_tm(cqt[b][:, 256:384].unsqueeze(1), ckn[b][:].unsqueeze(1), 1, 128, g_kv[:], sq, ssb,
                                    [("cqt", b)], [("ckn", b)])
                        S.dma("sp", self.o_ckv[l, tok0:tok0 + 128, :], ckn[b][:], reads=[("ckn", b)])
                        self.rms_tm(cqt[b][:, 384:416].unsqueeze(1), krn[b][:].unsqueeze(1), 1, 32, g_rope[:, 32:64], sq, ssb,
                                    [("cqt", b)], [("krn", b)])
                        S.dma("sp", self.o_krope[l, tok0:tok0 + 128, :], krn[b][:], reads=[("krn", b)])
                        S.op("dve", "tensor_copy", krr[:], krn[b][:], reads=[("krn", b)], writes=["krr"])
                        self.rope_tm(krr[:].unsqueeze(1), krr[:].unsqueeze(1), 1, rc[b], rs[b], t1, ["krr"], ["krr"])
                    else:
                        c0 = tok0 - T
                        S.dma("sp", ckn[b][:], self.c_ckv[l, c0:c0 + 128, :], writes=[("ckn", b)])
                        S.dma("sp", krr[:], self.c_krope[l, c0:c0 + 128, :], writes=["krr"])
                    S.op("pool", "tensor_copy", ckb[:], ckn[b][:], reads=[("ckn", b)], writes=["ckb"])
                    transposes(lambda h: ckb[:], 1, 128, ckT[:].unsqueeze(1), "ckb", "ckT", PB + 2)
                    S.op("pe", "matmul", self.psum[PB + 3][:, :], ckT[:], wukv[:], start=True, stop=True,
                         reads=["ckT", "wukv"], writes=[("ps", PB + 3)])
                    S.op("act", "activation", out=kv32[:], in_=self.psum[PB + 3][:, :], func=AF.Copy, reads=[("ps", PB + 3)], writes=["kv32"])
                    self.rms_tm(kv32v[:, :, 0:64], Kst[:, :, 0:64], 4, 64, g_nope[:, 64:128], sq, ssb, ["kv32"], ["Kst"],
                                gain_eng="pool")
                    S.op("pool", "tensor_copy", Va[:, t, :, 0:64], kv32v[:, :, 64:128], reads=["kv32"], writes=["V"])
                    S.op("pool", "tensor_copy", Kst[:, :, 64:96], krr[:].unsqueeze(1).to_broadcast([128, 4, 32]),
                         reads=["krr"], writes=["Kst"])
                    S.op("pool", "tensor_copy", Kst[:, :, 96:113], ka[b][:].unsqueeze(1).to_broadcast([128, 4, 17]),
                         reads=[("ka", b)], writes=["Kst"])
                    transposes(lambda h: Kst[:, h, :], 4, 113, KT[0:113, :, tok0:tok0 + 128], "Kst", "KT", PB + 2)
                for t0 in range(0, NKEY // 128, 2):
                    recs = []
                    for tt in (t0, t0 + 1):
                        r_ = Rec(tt % 2)
                        self.S = r_
                        mla_tile(tt, r_)
                        recs.append(r_)
                    self.S = realS
                    emit_interleaved(realS, recs)
                S = realS
                S.barrier()

            with ExitStack() as es3:
                o32 = [self.sb(es3, f"o32{i}", [128, 512], F32) for i in range(2)]
                yf = self.sb(es3, "yf", [64, 512], F32)
                gt = [self.sb(es3, f"gt{i}", [64, 512], BF16) for i in range(2)]
                yb = [self.sb(es3, f"yb{i}", [64, 512], BF16) for i in range(2)]
                cnt = [0]

                def fin(h, qs, pO, pkey):
                    i = cnt[0] % 2
                    cnt[0] += 1
                    S.dma("sp", gt[i][:], self.s_gT[h * 64:(h + 1) * 64, qs * 512:(qs + 1) * 512], writes=[("gt", i)])
                    S.op("act", "activation", out=o32[i][0:65, :], in_=pO[0:65, :], func=AF.Copy,
                         reads=[("ps", pkey)], writes=[("o32", i)])
                    S.op("dve", "reciprocal", o32[i][64:65, :], o32[i][64:65, :], reads=[("o32", i)], writes=[("o32", i)])
                    S.op("pe", "matmul", self.psum[4][0:64, :], self.ones_f[64:65, 0:64], o32[i][64:65, :], start=True, stop=True,
                         reads=[("o32", i)], writes=[("ps", 4)])
                    S.op("dve", "tensor_tensor", yf[:], o32[i][0:64, :], self.psum[4][0:64, :], ALU.mult,
                         reads=[("o32", i), ("ps", 4)], writes=["yf"])
                    S.op("pool", "tensor_tensor", yb[i][:], yf[:], gt[i][:], ALU.mult, reads=["yf", ("gt", i)], writes=[("yb", i)])
                    S.dma("sp", self.s_yT[h * 64:(h + 1) * 64, qs * 512:(qs + 1) * 512], yb[i][:], reads=[("yb", i)])

                self.attend(4, 113,
                            lambda g, qs: QT[0:113, g, qs * 512:(qs + 1) * 512],
                            lambda g, kc: KT[0:113, g, kc * 128:(kc + 1) * 128],
                            lambda h, kc: Va[:, kc, h, :],
                            lambda g: g, float((64 + 32) ** -0.5), fin)

    def phase_diff(self, l):
        nc, S = self.nc, self.S
        lam_init = 0.8 - 0.6 * math.exp(-0.3 * l)
        with ExitStack() as es:
            QdT = self.sb(es, "QdT", [128, 4, T], BF16)
            KdT = self.sb(es, "KdT", [128, 4, NKEY], BF16)
            Vd = self.sb(es, "Vd", [128, NKEY // 128, 4, 65], BF16)
            self.pt = [self.sb(es, f"pt{i}", [128, 512], BF16) for i in range(3)]
            g_qk = self.sb(es, "g_qk", [128, 64], F32)
            lpb = self.sb(es, "lpb", [128, 128], F32)
            lsm = self.sb(es, "lsm", [128, 72], F32)
            neglam = self.sb(es, "neglam", [128, 1], F32)
            gsub = self.sb(es, "gsub", [64, 1], F32)
            self.bcast_load("sp", g_qk[:], self.diff_qk_g[l].rearrange("a b -> (a b)"), "g_qk")
            self.bcast_load("sp", lpb[:], self.diff_lambda[l, :], "lpb")
            S.dma("sp", gsub[:], self.diff_subln_g[l], writes=["gsub"])
            S.op("dve", "tensor_scalar", gsub[:], gsub[:], float(1.0 - lam_init), None, ALU.mult, reads=["gsub"], writes=["gsub"])
            lp4 = lpb[:].rearrange("p (a b w) -> p a b w", a=2, b=2)
            S.op("dve", "tensor_tensor", lsm[:, 0:64].rearrange("p (a w) -> p a w", a=2), lp4[:, :, 0, :], lp4[:, :, 1, :], ALU.mult,
                 reads=["lpb"], writes=["lsm"])
            S.op("dve", "tensor_reduce", lsm[:, 64:66], lsm[:, 0:64].rearrange("p (a w) -> p a w", a=2), AX.X, ALU.add,
                 reads=["lsm"], writes=["lsm2"])
            S.op("act", "activation", out=lsm[:, 66:68], in_=lsm[:, 64:66], func=AF.Exp, reads=["lsm2"], writes=["lsm3"])
            S.op("dve", "tensor_tensor", neglam[:], lsm[:, 67:68], lsm[:, 66:67], ALU.subtract, reads=["lsm3"], writes=["neglam"])
            S.op("dve", "tensor_scalar", neglam[:], neglam[:], float(-lam_init), None, ALU.add, reads=["neglam"], writes=["neglam"])
            S.dma("sp", Vd[:, 0:NT, :, :].rearrange("p c h e -> p c (h e)"),
                  self.s_vd[0:T, :].rearrange("(c p) e -> p c e", p=128), writes=["V"])
            with ExitStack() as es1:
                cv32 = self.sb(es1, "cv32", [128, 4, 256], F32)
                S.dma("sp", cv32[:], self.c_dv[l].rearrange("(c p) e -> p c e", p=128), writes=["cv32"])
                S.op("pool", "memset", Vd[:, NT:NT + 4, :, :], 1.0, reads=["V"], writes=["V"])
                for c in range(4):
                    S.op("pool", "tensor_copy", Vd[:, NT + c, :, 0:64], cv32[:, c, :].rearrange("p (h e) -> p h e", h=4),
                         reads=["cv32", "V"], writes=["V"])
                S.barrier()
            with ExitStack() as es2:
                dqk = [self.sb(es2, f"dqk{i}", [128, 512], F32) for i in range(2)]
                rc = [self.sb(es2, f"rc{i}", [128, 256], F32) for i in range(2)]
                rs = [self.sb(es2, f"rs{i}", [128, 256], F32) for i in range(2)]
                qa = [self.sb(es2, f"qa{i}", [128, 17], F32) for i in range(2)]
                ka = [self.sb(es2, f"ka{i}", [128, 17], F32) for i in range(2)]
                kn = [self.sb(es2, f"kn{i}", [128, 8, 32], F32) for i in range(2)]
                BS = []
                for si_ in range(2):
                    BS.append(dict(
                        sq=self.sb(es2, "sq", [128, 256], F32), ssb=self.sb(es2, "ssb", [128, 16], F32),
                        t1=self.sb(es2, "t1", [128, 256], F32), qn=self.sb(es2, "qn", [128, 8, 32], F32),
                        kr=self.sb(es2, "kr", [128, 8, 32], F32), Qdst=self.sb(es2, "Qdst", [128, 8, 64], BF16),
                        Kdst=self.sb(es2, "Kdst", [128, 8, 64], BF16)))
                    S.op("pool", "memset", BS[si_]["Qdst"][:], 0.0, writes=[("Qdst0", si_)])
                    S.op("pool", "memset", BS[si_]["Kdst"][:], 0.0, writes=[("Kdst0", si_)])
                realS = self.S
                realS.barrier()

                def transposes(src, dst_ap, rkey, wkey, pi):
                    S = self.S
                    pst = self.psum[pi][:, :].bitcast(BF16)
                    for c in range(4):
                        S.op("pe", "transpose", pst[:, c * 128:(c + 1) * 128],
                             src[:, 2 * c:2 * c + 2, :].rearrange("p a w -> p (a w)"), self.ident_bf[:],
                             reads=[rkey], writes=[("ps", pi)])
                    S.op("act", "activation", out=dst_ap, in_=pst[:, 0:512].rearrange("p (c t) -> p c t", c=4),
                         func=AF.Copy, reads=[("ps", pi)], writes=[wkey])

                def diff_tile(t, S):
                    b = t % 2
                    PB = 4 - 4 * b
                    d_ = BS[b]
                    sq, ssb, t1, qn, kr, Qdst, Kdst = d_["sq"], d_["ssb"], d_["t1"], d_["qn"], d_["kr"], d_["Qdst"], d_["Kdst"]
                    tok0 = t * 128
                    ctx = t >= NT
                    S.dma("sp", ka[b][:], self.kaug[tok0:tok0 + 128, :], writes=[("ka", b)])
                    if not ctx:
                        S.dma("sp", dqk[b][:], self.s_dqk[tok0:tok0 + 128, :], writes=[("dqk", b)])
                        S.dma("sp", rc[b][:], self.ropeC[tok0:tok0 + 128, :], writes=["rc"])
                        S.dma("sp", rs[b][:], self.ropeS[tok0:tok0 + 128, :], writes=["rs"])
                        S.dma("sp", qa[b][:], self.qaug[tok0:tok0 + 128, :], writes=[("qa", b)])
                        qv = dqk[b][:, 0:256].rearrange("p (g w) -> p g w", g=8)
                        kv = dqk[b][:, 256:512].rearrange("p (g w) -> p g w", g=8)
                        self.rms_tm(qv, qn[:], 8, 32, g_qk[:, 0:32], sq, ssb, [("dqk", b)], ["qn"])
                        self.rope_tm(qn[:], Qdst[:, :, 0:32], 8, rc[b], rs[b], t1, ["qn"], ["Qdst"])
                        S.op("pool", "tensor_copy", Qdst[:, :, 32:49], qa[b][:].unsqueeze(1).to_broadcast([128, 8, 17]),
                             reads=[("qa", b)], writes=["Qdst"])
                        transposes(Qdst, QdT[:, :, tok0:tok0 + 128], "Qdst", "QT", PB + 0)
                        self.rms_tm(kv, kn[b][:], 8, 32, g_qk[:, 32:64], sq, ssb, [("dqk", b)], [("kn", b)])
                        S.dma("sp", self.o_dk[l, tok0:tok0 + 128, :], kn[b][:].rearrange("p g w -> p (g w)"), reads=[("kn", b)])
                        S.op("dve", "tensor_copy", kr[:], kn[b][:], reads=[("kn", b)], writes=["kr"])
                        self.rope_tm(kr[:], Kdst[:, :, 0:32], 8, rc[b], rs[b], t1, ["kr"], ["Kdst"])
                    else:
                        c0 = tok0 - T
                        S.dma("sp", kn[b][:].rearrange("p g w -> p (g w)"), self.c_dk[l, c0:c0 + 128, :], writes=[("kn", b)])
                        S.op("pool", "tensor_copy", Kdst[:, :, 0:32], kn[b][:], reads=[("kn", b)], writes=["Kdst"])
                    S.op("pool", "tensor_copy", Kdst[:, :, 32:49], ka[b][:].unsqueeze(1).to_broadcast([128, 8, 17]),
                         reads=[("ka", b)], writes=["Kdst"])
                    transposes(Kdst, KdT[:, :, tok0:tok0 + 128], "Kdst", "KT", PB + 2)
                for t0 in range(0, NKEY // 128, 2):
                    recs = []
                    for tt in (t0, t0 + 1):
                        r_ = Rec(tt % 2)
                        self.S = r_
                        diff_tile(tt, r_)
                        recs.append(r_)
                    self.S = realS
                    emit_interleaved(realS, recs)
                S = realS
                S.barrier()

            with ExitStack() as es3:
                o32 = [self.sb(es3, f"o32{i}", [128, 512], F32) for i in range(2)]
                y1 = self.sb(es3, "y1", [64, 512], F32)
                y2 = self.sb(es3, "y2", [64, 512], F32)
                sqf = self.sb(es3, "sqf", [64, 512], F32)
                gt = [self.sb(es3, f"gt{i}", [64, 512], BF16) for i in range(2)]
                yb = [self.sb(es3, f"yb{i}", [64, 512], BF16) for i in range(2)]
                cnt = [0]

                def fin(g, qs, pO, pkey):
                    h, j = g // 2, g % 2
                    S.op("act", "activation", out=o32[j][0:65, :], in_=pO[0:65, :], func=AF.Copy,
                         reads=[("ps", pkey)], writes=[("o32", j)])
                    S.op("dve", "reciprocal", o32[j][64:65, :], o32[j][64:65, :], reads=[("o32", j)], writes=[("o32", j)])
                    if j == 0:
                        return
                    i = cnt[0] % 2
                    cnt[0] += 1
                    r0 = 256 + h * 64
                    S.dma("sp", gt[i][:], self.s_gT[r0:r0 + 64, qs * 512:(qs + 1) * 512], writes=[("gt", i)])
                    for jj in range(2):
                        S.op("pe", "matmul", self.psum[4 + jj][0:64, :], self.ones_f[64:65, 0:64], o32[jj][64:65, :],
                             start=True, stop=True, reads=[("o32", jj)], writes=[("ps", 4 + jj)])
                    S.op("dve", "tensor_tensor", y1[:], o32[0][0:64, :], self.psum[4][0:64, :], ALU.mult,
                         reads=[("o32", 0), ("ps", 4)], writes=["y1"])
                    S.op("dve", "tensor_tensor", y2[:], o32[1][0:64, :], self.psum[5][0:64, :], ALU.mult,
                         reads=[("o32", 1), ("ps", 5)], writes=["y2"])
                    S.op("dve", "scalar_tensor_tensor", y1[:], y2[:], neglam[0:64, 0:1], y1[:], ALU.mult, ALU.add,
                         reads=["y1", "y2", "neglam"], writes=["y1"])
                    S.op("pool", "tensor_tensor", sqf[:], y1[:], y1[:], ALU.mult, reads=["y1"], writes=["sqf"])
                    S.op("pe", "matmul", self.psum[6][0:64, :], self.ones_f[0:64, 0:64], sqf[:], start=True, stop=True,
                         reads=["sqf"], writes=[("ps", 6)])
                    S.op("dve", "tensor_scalar", y2[:], self.psum[6][0:64, :], 1.0 / 64, EPS, ALU.mult, ALU.add,
                         reads=[("ps", 6)], writes=["y2"])
                    S.op("act", "activation", out=y2[:], in_=y2[:], func=AF.Sqrt, reads=["y2"], writes=["y2"])
                    S.op("dve", "reciprocal", y2[:], y2[:], reads=["y2"], writes=["y2"])
                    S.op("dve", "scalar_tensor_tensor", y1[:], y1[:], gsub[0:64, 0:1], y2[:], ALU.mult, ALU.mult,
                         reads=["y1", "y2", "gsub"], writes=["y1"])
                    S.op("pool", "tensor_tensor", yb[i][:], y1[:], gt[i][:], ALU.mult, reads=["y1", ("gt", i)], writes=[("yb", i)])
                    S.dma("sp", self.s_yT[r0:r0 + 64, qs * 512:(qs + 1) * 512], yb[i][:], reads=[("yb", i)])

                order = [(2 * h + j, qs) for h in range(4) for qs in range(NST) for j in range(2)]
                self.attend(8, 64,
                            lambda g, qs: QdT[64 * (g % 2):64 * (g % 2) + 64, g // 2, qs * 512:(qs + 1) * 512],
                            lambda g, kc: KdT[64 * (g % 2):64 * (g % 2) + 64, g // 2, kc * 128:(kc + 1) * 128],
                            lambda h, kc: Vd[:, kc, h, :],
                            lambda g: g // 2, float(32 ** -0.5), fin, order=order)

    def phase_hgrn(self, l):
        nc, S = self.nc, self.S
        NCH = T // 64
        SL = 512
        with ExitStack() as es:
            rmask = self.sb(es, "rmask", [128, SL], F32)
            mT = self.sb(es, "mT", [64, 256], F32)
            og = self.sb(es, "og", [64, 1], F32)
            S.op("pool", "memset", rmask[:], 1.0, writes=["rmask"])
            S.op("pool", "memset", rmask[:].rearrange("p (c s) -> p c s", s=64)[:, :, 0:1], 0.0, writes=["rmask"])
            S.dma("sp", mT[:], self.maskT[:, :], writes=["mT"])
            S.dma("sp", og[:], self.hgrn_out_g[l], writes=["og"])
            for hp in range(2):
                with ExitStack() as es1:
                    lbl = self.sb(es1, "lbl", [128, 24], F32)
                    vsb = self.sb(es1, "vsb", [64, NCH, 128], BF16)
                    qt = [self.sb(es1, f"qt{d}", [128, T], BF16) for d in range(2)]
                    kt = [self.sb(es1, f"kt{d}", [128, T], BF16) for d in range(2)]
                    qo = [self.sb(es1, f"qo{d}", [128, T], BF16) for d in range(2)]
                    ko = [self.sb(es1, f"ko{d}", [128, T], BF16) for d in range(2)]
                    qb = [self.sb(es1, f"qb{d}", [128, T], BF16) for d in range(2)]
                    khat = [self.sb(es1, f"khat{d}", [64, NCH, 128], BF16) for d in range(2)]
                    stab = [self.sb(es1, f"stab{d}", [128, NCH, 128], BF16) for d in range(2)]
                    etot = [self.sb(es1, f"etot{d}", [128, NCH], F32) for d in range(2)]
                    car = [self.sb(es1, f"car{d}", [128, NCH], F32) for d in range(2)]
                    S.dma("sp", vsb[:], self.s_hv[:, hp * 128:(hp + 1) * 128].rearrange("(c s) e -> s c e", s=64), writes=["vsb"])
                    for d in range(2):
                        S.dma("sp", car[d][:], self.carry[d], writes=[("car", d)])
                    S.dma("sp", lbl[:, 0:8], self.lbT[hp * 128:(hp + 1) * 128, :], writes=["lbl"])
                    S.op("act", "activation", out=lbl[:, 8:16], in_=lbl[:, 0:8], func=AF.Exp, reads=["lbl"], writes=["lbe"])
                    S.op("dve", "tensor_reduce", lbl[:, 16:18], lbl[:, 8:16].rearrange("p (a b) -> p a b", a=2), AX.X, ALU.add,
                         reads=["lbe"], writes=["lbs"])
                    S.op("dve", "reciprocal", lbl[:, 16:18], lbl[:, 16:18], reads=["lbs"], writes=["lbs"])
                    for d in range(2):
                        if l == 0:
                            S.op("dve", "memset", lbl[:, 18 + d:19 + d], 0.0, writes=[("lb", d)])
                        else:
                            S.op("dve", "tensor_reduce", lbl[:, 18 + d:19 + d], lbl[:, 8 + 4 * d + 1:8 + 4 * d + 1 + l], AX.X, ALU.add,
                                 reads=["lbe"], writes=[("lb", d)])
                            S.op("dve", "tensor_tensor", lbl[:, 18 + d:19 + d], lbl[:, 18 + d:19 + d], lbl[:, 16 + d:17 + d], ALU.mult,
                                 reads=[("lb", d), "lbs"], writes=[("lb", d)])
                        S.op("dve", "tensor_scalar", lbl[:, 20 + d:21 + d], lbl[:, 18 + d:19 + d], -1.0, 1.0, ALU.mult, ALU.add,
                             reads=[("lb", d)], writes=[("oml", d)])
                    if int(os.environ.get("KHG", "3")) < 1:
                        continue
                    with ExitStack() as es2:
                        q32 = self.sb(es2, "hq32", [128, SL], F32)
                        z = self.sb(es2, "hz", [128, SL], F32)
                        g = self.sb(es2, "hg", [128, SL], F32)
                        k = self.sb(es2, "hk", [128, SL], F32)
                        bb = self.sb(es2, "hb_", [128, SL], F32)
                        bc = self.sb(es2, "hbc", [128, SL], F32)
                        e1 = self.sb(es2, "he1", [128, SL], F32)
                        khT = self.sb(es2, "khT", [128, SL], BF16)
                        k2 = self.sb(es2, "hk2", [128, SL], F32)
                        c3 = lambda t_: t_[:].rearrange("p (c s) -> p c s", s=64)
                        ncs = SL // 64
                        for sl in range(T // SL):
                            tsl = slice(sl * SL, (sl + 1) * SL)
                            S.dma("sp", q32[:], self.s_hT[hp * 128:(hp + 1) * 128, tsl], writes=["q32"])
                            for d in range(2):
                                r0 = 256 * (1 + d) + hp * 128
                                S.dma("sp", z[:], self.s_hT[r0:r0 + 128, tsl], writes=["z"])
                                S.op("act", "activation", out=z[:], in_=z[:], func=AF.Sigmoid, reads=["z"], writes=["z"])
                                S.op("dve", "tensor_scalar", z[:], z[:], lbl[:, 20 + d:21 + d], lbl[:, 18 + d:19 + d], ALU.mult, ALU.add,
                                     reads=["z", ("lb", d), ("oml", d)], writes=["z"])
                                S.op("act", "activation", out=g[:], in_=z[:], func=AF.Ln, reads=["z"], writes=["g"])
                                S.op("pool", "tensor_scalar", k[:], z[:], -1.0, 1.0, ALU.mult, ALU.add, reads=["z"], writes=["k"])
                                S.op("dve", "tensor_tensor_scan", bb[:], rmask[:], g[:], 0.0, ALU.mult, ALU.add,
                                     reads=["g", "rmask"], writes=["bb"])
                                tot = c3(bb)[:, :, 63:64]
                                totb = tot.to_broadcast([128, ncs, 64])
                                if d == 1:
                                    S.op("dve", "tensor_tensor", bc[:], g[:], bb[:], ALU.subtract, reads=["g", "bb"], writes=["bc"])
                                    S.op("dve", "tensor_tensor", c3(e1), c3(bc), totb, ALU.add, reads=["bc", "bb"], writes=["e1"])
                                    S.op("act", "activation", out=etot[d][:, sl * ncs:(sl + 1) * ncs].unsqueeze(2), in_=tot, func=AF.Exp,
                                         reads=["bb"], writes=[("etot", d)])
                                    S.op("dve", "tensor_copy", bb[:], e1[:], reads=["e1"], writes=["bb"])
                                    totb = c3(bb)[:, :, 0:1].to_broadcast([128, ncs, 64])
                                else:
                                    S.op("act", "activation", out=etot[d][:, sl * ncs:(sl + 1) * ncs].unsqueeze(2), in_=tot, func=AF.Exp,
                                         reads=["bb"], writes=[("etot", d)])
                                c32 = lambda t_: t_[:].rearrange("p (c s) -> p c s", s=32)
                                rix = 31 if d == 0 else 32
                                S.op("dve", "tensor_tensor", c3(bc), c3(bb), c3(bb)[:, :, rix:rix + 1].to_broadcast([128, ncs, 64]),
                                     ALU.subtract, reads=["bb"], writes=["bc"])
                                S.op("pool", "tensor_scalar", k2[:], bc[:], 0.0, None, ALU.max, reads=["bc"], writes=["k2"])
                                S.op("dve", "tensor_scalar", bc[:], bc[:], 0.0, None, ALU.min, reads=["bc"], writes=["bc"])
                                S.op("act", "activation", out=e1[:], in_=bc[:], func=AF.Exp, reads=["bc"], writes=["e1"])
                                S.op("pool", "tensor_tensor", qo[d][:, tsl], q32[:], e1[:], ALU.mult, reads=["q32", "e1"], writes=[("qo", d)])
                                S.op("act", "activation", out=e1[:], in_=k2[:], func=AF.Exp, scale=-1.0, reads=["k2"], writes=["e1"])
                                S.op("pool", "tensor_tensor", ko[d][:, tsl], k[:], e1[:], ALU.mult, reads=["k", "e1"], writes=[("ko", d)])
                                S.op("dve", "tensor_tensor", c32(bc), c32(bb), c32(bb)[:, :, 15:16].to_broadcast([128, 2 * ncs, 32]),
                                     ALU.subtract, reads=["bb"], writes=["bc"])
                                S.op("dve", "tensor_scalar", bc[:], bc[:], 40.0, -40.0, ALU.min, ALU.max, reads=["bc"], writes=["bc"])
                                S.op("act", "activation", out=e1[:], in_=bc[:], func=AF.Exp, reads=["bc"], writes=["e1"])
                                S.op("pool", "tensor_tensor", qt[d][:, tsl], q32[:], e1[:], ALU.mult, reads=["q32", "e1"], writes=[("qt", d)])
                                S.op("dve", "reciprocal", e1[:], e1[:], reads=["e1"], writes=["e1"])
                                S.op("pool", "tensor_tensor", kt[d][:, tsl], k[:], e1[:], ALU.mult, reads=["k", "e1"], writes=[("kt", d)])
                                S.op("act", "activation", out=e1[:], in_=bb[:], func=AF.Exp, reads=["bb"], writes=["e1"])
                                S.op("pool", "tensor_tensor", qb[d][:, tsl], q32[:], e1[:], ALU.mult, reads=["q32", "e1"], writes=[("qb", d)])
                                S.op("dve", "tensor_tensor", c3(bc), totb, c3(bb), ALU.subtract, reads=["bb"], writes=["bc"])
                                S.op("act", "activation", out=e1[:], in_=bc[:], func=AF.Exp, reads=["bc"], writes=["e1"])
                                S.op("pool", "tensor_tensor", khT[:], k[:], e1[:], ALU.mult, reads=["k", "e1"], writes=["khT"])
                                for c4 in range(ncs // 8):
                                    pi = 4 + c4 % 2
                                    pst = self.psum[pi][:, :].bitcast(BF16)
                                    for cc in range(8):
                                        c = c4 * 8 + cc
                                        S.op("pe", "transpose", pst[0:64, cc * 128:(cc + 1) * 128], khT[:, c * 64:(c + 1) * 64],
                                             self.ident_bf[:], reads=["khT"], writes=[("ps", pi)])
                                    c0 = sl * ncs + c4 * 8
                                    S.op("act", "activation", out=khat[d][:, c0:c0 + 8, :],
                                         in_=pst[0:64, 0:1024].rearrange("p (c e) -> p c e", c=8), func=AF.Copy,
                                         reads=[("ps", pi)], writes=[("khat", d)])
                        S.barrier()
                    khg = int(os.environ.get("KHG", "3"))
                    if khg < 2:
                        continue
                    with ExitStack() as es3:
                        Scur = [self.sb(es3, f"Scur{d}", [128, 128], F32) for d in range(2)]
                        Sout = [[self.sb(es3, f"Sout{d}{i}", [128, 128], F32) for i in range(2)] for d in range(2)]
                        for d in range(2):
                            S.op("pool", "memset", Scur[d][:], 0.0, writes=[("Scur", d)])
                            for hh in range(2):
                                S.dma("sp", Scur[d][hh * 64:(hh + 1) * 64, hh * 64:(hh + 1) * 64], self.s0[l, d, 2 * hp + hh],
                                      reads=[("Scur", d)], writes=[("Scur", d)])
                        for step in range(NCH):
                            for d in range(2):
                                c = step if d == 0 else NCH - 1 - step
                                i = step % 2
                                pi = 4 + d
                                S.op("pool", "tensor_copy", stab[d][:, c, :], Scur[d][:], reads=[("Scur", d)], writes=[("stab", d)])
                                S.op("pe", "matmul", self.psum[pi][:, 0:128], khat[d][:, c, :], vsb[:, c, :], start=True, stop=True,
                                     reads=[("khat", d), "vsb"], writes=[("ps", pi)])
                                S.op("dve", "scalar_tensor_tensor", Sout[d][i][:], Scur[d][:], etot[d][:, c:c + 1], self.psum[pi][:, 0:128],
                                     ALU.mult, ALU.add, reads=[("Scur", d), ("etot", d), ("ps", pi)], writes=[("Sout", d, i)])
                                is_end = (c % 4 == 3) if d == 0 else (c % 4 == 0)
                                if is_end:
                                    for hh in range(2):
                                        S.dma("sp", self.o_state[l, c // 4, d, 2 * hp + hh],
                                              Sout[d][i][hh * 64:(hh + 1) * 64, hh * 64:(hh + 1) * 64], reads=[("Sout", d, i)])
                                cn = c + 1 if d == 0 else c - 1
                                if 0 <= cn < NCH:
                                    S.op("dve", "tensor_scalar", Scur[d][:], Sout[d][i][:], car[d][:, cn:cn + 1], None, ALU.mult,
                                         reads=[("Sout", d, i), ("car", d)], writes=[("Scur", d)])
                        S.barrier()
                    if khg < 3:
                        continue
                    with ExitStack() as es4:
                        Af = [self.sb(es4, f"Af{i}", [64, 256], F32) for i in range(2)]
                        Am = [self.sb(es4, f"Am{i}", [64, 256], BF16) for i in range(2)]
                        o32 = self.sb(es4, "ho32", [64, 512], F32)
                        sqf = self.sb(es4, "hsq", [64, 512], F32)
                        rst = self.sb(es4, "hrst", [64, 512], F32)
                        gt = [self.sb(es4, f"hgt{i}", [64, 512], BF16) for i in range(2)]
                        yb = [self.sb(es4, f"hyb{i}", [64, 512], BF16) for i in range(2)]
                        def emit_A(c):
                            for hh in range(2):
                                pA = self.psum[hh]
                                hs_ = slice(hh * 64, (hh + 1) * 64)
                                h1 = slice(c * 64, c * 64 + 32)
                                h2 = slice(c * 64 + 32, c * 64 + 64)
                                for d in range(2):
                                    col = d * 64
                                    for (sh, th, so, to) in ((h1, h1, 0, 0), (h2, h2, 32, 32), (h1, h2, 0, 32), (h2, h1, 32, 0)):
                                        cross_valid = (so == 0 and to == 32) if d == 0 else (so == 32 and to == 0)
                                        kk_, qq_ = (ko[d], qo[d]) if cross_valid else (kt[d], qt[d])
                                        S.op("pe", "matmul", pA[so:so + 32, col + to:col + to + 32], kk_[hs_, sh], qq_[hs_, th],
                                             start=True, stop=True, reads=[("kt", d), ("qt", d), ("ko", d), ("qo", d)],
                                             writes=[("ps", hh)])

                        emit_A(0)
                        for st in range(NST):
                            for cc in range(8):
                                c = st * 8 + cc
                                i = c % 2
                                csl = slice(c * 64, (c + 1) * 64)
                                for hh in range(2):
                                    S.op("dve", "tensor_tensor", Af[i][:, hh * 128:(hh + 1) * 128], self.psum[hh][0:64, 0:128],
                                         mT[:, hh * 128:(hh + 1) * 128], ALU.mult, reads=[("ps", hh), "mT"], writes=[("Af", i)])
                                if c + 1 < NCH:
                                    emit_A(c + 1)
                                S.op("pool", "tensor_copy", Am[i][:], Af[i][:], reads=[("Af", i)], writes=[("Am", i)])
                                for hh in range(2):
                                    pO = self.psum[2 + hh]
                                    pI = self.psum[4 + hh]
                                    hs = slice(hh * 64, (hh + 1) * 64)
                                    ocol = slice(cc * 64, (cc + 1) * 64)
                                    kp2 = os.environ.get("KP2", "")
                                    if "nointra" in kp2:
                                        continue
                                    S.op("pe", "matmul", pO[0:64, ocol], vsb[:, c, hs], Am[i][:, (hh * 2) * 64:(hh * 2 + 1) * 64],
                                         start=True, stop=False, reads=["vsb", ("Am", i)], writes=[("ps", 2 + hh)])
                                    S.op("pe", "matmul", pO[0:64, ocol], vsb[:, c, hs], Am[i][:, (hh * 2 + 1) * 64:(hh * 2 + 2) * 64],
                                         start=False, stop=True, reads=["vsb", ("Am", i)], writes=[("ps", 2 + hh)])
                                    for d in range(0 if "nointer" in kp2 else 2):
                                        S.op("pe", "matmul", pI[0:64, ocol], stab[d][hs, c, hs], qb[d][hs, csl],
                                             start=(d == 0), stop=(d == 1), reads=[("stab", d), ("qb", d)], writes=[("ps", 4 + hh)])
                            for hh in range(2):
                                h = 2 * hp + hh
                                gi = hh
                                r0 = 512 + h * 64
                                tsl = slice(st * 512, (st + 1) * 512)
                                S.dma("sp", gt[gi][:], self.s_gT[r0:r0 + 64, tsl], writes=[("gt", gi)])
                                S.op("act", "activation", out=o32[:], in_=self.psum[2 + hh][0:64, :], func=AF.Copy,
                                     reads=[("ps", 2 + hh)], writes=["o32"])
                                S.op("dve", "tensor_tensor", o32[:], o32[:], self.psum[4 + hh][0:64, :], ALU.add,
                                     reads=["o32", ("ps", 4 + hh)], writes=["o32"])
                                S.op("pool", "tensor_tensor", sqf[:], o32[:], o32[:], ALU.mult, reads=["o32"], writes=["sqf"])
                                S.op("pe", "matmul", self.psum[6][0:64, :], self.ones_f[0:64, 0:64], sqf[:], start=True, stop=True,
                                     reads=["sqf"], writes=[("ps", 6)])
                                S.op("dve", "tensor_scalar", rst[:], self.psum[6][0:64, :], 1.0 / 64, EPS, ALU.mult, ALU.add,
                                     reads=[("ps", 6)], writes=["rst"])
                                S.op("act", "activation", out=rst[:], in_=rst[:], func=AF.Sqrt, reads=["rst"], writes=["rst"])
                                S.op("dve", "reciprocal", rst[:], rst[:], reads=["rst"], writes=["rst"])
                                S.op("dve", "scalar_tensor_tensor", o32[:], o32[:], og[0:64, 0:1], rst[:], ALU.mult, ALU.mult,
                                     reads=["o32", "rst", "og"], writes=["o32"])
                                S.op("pool", "tensor_tensor", yb[gi][:], o32[:], gt[gi][:], ALU.mult, reads=["o32", ("gt", gi)],
                                     writes=[("yb", gi)])
                                S.dma("sp", self.s_yT[r0:r0 + 64, tsl], yb[gi][:], reads=[("yb", gi)])
                        S.barrier()

    def hy_wrap(self, z, m, key):
        S = self.S
        PI = math.pi
        S.op("dve", "tensor_scalar", m, z, PI, None, ALU.is_gt, reads=[key], writes=[key + "m"])
        S.op("dve", "scalar_tensor_tensor", z, m, -2 * PI, z, ALU.mult, ALU.add, reads=[key, key + "m"], writes=[key])
        S.op("dve", "tensor_scalar", m, z, -PI, None, ALU.is_lt, reads=[key], writes=[key + "m"])
        S.op("dve", "scalar_tensor_tensor", z, m, 2 * PI, z, ALU.mult, ALU.add, reads=[key, key + "m"], writes=[key])

    def phase_hyena(self, l):
        nc, S = self.nc, self.S
        with ExitStack() as es:
            F1z = self.sb(es, "F1z", [32, 128], F32)
            F1f = self.sb(es, "F1f", [64, 128], F32)
            Em = self.sb(es, "Em", [128, 512], BF16)
            Qm = self.sb(es, "Qm", [64, 128 * 64], BF16)
            S.dma("sp", F1z[:], self.F1z[:, :], writes=["F1z"])
            S.dma("sp", F1f[:], self.F1f[:, :], writes=["F1f"])
            S.dma("sp", Em[:], self.Em[:, :], writes=["Em"])
            S.dma("sp", Qm[:], self.Qm[:, :], writes=["Qm"])
            with ExitStack() as es1:
                w1 = self.sb(es1, "hw1", [33, 64], F32)
                w2 = self.sb(es1, "hw2", [64, 64], F32)
                w3 = self.sb(es1, "hw3", [64, 1024], F32)
                cols = self.sb(es1, "hcols", [64, 8], F32)
                nd = self.sb(es1, "hnd", [128, 2], F32)
                ft = [self.sb(es1, f"hft{i}", [33, 512], F32) for i in range(2)]
                tl = [self.sb(es1, f"htl{i}", [128, 512], F32) for i in range(2)]
                dm = [self.sb(es1, f"hdm{i}", [128, 2, 512], F32) for i in range(2)]
                z1 = self.sb(es1, "hz1", [64, 512], F32)
                m1 = self.sb(es1, "hm1", [64, 512], F32)
                h1 = self.sb(es1, "hh1", [64, 512], F32)
                h2 = self.sb(es1, "hh2", [64, 512], F32)
                win = self.sb(es1, "hwin", [128, 2, 512], F32)
                fa = self.sb(es1, "hfa", [128, 512], F32)
                fo = [self.sb(es1, f"hfo{i}", [128, 512], F32) for i in range(2)]
                S.dma("sp", w1[:], self.hy_w1[l], writes=["w1"])
                S.dma("sp", w2[:], self.hy_w2[l], writes=["w2"])
                S.dma("sp", w3[:], self.hy_w3[l], writes=["w3"])
                S.dma("sp", cols[:, 0:4], self.hy_cols[l], writes=["cols"])
                S.dma("sp", nd[:, 0:1], self.negdelta[0:128, :], writes=["nd"])
                S.dma("sp", nd[:, 1:2], self.negdelta[128:256, :], writes=["nd"])
                S.op("dve", "tensor_tensor", cols[:, 4:6], cols[:, 0:2], cols[:, 2:4], ALU.mult, reads=["cols"], writes=["cols2"])
                for sl in range(16):
                    b = sl % 2
                    ssl = slice(sl * 512, (sl + 1) * 512)
                    S.dma("sp", ft[b][:], self.featsT[:, ssl], writes=[("ft", b)])
                    S.dma("sp", tl[b][:], self.featsT[0, ssl].partition_broadcast(128), writes=[("tl", b)])
                    for dd in range(2):
                        S.dma("sp", dm[b][:, dd, :], self.dmask[dd, ssl].partition_broadcast(128), writes=[("dm", b)])
                    S.op("pe", "matmul", self.psum[0][0:64, :], w1[:], ft[b][:], start=True, stop=True,
                         reads=["w1", ("ft", b)], writes=[("ps", 0)])
                    S.op("dve", "tensor_scalar", z1[:], self.psum[0][0:64, :], cols[:, 2:3], cols[:, 4:5], ALU.mult, ALU.add,
                         reads=[("ps", 0), "cols", "cols2"], writes=["z1"])
                    self.hy_wrap(z1[:], m1[:], "z1")
                    S.op("act", "activation", out=h1[:], in_=z1[:], func=AF.Sin, reads=["z1"], writes=["h1"])
                    S.op("pe", "matmul", self.psum[1][0:64, :], w2[:], h1[:], start=True, stop=True,
                         reads=["w2", "h1"], writes=[("ps", 1)])
                    S.op("dve", "tensor_scalar", z1[:], self.psum[1][0:64, :], cols[:, 3:4], cols[:, 5:6], ALU.mult, ALU.add,
                         reads=[("ps", 1), "cols", "cols2"], writes=["z1"])
                    self.hy_wrap(z1[:], m1[:], "z1")
                    S.op("act", "activation", out=h2[:], in_=z1[:], func=AF.Sin, reads=["z1"], writes=["h2"])
                    for cch in range(2):
                        S.op("act", "activation", out=win[:, cch, :], in_=tl[b][:], func=AF.Exp, scale=nd[:, cch:cch + 1],
                             reads=[("tl", b), "nd"], writes=[("win", cch)])
                    it = 0
                    for o in range(2):
                        for cch in range(2):
                            for dd in range(2):
                                c0 = (o * 2 + dd) * 256 + cch * 128
                                S.op("pe", "matmul", self.psum[2 + dd][:, :], w3[:, c0:c0 + 128], h2[:], start=True, stop=True,
                                     reads=["w3", "h2"], writes=[("ps", 2 + dd)])
                            S.op("dve", "tensor_tensor", fa[:], self.psum[2][:, :], dm[b][:, 0, :], ALU.mult,
                                 reads=[("ps", 2), ("dm", b)], writes=["fa"])
                            S.op("dve", "tensor_tensor", fo[it % 2][:], self.psum[3][:, :], dm[b][:, 1, :], ALU.mult,
                                 reads=[("ps", 3), ("dm", b)], writes=[("fo", it % 2)])
                            S.op("pool", "tensor_tensor", fo[it % 2][:], fo[it % 2][:], fa[:], ALU.add,
                                 reads=[("fo", it % 2), "fa"], writes=[("fo", it % 2)])
                            S.op("pool", "tensor_tensor", fo[it % 2][:], fo[it % 2][:], win[:, cch, :], ALU.mult,
                                 reads=[("fo", it % 2), ("win", cch)], writes=[("fo", it % 2)])
                            S.dma("sp", self.s_filtT[o, cch * 128:(cch + 1) * 128, ssl], fo[it % 2][:], reads=[("fo", it % 2)])
                            it += 1
                S.barrier()

            with ExitStack() as es2:
                AB = self.sb(es2, "hyAB", [128, 16384], BF16)
                Y = self.sb(es2, "hyY", [128, 2, 128, 64], BF16)
                xb = [self.sb(es2, f"hyxb{i}", [64, 8, 128], F32) for i in range(2)]
                Gt = [self.sb(es2, f"hyG{i}", [128, 4, 384], BF16) for i in range(2)]
                Xg = [self.sb(es2, f"hyXg{i}", [128, 2, 4, 128], F32) for i in range(2)]
                Hg = [self.sb(es2, f"hyHg{i}", [128, 2, 4, 128], F32) for i in range(2)]
                pt1 = self.sb(es2, "hyp1", [128, 4, 128], F32)
                pt2 = self.sb(es2, "hyp2", [128, 4, 128], F32)
                A4 = AB[:, :].rearrange("p (r k c) -> p r k c", r=2, k=64)
                Bt = AB[0:64, :].rearrange("p (r m c) -> p r m c", r=2, m=128)

                def forward(src, K1, F1, key_f1, mode, hidx):
                    for cg in range(16):
                        b = cg % 2
                        S.dma("sp", xb[b][0:K1, :, :], src[cg * 8:(cg + 1) * 8, :].rearrange("c (a n) -> a c n", n=128),
                              writes=[("xb", b)])
                        for q4 in range(2):
                            pi = q4 % 2
                            for cc in range(4):
                                ci = q4 * 4 + cc
                                S.op("pe", "matmul", self.psum[pi][:, cc * 128:(cc + 1) * 128], xb[b][0:K1, ci, :], F1[0:K1, :],
                                     start=True, stop=True, reads=[("xb", b), key_f1], writes=[("ps", pi)])
                            c0 = cg * 8 + q4 * 4
                            S.op("act", "activation", out=A4[:, :, :, c0:c0 + 4].rearrange("p r k c -> p c r k"),
                                 in_=self.psum[pi][:, :].rearrange("p (c r k) -> p c r k", c=4, r=2), func=AF.Copy,
                                 reads=[("ps", pi)], writes=["A"])
                    for kg in range(16):
                        b = kg % 2
                        S.dma("sp", Gt[b][:], self.Gm[kg * 4:(kg + 1) * 4].rearrange("k n x -> n k x"), writes=[("G", b)])
                        if mode == "Y":
                            S.dma("sp", Hg[b][:], self.s_H[hidx].rearrange("p (r k c) -> p r k c", r=2, k=64)[:, :, kg * 4:(kg + 1) * 4, :],
                                  writes=[("Hg", b)])
                        for kk in range(4):
                            k1 = kg * 4 + kk
                            pi = 2 + kk % 2
                            ps = self.psum[pi]
                            S.op("pe", "matmul", ps[:, 0:128], Gt[b][:, kk, 0:128], A4[:, 0, k1, :], start=True, stop=False,
                                 reads=[("G", b), "A"], writes=[("ps", pi)])
                            S.op("pe", "matmul", ps[:, 0:128], Gt[b][:, kk, 256:384], A4[:, 1, k1, :], start=False, stop=True,
                                 reads=[("G", b), "A"], writes=[("ps", pi)])
                            S.op("pe", "matmul", ps[:, 128:256], Gt[b][:, kk, 128:256], A4[:, 0, k1, :], start=True, stop=False,
                                 reads=[("G", b), "A"], writes=[("ps", pi)])
                            S.op("pe", "matmul", ps[:, 128:256], Gt[b][:, kk, 0:128], A4[:, 1, k1, :], start=False, stop=True,
                                 reads=[("G", b), "A"], writes=[("ps", pi)])
                            S.op("act", "activation", out=Xg[b][:, :, kk, :], in_=ps[:, 0:256].rearrange("p (r c) -> p r c", r=2),
                                 func=AF.Copy, reads=[("ps", pi)], writes=[("Xg", b)])
                        if mode == "H":
                            S.dma("sp", self.s_H[hidx].rearrange("p (r k c) -> p r k c", r=2, k=64)[:, :, kg * 4:(kg + 1) * 4, :],
                                  Xg[b][:], reads=[("Xg", b)])
                        else:
                            Xr, Xi, Hr, Hi = Xg[b][:, 0], Xg[b][:, 1], Hg[b][:, 0], Hg[b][:, 1]
                            ksl = slice(kg * 4, (kg + 1) * 4)
                            S.op("dve", "tensor_tensor", pt1[:], Xr, Hr, ALU.mult, reads=[("Xg", b), ("Hg", b)], writes=["pt1"])
                            S.op("pool", "tensor_tensor", pt2[:], Xi, Hi, ALU.mult, reads=[("Xg", b), ("Hg", b)], writes=["pt2"])
                            S.op("pool", "tensor_tensor", Y[:, 0, :, ksl].rearrange("p c k -> p k c"), pt1[:], pt2[:], ALU.subtract,
                                 reads=["pt1", "pt2"], writes=["Y"])
                            S.op("dve", "tensor_tensor", pt1[:], Xr, Hi, ALU.mult, reads=[("Xg", b), ("Hg", b)], writes=["pt1"])
                            S.op("pool", "tensor_tensor", pt2[:], Xi, Hr, ALU.mult, reads=[("Xg", b), ("Hg", b)], writes=["pt2"])
                            S.op("pool", "tensor_tensor", Y[:, 1, :, ksl].rearrange("p c k -> p k c"), pt1[:], pt2[:], ALU.add,
                                 reads=["pt1", "pt2"], writes=["Y"])

                def inverse(yT):
                    E3 = Em[:, :].rearrange("p (v x) -> p v x", v=2)
                    Q4 = Qm[:, :].rearrange("p (m v b) -> p m v b", m=128, v=2)
                    for half in range(2):
                        for cp in range(32):
                            pi = 4 + cp % 2
                            for cc in range(2):
                                c = half * 64 + cp * 2 + cc
                                S.op("pe", "matmul", self.psum[pi][0:64, cc * 256:(cc + 1) * 256], Y[:, 0, c, :], E3[:, 0, :],
                                     start=True, stop=False, reads=["Y", "Em"], writes=[("ps", pi)])
                                S.op("pe", "matmul", self.psum[pi][0:64, cc * 256:(cc + 1) * 256], Y[:, 1, c, :], E3[:, 1, :],
                                     start=False, stop=True, reads=["Y", "Em"], writes=[("ps", pi)])
                            S.op("act", "activation", out=Bt[:, :, :, cp * 2:cp * 2 + 2].rearrange("p r m c -> p c r m"),
                                 in_=self.psum[pi][0:64, :].rearrange("p (c r m) -> p c r m", c=2, r=2), func=AF.Copy,
                                 reads=[("ps", pi)], writes=["Bt"])
                        for mg in range(8):
                            pi = 6 + mg % 2
                            for mm in range(16):
                                ma = mg * 16 + mm
                                S.op("pe", "matmul", self.psum[pi][0:64, mm * 32:(mm + 1) * 32], Bt[:, 0, ma, :], Q4[:, ma, 0, :],
                                     start=True, stop=False, reads=["Bt", "Qm"], writes=[("ps", pi)])
                                S.op("pe", "matmul", self.psum[pi][0:64, mm * 32:(mm + 1) * 32], Bt[:, 1, ma, :], Q4[:, ma, 1, :],
                                     start=False, stop=True, reads=["Bt", "Qm"], writes=[("ps", pi)])
                            S.op("act", "activation",
                                 out=yT[half * 64:(half + 1) * 64, :].rearrange("p (b a) -> p a b", a=128)[:, mg * 16:(mg + 1) * 16, :],
                                 in_=self.psum[pi][0:64, :].rearrange("p (a b) -> p a b", a=16), func=AF.Copy,
                                 reads=[("ps", pi)], writes=["yT"])

                for o in range(2):
                    for cch in range(2):
                        forward(self.s_filtT[o, cch * 128:(cch + 1) * 128, :], 64, F1f, "F1f", "H", o * 2 + cch)
                S.barrier()

                with ExitStack() as es3:
                    u = self.sb(es3, "hyu", [128, T], F32)
                    us = self.sb(es3, "hyus", [128, T], F32)
                    zc = self.sb(es3, "hyz", [128, T], F32)
                    cw = self.sb(es3, "hycw", [128, 4], F32)
                    bm = self.sb(es3, "hybm", [128, 2, 16], F32)
                    bia = self.sb(es3, "hybia", [128, 2], F32)
                    yT = us
                    ub = u[:, :].bitcast(BF16)
                    S.dma("sp", bm[:, 0, :], self.bmask[0], writes=["bm"])
                    S.dma("sp", bm[:, 1, :], self.bmask[1], writes=["bm"])
                    u3 = lambda t_: t_[:].rearrange("p (s t) -> p s t", s=16)
                    for cch in range(2):
                        for o in range(2):
                            S.dma("sp", bia[:, o:o + 1], self.hy_biasT[l, o, cch * 128:(cch + 1) * 128, :], writes=["bia"])
                        for part in range(3):
                            r0 = part * 256 + cch * 128
                            S.dma("sp", u[:], self.s_huT[r0:r0 + 128, :], writes=["u"])
                            S.dma("sp", cw[:], self.hy_convT[l, r0:r0 + 128, :], writes=["cw"])
                            S.op("pool", "memset", us[:, 0:1], 0.0, writes=["us"])
                            S.op("pool", "tensor_copy", us[:, 1:T], u[:, 0:T - 1], reads=["u"], writes=["us"])
                            S.op("dve", "tensor_tensor", u3(us)[:, :, 0:1], u3(us)[:, :, 0:1], bm[:, 0, :].unsqueeze(2), ALU.mult,
                                 reads=["us", "bm"], writes=["us"])
                            S.op("dve", "tensor_scalar", zc[:], u[:], cw[:, 1:2], cw[:, 3:4], ALU.mult, ALU.add,
                                 reads=["u", "cw"], writes=["zc"])
                            S.op("dve", "scalar_tensor_tensor", zc[:], us[:], cw[:, 0:1], zc[:], ALU.mult, ALU.add,
                                 reads=["us", "zc", "cw"], writes=["zc"])
                            S.op("pool", "memset", us[:, T - 1:T], 0.0, reads=["us"], writes=["us"])
                            S.op("pool", "tensor_copy", us[:, 0:T - 1], u[:, 1:T], reads=["u"], writes=["us"])
                            S.op("dve", "tensor_tensor", u3(us)[:, :, 255:256], u3(us)[:, :, 255:256], bm[:, 1, :].unsqueeze(2), ALU.mult,
                                 reads=["us", "bm"], writes=["us"])
                            S.op("dve", "scalar_tensor_tensor", zc[:], us[:], cw[:, 2:3], zc[:], ALU.mult, ALU.add,
                                 reads=["us", "zc", "cw"], writes=["zc"])
                            S.dma("sp", self.s_z[part, :, :], zc[:], reads=["zc"])
                        S.barrier()
                        for o in range(2):
                            zsrc = self.s_z[0] if o == 0 else self.s_z[3]
                            forward(zsrc, 32, F1z, "F1z", "Y", o * 2 + cch)
                            inverse(yT)
                            S.dma("sp", zc[:], zsrc, writes=["zc"])
                            S.dma("sp", u[:], self.s_z[1 + o], writes=["u"])
                            S.op("dve", "scalar_tensor_tensor", zc[:], zc[:], bia[:, o:o + 1], yT[:], ALU.mult, ALU.add,
                                 reads=["zc", "yT", "bia"], writes=["zc"])
                            S.op("pool", "tensor_tensor", zc[:], zc[:], u[:], ALU.mult, reads=["zc", "u"], writes=["zc"])
                            if o == 0:
                                S.dma("sp", self.s_z[3], zc[:], reads=["zc"])
                            else:
                                r0 = 768 + cch * 128
                                S.dma("sp", ub[:, 0:T], self.s_gT[r0:r0 + 128, :], reads=["u"], writes=["u"])
                                S.op("pool", "tensor_tensor", ub[:, T:2 * T], zc[:], ub[:, 0:T], ALU.mult, reads=["zc", "u"], writes=["u"])
                                S.dma("sp", self.s_yT[r0:r0 + 128, :], ub[:, T:2 * T], reads=["u"])
                            S.barrier()

    def phase_po(self, l):
        nc, S = self.nc, self.S
        xsrc = self.x_in if l == 0 else self.y
        with ExitStack() as es:
            wo32 = self.sb(es, "wo32", [128, 8, D], F32)
            wo16 = self.sb(es, "wo16", [128, 8, D], BF16)
            yT = [self.sb(es, f"yT{i}", [128, 8, 512], BF16) for i in range(2)]
            xin = [self.sb(es, f"xo{i}", [128, 4, D], F32) for i in range(2)]
            xo = [self.sb(es, f"xn{i}", [128, D], F32) for i in range(2)]
            for k in range(8):
                S.dma("sp", wo32[:, k, :], self.w_out[l, k * 128:(k + 1) * 128, :], writes=[("wo32", k)])
                S.op("dve" if k % 2 == 0 else "pool", "tensor_tensor", wo16[:, k, :], wo32[:, k, :], self.gate_b[:], ALU.mult,
                     reads=[("wo32", k)], writes=[("wo16", k)])

            def load(st):
                b = st % 2
                S.dma("sp", yT[b][:], self.s_yT[:, st * 512:(st + 1) * 512].rearrange("(k p) t -> p k t", p=128),
                      writes=[("yT", b)])
                S.dma("sp", xin[b][:], xsrc[st * 512:(st + 1) * 512, :].rearrange("(j p) d -> p j d", p=128),
                      writes=[("xo", b)])

            load(0)
            for st in range(NST):
                b = st % 2
                if st + 1 < NST:
                    load(st + 1)
                for j in range(4):
                    tok0 = st * 512 + j * 128
                    xb = j % 2
                    for n in range(2):
                        pi = (j * 2 + n) % 4
                        ps = self.psum[pi]
                        for k in range(8):
                            S.op("pe", "matmul", ps[:, :], yT[b][:, k, j * 128:(j + 1) * 128], wo16[:, k, n * 512:(n + 1) * 512],
                                 start=(k == 0), stop=(k == 7), reads=[("yT", b), ("wo16", k)], writes=[("ps", pi)])
                        S.op("dve", "tensor_tensor", xo[xb][:, n * 512:(n + 1) * 512], ps[:, :], xin[b][:, j, n * 512:(n + 1) * 512],
                             ALU.add, reads=[("ps", pi), ("xo", b)], writes=[("xn", xb, n)])
                    S.dma("sp", self.y[tok0:tok0 + 128, :], xo[xb][:], reads=[("xn", xb, 0), ("xn", xb, 1)])


def host_consts():
    c = {}
    c["ident"] = np.eye(128, dtype=np.float32)
    return c


def hyena_tables(is_prompt):
    f32 = np.float32
    N = 8192
    n = np.arange(N)
    t = {}
    if not is_prompt:
        L = 4096
        tt = np.where(n < L, n, 2 * L - 1 - n)
        dm0 = (n < L)
        dm1 = ~dm0
    else:
        L = 256
        tt = np.clip(np.where(n < 256, n, 511 - n), 0, 255)
        dm0 = (n < 256)
        dm1 = (n >= 256) & (n < 512)
    tl = np.linspace(0.0, 1.0, L, dtype=f32)[tt]
    w = (f32(2.0 * math.pi) * np.arange(L, dtype=f32) / f32(L)).astype(f32)[tt]
    fb = np.linspace(1e-4, 15, 16, dtype=f32)
    ang = (w[:, None] * fb[None, :]).astype(f32)
    feats = np.concatenate([tl[:, None], np.cos(ang), -np.sin(ang)], axis=-1).astype(f32)
    t["featsT"] = np.ascontiguousarray(feats.T)
    t["dmask"] = np.stack([dm0, dm1]).astype(f32)
    bm = np.ones((2, 128, 16), f32)
    if is_prompt:
        bm[:] = 0.0
    t["bmask"] = bm
    two_pi = 2.0 * math.pi
    k1 = np.arange(64)
    if not is_prompt:
        n1 = np.arange(64)
        a = two_pi * ((n1[:, None] * k1[None, :]) % 64) / 64.0
        F1 = np.concatenate([np.cos(a), -np.sin(a)], axis=1)
        F1z, F1f = F1[:32], F1
        n2 = np.arange(128)[None, :, None]
        k2 = np.arange(128)[None, None, :]
        idx = (n2 * (k1[:, None, None] + 64 * k2)) % 8192
        th = two_pi * idx / 8192.0
        ma = np.arange(128)[None, :, None]
        mb = np.arange(32)[None, None, :]
        qi = (128 * mb * k1[:, None, None] + ma * k1[:, None, None]) % 8192
        ps = two_pi * qi / 8192.0
        Qr, Qi = np.cos(ps) / 8192.0, np.sin(ps) / 8192.0
    else:
        p = k1 // 4
        kp = k1 % 4
        n1 = np.arange(64)
        a = two_pi * (((n1[:, None] % 2) * kp[None, :]) % 4) / 4.0
        sel = ((n1[:, None] // 2) == p[None, :])
        F1z = np.concatenate([np.cos(a) * sel, -np.sin(a) * sel], axis=1)[:32]
        a2 = two_pi * ((n1[:, None] * kp[None, :]) % 4) / 4.0
        sel2 = (n1[:, None] < 4)
        F1f = np.concatenate([np.cos(a2) * sel2, -np.sin(a2) * sel2], axis=1)
        n2 = np.arange(128)[None, :, None]
        k2 = np.arange(128)[None, None, :]
        idx = (n2 * (kp[:, None, None] + 4 * k2)) % 512
        th = two_pi * idx / 512.0
        ma = np.arange(128)[None, :, None]
        mb = np.arange(32)[None, None, :]
        qi = (128 * (mb % 2) * kp[:, None, None] + ma * kp[:, None, None]) % 512
        ps = two_pi * qi / 512.0
        selq = ((mb // 2) == p[:, None, None])
        Qr, Qi = np.cos(ps) / 512.0 * selq, np.sin(ps) / 512.0 * selq
    t["F1z"] = np.ascontiguousarray(F1z.astype(f32))
    t["F1f"] = np.ascontiguousarray(F1f.astype(f32))
    G = np.concatenate([np.cos(th), -np.sin(th), np.sin(th)], axis=2)
    t["Gm"] = np.ascontiguousarray(G.astype(ml_dtypes.bfloat16))
    k2v = np.arange(128)[:, None]
    mav = np.arange(128)[None, :]
    ph = two_pi * ((k2v * mav) % 128) / 128.0
    Er, Ei = np.cos(ph), np.sin(ph)
    t["Em"] = np.ascontiguousarray(np.concatenate([Er, Ei, -Ei, Er], axis=1).astype(ml_dtypes.bfloat16))
    Q = np.stack([Qr, -Qi], axis=2)
    t["Qm"] = np.ascontiguousarray(Q.reshape(64, 128 * 64).astype(ml_dtypes.bfloat16))
    dl = np.linspace(math.log(0.01) / 1.5, math.log(0.01) / 0.3, 256, dtype=f32)
    t["negdelta"] = (-np.abs(dl)).reshape(256, 1).astype(f32)
    return t


def job_tables(is_prompt):
    t = {}
    if not is_prompt:
        pos = np.arange(T)
        row = (pos // 64).astype(np.float32)
        col = (pos % 64).astype(np.float32)
        inv = (np.float32(10000.0) ** (-np.arange(0, 16, 2, dtype=np.float32) / np.float32(16))).astype(np.float32)
        ar = (row[:, None] * inv[None, :]).astype(np.float32)
        ac = (col[:, None] * inv[None, :]).astype(np.float32)
        C = np.concatenate([np.cos(ar), np.cos(ar), np.cos(ac), np.cos(ac)], axis=1).astype(np.float32)
        Ssg = np.concatenate([-np.sin(ar), np.sin(ar), -np.sin(ac), np.sin(ac)], axis=1).astype(np.float32)
        qaug = np.zeros((T, 17), np.float32)
        kaug = np.zeros((NKEY, 17), np.float32)
    else:
        C = np.ones((T, 32), np.float32)
        Ssg = np.zeros((T, 32), np.float32)
        pid = np.arange(T) // 256
        qaug = np.zeros((T, 17), np.float32)
        qaug[np.arange(T), pid] = 1.0
        qaug[:, 16] = 1.0
        kaug = np.zeros((NKEY, 17), np.float32)
        kaug[np.arange(T), pid] = BIG
        kaug[:, 16] = -BIG
    carry = np.ones((2, 128, 64), np.float32)
    if is_prompt:
        cidx = np.arange(64)
        carry[0][:, cidx % 4 == 0] = 0.0
        carry[1][:, cidx % 4 == 3] = 0.0
    t["carry"] = carry
    sidx = np.arange(64)
    mf = (sidx[:, None] <= sidx[None, :]).astype(np.float32)
    mb = (sidx[:, None] >= sidx[None, :]).astype(np.float32)
    t["maskT"] = np.ascontiguousarray(np.concatenate([mf, mb, mf, mb], axis=1))
    t.update(hyena_tables(is_prompt))
    t["ropeC"] = np.ascontiguousarray(np.tile(C, (1, 8)))
    t["ropeS"] = np.ascontiguousarray(np.tile(Ssg, (1, 8)))
    t["qaug"] = qaug
    t["kaug"] = kaug
    return t


def make_in_maps(inp, n_cores=8):
    consts = host_consts()
    tabs = {False: job_tables(False), True: job_tables(True)}
    maps = []
    for core in range(n_cores):
        job = core if core < 5 else 4
        m = dict(consts)
        if job < 4:
            m["x"] = np.ascontiguousarray(inp["x_sample"][job])
            cv = inp["c"][job]
        else:
            m["x"] = np.ascontiguousarray(inp["x_prompt"].reshape(T, D))
            cv = inp["c_ctx"]
        m["cvecT"] = np.ascontiguousarray(cv.reshape(8, 128).T)
        m.update(tabs[job >= 4])
        if job < 4:
            m["c_ckv"] = np.ascontiguousarray(inp["cache_mla_ckv"][job])
            m["c_krope"] = np.ascontiguousarray(inp["cache_mla_krope"][job])
            m["c_dk"] = np.ascontiguousarray(inp["cache_diff_k"][job].reshape(DEPTH, PAST, 256))
            m["c_dv"] = np.ascontiguousarray(inp["cache_diff_v"][job].reshape(DEPTH, PAST, 256))
        else:
            m["c_ckv"] = np.zeros((DEPTH, PAST, 128), np.float32)
            m["c_krope"] = np.zeros((DEPTH, PAST, 32), np.float32)
            m["c_dk"] = np.zeros((DEPTH, PAST, 256), np.float32)
            m["c_dv"] = np.zeros((DEPTH, PAST, 256), np.float32)
        m["lbT"] = np.ascontiguousarray(inp["hgrn_lb_logits"].transpose(2, 0, 1).reshape(256, 8))
        m["hy_cols"] = np.ascontiguousarray(np.stack([inp["hy_b1"], inp["hy_b2"], inp["hy_sin_freq"][:, 0], inp["hy_sin_freq"][:, 1]], axis=-1))
        m["hy_convT"] = np.ascontiguousarray(np.concatenate([inp["hy_conv_w"].transpose(0, 2, 1), inp["hy_conv_b"][:, :, None]], axis=-1))
        m["hy_biasT"] = np.ascontiguousarray(inp["hy_bias"].reshape(DEPTH, 2, 256, 1))
        for k in ("hy_w1", "hy_w2", "hy_w3"):
            m[k] = inp[k]
        m["hgrn_out_gT"] = np.ascontiguousarray(inp["hgrn_out_g"].reshape(DEPTH, 64, 1))
        if job < 4:
            m["s0"] = np.ascontiguousarray(inp["state_hgrn"][job])
        else:
            m["s0"] = np.zeros((DEPTH, 2, 4, 64, 64), np.float32)
        m["diff_lambda"] = np.ascontiguousarray(inp["diff_lambda"].reshape(DEPTH, 128))
        m["diff_subln_gT"] = np.ascontiguousarray(inp["diff_subln_g"].reshape(DEPTH, 64, 1))
        for k in ("norm_g", "w_mod", "b_mod", "w_in", "w_out", "mla_q_norm_g", "mla_w_uq", "mla_kv_norm_g", "mla_w_ukv",
                  "mla_nope_g", "mla_rope_g", "diff_qk_g"):
            m[k] = inp[k]
        maps.append(m)
    return maps


_CACHE = {}


def kernel(**inputs):
    inp = {k: np.asarray(v) for k, v in inputs.items()}
    if "b" not in _CACHE:
        st = os.environ.get("KSTAGES")
        _CACHE["b"] = Builder(stages=tuple(st.split(","))) if st else Builder()
    b = _CACHE["b"]
    maps = make_in_maps(inp)
    maps = [{k: v for k, v in m.items() if k in b.inputs} for m in maps]
    res = run_bass_kernel_spmd(b.nc, maps, core_ids=list(range(8)))
    r = res.results
    y_sample = np.stack([np.asarray(r[j]["y"], dtype=np.float32) for j in range(4)], axis=0)
    p = r[4]
    y_prompt = np.asarray(p["y"], dtype=np.float32).reshape(16, 256, D)
    new_ckv = np.ascontiguousarray(np.asarray(p["o_ckv"], dtype=np.float32).reshape(DEPTH, 16, 256, 128).transpose(1, 0, 2, 3))
    new_krope = np.ascontiguousarray(np.asarray(p["o_krope"], dtype=np.float32).reshape(DEPTH, 16, 256, 32).transpose(1, 0, 2, 3))
    new_dk = np.ascontiguousarray(np.asarray(p["o_dk"], dtype=np.float32).reshape(DEPTH, 16, 256, 4, 2, 32).transpose(1, 0, 2, 3, 4, 5))
    new_dv = np.ascontiguousarray(np.asarray(p["o_dv"], dtype=np.float32).reshape(DEPTH, 16, 256, 4, 64).transpose(1, 0, 2, 3, 4))
    new_st = np.ascontiguousarray(np.asarray(p["o_state"], dtype=np.float32).transpose(1, 0, 2, 3, 4, 5))
    return (y_prompt, y_sample, new_ckv, new_krope, new_dk, new_dv, new_st)
```
